# Optimizing a Trainium2 kernel written in Bass

```python
import jax
import jax.numpy as jnp
from jax import lax
import numpy as np

D_MODEL = 1024
BATCH = 2
SEQ = 8192
DEPTH = 4

GRID_W = 64
CTX_LEN = 256
RET_HEADS = 4
RET_DK = 64
RET_DV = 128
RET_CHUNK = 128
RET_QK_W = RET_HEADS * RET_DK
RET_V_W = RET_HEADS * RET_DV
NA_HEADS = 8
NA_DH = 64
NA_KH = 8
NA_KW = 16
NA_W = NA_HEADS * NA_DH
MLA_HEADS = 8
MLA_Q_RANK = 256
MLA_KV_RANK = 256
MLA_NOPE = 64
MLA_ROPE = 32
MLA_DV = 64
MLA_V_W = MLA_HEADS * MLA_DV
D_FF = 4 * D_MODEL
ROPE_BASE = 10000.0
Q_BLOCK = 128
EPS = 1e-5
DEEPNORM_ALPHA = (2 * DEPTH) ** 0.25
DEEPNORM_BETA = (8 * DEPTH) ** -0.25
IN_SPLITS = (RET_QK_W, RET_QK_W, RET_V_W, RET_V_W, RET_V_W, NA_W, NA_W, NA_W,
             MLA_Q_RANK, MLA_KV_RANK, MLA_ROPE, D_MODEL, D_MODEL, D_MODEL)
IN_WIDTH = sum(IN_SPLITS)

kernel_name = 'hybrid_retention_natten_mla_dit_block'


def layer_norm(x, gain, bias):
    xf = x.astype(jnp.float32)
    mu = jnp.mean(xf, axis=-1, keepdims=True)
    var = jnp.mean(jnp.square(xf - mu), axis=-1, keepdims=True)
    return ((xf - mu) * lax.rsqrt(var + EPS) * gain + bias).astype(x.dtype)


def rms_norm(x, gain):
    xf = x.astype(jnp.float32)
    return (xf * lax.rsqrt(jnp.mean(jnp.square(xf), axis=-1, keepdims=True) + EPS) * gain).astype(x.dtype)


def to_heads(a, n_heads):
    b, t, _ = a.shape
    return a.reshape(b, t, n_heads, -1).transpose(0, 2, 1, 3)


def merge_heads(a):
    b, h, t, d = a.shape
    return a.transpose(0, 2, 1, 3).reshape(b, t, h * d)


def axial_rope(n_tok, rot_dim):
    t = jnp.arange(n_tok)
    row = (t // GRID_W).astype(jnp.float32)
    col = (t % GRID_W).astype(jnp.float32)
    n_freq = rot_dim // 4
    inv_freq = ROPE_BASE ** (-2.0 * jnp.arange(n_freq, dtype=jnp.float32) / (rot_dim // 2))
    ang = jnp.concatenate([row[:, None] * inv_freq, col[:, None] * inv_freq], axis=-1)
    return jnp.cos(ang), jnp.sin(ang)


def apply_rope(x, rope):
    cos, sin = rope
    shape = (1, cos.shape[0]) + (1,) * (x.ndim - 3) + (cos.shape[1],)
    cos = cos.reshape(shape).astype(x.dtype)
    sin = sin.reshape(shape).astype(x.dtype)
    x1, x2 = jnp.split(x, 2, axis=-1)
    return jnp.concatenate([x1 * cos - x2 * sin, x1 * sin + x2 * cos], axis=-1)


def attend_blocks(q, k, v):
    b, h, t, d = q.shape
    nb = t // Q_BLOCK
    qb = jnp.moveaxis(q.reshape(b, h, nb, Q_BLOCK, d), 2, 0)

    def one_block(qi):
        s = jnp.einsum('bhqd,bhkd->bhqk', qi, k).astype(jnp.float32)
        p = jax.nn.softmax(s, axis=-1).astype(v.dtype)
        return jnp.einsum('bhqk,bhkv->bhqv', p, v)

    o = lax.map(one_block, qb)
    return jnp.moveaxis(o, 0, 2).reshape(b, h, t, v.shape[-1])


def retention_chunkwise(q, k, v, log_gamma, s0):
    b, h, t, dk = q.shape
    dv = v.shape[-1]
    n = t // RET_CHUNK
    qc = q.reshape(b, h, n, RET_CHUNK, dk)
    kc = k.reshape(b, h, n, RET_CHUNK, dk)
    vc = v.reshape(b, h, n, RET_CHUNK, dv)
    pos = jnp.arange(RET_CHUNK, dtype=jnp.float32)
    lg = log_gamma[:, None]
    diff = pos[:, None] - pos[None, :]
    decay_mat = jnp.where(diff >= 0, jnp.exp(lg[:, :, None] * jnp.maximum(diff, 0.0)), 0.0)
    q_decay = jnp.exp(lg * (pos + 1.0))
    k_decay = jnp.exp(lg * (RET_CHUNK - 1.0 - pos))
    chunk_decay = jnp.exp(log_gamma * RET_CHUNK)
    scores = jnp.einsum('bhncd,bhnsd->bhncs', qc, kc) * decay_mat[None, :, None]
    inner = jnp.einsum('bhncs,bhnsv->bhncv', scores, vc)
    kv_chunk = jnp.einsum('bhnsd,bhnsv->nbhdv', kc * k_decay[None, :, None, :, None], vc)

    def step(s, kv):
        return s * chunk_decay[None, :, None, None] + kv, s

    s_final, s_prev = lax.scan(step, s0, kv_chunk)
    cross = jnp.einsum('bhncd,nbhdv->bhncv', qc * q_decay[None, :, None, :, None], s_prev)
    return (inner + cross).reshape(b, h, t, dv), s_final


def retention_final_state(k, v, log_gamma):
    t = k.shape[2]
    w = jnp.exp(log_gamma[:, None] * (t - 1.0 - jnp.arange(t, dtype=jnp.float32)))
    return jnp.einsum('bhtd,bhtv->bhdv', k * w[None, :, :, None], v)


def head_group_norm(o, gain):
    of = o.astype(jnp.float32)
    mu = jnp.mean(of, axis=-1, keepdims=True)
    var = jnp.mean(jnp.square(of - mu), axis=-1, keepdims=True)
    return merge_heads((of - mu) * lax.rsqrt(var + EPS)) * gain


def retention_qkv(pq, pk, pv, rope):
    b, t, _ = pq.shape
    q = pq.reshape(b, t, RET_HEADS, RET_DK)
    k = pk.reshape(b, t, RET_HEADS, RET_DK) * (RET_DK ** -0.5)
    if rope is not None:
        q = apply_rope(q, rope)
        k = apply_rope(k, rope)
    return q.transpose(0, 2, 1, 3), k.transpose(0, 2, 1, 3), to_heads(pv, RET_HEADS)


def retention_branch(parts_x, parts_z, p, rope, need_ctx):
    log_gamma = jnp.log1p(-jnp.exp(p['ret_log_decay'].astype(jnp.float32)))
    qx, kx, vx = retention_qkv(parts_x[0], parts_x[1], parts_x[2], rope)
    qz, kz, vz = retention_qkv(parts_z[0], parts_z[1], parts_z[2], None)
    s_zero = jnp.zeros((qx.shape[0], RET_HEADS, RET_DK, RET_DV), jnp.float32)
    out_x, out_z = [], []
    for d in range(2):
        f = (lambda a: a) if d == 0 else (lambda a: jnp.flip(a, axis=2))
        if need_ctx:
            o_z, s_ctx = retention_chunkwise(f(qz), f(kz), f(vz), log_gamma[d], s_zero)
            out_z.append(f(o_z))
        else:
            s_ctx = retention_final_state(f(kz), f(vz), log_gamma[d])
        o_x, _ = retention_chunkwise(f(qx), f(kx), f(vx), log_gamma[d], s_ctx)
        out_x.append(f(o_x))

    def gated(outs, g_fwd, g_bwd):
        return (jax.nn.silu(g_fwd) * head_group_norm(outs[0], p['ret_gn_gain'])
                + jax.nn.silu(g_bwd) * head_group_norm(outs[1], p['ret_gn_gain']))

    y_x = gated(out_x, parts_x[3], parts_x[4])
    y_z = gated(out_z, parts_z[3], parts_z[4]) if need_ctx else None
    return y_x, y_z


def neighborhood_attention(q, k, v, k_ctx, v_ctx, rpb):
    b, h, t, d = q.shape
    rows = t // GRID_W
    kh = min(NA_KH, rows)
    kg = k.reshape(b, h, rows, GRID_W, d)
    vg = v.reshape(b, h, rows, GRID_W, d)
    qg = jnp.moveaxis(q.reshape(b, h, rows, GRID_W, d), 2, 0)
    cols = np.arange(GRID_W)
    col_start = np.clip(cols - NA_KW // 2, 0, GRID_W - NA_KW)
    col_idx = col_start[:, None] + np.arange(NA_KW)[None, :]
    dc_idx = col_idx - cols[:, None] + (NA_KW - 1)
    n_loc = kh * NA_KW

    def one_row(args):
        r, q_row = args
        r0 = jnp.clip(r - kh // 2, 0, rows - kh)
        k_rows = lax.dynamic_slice_in_dim(kg, r0, kh, axis=2)
        v_rows = lax.dynamic_slice_in_dim(vg, r0, kh, axis=2)
        k_win = k_rows[:, :, :, col_idx]
        v_win = v_rows[:, :, :, col_idx]
        dr_idx = r0 + jnp.arange(kh) - r + (NA_KH - 1)
        bias = rpb[:, dr_idx[:, None, None], dc_idx[None, :, :]]
        s_loc = jnp.einsum('bhwd,bhawkd->bhwak', q_row, k_win) + jnp.transpose(bias, (0, 2, 1, 3))[None]
        s_loc = s_loc.reshape(b, h, GRID_W, n_loc)
        s_ctx = jnp.einsum('bhwd,bhld->bhwl', q_row, k_ctx)
        s_all = jnp.concatenate([s_loc.astype(jnp.float32), s_ctx.astype(jnp.float32)], axis=-1)
        prob = jax.nn.softmax(s_all, axis=-1).astype(v.dtype)
        p_loc = prob[..., :n_loc].reshape(b, h, GRID_W, kh, NA_KW)
        p_ctx = prob[..., n_loc:]
        return (jnp.einsum('bhwak,bhawkv->bhwv', p_loc, v_win)
                + jnp.einsum('bhwl,bhlv->bhwv', p_ctx, v_ctx))

    out = lax.map(one_row, (jnp.arange(rows), qg))
    return jnp.moveaxis(out, 0, 2).reshape(b, h, t, d)


def na_branch(parts_x, parts_z, p, need_ctx):
    scale = NA_DH ** -0.5
    qx, kx, vx = [to_heads(a, NA_HEADS) for a in parts_x]
    qz, kz, vz = [to_heads(a, NA_HEADS) for a in parts_z]
    y_x = merge_heads(neighborhood_attention(qx * scale, kx, vx, kz, vz, p['na_rpb']))
    y_z = merge_heads(attend_blocks(qz * scale, kz, vz)) if need_ctx else None
    return y_x, y_z


def mla_project(pq, pkv, pkr, p, rope):
    b, t, _ = pq.shape
    q = (rms_norm(pq, p['mla_q_norm']) @ p['mla_w_qup']).reshape(b, t, MLA_HEADS, MLA_NOPE + MLA_ROPE)
    kv = (rms_norm(pkv, p['mla_kv_norm']) @ p['mla_w_kvup']).reshape(b, t, MLA_HEADS, MLA_NOPE + MLA_DV)
    q_nope, q_rope = q[..., :MLA_NOPE], q[..., MLA_NOPE:]
    k_nope, v = kv[..., :MLA_NOPE], kv[..., MLA_NOPE:]
    k_rope = pkr
    if rope is not None:
        q_rope = apply_rope(q_rope, rope)
        k_rope = apply_rope(k_rope, rope)
    q = jnp.concatenate([q_nope, q_rope], axis=-1) * ((MLA_NOPE + MLA_ROPE) ** -0.5)
    k = jnp.concatenate([k_nope, jnp.broadcast_to(k_rope[:, :, None, :], (b, t, MLA_HEADS, MLA_ROPE))], axis=-1)
    return q.transpose(0, 2, 1, 3), k.transpose(0, 2, 1, 3), v.transpose(0, 2, 1, 3)


def mla_branch(parts_x, parts_z, p, rope, need_ctx):
    qx, kx, vx = mla_project(parts_x[0], parts_x[1], parts_x[2], p, rope)
    qz, kz, vz = mla_project(parts_z[0], parts_z[1], parts_z[2], p, None)
    k_all = jnp.concatenate([kx, kz], axis=2)
    v_all = jnp.concatenate([vx, vz], axis=2)
    y_x = merge_heads(attend_blocks(qx, k_all, v_all))
    y_z = merge_heads(attend_blocks(qz, kz, vz)) if need_ctx else None
    return y_x, y_z


def token_mixers(hx, hz, p, rope_ret, rope_mla, need_ctx):
    offsets = np.cumsum(IN_SPLITS)[:-1].tolist()
    px = jnp.split(hx @ p['w_in'], offsets, axis=-1)
    pz = jnp.split(hz @ p['w_in'], offsets, axis=-1)
    ya_x, ya_z = retention_branch(px[0:5], pz[0:5], p, rope_ret, need_ctx)
    yb_x, yb_z = na_branch(px[5:8], pz[5:8], p, need_ctx)
    yc_x, yc_z = mla_branch(px[8:11], pz[8:11], p, rope_mla, need_ctx)

    def merge(parts, ya, yb, yc):
        y = (jax.nn.sigmoid(parts[11]) * (ya @ p['w_branch_ret'])
             + jax.nn.sigmoid(parts[12]) * (yb @ p['w_branch_na'])
             + jax.nn.sigmoid(parts[13]) * (yc @ p['w_branch_mla']))
        return y @ p['w_out']

    mix_x = merge(px, ya_x, yb_x, yc_x)
    mix_z = merge(pz, ya_z, yb_z, yc_z) if need_ctx else None
    return mix_x, mix_z


def sq_relu_mlp(h, p):
    return jnp.square(jax.nn.relu(h @ p['w_ff1'])) @ p['w_ff2']


def hybrid_layer(x, z, c, c_ctx, p, rope_ret, rope_mla, need_ctx):
    mod_x = jax.nn.silu(c) @ p['w_ada'] + p['b_ada']
    mod_z = jax.nn.silu(c_ctx) @ p['w_ada'] + p['b_ada']
    sh1x, sc1x, g1x, sh2x, sc2x, g2x = [m[:, None, :] for m in jnp.split(mod_x, 6, axis=-1)]
    sh1z, sc1z, g1z, sh2z, sc2z, g2z = jnp.split(mod_z, 6, axis=-1)
    hx = x * (1.0 + sc1x) + sh1x
    hz = z * (1.0 + sc1z) + sh1z
    mix_x, mix_z = token_mixers(hx, hz, p, rope_ret, rope_mla, need_ctx)
    x = layer_norm(DEEPNORM_ALPHA * x + g1x * mix_x, p['ln_gain'][0], p['ln_bias'][0])
    x = layer_norm(DEEPNORM_ALPHA * x + g2x * sq_relu_mlp(x * (1.0 + sc2x) + sh2x, p),
                   p['ln_gain'][1], p['ln_bias'][1])
    if need_ctx:
        z = layer_norm(DEEPNORM_ALPHA * z + g1z * mix_z, p['ln_gain'][0], p['ln_bias'][0])
        z = layer_norm(DEEPNORM_ALPHA * z + g2z * sq_relu_mlp(z * (1.0 + sc2z) + sh2z, p),
                       p['ln_gain'][1], p['ln_bias'][1])
    else:
        z = None
    return x, z


def setup_inputs(seed: int = 0) -> dict:
    key = jax.random.key(seed)
    ks = jax.random.split(key, 24)
    f32 = jnp.float32
    L = DEPTH
    D = D_MODEL

    def nrm(k, shape, s):
        return s * jax.random.normal(k, shape, f32)

    base_decay = -(5.0 + jnp.arange(RET_HEADS, dtype=f32)) * float(np.log(2.0))
    return {
        'x': nrm(ks[0], (BATCH, SEQ, D), 1.0),
        'c': nrm(ks[1], (BATCH, D), 1.0),
        'ctx': nrm(ks[2], (BATCH, CTX_LEN, D), 1.0),
        'c_ctx': nrm(ks[3], (D,), 1.0),
        'w_ada': nrm(ks[4], (L, D, 6 * D), 0.5 * D ** -0.5),
        'b_ada': nrm(ks[5], (L, 6 * D), 0.01),
        'w_in': nrm(ks[6], (L, D, IN_WIDTH), D ** -0.5),
        'ret_log_decay': base_decay + nrm(ks[7], (L, 2, RET_HEADS), 0.1),
        'ret_gn_gain': 1.0 + nrm(ks[8], (L, RET_V_W), 0.02),
        'na_rpb': nrm(ks[9], (L, NA_HEADS, 2 * NA_KH - 1, 2 * NA_KW - 1), 0.1),
        'mla_q_norm': 1.0 + nrm(ks[10], (L, MLA_Q_RANK), 0.02),
        'mla_w_qup': nrm(ks[11], (L, MLA_Q_RANK, MLA_HEADS * (MLA_NOPE + MLA_ROPE)), MLA_Q_RANK ** -0.5),
        'mla_kv_norm': 1.0 + nrm(ks[12], (L, MLA_KV_RANK), 0.02),
        'mla_w_kvup': nrm(ks[13], (L, MLA_KV_RANK, MLA_HEADS * (MLA_NOPE + MLA_DV)), MLA_KV_RANK ** -0.5),
        'w_branch_ret': nrm(ks[14], (L, RET_V_W, D), DEEPNORM_BETA * RET_V_W ** -0.5),
        'w_branch_na': nrm(ks[15], (L, NA_W, D), DEEPNORM_BETA * NA_W ** -0.5),
        'w_branch_mla': nrm(ks[16], (L, MLA_V_W, D), DEEPNORM_BETA * MLA_V_W ** -0.5),
        'w_out': nrm(ks[17], (L, D, D), DEEPNORM_BETA * D ** -0.5),
        'w_ff1': nrm(ks[18], (L, D, D_FF), D ** -0.5),
        'w_ff2': nrm(ks[19], (L, D_FF, D), DEEPNORM_BETA * D_FF ** -0.5),
        'ln_gain': 1.0 + nrm(ks[20], (L, 2, D), 0.02),
        'ln_bias': nrm(ks[21], (L, 2, D), 0.01),
    }


def reference(x, c, ctx, c_ctx, w_ada, b_ada, w_in, ret_log_decay, ret_gn_gain, na_rpb,
              mla_q_norm, mla_w_qup, mla_kv_norm, mla_w_kvup, w_branch_ret, w_branch_na,
              w_branch_mla, w_out, w_ff1, w_ff2, ln_gain, ln_bias):
    n_tok = x.shape[1]
    rope_ret = axial_rope(n_tok, RET_DK)
    rope_mla = axial_rope(n_tok, MLA_ROPE)
    z = ctx
    for l in range(DEPTH):
        p = {
            'w_ada': w_ada[l], 'b_ada': b_ada[l], 'w_in': w_in[l],
            'ret_log_decay': ret_log_decay[l], 'ret_gn_gain': ret_gn_gain[l], 'na_rpb': na_rpb[l],
            'mla_q_norm': mla_q_norm[l], 'mla_w_qup': mla_w_qup[l],
            'mla_kv_norm': mla_kv_norm[l], 'mla_w_kvup': mla_w_kvup[l],
            'w_branch_ret': w_branch_ret[l], 'w_branch_na': w_branch_na[l], 'w_branch_mla': w_branch_mla[l],
            'w_out': w_out[l], 'w_ff1': w_ff1[l], 'w_ff2': w_ff2[l],
            'ln_gain': ln_gain[l], 'ln_bias': ln_bias[l],
        }
        x, z = hybrid_layer(x, z, c, c_ctx, p, rope_ret, rope_mla, l < DEPTH - 1)
    return x
```

```python
import numpy as np
from contextlib import ExitStack
import ml_dtypes
import concourse.bass as bass
import concourse.mybir as mybir
from concourse.bass_utils import run_bass_kernel_spmd

F32 = mybir.dt.float32
BF16 = mybir.dt.bfloat16
AF = mybir.ActivationFunctionType
ALU = mybir.AluOpType
AX = mybir.AxisListType

D = 1024
NT = 2048
NZ = 256
T = NT + NZ
NTILE = T // 128
EPS = 1e-5
ALPHA = 8.0 ** 0.25
EPOCH = 20000
C_RQ, C_RK, C_RV, C_GF, C_GB, C_NQ, C_NK, C_NV, C_MQ, C_MKV, C_MKR, C_GA, C_GBR, C_GC = (
    0, 256, 512, 1024, 1536, 2048, 2560, 3072, 3584, 3840, 4096, 4128, 5152, 6176)
TOKBLKS = [(0, 512), (512, 512), (1024, 512), (1536, 512), (2048, 256)]


class Sched:
    ENGS = ("pe", "dve", "act", "pool", "sp")

    def __init__(self, nc, stack, n_dma_sems=12, n_eng_sems=8):
        self.nc = nc
        self.ops = {e: [] for e in self.ENGS}
        self.esems = {e: [stack.enter_context(nc.semaphore(f"s_{e}_{i}")) for i in range(n_eng_sems)]
                      for e in ("pe", "dve", "act", "pool")}
        self.ecnt = {e: 0 for e in ("pe", "dve", "act", "pool")}
        self.eep = {e: 0 for e in ("pe", "dve", "act", "pool")}
        self.dsems = {q: [stack.enter_context(nc.semaphore(f"d_{q}_{i}")) for i in range(n_dma_sems)]
                      for q in ("sp", "act", "pool")}
        self.dval = {q: [0] * n_dma_sems for q in ("sp", "act", "pool")}
        self.drr = {q: 0 for q in ("sp", "act", "pool")}
        self.lastw = {}
        self.reads = {}
        self.seen = {e: {} for e in self.ENGS}
        self.final_tokens = []
        self.n_ops = 0
        self.csem = stack.enter_context(nc.semaphore("s_coll"))
        self.cval = 0

    def _need(self, eng, tok, waits):
        if tok is None:
            return
        sem, val = tok
        if eng == "pe" and any(sem is x for x in self.esems["pe"]):
            return
        sid = id(sem)
        cur = self.seen[eng].get(sid)
        if cur is not None and cur >= val:
            return
        self.seen[eng][sid] = val
        waits.append((sem, val))

    def _deps(self, eng, reads, writes):
        waits = []
        for k in reads:
            self._need(eng, self.lastw.get(k), waits)
        for k in writes:
            self._need(eng, self.lastw.get(k), waits)
            for t in self.reads.get(k, ()):
                self._need(eng, t, waits)
        return waits

    def _commit(self, tok, reads, writes):
        for k in reads:
            self.reads.setdefault(k, []).append(tok)
        for k in writes:
            self.lastw[k] = tok
            self.reads[k] = []

    def op(self, eng, fn, reads=(), writes=()):
        waits = self._deps(eng, reads, writes)
        if self.ecnt[eng] >= EPOCH:
            self.eep[eng] += 1
            self.ecnt[eng] = 0
        sem = self.esems[eng][self.eep[eng]]
        self.ecnt[eng] += 1
        tok = (sem, self.ecnt[eng])
        self.ops[eng].append((waits, fn, (sem, 1)))
        self._commit(tok, reads, writes)
        self.n_ops += 1
        return tok

    def dma(self, q, fn, reads=(), writes=(), final=False):
        waits = self._deps(q, reads, writes)
        i = self.drr[q]
        self.drr[q] = (i + 1) % len(self.dsems[q])
        sem = self.dsems[q][i]
        prev = self.dval[q][i]
        if prev > 0:
            self._need(q, (sem, prev), waits)
        self.dval[q][i] = prev + 16
        tok = (sem, prev + 16)
        self.ops[q].append((waits, fn, (sem, 16)))
        self._commit(tok, reads, writes)
        if final:
            self.final_tokens.append(tok)
        self.n_ops += 1
        return tok

    def coll(self, fn, reads=(), writes=()):
        waits = self._deps("pool", reads, writes)
        if not hasattr(self, "csem"):
            raise RuntimeError("no collective semaphore")
        self.cval += 1
        tok = (self.csem, self.cval)
        self.ops["pool"].append((waits, fn, (self.csem, 1)))
        self._commit(tok, reads, writes)
        self.ctoks = tok
        return tok

    def barrier(self):
        toks = []
        if getattr(self, "cval", 0) > 0 and getattr(self, "bar_coll", True):
            toks.append((self.csem, self.cval))
        for e in ("pe", "dve", "act", "pool"):
            if self.ecnt[e] > 0:
                toks.append((self.esems[e][self.eep[e]], self.ecnt[e]))
        for q in ("sp", "act", "pool"):
            for i, v in enumerate(self.dval[q]):
                if v > 0:
                    toks.append((self.dsems[q][i], v))
        for e in self.ENGS:
            waits = []
            for t in toks:
                self._need(e, t, waits)
            if waits:
                self.ops[e].append((waits, None, None))

    def emit(self, final=False):
        nc = self.nc
        fin = []
        if final:
            mx = {}
            for (sm, v) in self.final_tokens:
                if id(sm) not in mx or mx[id(sm)][1] < v:
                    mx[id(sm)] = (sm, v)
            fin = list(mx.values())
        handles = {"pe": "tensor", "dve": "vector", "act": "scalar", "pool": "gpsimd", "sp": "sync"}
        with nc.Block() as block:
            for e in self.ENGS:
                ops = self.ops[e]
                extra = fin if e == "sp" else []
                if not ops and not extra:
                    continue

                def body(h, ops=ops, extra=extra):
                    for waits, fn, si in ops:
                        for (ws, wv) in waits:
                            h.wait_ge(ws, wv)
                        if fn is not None:
                            fn(h).then_inc(si[0], si[1])
                    for (ws, wv) in extra:
                        h.wait_ge(ws, wv)

                getattr(block, handles[e])(body)
        self.ops = {e: [] for e in self.ENGS}


class Ring:
    def __init__(self, tiles, name, keys=None):
        self.tiles = tiles
        self.name = name
        self.keys = keys
        self.i = 0

    def next(self):
        j = self.i % len(self.tiles)
        t = self.tiles[j]
        k = (self.name, j) if self.keys is None else self.keys[j]
        self.i += 1
        return t, k


class KB:
    def __init__(self, nc, S):
        self.nc = nc
        self.S = S
        self.dq = 0
        self.uid = 0

    def sb(self, st, name, shape, dt):
        self.uid += 1
        return st.enter_context(self.nc.sbuf_tensor(f"sb{self.uid}_{name}", shape, dt))

    def ps(self, st, name, shape, dt=F32):
        self.uid += 1
        return st.enter_context(self.nc.psum_tensor(f"ps{self.uid}_{name}", shape, dt))

    def ring_sb(self, st, name, shape, dt, n):
        return Ring([self.sb(st, f"{name}{i}", shape, dt) for i in range(n)], name)

    def ring_ps_sliced(self, st, name, width, n, per_bank=4):
        tiles, keys = [], []
        nb = (n + per_bank - 1) // per_bank
        for b in range(nb):
            t = self.ps(st, f"{name}{b}", [128, width * per_bank], F32)
            for j in range(per_bank):
                if len(tiles) < n:
                    tiles.append(t[:, j * width:(j + 1) * width])
                    keys.append((name, "bank", b))
        return Ring(tiles, name, keys)

    def ring_ps(self, st, name, shape, n, dt=F32):
        return Ring([self.ps(st, f"{name}{i}", shape, dt) for i in range(n)], name)

    def load(self, out, in_, writes, reads=(), q=None):
        if q is None:
            q = ("sp", "act")[self.dq % 2]
            self.dq += 1
        return self.S.dma(q, lambda h: h.dma_start(out=out, in_=in_), reads=reads, writes=writes)

    def load_cast(self, out, in_, writes, reads=()):
        return self.S.dma("pool", lambda h: h.dma_start(out=out, in_=in_), reads=reads, writes=writes)

    def store(self, out, in_, reads, writes=(), final=False, q="sp"):
        return self.S.dma(q, lambda h: h.dma_start(out=out, in_=in_), reads=reads, writes=writes, final=final)

    def mm_group(self, out, pairs, reads, writes):
        n = len(pairs)

        def fn(h):
            r = None
            for i, (l, rr) in enumerate(pairs):
                r = h.matmul(out, lhsT=l, rhs=rr, start=(i == 0), stop=(i == n - 1))
            return r
        return self.S.op("pe", fn, reads=reads, writes=writes)


def wview(w, c0, c1):
    return w.rearrange("(k p) n -> p k n", p=128)[:, :, c0:c1]


def emit_ret_tables(kb, st, rld, cst):
    S = kb.S
    lg = kb.sb(st, "lg", [128, 8], F32)
    cs = kb.sb(st, "cst", [128, 128 * 6 + 8], F32)
    DT = kb.sb(st, "DT", [128, 8, 128], F32)
    QD = kb.sb(st, "QD", [128, 8, 128], F32)
    kdec = kb.sb(st, "kdec", [128, 8], F32)
    cd = kb.sb(st, "cd", [128, 8], F32)
    tmp = kb.sb(st, "rt_tmp", [128, 128], F32)
    kb.load(lg[:], rld.partition_broadcast(128), writes=["lg"])
    kb.load(cs[:], cst, writes=["cst"])
    S.op("act", lambda h: h.activation(out=lg[:], in_=lg[:], func=AF.Exp), reads=["lg"], writes=["lg"])
    S.op("act", lambda h: h.activation(out=lg[:], in_=lg[:], func=AF.Ln, scale=-1.0, bias=cs[:, 768 + 4:768 + 5]),
         reads=["lg", "cst"], writes=["lg"])
    for d in range(2):
        for hh in range(4):
            i = d * 4 + hh
            S.op("act", lambda h, i=i, d=d: h.activation(out=tmp[:], in_=cs[:, d * 128:(d + 1) * 128], func=AF.Exp,
                                                          scale=lg[:, i:i + 1]),
                 reads=["lg", "cst"], writes=["rt_tmp"])
            S.op("dve", lambda h, i=i, d=d: h.tensor_tensor(out=DT[:, i, :], in0=tmp[:], in1=cs[:, (2 + d) * 128:(3 + d) * 128],
                                                             op=ALU.mult),
                 reads=["rt_tmp", "cst"], writes=["DT"])
            S.op("act", lambda h, i=i, d=d: h.activation(out=QD[:, i, :], in_=cs[:, (4 + d) * 128:(5 + d) * 128], func=AF.Exp,
                                                          scale=lg[:, i:i + 1]),
                 reads=["lg", "cst"], writes=["QD"])
            S.op("act", lambda h, i=i, d=d: h.activation(out=kdec[:, i:i + 1], in_=cs[:, 768 + d:768 + d + 1], func=AF.Exp,
                                                          scale=lg[:, i:i + 1]),
                 reads=["lg", "cst"], writes=["kdec"])
    S.op("act", lambda h: h.activation(out=cd[:], in_=lg[:], func=AF.Exp, scale=128.0), reads=["lg"], writes=["cd"])
    return dict(lg=lg, DT=DT, QD=QD, kdec=kdec, cd=cd, cs=cs)


def ret_consts():
    s = np.arange(128)[:, None].astype(np.float32)
    c = np.arange(128)[None, :].astype(np.float32)
    E0 = np.maximum(c - s, 0.0)
    E1 = np.maximum(s - c, 0.0)
    M0 = (c >= s).astype(np.float32)
    M1 = (s >= c).astype(np.float32)
    R0 = np.broadcast_to(c + 1.0, (128, 128))
    R1 = np.broadcast_to(128.0 - c, (128, 128))
    tail = np.zeros((128, 8), np.float32)
    tail[:, 0] = 127.0 - s[:, 0]
    tail[:, 1] = s[:, 0]
    tail[:, 4] = 1.0
    tail[:, 5] = EPS
    return np.concatenate([E0, E1, M0, M1, R0, R1, tail], axis=1).astype(np.float32)


def build_A():
    nc = bass.Bass("TRN2", target_bir_lowering=False)
    din = lambda name, shape, dt=F32: nc.dram_tensor(name, shape, dt, kind="ExternalInput").ap()
    dout = lambda name, shape, dt=F32: nc.dram_tensor(name, shape, dt, kind="ExternalOutput").ap()
    xT = din("xT", [D, T])
    c2 = din("c2", [128, 16])
    w_ada = din("w_ada", [D, 6 * D])
    b_ada = din("b_ada", [128, 48])
    w_in = din("w_in", [D, 7200])
    rld = din("rld", [1, 8])
    rcst = din("rcst", [128, 776])
    kvg = din("kvg", [1, 256])
    rope_r = din("rope_r", [T, 64])
    rope_m = din("rope_m", [T, 32])
    o_mods = dout("mods", [128, 96])
    o_hxT = dout("hxT", [128, 8, T], BF16)
    o_retK = dout("retK", [T, 256], BF16)
    o_retKT = dout("retKT", [64, 4, T], BF16)
    o_retV = dout("retV", [T, 512], BF16)
    o_retF = dout("retF", [64, 8, 128])
    o_naKT = dout("naKT", [128, 4, T], BF16)
    o_naV = dout("naV", [T, 8, 128], BF16)
    o_ckvnT = dout("ckvnT", [128, 2, T], BF16)
    o_krT = dout("krT", [32, T], BF16)

    with ExitStack() as st0:
        S = Sched(nc, st0)
        kb = KB(nc, S)
        with ExitStack() as st:
            hxT = kb.sb(st, "hxT", [128, 8, T], BF16)
            mods = kb.sb(st, "mods", [128, 48, 2], F32)
            ident = kb.sb(st, "ident", [128, 128], BF16)
            identf = kb.sb(st, "identf", [128, 128], F32)
            S.op("pool", lambda h: h.memset(identf[:], 0.0), writes=["identf"])
            S.op("pool", lambda h: h.affine_select(out=identf[:], in_=identf[:], pattern=[[-1, 128]],
                                                     compare_op=ALU.not_equal, fill=1.0, base=0, channel_multiplier=1),
                 reads=["identf"], writes=["identf"])
            S.op("dve", lambda h: h.tensor_copy(out=ident[:], in_=identf[:]), reads=["identf"], writes=["ident"])
            with ExitStack() as p1:
                emit_mod_hx(kb, p1, xT, c2, w_ada, b_ada, mods, hxT)
                kb.store(o_mods, mods[:].rearrange("p a b -> p (a b)"), reads=["mods"], final=True)
                kb.store(o_hxT, hxT[:], reads=[("hxT", j) for j in range(8)], final=True)
                S.barrier()
                S.emit()
            with ExitStack() as p2:
                emit_kv_side(kb, p2, hxT, ident, w_in, rld, rcst, kvg, rope_r, rope_m,
                             o_retK, o_retKT, o_retV, o_retF, o_naKT, o_naV, o_ckvnT, o_krT)
                S.barrier()
                S.emit(final=True)
    return nc


def emit_mod_hx(kb, st, xT, c2, w_ada, b_ada, mods, hxT, mods_src=None):
    S = kb.S
    if mods_src is not None:
        sc1p = kb.sb(st, "sc1p", [128, 8, 2], F32)
        xs = kb.ring_sb(st, "xs", [128, T], F32, 2)
        kb.load(mods[:].rearrange("p a b -> p (a b)"), mods_src, writes=["mods"])
        emit_hx_only(kb, S, xT, mods, sc1p, xs, hxT)
        return
    c2s = kb.sb(st, "c2s", [128, 16], F32)
    bad = kb.sb(st, "bad", [128, 48], F32)
    sc1p = kb.sb(st, "sc1p", [128, 8, 2], F32)
    wa = kb.ring_sb(st, "wa", [128, 8, 768], F32, 2)
    mps = kb.ps(st, "mod_ps", [128, 48, 2])
    xs = kb.ring_sb(st, "xs", [128, T], F32, 2)
    kb.load(c2s[:], c2, writes=["c2s"])
    kb.load(bad[:], b_ada, writes=["bad"])
    S.op("act", lambda h: h.activation(out=c2s[:], in_=c2s[:], func=AF.Silu), reads=["c2s"], writes=["c2s"])
    c2v = c2s[:].rearrange("p (k w) -> p k w", w=2)
    if MOD_ROWMAJOR:
        mrow = kb.sb(st, "mrow", [2, 6 * D], F32)
        identf2 = kb.sb(st, "identf2", [128, 128], F32)
        E(S, "pool", "memset", [], ["identf2"], identf2[:], 0.0)
        E(S, "pool", "affine_select", ["identf2"], ["identf2"], out=identf2[:], in_=identf2[:], pattern=[[-1, 128]],
          compare_op=ALU.not_equal, fill=1.0, base=0, channel_multiplier=1)
        mrp = kb.ring_ps(st, "mrow_ps", [2, 768], 2)
        for blk in range(8):
            wt, wk = wa.next()
            kb.load(wt[:], wview(w_ada, blk * 768, (blk + 1) * 768), writes=[wk])
            mp, mpk = mrp.next()
            kb.mm_group(mp[:, 0:512], [(c2v[:, k, :], wt[:, k, 0:512]) for k in range(8)], reads=[wk, "c2s"], writes=[(mpk, 0)])
            kb.mm_group(mp[:, 512:768], [(c2v[:, k, :], wt[:, k, 512:768]) for k in range(8)], reads=[wk, "c2s"], writes=[(mpk, 1)])
            E(S, "act", "copy", [(mpk, 0), (mpk, 1)], [("mrow", blk)], out=mrow[:, blk * 768:(blk + 1) * 768], in_=mp[:, :])
        for j in range(48):
            E(S, "pe", "transpose", [("mrow", j // 6), "identf2"], [("mps", j)], mps[:, j, :], mrow[:, j * 128:(j + 1) * 128], identf2[0:2, 0:2])
    else:
        for blk in range(8):
            wt, wk = wa.next()
            kb.load(wt[:], wview(w_ada, blk * 768, (blk + 1) * 768), writes=[wk])
            for jj in range(6):
                j = blk * 6 + jj
                kb.mm_group(mps[:, j, :], [(wt[:, k, jj * 128:(jj + 1) * 128], c2v[:, k, :]) for k in range(8)],
                            reads=[wk, "c2s"], writes=[("mps", j)])
    for w in range(2):
        S.op("dve", lambda h, w=w: h.tensor_tensor(out=mods[:, :, w], in0=mps[:, :, w], in1=bad[:], op=ALU.add),
             reads=[("mps", j) for j in range(48)] + ["bad"], writes=["mods"])
    emit_hx_only(kb, S, xT, mods, sc1p, xs, hxT)


def emit_hx_only(kb, S, xT, mods, sc1p, xs, hxT):
    S.op("dve", lambda h: h.tensor_scalar_add(out=sc1p[:], in0=mods[:, 8:16, :], scalar1=1.0), reads=["mods"], writes=["sc1p"])
    xv = xT.rearrange("(j p) t -> p j t", p=128)
    for j in range(8):
        xt, xk = xs.next()
        kb.load(xt[:], xv[:, j, :], writes=[xk])
        eng = "dve" if j % 2 == 0 else "pool"
        S.op(eng, lambda h, j=j, xt=xt: h.tensor_scalar(out=hxT[:, j, 0:NT], in0=xt[:, 0:NT], scalar1=sc1p[:, j, 0:1],
                                                        scalar2=mods[:, j, 0:1], op0=ALU.mult, op1=ALU.add),
             reads=[xk, "sc1p", "mods"], writes=[("hxTa", j)])
        S.op(eng, lambda h, j=j, xt=xt: h.tensor_scalar(out=hxT[:, j, NT:T], in0=xt[:, NT:T], scalar1=sc1p[:, j, 1:2],
                                                        scalar2=mods[:, j, 1:2], op0=ALU.mult, op1=ALU.add),
             reads=[xk, "sc1p", "mods", ("hxTa", j)], writes=[("hxT", j)])


def rope_tm(S, eng, out, src, cos, sin, t1, t2, nh, half, rkeys, wkey, tkey):
    cb = cos.unsqueeze(1).to_broadcast([128, nh, half]) if nh > 1 else cos
    sn = sin.unsqueeze(1).to_broadcast([128, nh, half]) if nh > 1 else sin
    if nh > 1:
        x1, x2 = src[:, :, 0:half], src[:, :, half:2 * half]
        o1, o2 = out[:, :, 0:half], out[:, :, half:2 * half]
    else:
        x1, x2 = src[:, 0:half], src[:, half:2 * half]
        o1, o2 = out[:, 0:half], out[:, half:2 * half]
    S.op(eng, lambda h: h.tensor_tensor(out=t1, in0=x1, in1=cb, op=ALU.mult), reads=rkeys, writes=[tkey + "1"])
    S.op(eng, lambda h: h.tensor_tensor(out=t2, in0=x2, in1=sn, op=ALU.mult), reads=rkeys, writes=[tkey + "2"])
    S.op(eng, lambda h: h.tensor_tensor(out=o1, in0=t1, in1=t2, op=ALU.subtract), reads=[tkey + "1", tkey + "2"], writes=[(wkey, "a")])
    S.op(eng, lambda h: h.tensor_tensor(out=t1, in0=x1, in1=sn, op=ALU.mult), reads=rkeys + [(wkey, "a")], writes=[tkey + "1"])
    S.op(eng, lambda h: h.tensor_tensor(out=t2, in0=x2, in1=cb, op=ALU.mult), reads=rkeys + [(wkey, "a")], writes=[tkey + "2"])
    S.op(eng, lambda h: h.tensor_tensor(out=o2, in0=t1, in1=t2, op=ALU.add), reads=[tkey + "1", tkey + "2", (wkey, "a")], writes=[wkey])


def emit_kv_side(kb, st, hxT, ident, w_in, rld, rcst, kvg, rope_r, rope_m,
                 o_retK, o_retKT, o_retV, o_retF, o_naKT, o_naV, o_ckvnT, o_krT):
    S = kb.S
    hx_keys = [("hxT", j) for j in range(8)]
    RT = emit_ret_tables(kb, st, rld, rcst)
    w_rkv = kb.sb(st, "w_rkv", [128, 8, 768], BF16)
    w_nk = kb.sb(st, "w_nk", [128, 8, 512], BF16)
    w_nv = kb.sb(st, "w_nv", [128, 8, 512], BF16)
    w_mk = kb.sb(st, "w_mk", [128, 8, 288], BF16)
    kb.load_cast(w_rkv[:], wview(w_in, C_RK, C_GF), writes=["w_rkv"])
    kb.load_cast(w_mk[:], wview(w_in, C_MKV, C_GA), writes=["w_mk"])
    kb.load_cast(w_nv[:], wview(w_in, C_NV, C_MQ), writes=["w_nv"])
    kb.load_cast(w_nk[:], wview(w_in, C_NK, C_NV), writes=["w_nk"])
    rr = kb.sb(st, "rr", [128, NTILE, 64], F32)
    rm = kb.sb(st, "rm", [128, NTILE, 32], F32)
    kb.load(rr[:], rope_r.rearrange("(t p) f -> p t f", p=128), writes=["rr"])
    kb.load(rm[:], rope_m.rearrange("(t p) f -> p t f", p=128), writes=["rm"])
    gain = kb.sb(st, "kvgain", [128, 256], F32)
    kb.load(gain[:], kvg.partition_broadcast(128), writes=["kvgain"])
    k_tm = kb.sb(st, "k_tm", [128, NTILE, 256], BF16)
    v_tm = kb.sb(st, "v_tm", [128, NTILE, 512], BF16)
    kT = kb.sb(st, "kT", [64, 4, T], BF16)
    vaug_r = kb.ring_sb(st, "vaug", [128, 8, 128], BF16, 2)
    ckvnT = kb.sb(st, "ckvnT", [128, 2, T], BF16)
    krT = kb.sb(st, "krT", [32, T], BF16)
    naKT_r = kb.ring_sb(st, "naKT", [128, T], BF16, 2)
    for _ in range(2):
        vt_, vk_ = vaug_r.next()
        S.op("pool", lambda h, vt_=vt_: h.memset(vt_[:], 1.0), writes=[vk_])
    ps_a = kb.ring_ps(st, "ps_a", [128, 512], 2)
    ps_b = kb.ring_ps(st, "ps_b", [128, 512], 2)
    ps_t = kb.ring_ps(st, "ps_t", [128, 128], 2, BF16)
    kf = kb.ring_sb(st, "kf", [128, 256], F32, 2)
    kr32 = kb.ring_sb(st, "kr32", [128, 32], F32, 2)
    krb = kb.ring_sb(st, "krb", [128, 32], BF16, 2)
    ckvn = kb.ring_sb(st, "ckvn", [128, 256], BF16, 2)
    t1 = kb.sb(st, "t1", [128, 128], F32)
    t2 = kb.sb(st, "t2", [128, 128], F32)
    t1b = kb.sb(st, "t1b", [128, 128], F32)
    t2b = kb.sb(st, "t2b", [128, 128], F32)
    u1 = kb.sb(st, "u1", [128, 16], F32)
    u2 = kb.sb(st, "u2", [128, 16], F32)
    junk = kb.sb(st, "junk", [128, 256], F32)
    ss = kb.ring_sb(st, "ss", [128, 2], F32, 2)
    eps_ap = RT["cs"][:, 768 + 5:768 + 6]
    Fst = kb.sb(st, "Fst", [64, 8, 128], F32)
    S.op("dve", lambda h: h.memset(Fst[:], 0.0), writes=[("F", i) for i in range(8)])
    kd_r = kb.ring_sb(st, "kdA", [128, 64], BF16, 4)
    ps_sA = kb.ring_ps(st, "ps_sA", [64, 128], 2)

    def scan_step(d, t):
        for hh in range(4):
            i = d * 4 + hh
            kdt, kdk = kd_r.next()
            E(S, "act", "activation", [("k_tm", t), "kdec"], [kdk], out=kdt[:], in_=k_tm[:, t, hh * 64:(hh + 1) * 64], func=AF.Copy,
              scale=RT["kdec"][:, i:i + 1])
            pst, psk = ps_sA.next()
            kb.mm_group(pst[:], [(kdt[:], v_tm[:, t, hh * 128:(hh + 1) * 128])], reads=[kdk, ("v_tm", t)], writes=[psk])
            E(S, "dve", "scalar_tensor_tensor", [psk, ("F", i), "cd"], [("F", i)], out=Fst[:, i, :], in0=Fst[:, i, :], scalar=RT["cd"][0:64, i:i + 1],
              in1=pst[:], op0=ALU.mult, op1=ALU.add)

    for t in range(NTILE):
        tok = slice(t * 128, (t + 1) * 128)
        if 1 <= t <= 16:
            scan_step(0, t - 1)
        pa, pak = ps_a.next()
        kb.mm_group(pa[:, 0:512], [(hxT[:, k, tok], w_rkv[:, k, 0:512]) for k in range(8)],
                    reads=hx_keys + ["w_rkv"], writes=[pak])
        pb, pbk = ps_b.next()
        kb.mm_group(pb[:, 0:256], [(hxT[:, k, tok], w_rkv[:, k, 512:768]) for k in range(8)],
                    reads=hx_keys + ["w_rkv"], writes=[pbk])
        S.op("act", lambda h, pa=pa, t=t: h.copy(out=v_tm[:, t, 0:256], in_=pa[:, 256:512]), reads=[pak], writes=[("v_tm_a", t)])
        S.op("act", lambda h, pb=pb, t=t: h.copy(out=v_tm[:, t, 256:512], in_=pb[:, 0:256]), reads=[pbk, ("v_tm_a", t)], writes=[("v_tm", t)])
        kft, kfk = kf.next()
        S.op("act", lambda h, pa=pa, kft=kft: h.copy(out=kft[:], in_=pa[:, 0:256]), reads=[pak], writes=[kfk])
        rope_tm(S, "pool" if t % 2 == 0 else "dve", k_tm[:, t, :].rearrange("p (h f) -> p h f", h=4), kft[:].rearrange("p (h f) -> p h f", h=4),
                rr[:, t, 0:32], rr[:, t, 32:64], (t1 if t % 2 == 0 else t1b)[:].rearrange("p (h f) -> p h f", h=4),
                (t2 if t % 2 == 0 else t2b)[:].rearrange("p (h f) -> p h f", h=4),
                4, 32, [kfk, "rr"], ("k_tm", t), "ropeA" if t % 2 == 0 else "ropeAb")
        kb.store(o_retK[tok, :], k_tm[:, t, :], reads=[("k_tm", t)], final=True)
        kb.store(o_retV[tok, :], v_tm[:, t, :], reads=[("v_tm", t)], final=True, q="act")
        for hh in range(4):
            pt, ptk = ps_t.next()
            S.op("pe", lambda h, pt=pt, t=t, hh=hh: h.transpose(pt[0:64, :], k_tm[:, t, hh * 64:(hh + 1) * 64], ident[:]),
                 reads=[("k_tm", t), "ident"], writes=[ptk])
            S.op("dve", lambda h, pt=pt, hh=hh, tok=tok: h.tensor_copy(out=kT[:, hh, tok], in_=pt[0:64, :]), reads=[ptk], writes=[("kT", t, hh)])
        pa, pak = ps_a.next()
        kb.mm_group(pa[:, 0:512], [(hxT[:, k, tok], w_nv[:, k, :]) for k in range(8)], reads=hx_keys + ["w_nv"], writes=[pak])
        vg, vgk = vaug_r.next()
        S.op("act", lambda h, pa=pa, vg=vg: h.copy(out=vg[:, :, 0:64], in_=pa[:, 0:512].rearrange("p (h f) -> p h f", h=8)),
             reads=[pak, vgk], writes=[vgk])
        kb.store(o_naV[tok, :, :], vg[:], reads=[vgk], final=True)
        pb, pbk = ps_b.next()
        kb.mm_group(pb[:, 0:288], [(hxT[:, k, tok], w_mk[:, k, :]) for k in range(8)], reads=hx_keys + ["w_mk"], writes=[pbk])
        sst, ssk = ss.next()
        S.op("act", lambda h, pb=pb, sst=sst: h.activation(out=junk[:], in_=pb[:, 0:256], func=AF.Square, accum_out=sst[:, 0:1]),
             reads=[pbk], writes=["junk", ssk])
        S.op("act", lambda h, sst=sst: h.activation(out=sst[:, 1:2], in_=sst[:, 0:1], func=AF.Sqrt, scale=1.0 / 256.0, bias=eps_ap),
             reads=[ssk, "cst"], writes=[ssk])
        k32, k32k = kr32.next()
        S.op("act", lambda h, pb=pb, k32=k32: h.copy(out=k32[:], in_=pb[:, 256:288]), reads=[pbk], writes=[k32k])
        S.op("dve", lambda h, sst=sst: h.reciprocal(out=sst[:, 1:2], in_=sst[:, 1:2]), reads=[ssk], writes=[ssk])
        cn, cnk = ckvn.next()
        S.op("dve", lambda h, pb=pb, sst=sst, cn=cn: h.scalar_tensor_tensor(out=cn[:], in0=pb[:, 0:256], scalar=sst[:, 1:2], in1=gain[:],
                                                                           op0=ALU.mult, op1=ALU.mult),
             reads=[pbk, ssk, "kvgain", k32k], writes=[cnk])
        kbt, kbk = krb.next()
        rope_tm(S, "dve", kbt[:], k32[:], rm[:, t, 0:16], rm[:, t, 16:32], u1[:], u2[:], 1, 16, [k32k, "rm"], kbk, "ropeB")
        for kc in range(2):
            pt, ptk = ps_t.next()
            S.op("pe", lambda h, pt=pt, cn=cn, kc=kc: h.transpose(pt[:], cn[:, kc * 128:(kc + 1) * 128], ident[:]),
                 reads=[cnk, "ident"], writes=[ptk])
            S.op("dve", lambda h, pt=pt, kc=kc, tok=tok: h.tensor_copy(out=ckvnT[:, kc, tok], in_=pt[:]), reads=[ptk], writes=[("ckvnT", t, kc)])
        pt, ptk = ps_t.next()
        S.op("pe", lambda h, pt=pt, kbt=kbt: h.transpose(pt[0:32, :], kbt[:], ident[:]), reads=[kbk, "ident"], writes=[ptk])
        S.op("dve", lambda h, pt=pt, tok=tok: h.tensor_copy(out=krT[:, tok], in_=pt[0:32, :]), reads=[ptk], writes=[("krT", t)])
    bwd_t = list(range(15, -1, -1))
    for hp in range(4):
        nk, nkk = naKT_r.next()
        for (b0, bn) in TOKBLKS:
            if bwd_t:
                scan_step(1, bwd_t.pop(0))
            pa, pak = ps_a.next()
            kb.mm_group(pa[:, 0:bn], [(w_nk[:, k, hp * 128:(hp + 1) * 128], hxT[:, k, b0:b0 + bn]) for k in range(8)],
                        reads=hx_keys + ["w_nk"], writes=[pak])
            S.op("act", lambda h, pa=pa, nk=nk, b0=b0, bn=bn: h.copy(out=nk[:, b0:b0 + bn], in_=pa[:, 0:bn]),
                 reads=[pak], writes=[nkk])
        kb.store(o_naKT[:, hp, :], nk[:], reads=[nkk], final=True)
    kb.store(o_ckvnT, ckvnT[:], reads=[("ckvnT", t, kc) for t in range(NTILE) for kc in range(2)], final=True)
    kb.store(o_krT, krT[:], reads=[("krT", t) for t in range(NTILE)], final=True)
    kb.store(o_retKT, kT[:], reads=[("kT", t, hh) for t in range(NTILE) for hh in range(4)], final=True)
    while bwd_t:
        scan_step(1, bwd_t.pop(0))
    kb.store(o_retF, Fst[:], reads=[("F", i) for i in range(8)], final=True)


def emit_ret_scan(kb, st, RT, k_tm, v_tm, Sst, tiles, sprev, tag):
    S = kb.S
    kd = kb.ring_sb(st, "kd" + tag, [128, 64], BF16, 3)
    ps_s = kb.ring_ps(st, "ps_s" + tag, [64, 128], 2)
    for d in range(2):
        order = tiles if d == 0 else tiles[::-1]
        for t in order:
            for hh in range(4):
                i = d * 4 + hh
                if sprev is not None:
                    S.op("act", lambda h, i=i, t=t: h.copy(out=sprev[:, i, t, :], in_=Sst[:, i, :]), reads=[("F", i)],
                         writes=[("sprev", i, t)])
                kdt, kdk = kd.next()
                S.op("act", lambda h, kdt=kdt, t=t, hh=hh, i=i: h.activation(out=kdt[:], in_=k_tm[:, t, hh * 64:(hh + 1) * 64], func=AF.Copy,
                                                                              scale=RT["kdec"][:, i:i + 1]),
                     reads=[("k_tm", t), "kdec"], writes=[kdk])
                pst, psk = ps_s.next()
                kb.mm_group(pst[:], [(kdt[:], v_tm[:, t, hh * 128:(hh + 1) * 128])], reads=[kdk, ("v_tm", t)], writes=[psk])
                S.op("dve", lambda h, pst=pst, i=i: h.scalar_tensor_tensor(out=Sst[:, i, :], in0=Sst[:, i, :], scalar=RT["cd"][0:64, i:i + 1],
                                                                            in1=pst[:], op0=ALU.mult, op1=ALU.add),
                     reads=[psk, ("F", i), "cd"], writes=[("F", i)])


def rope_tables(q):
    idx = q * NT + np.arange(NT)
    row = (idx // 64).astype(np.float32)
    col = (idx % 64).astype(np.float32)

    def tab(rot_dim):
        nf = rot_dim // 4
        inv = (10000.0 ** (-2.0 * np.arange(nf, dtype=np.float32) / (rot_dim // 2))).astype(np.float32)
        ang = np.concatenate([row[:, None] * inv, col[:, None] * inv], axis=-1).astype(np.float32)
        cs = np.concatenate([np.cos(ang), np.sin(ang)], axis=-1).astype(np.float32)
        z = np.concatenate([np.ones((NZ, rot_dim // 2), np.float32), np.zeros((NZ, rot_dim // 2), np.float32)], axis=-1)
        return np.concatenate([cs, z], axis=0)
    return tab(64), tab(32)


def fm(v):
    return np.ascontiguousarray(v.reshape(-1, 128).T)


def prep_A(inp, l, core, xT_core):
    b, q = core // 4, core % 4
    rr, rm = rope_tables(q)
    c2 = np.stack([inp["c"][b], inp["c_ctx"]], axis=-1)
    c2 = np.ascontiguousarray(c2.reshape(8, 128, 2).transpose(1, 0, 2).reshape(128, 16))
    return {
        "xT": xT_core, "c2": c2, "w_ada": inp["w_ada"][l], "b_ada": fm(inp["b_ada"][l]),
        "w_in": inp["w_in"][l], "rld": np.ascontiguousarray(inp["ret_log_decay"][l].reshape(1, 8)),
        "rcst": ret_consts(), "kvg": np.ascontiguousarray(inp["mla_kv_norm"][l].reshape(1, 256)),
        "rope_r": (rr * np.float32(0.125)).astype(np.float32), "rope_m": rm,
    }


NKEY = 2816
NKM = 8192 + 256
FFBLK = 256
MOD_ROWMAJOR = True
MOD_PREFETCH = True
MLA_PAIRS = True


def E(S, eng, meth, reads, writes, *args, **kw):
    return S.op(eng, lambda h: getattr(h, meth)(*args, **kw), reads=reads, writes=writes)


def make_ident(kb, st):
    S = kb.S
    ident = kb.sb(st, "ident", [128, 128], BF16)
    identf = kb.sb(st, "identf", [128, 128], F32)
    E(S, "pool", "memset", [], ["identf"], identf[:], 0.0)
    E(S, "pool", "affine_select", ["identf"], ["identf"], out=identf[:], in_=identf[:], pattern=[[-1, 128]],
      compare_op=ALU.not_equal, fill=1.0, base=0, channel_multiplier=1)
    E(S, "dve", "tensor_copy", ["identf"], ["ident"], out=ident[:], in_=identf[:])
    return ident, identf


def rope2(S, eng, x1, x2, o1, o2, cb, sn, t1, t2, rkeys, wkey, tkey):
    E(S, eng, "tensor_tensor", rkeys, [(tkey, 1)], out=t1, in0=x1, in1=cb, op=ALU.mult)
    E(S, eng, "tensor_tensor", rkeys, [(tkey, 2)], out=t2, in0=x2, in1=sn, op=ALU.mult)
    E(S, eng, "tensor_tensor", [(tkey, 1), (tkey, 2)], [(wkey, "a")], out=o1, in0=t1, in1=t2, op=ALU.subtract)
    E(S, eng, "tensor_tensor", rkeys + [(wkey, "a")], [(tkey, 1)], out=t1, in0=x1, in1=sn, op=ALU.mult)
    E(S, eng, "tensor_tensor", rkeys + [(wkey, "a")], [(tkey, 2)], out=t2, in0=x2, in1=cb, op=ALU.mult)
    E(S, eng, "tensor_tensor", [(tkey, 1), (tkey, 2), (wkey, "a")], [wkey], out=o2, in0=t1, in1=t2, op=ALU.add)


def build_B():
    nc = bass.Bass("TRN2", target_bir_lowering=False)
    din = lambda name, shape, dt=F32: nc.dram_tensor(name, shape, dt, kind="ExternalInput").ap()
    dout = lambda name, shape, dt=F32: nc.dram_tensor(name, shape, dt, kind="ExternalOutput").ap()
    I = dict(
        xT=din("xT", [D, T]), mods=din("mods", [128, 96]), hxT=din("hxT", [128, 8, T], BF16),
        retK=din("retK", [T, 256], BF16), retKT=din("retKT", [64, 4, T], BF16), retV=din("retV", [T, 512], BF16),
        Fsl=din("Fsl", [64, 8, 3, 128]), fexp=din("fexp", [128, 8]), rld=din("rld", [1, 8]), rcst=din("rcst", [128, 776]),
        gng=din("gng", [1, 512]), rope_q=din("rope_q", [T, 64]), rope_mq=din("rope_mq", [T, 32]),
        naKT=din("naKT", [128, 4, NKEY], BF16), naV=din("naV", [NKEY, 8, 128], BF16),
        nabias=din("nabias", [128, 8, 24, 64]), namask=din("namask", [128, 32, 512], BF16),
        ckvnT=din("ckvnT", [128, 2, NKM], BF16), krT=din("krT", [32, NKM], BF16),
        qg=din("qg", [1, 256]), w_qup=din("w_qup", [256, 768]), w_kvup=din("w_kvup", [256, 1024]),
        w_in=din("w_in", [D, 7200]), w_br=din("w_br", [3, 512, D]), w_out=din("w_out", [D, D]),
        w_ff1=din("w_ff1", [D, 4 * D]), w_ff2=din("w_ff2", [4 * D, D]), lng=din("lng", [128, 32]),
    )
    xoT = dout("xoT", [D, T])
    dbg = dict(yaT=dout("yaT", [128, 4, T], BF16), ybT=dout("ybT", [128, 4, T], BF16), ycT=dout("ycT", [128, 4, T], BF16),
               x1T=dout("x1T", [128, 8, T]))
    with ExitStack() as st0:
        S = Sched(nc, st0)
        kb = KB(nc, S)
        with ExitStack() as ph:
            emit_retention(kb, ph, I, dbg["yaT"])
            S.barrier(); S.emit()
        with ExitStack() as ph:
            emit_na(kb, ph, I, dbg["ybT"])
            S.barrier(); S.emit()
        with ExitStack() as ph:
            emit_mla(kb, ph, I, dbg["ycT"])
            S.barrier(); S.emit()
        with ExitStack() as ph:
            emit_merge(kb, ph, I, dbg)
            S.barrier(); S.emit()
        with ExitStack() as ph:
            emit_ffn(kb, ph, I, dbg["x1T"], xoT)
            S.barrier(); S.emit(final=True)
    return nc


def emit_retention(kb, st, I, o_yaT):
    S = kb.S
    RT = emit_ret_tables(kb, st, I["rld"], I["rcst"])
    ident, _ = make_ident(kb, st)
    eps_ap = RT["cs"][:, 768 + 5:768 + 6]
    hxT = kb.sb(st, "hxT", [128, 8, T], BF16)
    kb.load(hxT[:], I["hxT"], writes=["hxT"])
    k_tm = kb.sb(st, "k_tm", [128, NTILE, 256], BF16)
    v_tm = kb.sb(st, "v_tm", [128, NTILE, 512], BF16)
    kT = kb.sb(st, "kT", [64, 4, T], BF16)
    qT = kb.sb(st, "qT", [64, 4, T], BF16)
    kb.load(k_tm[:], I["retK"].rearrange("(t p) f -> p t f", p=128), writes=["k_tm"])
    kb.load(v_tm[:], I["retV"].rearrange("(t p) f -> p t f", p=128), writes=["v_tm"])
    kb.load(kT[:], I["retKT"], writes=["kT"])
    w_g = kb.sb(st, "w_g", [128, 8, 1024], BF16)
    kb.load_cast(w_g[:], wview(I["w_in"], C_GF, C_NQ), writes=["w_g"])
    gng = kb.sb(st, "gng", [128, 512], F32)
    kb.load(gng[:], I["gng"].partition_broadcast(128), writes=["gng"])
    Fsl = kb.sb(st, "Fsl", [64, 8, 4, 128], F32)
    for s_ in range(4):
        kb.load(Fsl[:, :, s_, :], I["FB_out"][s_ * 64:(s_ + 1) * 64, :].rearrange("p (i v) -> p i v", i=8), writes=[("Fsl", s_)], reads=["FB_out"])
    cc = kb.sb(st, "cc", [128, 32], F32)
    kb.load(cc[:], I["cc"], writes=["cc"])
    coef = kb.sb(st, "coef", [128, 8, 5], F32)
    for d in range(2):
        fo = 12 if d == 0 else 17
        mo = 22 if d == 0 else 26
        for hh in range(4):
            i = d * 4 + hh
            E(S, "act", "activation", ["cc", "lg"], [("coefe", i)], out=coef[:, i, :], in_=cc[:, fo:fo + 5], func=AF.Exp, scale=RT["lg"][:, i:i + 1])
            E(S, "dve", "tensor_tensor", [("coefe", i), "cc"], [("coef", i)], out=coef[:, i, 0:4], in0=coef[:, i, 0:4], in1=cc[:, mo:mo + 4], op=ALU.mult)
    ya = kb.sb(st, "ya_acc", [128, NTILE, 512], F32)
    Sst = kb.sb(st, "Sst", [64, 8, 128], F32)
    hv = lambda ap: ap.rearrange("p (h f) -> p h f", h=4)
    with ExitStack() as s2:
        w_q = kb.sb(s2, "w_q", [128, 8, 256], BF16)
        kb.load_cast(w_q[:], wview(I["w_in"], C_RQ, C_RK), writes=["w_q"])
        rq = kb.sb(s2, "rq", [128, NTILE, 64], F32)
        kb.load(rq[:], I["rope_q"].rearrange("(t p) f -> p t f", p=128), writes=["rq"])
        ps_q = kb.ring_ps(s2, "ps_q", [128, 256], 2)
        ps_t = kb.ring_ps(s2, "ps_tq", [128, 128], 4, BF16)
        qf = kb.ring_sb(s2, "qf", [128, 256], F32, 3)
        qb = kb.ring_sb(s2, "qb", [128, 256], BF16, 3)
        t1 = kb.ring_sb(s2, "t1", [128, 128], F32, 2)
        t2 = kb.ring_sb(s2, "t2", [128, 128], F32, 2)
        for t in range(NTILE):
            tok = slice(t * 128, (t + 1) * 128)
            pq, pqk = ps_q.next()
            kb.mm_group(pq[:], [(hxT[:, k, tok], w_q[:, k, :]) for k in range(8)], reads=["hxT", "w_q"], writes=[pqk])
            qft, qfk = qf.next()
            E(S, "act", "copy", [pqk], [qfk], out=qft[:], in_=pq[:])
            qbt, qbk = qb.next()
            cb = rq[:, t, 0:32].unsqueeze(1).to_broadcast([128, 4, 32])
            sn = rq[:, t, 32:64].unsqueeze(1).to_broadcast([128, 4, 32])
            t1t, t1k = t1.next()
            t2t, _ = t2.next()
            rope2(S, "dve" if t % 2 == 0 else "pool", hv(qft[:])[:, :, 0:32], hv(qft[:])[:, :, 32:64], hv(qbt[:])[:, :, 0:32], hv(qbt[:])[:, :, 32:64],
                  cb, sn, hv(t1t[:]), hv(t2t[:]), [qfk, "rq"], qbk, t1k)
            for hh in range(4):
                pt, ptk = ps_t.next()
                E(S, "pe", "transpose", [qbk, "ident"], [ptk], pt[0:64, :], qbt[:, hh * 64:(hh + 1) * 64], ident[:])
                E(S, "act" if hh % 2 else "dve", "copy" if hh % 2 else "tensor_copy", [ptk], [("qT", t, hh)], out=qT[:, hh, tok], in_=pt[0:64, :])
        S.barrier(); S.emit()
    import os as _os
    if _os.environ.get("RET_STOP") == "pre":
        return
    NCH = 8
    ps_g = kb.ring_ps(st, "ps_g", [128, 512], 2)
    ps_sc = kb.ring_ps_sliced(st, "ps_sc", 128, NCH)
    ps_o = kb.ring_ps_sliced(st, "ps_o", 128, NCH)
    ps_t = kb.ring_ps(st, "ps_t", [128, 128], 1, BF16)
    sg = kb.ring_sb(st, "sg", [128, 512], F32, 2)
    sT = kb.ring_sb(st, "sT", [128, 128], BF16, NCH)
    qs = kb.ring_sb(st, "qs", [64, 128], BF16, NCH)
    Sb = kb.ring_sb(st, "Sb", [64, 128], BF16, NCH)
    kd = kb.ring_sb(st, "kd", [128, 64], BF16, NCH)
    stats = kb.ring_sb(st, "stats", [128, 6], F32, NCH)
    mv = kb.ring_sb(st, "mv", [128, 4], F32, NCH)
    yn = kb.ring_sb(st, "yn", [128, 128], F32, NCH)
    yab = kb.ring_sb(st, "yab", [128, 512], BF16, 2)
    yaT_r = kb.ring_sb(st, "yaT", [128, 4, 128], BF16, 2)
    E(S, "pool", "memset", [], [("ya", t) for t in range(NTILE)], ya[:], 0.0)
    E(S, "dve", "memset", [], [("F", i) for i in range(8)], Sst[:], 0.0)

    def do_pairs(pairs):
        ch = []
        for (d, t) in pairs:
            tok = slice(t * 128, (t + 1) * 128)
            pg, pgk = ps_g.next()
            kb.mm_group(pg[:], [(hxT[:, k, tok], w_g[:, k, d * 512:(d + 1) * 512]) for k in range(8)], reads=["hxT", "w_g"], writes=[pgk])
            sgt, sgk = sg.next()
            E(S, "act", "activation", [pgk], [sgk], out=sgt[:], in_=pg[:], func=AF.Silu)
            for hh in range(4):
                ch.append(dict(d=d, t=t, hh=hh, i=d * 4 + hh, tok=tok, sg=sgt, sgk=sgk))
        for c in ch:
            c["psc"], c["psck"] = ps_sc.next()
            kb.mm_group(c["psc"][:], [(kT[:, c["hh"], c["tok"]], qT[:, c["hh"], c["tok"]])], reads=["kT", ("qT", c["t"], c["hh"])], writes=[c["psck"]])
        for c in ch:
            i, hh, t, tok = c["i"], c["hh"], c["t"], c["tok"]
            c["sT"], c["sTk"] = sT.next()
            E(S, "dve", "tensor_tensor", [c["psck"], "DT"], [c["sTk"]], out=c["sT"][:], in0=c["psc"][:], in1=RT["DT"][:, i, :], op=ALU.mult)
            c["qs"], c["qsk"] = qs.next()
            E(S, "pool", "tensor_tensor", [("qT", t, hh), "QD"], [c["qsk"]], out=c["qs"][:], in0=qT[:, hh, tok], in1=RT["QD"][0:64, i, :], op=ALU.mult)
            c["Sb"], c["Sbk"] = Sb.next()
            E(S, "act", "copy", [("F", i)], [c["Sbk"]], out=c["Sb"][:], in_=Sst[:, i, :])
            c["kd"], c["kdk"] = kd.next()
            E(S, "act", "activation", ["k_tm", "kdec"], [c["kdk"]], out=c["kd"][:], in_=k_tm[:, t, hh * 64:(hh + 1) * 64], func=AF.Copy, scale=RT["kdec"][:, i:i + 1])
        for c in ch:
            hh, t = c["hh"], c["t"]
            c["po"], c["pok"] = ps_o.next()
            kb.mm_group(c["po"][:], [(c["sT"][:], v_tm[:, t, hh * 128:(hh + 1) * 128]), (c["qs"][:], c["Sb"][:])],
                        reads=[c["sTk"], "v_tm", c["qsk"], c["Sbk"]], writes=[c["pok"]])
            kb.mm_group(c["psc"][0:64, :], [(c["kd"][:], v_tm[:, t, hh * 128:(hh + 1) * 128])], reads=[c["kdk"], "v_tm", c["sTk"]], writes=[c["psck"]])
        for c in ch:
            i = c["i"]
            E(S, "dve", "scalar_tensor_tensor", [c["psck"], ("F", i), "cd", c["Sbk"]], [("F", i)], out=Sst[:, i, :], in0=Sst[:, i, :],
              scalar=RT["cd"][0:64, i:i + 1], in1=c["psc"][0:64, :], op0=ALU.mult, op1=ALU.add)
            c["st"], c["stk"] = stats.next()
            E(S, "dve", "bn_stats", [c["pok"]], [c["stk"]], out=c["st"][:], in_=c["po"][:])
        for c in ch:
            c["mv"], c["mvk"] = mv.next()
            E(S, "dve", "bn_aggr", [c["stk"]], [c["mvk"]], out=c["mv"][:, 0:2], in_=c["st"][:])
        for c in ch:
            E(S, "act", "activation", [c["mvk"], "cst"], [c["mvk"]], out=c["mv"][:, 2:3], in_=c["mv"][:, 1:2], func=AF.Sqrt, bias=eps_ap, scale=1.0)
        for c in ch:
            E(S, "dve", "reciprocal", [c["mvk"]], [c["mvk"]], out=c["mv"][:, 2:3], in_=c["mv"][:, 2:3])
        for c in ch:
            c["yn"], c["ynk"] = yn.next()
            E(S, "dve", "tensor_scalar", [c["pok"], c["mvk"]], [c["ynk"]], out=c["yn"][:], in0=c["po"][:], scalar1=c["mv"][:, 0:1], scalar2=c["mv"][:, 2:3],
              op0=ALU.subtract, op1=ALU.mult)
        for c in ch:
            hh, t = c["hh"], c["t"]
            yslc = ya[:, t, hh * 128:(hh + 1) * 128]
            E(S, "pool", "tensor_tensor", [c["ynk"], c["sgk"]], [c["ynk"]], out=c["yn"][:], in0=c["yn"][:], in1=c["sg"][:, hh * 128:(hh + 1) * 128], op=ALU.mult)
            E(S, "pool", "tensor_tensor", [c["ynk"], ("ya", t)], [("ya", t)], out=yslc, in0=yslc, in1=c["yn"][:], op=ALU.add)

    if _os.environ.get("RET_STOP") == "memset":
        return
    do_pairs([(0, 16), (1, 17)])
    if _os.environ.get("RET_STOP") == "one":
        return
    do_pairs([(0, 17), (1, 16)])
    for d in range(2):
        for hh in range(4):
            i = d * 4 + hh
            E(S, "dve", "tensor_scalar_mul", [("F", i), ("coef", i)], [("F", i)], out=Sst[:, i, :], in0=Sst[:, i, :], scalar1=coef[0:64, i, 4:5])
            for j in range(4):
                E(S, "dve", "scalar_tensor_tensor", [("F", i), ("coef", i), ("Fsl", j)], [("F", i)], out=Sst[:, i, :], in0=Fsl[:, i, j, :],
                  scalar=coef[0:64, i, j:j + 1], in1=Sst[:, i, :], op0=ALU.mult, op1=ALU.add)
    for k in range(16):
        do_pairs([(0, k), (1, 15 - k)])
    for t in range(NTILE):
        tok = slice(t * 128, (t + 1) * 128)
        ybt, ybk = yab.next()
        E(S, "dve", "tensor_tensor", [("ya", t), "gng"], [ybk], out=ybt[:], in0=ya[:, t, :], in1=gng[:], op=ALU.mult)
        yat, yatk = yaT_r.next()
        for kc in range(4):
            pt, ptk = ps_t.next()
            E(S, "pe", "transpose", [ybk, "ident"], [ptk], pt[:], ybt[:, kc * 128:(kc + 1) * 128], ident[:])
            E(S, "act", "copy", [ptk], [(yatk, kc)], out=yat[:, kc, :], in_=pt[:])
        kb.store(o_yaT[:, :, tok], yat[:], reads=[(yatk, kc) for kc in range(4)], writes=["yaT_d"])


class AttnPipe:
    def __init__(self, kb, S, rings, ident=None, depth=2):
        self.kb, self.S, self.rings, self.ident, self.depth = kb, S, rings, ident, depth
        self.steps = []

    def add(self, kT_list, q_ap, qkeys, v_list, out_ap, okey, extra=None):
        n = len(kT_list)
        blk = dict(q=q_ap, qk=list(qkeys), out=out_ap, okey=okey, n=n, nq=q_ap.shape[-1], po=None)
        for i in range(n):
            self.steps.append(dict(blk=blk, i=i, k=kT_list[i], v=v_list[i], ex=(extra[i] if extra is not None else None)))

    def _qk(self, st):
        ps_s = self.rings[0]
        blk = st["blk"]; nq = blk["nq"]
        ps, psk = ps_s.next()
        st["ps"], st["psk"] = ps, psk
        kl, kkeys = st["k"]
        pairs = [(kl, blk["q"])]
        rk = list(kkeys) + blk["qk"]
        if st["ex"] is not None:
            pairs.append((self.ident, st["ex"][0]))
            rk += list(st["ex"][2]) + ["ident"]
        self.kb.mm_group(ps[:, 0:nq], pairs, reads=rk, writes=[psk])

    def _exp(self, st):
        S = self.S
        _, _, e_r, p_r, _ = self.rings
        nq = st["blk"]["nq"]
        et, ek = e_r.next()
        E(S, "act", "activation", [st["psk"]], [ek], out=et[:, 0:nq], in_=st["ps"][:, 0:nq], func=AF.Exp)
        if st["ex"] is not None:
            pt, pk = p_r.next()
            E(S, "dve", "tensor_tensor", [ek] + list(st["ex"][2]), [pk], out=pt[:, 0:nq], in0=et[:, 0:nq], in1=st["ex"][1], op=ALU.mult)
            et, ek = pt, pk
        st["e"], st["ek"] = et, ek

    def _pv(self, st):
        S = self.S
        _, ps_o, _, _, rden_r = self.rings
        blk = st["blk"]; nq = blk["nq"]; i = st["i"]; n = blk["n"]
        if i == 0:
            blk["po"], blk["pok"] = ps_o.next()
        po, pok = blk["po"], blk["pok"]
        vl, vkeys = st["v"]
        et = st["e"]
        S.op("pe", lambda h: h.matmul(po[:, 0:nq], lhsT=vl, rhs=et[:, 0:nq], start=(i == 0), stop=(i == n - 1)),
             reads=[st["ek"]] + list(vkeys), writes=[pok])
        if i == n - 1:
            rd, rdk = rden_r.next()
            E(S, "dve", "reciprocal", [pok], [rdk], out=rd[:, 0:nq], in_=po[64:128, 0:nq])
            E(S, "dve", "tensor_tensor", [pok, rdk], [blk["okey"]], out=blk["out"], in0=po[0:64, 0:nq], in1=rd[:, 0:nq], op=ALU.mult)

    def run_pairs(self):
        S, kb = self.S, self.kb
        ps_s, ps_o, e_r, _, rden_r = self.rings
        st = self.steps
        assert len(st) % 2 == 0
        units = [(st[2 * u], st[2 * u + 1]) for u in range(len(st) // 2)]

        def qk(u):
            a, b = units[u]
            assert a["blk"] is b["blk"]
            ps, psk = ps_s.next()
            nq = a["blk"]["nq"]
            for half, s_ in enumerate((a, b)):
                kl, kkeys = s_["k"]
                kb.mm_group(ps[:, half * 512:half * 512 + nq], [(kl, s_["blk"]["q"])], reads=list(kkeys) + s_["blk"]["qk"], writes=[psk])
            a["ps"], a["psk"] = ps, psk

        def ex(u):
            a, b = units[u]
            nq = a["blk"]["nq"]
            et, ek = e_r.next()
            if nq == 512:
                E(S, "act", "activation", [a["psk"]], [ek], out=et[:, 0:1024], in_=a["ps"][:, 0:1024], func=AF.Exp)
            else:
                for half in range(2):
                    E(S, "act", "activation", [a["psk"]], [ek], out=et[:, half * 512:half * 512 + nq], in_=a["ps"][:, half * 512:half * 512 + nq], func=AF.Exp)
            a["e"], a["ek"] = et, ek

        def pv(u):
            a, b = units[u]
            et, ek = a["e"], a["ek"]
            for half, s_ in enumerate((a, b)):
                s_["e"], s_["ek"] = et[:, half * 512:(half + 1) * 512], ek
                self._pv(s_)

        qk(0)
        for u in range(len(units)):
            ex(u)
            if u + 1 < len(units):
                qk(u + 1)
            pv(u)
        self.steps = []

    def run(self):
        st = self.steps
        for j in range(min(self.depth, len(st))):
            self._qk(st[j])
        for j in range(len(st)):
            self._exp(st[j])
            if j + self.depth < len(st):
                self._qk(st[j + self.depth])
            self._pv(st[j])
        self.steps = []


def attn_rings_pairs(kb, st, tag):
    return (kb.ring_ps(st, "ps_s" + tag, [128, 1024], 2), kb.ring_ps(st, "ps_o" + tag, [128, 512], 2),
            kb.ring_sb(st, "e_r" + tag, [128, 1024], BF16, 3), None,
            kb.ring_sb(st, "rden" + tag, [64, 512], F32, 2))


def attn_rings(kb, st, tag, n_s=4):
    return (kb.ring_ps(st, "ps_s" + tag, [128, 512], n_s), kb.ring_ps(st, "ps_o" + tag, [128, 512], 2),
            kb.ring_sb(st, "e_r" + tag, [128, 512], BF16, 4), kb.ring_sb(st, "p_r" + tag, [128, 512], BF16, 3),
            kb.ring_sb(st, "rden" + tag, [64, 512], F32, 2))


def emit_na(kb, st, I, o_ybT, modpre=None):
    S = kb.S
    ident, _ = make_ident(kb, st)
    qT = kb.sb(st, "naqT", [128, 4, T], BF16)
    with ExitStack() as s2:
        hxT = kb.sb(s2, "hxT", [128, 8, T], BF16)
        kb.load(hxT[:], I["hxT"], writes=["hxT"])
        w_nq = kb.sb(s2, "w_nq", [128, 8, 512], BF16)
        kb.load_cast(w_nq[:], wview(I["w_in"], C_NQ, C_NK), writes=["w_nq"])
        ps_a = kb.ring_ps(s2, "ps_a", [128, 512], 2)
        for hp in range(4):
            for (b0, bn) in TOKBLKS:
                pa, pak = ps_a.next()
                kb.mm_group(pa[:, 0:bn], [(w_nq[:, k, hp * 128:(hp + 1) * 128], hxT[:, k, b0:b0 + bn]) for k in range(8)],
                            reads=["hxT", "w_nq"], writes=[pak])
                E(S, "act", "activation", [pak], [("naqT", hp, b0)], out=qT[:, hp, b0:b0 + bn], in_=pa[:, 0:bn], func=AF.Copy, scale=0.125)
        S.barrier(); S.emit()
    kT = kb.sb(st, "nakT", [128, 4, NKEY], BF16)
    V = kb.sb(st, "naV", [128, NKEY // 128, 8, 128], BF16)
    kb.load(kT[:, :, 256:2304], I["naKT"][:, :, 0:NT], writes=[("nakT", "own")])
    kb.load(kT[:, :, 2560:NKEY], I["naKT"][:, :, NT:T], writes=[("nakT", "ctx")])
    nvv = I["naV"].rearrange("(u p) h f -> p u h f", p=128)
    kb.load(V[:, 2:18], nvv[:, 0:16], writes=[("naV", "own")])
    kb.load(V[:, 20:22], nvv[:, 16:18], writes=[("naV", "ctx")])
    cc = kb.sb(st, "cc", [128, 32], F32)
    kb.load(cc[:], I["cc"], writes=["cc"])
    xbo = [g_.rearrange("(s p) c -> p s c", p=128) for g_ in I["XB_out"]]
    with ExitStack() as s3:
        hal = kb.sb(s3, "hal", [128, 4, 6144], BF16)
        kb.load(hal[:, :, 0:2048], xbo[1][:, :, 2048:4096], writes=[("hal", 0)], reads=[("XB_out", 1)])
        kb.load(hal[:, :, 2048:6144], xbo[2][:, :, 0:4096], writes=[("hal", 1)], reads=[("XB_out", 2)])
        E(S, "dve", "tensor_copy", [("hal", 0), ("hal", 1)], ["hal"], out=hal[0:1, 0, 0:1], in_=hal[0:1, 0, 0:1])
        kt_top = kT[:, :, 0:256]
        kt_bot = kT[:, :, 2304:2560]
        v_top = V[:, 0:2].rearrange("p u h f -> p u (h f)")
        v_bot = V[:, 18:20].rearrange("p u h f -> p u (h f)")
        for s_ in range(4):
            srcs = [(kt_top, hal[:, s_, 1024:2048].rearrange("p (a t) -> p a t", a=4), 4 + s_, "dve"),
                    (kt_bot, hal[:, s_, 0:1024].rearrange("p (a t) -> p a t", a=4), 8 + s_, "pool"),
                    (v_top, hal[:, s_, 4096:6144].rearrange("p (u f) -> p u f", u=2), 4 + s_, "dve"),
                    (v_bot, hal[:, s_, 2048:4096].rearrange("p (u f) -> p u f", u=2), 8 + s_, "pool")]
            for j, (dst, src, col, eng) in enumerate(srcs):
                if s_ == 0:
                    E(S, eng, "tensor_scalar_mul", ["hal", "cc"], [("halo", j)], out=dst, in0=src, scalar1=cc[:, col:col + 1])
                else:
                    E(S, "dve", "scalar_tensor_tensor", ["hal", "cc", ("halo", j)], [("halo", j)], out=dst, in0=src, scalar=cc[:, col:col + 1], in1=dst,
                      op0=ALU.mult, op1=ALU.add)
        S.barrier(); S.emit()
    E(S, "dve", "tensor_copy", [("nakT", "own"), ("nakT", "ctx"), ("halo", 0), ("halo", 1)], ["nakT"], out=kT[0:1, 0, 0:1], in_=kT[0:1, 0, 0:1])
    E(S, "dve", "tensor_copy", [("naV", "own"), ("naV", "ctx"), ("halo", 2), ("halo", 3)], ["naV"], out=V[0:1, 0, 0, 0:1], in_=V[0:1, 0, 0, 0:1])
    bias = kb.sb(st, "nabias", [128, 8, 24 * 64], BF16)
    kb.load_cast(bias[:], I["nabias"].rearrange("p h e c -> p h (e c)"), writes=["nabias"])
    mask = kb.sb(st, "namask", [128, 32, 512], BF16)
    kb.load(mask[:], I["namask"], writes=["namask"])
    ybT = kb.sb(st, "ybT", [128, 4, T], BF16)
    rings = attn_rings(kb, st, "na")
    pipe = AttnPipe(kb, S, rings, ident=ident[:])
    mp_ = modpre(kb, st) if modpre is not None else None
    for h in range(8):
        hp, hs = h // 2, (h % 2) * 64
        for b in range(5):
            b0, bn = TOKBLKS[b]
            qa = qT[hs:hs + 64, hp, b0:b0 + bn]
            qk = [("naqT", hp, b0)]
            kl, vl, ex = [], [], []
            if b < 4:
                for t in range(8):
                    u = 4 * b + t
                    e0 = (4 - 2 * t) + 10
                    kl.append((kT[hs:hs + 64, hp, u * 128:(u + 1) * 128], ["nakT"]))
                    vl.append((V[:, u, h, :], ["naV"]))
                    ex.append((bias[:, h, e0 * 64:(e0 + 8) * 64], mask[:, b * 8 + t, :], ["nabias", "namask"]))
            for u in (20, 21):
                kl.append((kT[hs:hs + 64, hp, u * 128:(u + 1) * 128], ["nakT"]))
                vl.append((V[:, u, h, :], ["naV"]))
                ex.append(None)
            pipe.add(kl, qa, qk, vl, ybT[hs:hs + 64, hp, b0:b0 + bn], ("ybT", h, b), extra=ex)
        if mp_ is not None:
            mp_.step()
            pipe.run()
    pipe.run()
    if mp_ is not None:
        mp_.finish()
    kb.store(o_ybT, ybT[:], reads=[("ybT", h, b) for h in range(8) for b in range(5)], final=True)


class ModPrefetch:
    NB = 16
    BW = 384

    def __init__(self, kb, st, c2, w_ada, b_ada, out_dram):
        S = self.S = kb.S
        self.kb, self.w_ada, self.out = kb, w_ada, out_dram
        self.c2s = kb.sb(st, "pc2s", [128, 16], F32)
        self.bad = kb.sb(st, "pbad", [128, 48], F32)
        self.wa = kb.ring_sb(st, "pwa", [128, 8, self.BW], F32, 2)
        self.mrow = kb.ring_sb(st, "pmrow", [2, self.BW], F32, 2)
        self.mods = kb.sb(st, "pmods", [128, 48, 2], F32)
        self.idf = kb.sb(st, "pidf", [128, 128], F32)
        self.mps = kb.ps(st, "pmod_ps", [128, 48, 2])
        self.mrp = kb.ring_ps(st, "pmrow_ps", [2, self.BW], 1)
        kb.load(self.c2s[:], c2, writes=["pc2s"])
        kb.load(self.bad[:], b_ada, writes=["pbad"])
        E(S, "act", "activation", ["pc2s"], ["pc2s"], out=self.c2s[:], in_=self.c2s[:], func=AF.Silu)
        E(S, "pool", "memset", [], ["pidf"], self.idf[:], 0.0)
        E(S, "pool", "affine_select", ["pidf"], ["pidf"], out=self.idf[:], in_=self.idf[:], pattern=[[-1, 128]],
          compare_op=ALU.not_equal, fill=1.0, base=0, channel_multiplier=1)
        self.c2v = self.c2s[:].rearrange("p (k w) -> p k w", w=2)
        self.pending = []
        self.nxt = 0

    def _compute(self, blk, wt, wk):
        S, kb = self.S, self.kb
        mp, mpk = self.mrp.next()
        kb.mm_group(mp[:, :], [(self.c2v[:, k, :], wt[:, k, :]) for k in range(8)], reads=[wk, "pc2s"], writes=[mpk])
        mr, mrk = self.mrow.next()
        E(S, "dve", "tensor_copy", [mpk], [mrk], out=mr[:], in_=mp[:, :])
        for jj in range(self.BW // 128):
            j = blk * (self.BW // 128) + jj
            E(S, "pe", "transpose", [mrk, "pidf"], [("pmps", j)], self.mps[:, j, :], mr[:, jj * 128:(jj + 1) * 128], self.idf[0:2, 0:2])

    def step(self):
        for (blk, wt, wk) in self.pending:
            self._compute(blk, wt, wk)
        self.pending = []
        for _ in range(2):
            if self.nxt < self.NB:
                blk = self.nxt
                self.nxt += 1
                wt, wk = self.wa.next()
                self.kb.load(wt[:], wview(self.w_ada, blk * self.BW, (blk + 1) * self.BW), writes=[wk])
                self.pending.append((blk, wt, wk))

    def finish(self):
        while self.pending or self.nxt < self.NB:
            self.step()
        S = self.S
        for w in range(2):
            E(S, "dve", "tensor_tensor", [("pmps", j) for j in range(48)] + ["pbad"], ["pmods"], out=self.mods[:, :, w], in0=self.mps[:, :, w],
              in1=self.bad[:], op=ALU.add)
        self.kb.store(self.out, self.mods[:].rearrange("p a b -> p (a b)"), reads=["pmods"])


def emit_mla(kb, st, I, o_ycT, modpre=None):
    S = kb.S
    ident, _ = make_ident(kb, st)
    qTm = kb.sb(st, "qTm", [96, 8, T], BF16)
    cst = kb.sb(st, "mcst", [128, 8], F32)
    kb.load(cst[:], I["rcst"][:, 768:776], writes=["mcst"])
    eps_ap = cst[:, 5:6]
    with ExitStack() as s2:
        cqnT = kb.sb(s2, "cqnT", [128, 2, T], BF16)
        with ExitStack() as s3:
            hxT = kb.sb(s3, "hxT", [128, 8, T], BF16)
            kb.load(hxT[:], I["hxT"], writes=["hxT"])
            w_mq = kb.sb(s3, "w_mq", [128, 8, 256], BF16)
            kb.load_cast(w_mq[:], wview(I["w_in"], C_MQ, C_MKV), writes=["w_mq"])
            qg = kb.sb(s3, "qg", [128, 256], F32)
            kb.load(qg[:], I["qg"].partition_broadcast(128), writes=["qg"])
            GA = 3
            ps_a = kb.ring_ps(s3, "ps_a", [128, 256], GA)
            ps_t = kb.ring_ps(s3, "ps_t", [128, 128], 4, BF16)
            ss = kb.ring_sb(s3, "ss", [128, 2], F32, 2 * GA)
            junk = kb.sb(s3, "junk", [128, 256], F32)
            cn_r = kb.ring_sb(s3, "cn", [128, 256], BF16, 2 * GA)
            for g0 in range(0, NTILE, GA):
                grp = []
                for t in range(g0, min(g0 + GA, NTILE)):
                    tok = slice(t * 128, (t + 1) * 128)
                    pa, pak = ps_a.next()
                    kb.mm_group(pa[:, 0:256], [(hxT[:, k, tok], w_mq[:, k, :]) for k in range(8)], reads=["hxT", "w_mq"], writes=[pak])
                    sst, ssk = ss.next()
                    cn, cnk = cn_r.next()
                    grp.append(dict(t=t, tok=tok, pa=pa, pak=pak, sst=sst, ssk=ssk, cn=cn, cnk=cnk))
                for c in grp:
                    E(S, "act", "activation", [c["pak"]], ["junk", c["ssk"]], out=junk[:], in_=c["pa"][:, 0:256], func=AF.Square, accum_out=c["sst"][:, 0:1])
                for c in grp:
                    E(S, "act", "activation", [c["ssk"], "mcst"], [c["ssk"]], out=c["sst"][:, 1:2], in_=c["sst"][:, 0:1], func=AF.Sqrt, scale=1.0 / 256.0, bias=eps_ap)
                for c in grp:
                    E(S, "dve", "reciprocal", [c["ssk"]], [c["ssk"]], out=c["sst"][:, 1:2], in_=c["sst"][:, 1:2])
                for c in grp:
                    E(S, "dve", "scalar_tensor_tensor", [c["pak"], c["ssk"], "qg"], [c["cnk"]], out=c["cn"][:], in0=c["pa"][:, 0:256], scalar=c["sst"][:, 1:2], in1=qg[:],
                      op0=ALU.mult, op1=ALU.mult)
                pts = []
                for c in grp:
                    for kc in range(2):
                        pt, ptk = ps_t.next()
                        E(S, "pe", "transpose", [c["cnk"], "ident"], [ptk], pt[:], c["cn"][:, kc * 128:(kc + 1) * 128], ident[:])
                        E(S, "dve" if kc == 0 else "act", "tensor_copy" if kc == 0 else "copy", [ptk], [("cqnT", c["t"])], out=cqnT[:, kc, c["tok"]], in_=pt[:])
            S.barrier(); S.emit()
        w_qup = kb.sb(s2, "w_qup", [128, 2, 768], BF16)
        kb.load_cast(w_qup[:], wview(I["w_qup"], 0, 768), writes=["w_qup"])
        rmq = kb.sb(s2, "rmq", [128, NTILE, 32], F32)
        kb.load(rmq[:], I["rope_mq"].rearrange("(t p) f -> p t f", p=128), writes=["rmq"])
        ps_a = kb.ring_ps(s2, "ps_qa", [128, 512], 2)
        ps_b = kb.ring_ps(s2, "ps_qb", [128, 256], 2)
        ps_t = kb.ring_ps(s2, "ps_qt", [128, 128], 4, BF16)
        qf_r = kb.ring_sb(s2, "mqf", [128, 768], F32, 4)
        qb_r = kb.ring_sb(s2, "mqb", [128, 768], BF16, 4)
        u1_r = kb.ring_sb(s2, "mu1", [128, 8, 16], F32, 2)
        u2_r = kb.ring_sb(s2, "mu2", [128, 8, 16], F32, 2)
        scl = float(96.0 ** -0.5)
        for g0 in range(0, NTILE, 2):
            grp = []
            for t in range(g0, min(g0 + 2, NTILE)):
                tok = slice(t * 128, (t + 1) * 128)
                pa, pak = ps_a.next()
                pb, pbk = ps_b.next()
                kb.mm_group(pa[:, 0:512], [(cqnT[:, kc, tok], w_qup[:, kc, 0:512]) for kc in range(2)], reads=[("cqnT", t), "w_qup"], writes=[pak])
                kb.mm_group(pb[:, 0:256], [(cqnT[:, kc, tok], w_qup[:, kc, 512:768]) for kc in range(2)], reads=[("cqnT", t), "w_qup"], writes=[pbk])
                qf, qfk = qf_r.next()
                qb, qbk = qb_r.next()
                grp.append(dict(t=t, tok=tok, pa=pa, pak=pak, pb=pb, pbk=pbk, qf=qf, qfk=qfk, qb=qb, qbk=qbk))
            for c in grp:
                E(S, "act", "copy", [c["pak"]], [(c["qfk"], 0)], out=c["qf"][:, 0:512], in_=c["pa"][:, 0:512])
                E(S, "act", "copy", [c["pbk"], (c["qfk"], 0)], [c["qfk"]], out=c["qf"][:, 512:768], in_=c["pb"][:, 0:256])
            for c in grp:
                qf3 = c["qf"][:].rearrange("p (h f) -> p h f", h=8)
                qb3 = c["qb"][:].rearrange("p (h f) -> p h f", h=8)
                E(S, "dve", "tensor_scalar_mul", [c["qfk"]], [(c["qbk"], "n")], out=qb3[:, :, 0:64], in0=qf3[:, :, 0:64], scalar1=scl)
                cb = rmq[:, c["t"], 0:16].unsqueeze(1).to_broadcast([128, 8, 16])
                sn = rmq[:, c["t"], 16:32].unsqueeze(1).to_broadcast([128, 8, 16])
                u1, u1k = u1_r.next()
                u2, _ = u2_r.next()
                rope2(S, "pool" if c["t"] % 2 == 0 else "dve", qf3[:, :, 64:80], qf3[:, :, 80:96], qb3[:, :, 64:80], qb3[:, :, 80:96], cb, sn, u1[:], u2[:],
                      [c["qfk"], "rmq", (c["qbk"], "n")], c["qbk"], u1k)
            for c in grp:
                for h in range(8):
                    pt, ptk = ps_t.next()
                    E(S, "pe", "transpose", [c["qbk"], "ident"], [ptk], pt[0:96, :], c["qb"][:, h * 96:(h + 1) * 96], ident[:])
                    E(S, "dve" if h % 2 == 0 else "act", "tensor_copy" if h % 2 == 0 else "copy", [ptk], [("qTm", c["t"])], out=qTm[:, h, c["tok"]], in_=pt[0:96, :])
        S.barrier(); S.emit()
    ck = kb.sb(st, "ckvnTa", [128, 2, NKM], BF16)
    xbo = [g_.rearrange("(s p) c -> p s c", p=128) for g_ in I["XB_out"]]
    for s_ in range(4):
        kb.load(ck[:, :, s_ * NT:(s_ + 1) * NT], xbo[0][:, s_, 0:4096].rearrange("p (k t) -> p k t", k=2), writes=[("ck", s_)], reads=[("XB_out", 0)])
    kb.load(ck[:, :, 4 * NT:NKM], I["ckvnT"][:, :, NT:T], writes=[("ck", 4)])
    E(S, "dve", "tensor_copy", [("ck", s_) for s_ in range(5)], ["ck"], out=ck[0:1, 0, 0:1], in_=ck[0:1, 0, 0:1])
    w_kv = kb.sb(st, "w_kvup", [128, 2, 1024], BF16)
    kb.load_cast(w_kv[:], wview(I["w_kvup"], 0, 1024), writes=["w_kv"])
    KT = kb.sb(st, "mKT", [96, 2, NKM], BF16)
    VA = kb.sb(st, "mVA", [128, NKM // 128, 2, 128], BF16)
    ycT = kb.sb(st, "ycT", [128, 4, T], BF16)
    E(S, "pool", "memset", [], ["mVA"], VA[:], 1.0)
    mp_ = modpre(kb, st) if modpre is not None else None
    ps_k = kb.ring_ps(st, "ps_k", [128, 512], 1 if mp_ is not None else 2)
    rings = attn_rings_pairs(kb, st, "ml") if (MLA_PAIRS and mp_ is None) else attn_rings(kb, st, "ml", n_s=3)
    pipe = AttnPipe(kb, S, rings)
    NU = NKM // 128
    for hp in range(4):
        for hh in range(2):
            h = hp * 2 + hh
            for s_ in range(4):
                kb.load(KT[64:96, hh, s_ * NT:(s_ + 1) * NT], xbo[1][0:32, s_, 0:2048], writes=[("mKTr", hh, s_)], reads=[("XB_out", 1)])
            kb.load(KT[64:96, hh, 4 * NT:NKM], I["krT"][:, NT:T], writes=[("mKTr", hh, 4)])
            for c0 in range(0, NKM, 512):
                cn = min(512, NKM - c0)
                pk, pkk = ps_k.next()
                kb.mm_group(pk[0:64, 0:cn], [(w_kv[:, kc, h * 128:h * 128 + 64], ck[:, kc, c0:c0 + cn]) for kc in range(2)],
                            reads=["ck", "w_kv"], writes=[pkk])
                E(S, "act" if (c0 // 512) % 2 else "dve", "copy" if (c0 // 512) % 2 else "tensor_copy", [pkk], [("mKT", hh, c0)],
                  out=KT[0:64, hh, c0:c0 + cn], in_=pk[0:64, 0:cn])
        for u in range(NU):
            pk, pkk = ps_k.next()
            wv = w_kv[:, :, hp * 256:(hp + 1) * 256].rearrange("p k (h f) -> p k h f", h=2)[:, :, :, 64:128]
            kb.mm_group(pk[:, 0:128].rearrange("p (h f) -> p h f", h=2), [(ck[:, kc, u * 128:(u + 1) * 128], wv[:, kc, :, :]) for kc in range(2)],
                        reads=["ck", "w_kv"], writes=[pkk])
            E(S, "act" if u % 2 else "dve", "copy" if u % 2 else "tensor_copy", [pkk, "mVA"], [("mVAu", u)],
              out=VA[:, u, :, 0:64], in_=pk[:, 0:128].rearrange("p (h f) -> p h f", h=2))
        for hh in range(2):
            h = hp * 2 + hh
            for b in range(5):
                b0, bn = TOKBLKS[b]
                us = list(range(NU)) if b < 4 else [NU - 2, NU - 1]
                kl = [(KT[:, hh, u * 128:(u + 1) * 128], [("mKT", hh, (u // 4) * 512), ("mKTr", hh, u // 16)]) for u in us]
                vl = [(VA[:, u, hh, :], [("mVAu", u)]) for u in us]
                pipe.add(kl, qTm[:, h, b0:b0 + bn], [("qTm", t) for t in range(b0 // 128, (b0 + bn) // 128)], vl,
                         ycT[hh * 64:hh * 64 + 64, hp, b0:b0 + bn], ("ycT", h, b))
            if mp_ is not None:
                mp_.step()
            if MLA_PAIRS and mp_ is None:
                pipe.run_pairs()
            else:
                pipe.run()
    if mp_ is not None:
        mp_.finish()
    kb.store(o_ycT, ycT[:], reads=[("ycT", h, b) for h in range(8) for b in range(5)], final=True)


def emit_ln(kb, S, x, xkeys, n, ones, lng, gcol, cst, ps_ln, sq_r, lnt, okeys, out=None):
    p1, p1k = ps_ln.next()
    p2, p2k = ps_ln.next()
    kb.mm_group(p1[:, 0:n], [(ones[:], x[:, oc, :]) for oc in range(8)], reads=list(xkeys) + ["ones"], writes=[p1k])
    for oc in range(8):
        sq, sqk = sq_r.next()
        E(S, "act", "activation", [xkeys[oc]], [sqk], out=sq[:, 0:n], in_=x[:, oc, :], func=AF.Square)
        S.op("pe", lambda h, sq=sq, oc=oc, p2=p2: h.matmul(p2[:, 0:n], lhsT=ones[:], rhs=sq[:, 0:n], start=(oc == 0), stop=(oc == 7)),
             reads=[sqk, "ones"], writes=[p2k])
    mean, msq, rstd = lnt
    E(S, "act", "activation", [p1k], ["ln_mean"], out=mean[:, 0:n], in_=p1[:, 0:n], func=AF.Copy, scale=1.0 / 1024.0)
    E(S, "pool", "tensor_tensor", ["ln_mean"], ["ln_msq"], out=msq[:, 0:n], in0=mean[:, 0:n], in1=mean[:, 0:n], op=ALU.mult)
    E(S, "dve", "scalar_tensor_tensor", [p2k, "ln_msq"], ["ln_rstd"], out=rstd[:, 0:n], in0=p2[:, 0:n], scalar=1.0 / 1024.0, in1=msq[:, 0:n],
      op0=ALU.mult, op1=ALU.subtract)
    E(S, "act", "activation", ["ln_rstd", "mcst"], ["ln_rstd"], out=rstd[:, 0:n], in_=rstd[:, 0:n], func=AF.Sqrt, bias=cst[:, 5:6], scale=1.0)
    E(S, "dve", "reciprocal", ["ln_rstd"], ["ln_rstd"], out=rstd[:, 0:n], in_=rstd[:, 0:n])
    for oc in range(8):
        eng = "dve" if oc % 2 == 0 else "pool"
        o = x[:, oc, :] if out is None else out[:, oc, :]
        E(S, eng, "tensor_tensor", [xkeys[oc], "ln_mean"], [xkeys[oc]], out=x[:, oc, :], in0=x[:, oc, :], in1=mean[:, 0:n], op=ALU.subtract)
        E(S, eng, "tensor_tensor", [xkeys[oc], "ln_rstd"], [xkeys[oc]], out=x[:, oc, :], in0=x[:, oc, :], in1=rstd[:, 0:n], op=ALU.mult)
        E(S, eng, "tensor_scalar", [xkeys[oc], "lng"], [okeys[oc]], out=o, in0=x[:, oc, :], scalar1=lng[:, gcol + oc:gcol + oc + 1],
          scalar2=lng[:, gcol + 8 + oc:gcol + 9 + oc], op0=ALU.mult, op1=ALU.add)


def ln_common(kb, st, I):
    S = kb.S
    ones = kb.sb(st, "ones", [128, 128], F32)
    E(S, "pool", "memset", [], ["ones"], ones[:], 1.0)
    lng = kb.sb(st, "lng", [128, 32], F32)
    kb.load(lng[:], I["lng"], writes=["lng"])
    cst = kb.sb(st, "mcst", [128, 8], F32)
    kb.load(cst[:], I["rcst"][:, 768:776], writes=["mcst"])
    mods = kb.sb(st, "mods", [128, 48, 2], F32)
    kb.load(mods[:].rearrange("p a b -> p (a b)"), I["mods"], writes=["mods"])
    return ones, lng, cst, mods


def emit_merge(kb, st, I, dbg):
    S = kb.S
    ones, lng, cst, mods = ln_common(kb, st, I)
    w_gt = kb.sb(st, "w_gt", [128, 8, 3072], BF16)
    w_br = kb.sb(st, "w_br", [128, 3, 4, 1024], BF16)
    w_out = kb.sb(st, "w_out", [128, 8, 1024], BF16)
    for b in range(3):
        kb.load_cast(w_gt[:, :, b * 1024:(b + 1) * 1024], wview(I["w_in"], C_GA + b * 1024, C_GA + (b + 1) * 1024), writes=[("w_gt", b)])
        kb.load_cast(w_br[:, b, :, :], I["w_br"][b].rearrange("(k p) n -> p k n", p=128), writes=[("w_br", b)])
    kb.load_cast(w_out[:], wview(I["w_out"], 0, 1024), writes=["w_out"])
    wkeys = [("w_gt", b) for b in range(3)] + [("w_br", b) for b in range(3)]
    hx_r = kb.ring_sb(st, "hxb", [128, 8, 512], BF16, 2)
    y_r = [kb.ring_sb(st, f"yb{b}", [128, 4, 512], BF16, 2) for b in range(3)]
    x_r = kb.ring_sb(st, "xb", [128, 8, 512], F32, 1)
    yT = kb.sb(st, "yTm", [128, 8, 512], BF16)
    x1 = kb.ring_sb(st, "x1", [128, 8, 512], F32, 1)
    ps_g = kb.ring_ps(st, "ps_g", [128, 512], 3)
    ps_b = kb.ring_ps(st, "ps_b", [128, 512], 3)
    ps_ln = kb.ring_ps(st, "ps_ln", [128, 512], 2)
    sg_r = kb.ring_sb(st, "sgm", [128, 512], F32, 3)
    acc_r = kb.ring_sb(st, "accm", [128, 512], F32, 2)
    tmp_r = kb.ring_sb(st, "tmpm", [128, 512], F32, 2)
    sq_r = kb.ring_sb(st, "sqm", [128, 512], F32, 2)
    lnt = (kb.sb(st, "ln_mean", [128, 512], F32), kb.sb(st, "ln_msq", [128, 512], F32), kb.sb(st, "ln_rstd", [128, 512], F32))
    srcs = [dbg["yaT"], dbg["ybT"], dbg["ycT"]]
    xv = I["xT"].rearrange("(j p) t -> p j t", p=128)
    for (b0, bn) in TOKBLKS:
        w = 0 if b0 < NT else 1
        hx, hxk = hx_r.next()
        kb.load(hx[:, :, 0:bn], I["hxT"][:, :, b0:b0 + bn], writes=[hxk])
        ys = []
        for b in range(3):
            yt, yk = y_r[b].next()
            kb.load(yt[:, :, 0:bn], srcs[b][:, :, b0:b0 + bn], writes=[yk])
            ys.append((yt, yk))
        xt, xk = x_r.next()
        kb.load(xt[:, :, 0:bn], xv[:, :, b0:b0 + bn], writes=[xk])
        for oc in range(8):
            ocs = slice(oc * 128, (oc + 1) * 128)
            acc, acck = acc_r.next()
            for b in range(3):
                pg, pgk = ps_g.next()
                kb.mm_group(pg[:, 0:bn], [(w_gt[:, k, b * 1024 + oc * 128:b * 1024 + (oc + 1) * 128], hx[:, k, 0:bn]) for k in range(8)],
                            reads=[hxk, ("w_gt", b)], writes=[pgk])
                pb, pbk = ps_b.next()
                kb.mm_group(pb[:, 0:bn], [(w_br[:, b, k, ocs], ys[b][0][:, k, 0:bn]) for k in range(4)], reads=[ys[b][1], ("w_br", b)], writes=[pbk])
                sgt, sgk = sg_r.next()
                E(S, "act", "activation", [pgk], [sgk], out=sgt[:, 0:bn], in_=pg[:, 0:bn], func=AF.Sigmoid)
                if b == 0:
                    E(S, "dve", "tensor_tensor", [sgk, pbk], [acck], out=acc[:, 0:bn], in0=pb[:, 0:bn], in1=sgt[:, 0:bn], op=ALU.mult)
                else:
                    E(S, "dve", "tensor_tensor", [sgk, pbk], [sgk], out=sgt[:, 0:bn], in0=pb[:, 0:bn], in1=sgt[:, 0:bn], op=ALU.mult)
                    if b == 1:
                        E(S, "pool", "tensor_tensor", [sgk, acck], [acck], out=acc[:, 0:bn], in0=acc[:, 0:bn], in1=sgt[:, 0:bn], op=ALU.add)
                    else:
                        E(S, "pool", "tensor_tensor", [sgk, acck], [("yTm", oc)], out=yT[:, oc, 0:bn], in0=acc[:, 0:bn], in1=sgt[:, 0:bn], op=ALU.add)
        x1t, x1k = x1.next()
        xkeys = [(x1k, oc) for oc in range(8)]
        for oc in range(8):
            ocs = slice(oc * 128, (oc + 1) * 128)
            pm, pmk = ps_g.next()
            kb.mm_group(pm[:, 0:bn], [(w_out[:, k, ocs], yT[:, k, 0:bn]) for k in range(8)], reads=[("yTm", k) for k in range(8)] + ["w_out"], writes=[pmk])
            tmp, tmpk = tmp_r.next()
            E(S, "act", "activation", [pmk, "mods"], [tmpk], out=tmp[:, 0:bn], in_=pm[:, 0:bn], func=AF.Copy, scale=mods[:, 16 + oc, w:w + 1])
            E(S, "dve", "scalar_tensor_tensor", [tmpk, xk], [xkeys[oc]], out=x1t[:, oc, 0:bn], in0=xt[:, oc, 0:bn], scalar=float(ALPHA), in1=tmp[:, 0:bn],
              op0=ALU.mult, op1=ALU.add)
        emit_ln(kb, S, x1t[:, :, 0:bn], xkeys, bn, ones, lng, 0, cst, ps_ln, sq_r, lnt, xkeys)
        kb.store(dbg["x1T"][:, :, b0:b0 + bn], x1t[:, :, 0:bn], reads=xkeys, writes=[("x1T_d", b0)], final=True)


def emit_ffn(kb, st, I, x1T_d, xoT, last=True):
    S = kb.S
    ones, lng, cst, mods = ln_common(kb, st, I)
    sc2p = kb.sb(st, "sc2p", [128, 8, 2], F32)
    E(S, "dve", "tensor_scalar_add", ["mods"], ["sc2p"], out=sc2p[:], in0=mods[:, 32:40, :], scalar1=1.0)
    w1 = kb.sb(st, "w_ff1", [128, 8, 4096], BF16)
    w2 = kb.sb(st, "w_ff2", [128, 32, 1024], BF16)
    for c in range(4):
        kb.load_cast(w1[:, :, c * 1024:(c + 1) * 1024], wview(I["w_ff1"], c * 1024, (c + 1) * 1024), writes=[("w1", c)])
    for c in range(4):
        kb.load_cast(w2[:, c * 8:(c + 1) * 8, :], I["w_ff2"].rearrange("(k p) n -> p k n", p=128)[:, c * 8:(c + 1) * 8, :], writes=[("w2", c)])
    w1k = [("w1", c) for c in range(4)]
    w2k = [("w2", c) for c in range(4)]
    n = FFBLK
    x_r = kb.ring_sb(st, "xf", [128, 8, n], F32, 3)
    h_r = kb.ring_sb(st, "hf", [128, 8, n], BF16, 2)
    a_r = kb.ring_sb(st, "af", [128, 32, n], BF16, 2)
    r_r = kb.ring_sb(st, "rf", [128, n], F32, 3)
    ps_f = kb.ring_ps(st, "ps_f", [128, n], 4)
    ps_ln = kb.ring_ps(st, "ps_lnf", [128, n], 2)
    tmp_r = kb.ring_sb(st, "tmpf", [128, n], F32, 2)
    sq_r = kb.ring_sb(st, "sqf", [128, n], F32, 2)
    lnt = (kb.sb(st, "ln_mean", [128, n], F32), kb.sb(st, "ln_msq", [128, n], F32), kb.sb(st, "ln_rstd", [128, n], F32))
    xov = xoT.rearrange("(j p) t -> p j t", p=128)
    blocks = list(range(0, T, n))
    ctx = {}

    def stage_A(b0):
        w = 0 if b0 < NT else 1
        xt, xk = x_r.next()
        kb.load(xt[:], x1T_d[:, :, b0:b0 + n], writes=[xk], reads=[("x1T_d", (b0 // 512) * 512)])
        ht, hk = h_r.next()
        for j in range(8):
            E(S, "dve" if j % 2 == 0 else "pool", "tensor_scalar", [xk, "sc2p", "mods"], [(hk, j)], out=ht[:, j, :], in0=xt[:, j, :],
              scalar1=sc2p[:, j, w:w + 1], scalar2=mods[:, 24 + j, w:w + 1], op0=ALU.mult, op1=ALU.add)
        at, ak = a_r.next()
        for fc in range(32):
            pf, pfk = ps_f.next()
            kb.mm_group(pf[:], [(w1[:, k, fc * 128:(fc + 1) * 128], ht[:, k, :]) for k in range(8)], reads=[(hk, j) for j in range(8)] + [("w1", fc // 8)], writes=[pfk])
            rt, rk = r_r.next()
            E(S, "act", "activation", [pfk], [rk], out=rt[:], in_=pf[:], func=AF.Relu)
            E(S, "pool" if fc % 2 == 0 else "dve", "tensor_tensor", [rk], [(ak, fc)], out=at[:, fc, :], in0=rt[:], in1=rt[:], op=ALU.mult)
        ctx[b0] = dict(w=w, xt=xt, xk=xk, hk=hk, at=at, ak=ak, xkeys=[(xk, oc) for oc in range(8)])

    def stage_B1(b0):
        c = ctx[b0]
        xt, xk, at, ak, w = c["xt"], c["xk"], c["at"], c["ak"], c["w"]
        for oc in range(8):
            pm, pmk = ps_f.next()
            kb.mm_group(pm[:], [(w2[:, k, oc * 128:(oc + 1) * 128], at[:, k, :]) for k in range(32)], reads=[(ak, fc) for fc in range(32)] + w2k, writes=[pmk])
            tmp, tmpk = tmp_r.next()
            E(S, "act", "activation", [pmk, "mods"], [tmpk], out=tmp[:], in_=pm[:], func=AF.Copy, scale=mods[:, 40 + oc, w:w + 1])
            E(S, "dve", "scalar_tensor_tensor", [tmpk, xk] + [(c["hk"], j) for j in range(8)], [c["xkeys"][oc]], out=xt[:, oc, :], in0=xt[:, oc, :], scalar=float(ALPHA), in1=tmp[:],
              op0=ALU.mult, op1=ALU.add)

    def stage_B2(b0):
        c = ctx.pop(b0)
        emit_ln(kb, S, c["xt"][:], c["xkeys"], n, ones, lng, 16, cst, ps_ln, sq_r, lnt, c["xkeys"])
        kb.store(xov[:, :, b0:b0 + n], c["xt"][:], reads=c["xkeys"], final=last)

    nb = len(blocks)
    stage_A(blocks[0])
    if nb > 1:
        stage_A(blocks[1])
    for i in range(nb):
        stage_B1(blocks[i])
        if i + 2 < nb:
            stage_A(blocks[i + 2])
        stage_B2(blocks[i])


_BF = ml_dtypes.bfloat16


def _bf(a):
    a = np.asarray(a)
    if a.dtype.kind == "V":
        a = a.view(_BF)
    return a


def na_bias_table(rpb):
    a = np.arange(2)[:, None, None, None]
    kc = np.arange(64)[None, :, None, None]
    e = np.arange(24)[None, None, :, None]
    qc = np.arange(64)[None, None, None, :]
    dr = 10 + a - e + 0 * kc + 0 * qc
    dc = kc - qc + 0 * a + 0 * e
    ok = (np.abs(dr) <= 7) & (np.abs(dc) <= 15)
    dri = np.clip(dr + 7, 0, 14)
    dci = np.clip(dc + 15, 0, 30)
    out = np.zeros((2, 64, 8, 24, 64), np.float32)
    for h in range(8):
        g = rpb[h][dri, dci]
        out[:, :, h] = np.where(ok, g, np.float32(0.0))
    return np.ascontiguousarray(out.reshape(128, 8, 24, 64))


def na_mask_table(q):
    R0 = 32 * q
    m = np.zeros((128, 4, 8, 8, 64), np.float32)
    kc = np.arange(64)[:, None]
    qc = np.arange(64)[None, :]
    cs = np.clip(qc - 8, 0, 48)
    colok = (kc >= cs) & (kc < cs + 16)
    for b in range(4):
        for t in range(8):
            for a in range(2):
                krow = R0 + 8 * b - 4 + 2 * t + a
                for j in range(8):
                    qrow = R0 + 8 * b + j
                    r0 = min(max(qrow - 4, 0), 120)
                    if 0 <= krow < 128 and r0 <= krow < r0 + 8:
                        m[a * 64:(a + 1) * 64, b, t, j, :] = colok
    return np.ascontiguousarray(m.reshape(128, 32, 512)).astype(_BF)


def prep_B(inp, l, core, xT_core, A):
    b, q = core // 4, core % 4
    grp = [b * 4 + i for i in range(4)]
    me = A[core]
    rr, rm = rope_tables(q)
    Fsl = np.zeros((64, 8, 3, 128), np.float32)
    fexp = np.zeros((128, 8), np.float32)
    for j in range(3):
        if q - 1 - j >= 0:
            Fsl[:, 0:4, j, :] = np.asarray(A[grp[q - 1 - j]]["retF"])[:, 0:4, :]
        if q + 1 + j <= 3:
            Fsl[:, 4:8, j, :] = np.asarray(A[grp[q + 1 + j]]["retF"])[:, 4:8, :]
        fexp[:, j] = NT * j
        fexp[:, 4 + j] = NT * j
    fexp[:, 3] = NT * q
    fexp[:, 7] = NT * (3 - q)
    kT = _bf(me["naKT"]); V = _bf(me["naV"])
    naKT = np.zeros((128, 4, NKEY), _BF)
    naV = np.zeros((NKEY, 8, 128), _BF)
    naKT[:, :, 256:2304] = kT[:, :, 0:NT]; naV[256:2304] = V[0:NT]
    naKT[:, :, 2560:] = kT[:, :, NT:]; naV[2560:] = V[NT:]
    if q > 0:
        p = A[grp[q - 1]]
        naKT[:, :, 0:256] = _bf(p["naKT"])[:, :, NT - 256:NT]; naV[0:256] = _bf(p["naV"])[NT - 256:NT]
    if q < 3:
        p = A[grp[q + 1]]
        naKT[:, :, 2304:2560] = _bf(p["naKT"])[:, :, 0:256]; naV[2304:2560] = _bf(p["naV"])[0:256]
    ck = np.concatenate([_bf(A[g]["ckvnT"])[:, :, 0:NT] for g in grp] + [_bf(me["ckvnT"])[:, :, NT:]], axis=2)
    kr = np.concatenate([_bf(A[g]["krT"])[:, 0:NT] for g in grp] + [_bf(me["krT"])[:, NT:]], axis=1)
    lng = np.concatenate([fm(inp["ln_gain"][l, 0]), fm(inp["ln_bias"][l, 0]), fm(inp["ln_gain"][l, 1]), fm(inp["ln_bias"][l, 1])], axis=1)
    return {
        "xT": xT_core, "mods": np.asarray(me["mods"]), "hxT": _bf(me["hxT"]), "retK": _bf(me["retK"]), "retKT": _bf(me["retKT"]),
        "retV": _bf(me["retV"]), "Fsl": Fsl, "fexp": fexp, "rld": np.ascontiguousarray(inp["ret_log_decay"][l].reshape(1, 8)),
        "rcst": ret_consts(), "gng": np.ascontiguousarray(inp["ret_gn_gain"][l].reshape(1, 512)), "rope_q": rr,
        "rope_mq": (rm * np.float32(96.0 ** -0.5)).astype(np.float32),
        "naKT": naKT, "naV": naV, "nabias": na_bias_table(inp["na_rpb"][l]), "namask": na_mask_table(q),
        "ckvnT": np.ascontiguousarray(ck), "krT": np.ascontiguousarray(kr),
        "qg": np.ascontiguousarray(inp["mla_q_norm"][l].reshape(1, 256)), "w_qup": inp["mla_w_qup"][l], "w_kvup": inp["mla_w_kvup"][l],
        "w_in": inp["w_in"][l],
        "w_br": np.ascontiguousarray(np.stack([inp["w_branch_ret"][l], inp["w_branch_na"][l], inp["w_branch_mla"][l]])),
        "w_out": inp["w_out"][l], "w_ff1": inp["w_ff1"][l], "w_ff2": inp["w_ff2"][l], "lng": np.ascontiguousarray(lng),
    }


XBW = 12288


def emit_exchange(kb, st, G):
    S = kb.S
    cc = kb.sb(st, "cc", [128, 32], F32)
    kb.load(cc[:], G["cc"], writes=["cc"])
    stg1 = kb.sb(st, "xstg1", [128, 4096], BF16)
    stg2 = kb.sb(st, "xstg2", [32, 2048], BF16)
    stg3 = kb.sb(st, "xstg3", [128, 6144], BF16)
    m1 = kb.sb(st, "xm1", [128, 4, 4096], BF16)
    m2 = kb.sb(st, "xm2", [32, 4, 2048], BF16)
    m3 = kb.sb(st, "xm3", [128, 4, 6144], BF16)
    fst = kb.sb(st, "xf", [64, 1024], F32)
    fm_ = kb.sb(st, "xfm", [64, 4, 1024], F32)
    xin = [g_.rearrange("(s p) c -> p s c", p=128) for g_ in G["XB_in"]]
    nvv = G["naV"].rearrange("(u p) h f -> p u (h f)", p=128)
    kb.load(fst[:], G["retF"].rearrange("p i v -> p (i v)"), writes=["xf"])
    kb.load(stg1[:].rearrange("p (k t) -> p k t", k=2), G["ckvnT"][:, :, 0:NT], writes=["xstg1"])
    kb.load(stg2[:], G["krT"][:, 0:NT], writes=["xstg2"])
    kb.load(stg3[:, 0:1024].rearrange("p (a t) -> p a t", a=4), G["naKT"][:, :, 0:256], writes=[("xstg3", 0)])
    kb.load(stg3[:, 1024:2048].rearrange("p (a t) -> p a t", a=4), G["naKT"][:, :, NT - 256:NT], writes=[("xstg3", 1)])
    kb.load(stg3[:, 2048:4096].rearrange("p (u f) -> p u f", u=2), nvv[:, 0:2, :], writes=[("xstg3", 2)])
    kb.load(stg3[:, 4096:6144].rearrange("p (u f) -> p u f", u=2), nvv[:, 14:16, :], writes=[("xstg3", 3)])
    rg = [[0, 1, 2, 3], [4, 5, 6, 7]]
    for s_ in range(4):
        E(S, "dve", "tensor_scalar_mul", ["xf", "cc"], [("xfm", s_)], out=fm_[:, s_, :], in0=fst[:], scalar1=cc[0:64, s_:s_ + 1])
    kb.store(G["FB_in"].rearrange("(s p) c -> p s c", p=64), fm_[:], reads=[("xfm", s_) for s_ in range(4)], writes=["FB_in"])
    fi, fo = G["FB_in"], G["FB_out"]
    S.coll(lambda h: h.collective_compute("AllReduce", ALU.add, replica_groups=rg, ins=[fi.opt()], outs=[fo.opt()]),
           reads=["FB_in"], writes=["FB_out"])
    for s_ in range(4):
        E(S, "dve", "tensor_scalar_mul", [("xstg3", j) for j in range(4)] + ["cc"], [("xm3", s_)], out=m3[:, s_, :], in0=stg3[:],
          scalar1=cc[:, s_:s_ + 1])
    for s_ in range(4):
        E(S, "dve", "tensor_scalar_mul", ["xstg2", "cc"], [("xm2", s_)], out=m2[:, s_, :], in0=stg2[:], scalar1=cc[0:32, s_:s_ + 1])
    kb.store(xin[1][0:32, :, 0:2048], m2[:], reads=[("xm2", s_) for s_ in range(4)], writes=[("XB_in", 1, "a")])
    kb.store(xin[1][:, :, 2048:4096], m3[:, :, 0:2048], reads=[("xm3", s_) for s_ in range(4)], writes=[("XB_in", 1, "b")], q="act")
    kb.store(xin[2][:, :, 0:4096], m3[:, :, 2048:6144], reads=[("xm3", s_) for s_ in range(4)], writes=[("XB_in", 2)])
    for s_ in range(4):
        E(S, "dve", "tensor_scalar_mul", ["xstg1", "cc"], [("xm1", s_)], out=m1[:, s_, :], in0=stg1[:], scalar1=cc[:, s_:s_ + 1])
    kb.store(xin[0][:, :, 0:4096], m1[:], reads=[("xm1", s_) for s_ in range(4)], writes=[("XB_in", 0)], q="act")
    rk = {0: [("XB_in", 0)], 1: [("XB_in", 1, "a"), ("XB_in", 1, "b"), "XB_in"], 2: [("XB_in", 2)]}
    for j in (1, 2, 0):
        xi, xo = G["XB_in"][j], G["XB_out"][j]
        S.coll(lambda h, xi=xi, xo=xo: h.collective_compute("AllReduce", ALU.add, replica_groups=rg, ins=[xi.opt()], outs=[xo.opt()]),
               reads=rk[j], writes=[("XB_out", j)])


def build_fused(n_layers=4):
    nc = bass.Bass("TRN2", target_bir_lowering=False)
    din = lambda name, shape, dt=F32: nc.dram_tensor(name, shape, dt, kind="ExternalInput").ap()
    dsc = lambda name, shape, dt=F32: nc.dram_tensor(name, shape, dt).ap()
    L = n_layers
    X = dict(
        xT=din("xT", [D, T]), c2=din("c2", [128, 16]), w_ada=din("w_ada", [L, D, 6 * D]), b_ada=din("b_ada", [L, 128, 48]),
        w_in=din("w_in", [L, D, 7200]), rld=din("rld", [L, 1, 8]), rcst=din("rcst", [128, 776]), kvg=din("kvg", [L, 1, 256]),
        rope_rk=din("rope_rk", [T, 64]), rope_m=din("rope_m", [T, 32]), rope_q=din("rope_q", [T, 64]), rope_mq=din("rope_mq", [T, 32]),
        gng=din("gng", [L, 1, 512]), nabias=din("nabias", [L, 128, 8, 24, 64]), namask=din("namask", [128, 32, 512], BF16),
        qg=din("qg", [L, 1, 256]), w_qup=din("w_qup", [L, 256, 768]), w_kvup=din("w_kvup", [L, 256, 1024]),
        w_br=din("w_br", [L, 3, 512, D]), w_out=din("w_out", [L, D, D]), w_ff1=din("w_ff1", [L, D, 4 * D]), w_ff2=din("w_ff2", [L, 4 * D, D]),
        lng=din("lng", [L, 128, 32]), cc=din("cc", [128, 32]),
    )
    xoT = nc.dram_tensor("xoT", [D, T], F32, kind="ExternalOutput").ap()
    G = dict(
        mods=dsc("mods_d", [128, 96]), hxT=dsc("hxT_d", [128, 8, T], BF16), retK=dsc("retK_d", [T, 256], BF16),
        retKT=dsc("retKT_d", [64, 4, T], BF16), retV=dsc("retV_d", [T, 512], BF16), retF=dsc("retF_d", [64, 8, 128]),
        naKT=dsc("naKT_d", [128, 4, T], BF16), naV=dsc("naV_d", [T, 8, 128], BF16), ckvnT=dsc("ckvnT_d", [128, 2, T], BF16),
        krT=dsc("krT_d", [32, T], BF16), XB_in=[dsc(f"XB_in{j}", [512, 4096], BF16) for j in range(3)],
        XB_out=[dsc(f"XB_out{j}", [512, 4096], BF16) for j in range(3)],
        FB_in=dsc("FB_in", [256, 1024]), FB_out=dsc("FB_out", [256, 1024]),
        yaT=dsc("yaT_d", [128, 4, T], BF16), ybT=dsc("ybT_d", [128, 4, T], BF16), ycT=dsc("ycT_d", [128, 4, T], BF16),
        x1T=dsc("x1T_d", [128, 8, T]), cc=X["cc"], modsN=dsc("modsN_d", [128, 96]),
    )
    xbuf = [dsc("xping", [D, T]), dsc("xpong", [D, T])]
    with ExitStack() as st0:
        S = Sched(nc, st0)
        kb = KB(nc, S)
        with ExitStack() as ph:
            zt = kb.sb(ph, "zt", [128, 4, 2048], BF16)
            E(S, "pool", "memset", [], ["zt"], zt[:], 0.0)
            kb.store(G["XB_in"][1].rearrange("(s p) c -> p s c", p=128)[:, :, 0:2048], zt[:], reads=["zt"], writes=["XB_in"])
            S.barrier(); S.emit()
        for l in range(L):
            x_in = X["xT"] if l == 0 else xbuf[(l - 1) % 2]
            x_out = xoT if l == L - 1 else xbuf[l % 2]
            with ExitStack() as st:
                hxT = kb.sb(st, "hxT", [128, 8, T], BF16)
                mods = kb.sb(st, "mods", [128, 48, 2], F32)
                ident, _ = make_ident(kb, st)
                with ExitStack() as p1:
                    emit_mod_hx(kb, p1, x_in, X["c2"], X["w_ada"][l], X["b_ada"][l], mods, hxT,
                                mods_src=(G["modsN"] if (l > 0 and MOD_PREFETCH) else None))
                    kb.store(G["mods"], mods[:].rearrange("p a b -> p (a b)"), reads=["mods"])
                    kb.store(G["hxT"], hxT[:], reads=[("hxT", j) for j in range(8)])
                    S.barrier(); S.emit()
                with ExitStack() as p2:
                    emit_kv_side(kb, p2, hxT, ident, X["w_in"][l], X["rld"][l], X["rcst"], X["kvg"][l], X["rope_rk"], X["rope_m"],
                                 G["retK"], G["retKT"], G["retV"], G["retF"], G["naKT"], G["naV"], G["ckvnT"], G["krT"])
                    S.barrier(); S.emit()
            with ExitStack() as ph:
                emit_exchange(kb, ph, G)
                S.bar_coll = False
                S.barrier(); S.emit()
            I = dict(G)
            I.update(xT=x_in, rld=X["rld"][l], rcst=X["rcst"], gng=X["gng"][l], rope_q=X["rope_q"], rope_mq=X["rope_mq"],
                     nabias=X["nabias"][l], namask=X["namask"], qg=X["qg"][l], w_qup=X["w_qup"][l], w_kvup=X["w_kvup"][l],
                     w_in=X["w_in"][l], w_br=X["w_br"][l], w_out=X["w_out"][l], w_ff1=X["w_ff1"][l], w_ff2=X["w_ff2"][l], lng=X["lng"][l])
            dbg = dict(yaT=G["yaT"], ybT=G["ybT"], ycT=G["ycT"], x1T=G["x1T"])
            with ExitStack() as ph:
                emit_retention(kb, ph, I, dbg["yaT"])
                S.barrier(); S.emit()
            mpre = None
            if MOD_PREFETCH and l + 1 < L:
                mpre = (lambda kb_, st_, l=l: ModPrefetch(kb_, st_, X["c2"], X["w_ada"][l + 1], X["b_ada"][l + 1], G["modsN"]))
            with ExitStack() as ph:
                emit_na(kb, ph, I, dbg["ybT"], modpre=mpre)
                S.barrier(); S.emit()
            with ExitStack() as ph:
                emit_mla(kb, ph, I, dbg["ycT"], modpre=None)
                S.barrier(); S.emit()
            with ExitStack() as ph:
                emit_merge(kb, ph, I, dbg)
                S.barrier(); S.emit()
            with ExitStack() as ph:
                emit_ffn(kb, ph, I, dbg["x1T"], x_out, last=(l == L - 1))
                S.barrier(); S.emit(final=(l == L - 1))
    return nc


def core_consts(q):
    cc = np.zeros((128, 32), np.float32)
    cc[:, q] = 1.0
    if q > 0:
        cc[:, 4 + q - 1] = 1.0
    if q < 3:
        cc[:, 8 + q + 1] = 1.0
    for s_ in range(4):
        if s_ < q:
            cc[:, 12 + s_] = NT * (q - 1 - s_)
            cc[:, 22 + s_] = 1.0
        if s_ > q:
            cc[:, 17 + s_] = NT * (s_ - q - 1)
            cc[:, 26 + s_] = 1.0
    cc[:, 16] = NT * q
    cc[:, 21] = NT * (3 - q)
    return cc


def prep_fused(inp, core, L=4):
    b, q = core // 4, core % 4
    rr, rm = rope_tables(q)
    xc = np.concatenate([inp["x"][b, q * NT:(q + 1) * NT], inp["ctx"][b]], axis=0)
    c2 = np.stack([inp["c"][b], inp["c_ctx"]], axis=-1)
    c2 = np.ascontiguousarray(c2.reshape(8, 128, 2).transpose(1, 0, 2).reshape(128, 16))
    lng = np.stack([np.concatenate([fm(inp["ln_gain"][l, 0]), fm(inp["ln_bias"][l, 0]), fm(inp["ln_gain"][l, 1]), fm(inp["ln_bias"][l, 1])], axis=1)
                    for l in range(L)])
    return {
        "xT": np.ascontiguousarray(xc.T), "c2": c2, "w_ada": inp["w_ada"][:L], "b_ada": np.stack([fm(inp["b_ada"][l]) for l in range(L)]),
        "w_in": inp["w_in"][:L], "rld": np.ascontiguousarray(inp["ret_log_decay"][:L].reshape(L, 1, 8)), "rcst": ret_consts(),
        "kvg": np.ascontiguousarray(inp["mla_kv_norm"][:L].reshape(L, 1, 256)),
        "rope_rk": (rr * np.float32(0.125)).astype(np.float32), "rope_m": rm, "rope_q": rr,
        "rope_mq": (rm * np.float32(96.0 ** -0.5)).astype(np.float32),
        "gng": np.ascontiguousarray(inp["ret_gn_gain"][:L].reshape(L, 1, 512)),
        "nabias": np.stack([na_bias_table(inp["na_rpb"][l]) for l in range(L)]), "namask": na_mask_table(q),
        "qg": np.ascontiguousarray(inp["mla_q_norm"][:L].reshape(L, 1, 256)), "w_qup": inp["mla_w_qup"][:L], "w_kvup": inp["mla_w_kvup"][:L],
        "w_br": np.ascontiguousarray(np.stack([inp["w_branch_ret"][:L], inp["w_branch_na"][:L], inp["w_branch_mla"][:L]], axis=1)),
        "w_out": inp["w_out"][:L], "w_ff1": inp["w_ff1"][:L], "w_ff2": inp["w_ff2"][:L], "lng": np.ascontiguousarray(lng),
        "cc": core_consts(q),
    }


_NC = {}


def kernel(**inp):
    inp = {k: np.asarray(v) for k, v in inp.items()}
    if "F" not in _NC:
        _NC["F"] = build_fused(4)
    res = run_bass_kernel_spmd(_NC["F"], [prep_fused(inp, c) for c in range(8)], core_ids=list(range(8))).results
    out = np.zeros((2, 8192, D), np.float32)
    for core in range(8):
        b, q = core // 4, core % 4
        out[b, q * NT:(q + 1) * NT] = np.asarray(res[core]["xoT"])[:, 0:NT].T
    return out
```

```python
import numpy as np
from contextlib import ExitStack
import ml_dtypes
import concourse.bass as bass
import concourse.mybir as mybir
from concourse.bass_utils import run_bass_kernel_spmd

F32 = mybir.dt.float32
BF16 = mybir.dt.bfloat16
AF = mybir.ActivationFunctionType
ALU = mybir.AluOpType
AX = mybir.AxisListType

D = 1024
NT = 2048
NZ = 256
T = NT + NZ
NTILE = T // 128
EPS = 1e-5
ALPHA = 8.0 ** 0.25
EPOCH = 20000
C_RQ, C_RK, C_RV, C_GF, C_GB, C_NQ, C_NK, C_NV, C_MQ, C_MKV, C_MKR, C_GA, C_GBR, C_GC = (
    0, 256, 512, 1024, 1536, 2048, 2560, 3072, 3584, 3840, 4096, 4128, 5152, 6176)
TOKBLKS = [(0, 512), (512, 512), (1024, 512), (1536, 512), (2048, 256)]


class Sched:
    ENGS = ("pe", "dve", "act", "pool", "sp")

    def __init__(self, nc, stack, n_dma_sems=12, n_eng_sems=8):
        self.nc = nc
        self.ops = {e: [] for e in self.ENGS}
        self.esems = {e: [stack.enter_context(nc.semaphore(f"s_{e}_{i}")) for i in range(n_eng_sems)]
                      for e in ("pe", "dve", "act", "pool")}
        self.ecnt = {e: 0 for e in ("pe", "dve", "act", "pool")}
        self.eep = {e: 0 for e in ("pe", "dve", "act", "pool")}
        self.dsems = {q: [stack.enter_context(nc.semaphore(f"d_{q}_{i}")) for i in range(n_dma_sems)]
                      for q in ("sp", "act", "pool")}
        self.dval = {q: [0] * n_dma_sems for q in ("sp", "act", "pool")}
        self.drr = {q: 0 for q in ("sp", "act", "pool")}
        self.lastw = {}
        self.reads = {}
        self.seen = {e: {} for e in self.ENGS}
        self.final_tokens = []
        self.n_ops = 0
        self.csem = stack.enter_context(nc.semaphore("s_coll"))
        self.cval = 0

    def _need(self, eng, tok, waits):
        if tok is None:
            return
        sem, val = tok
        if eng == "pe" and any(sem is x for x in self.esems["pe"]):
            return
        sid = id(sem)
        cur = self.seen[eng].get(sid)
        if cur is not None and cur >= val:
            return
        self.seen[eng][sid] = val
        waits.append((sem, val))

    def _deps(self, eng, reads, writes):
        waits = []
        for k in reads:
            self._need(eng, self.lastw.get(k), waits)
        for k in writes:
            self._need(eng, self.lastw.get(k), waits)
            for t in self.reads.get(k, ()):
                self._need(eng, t, waits)
        return waits

    def _commit(self, tok, reads, writes):
        for k in reads:
            self.reads.setdefault(k, []).append(tok)
        for k in writes:
            self.lastw[k] = tok
            self.reads[k] = []

    def op(self, eng, fn, reads=(), writes=()):
        waits = self._deps(eng, reads, writes)
        if self.ecnt[eng] >= EPOCH:
            self.eep[eng] += 1
            self.ecnt[eng] = 0
        sem = self.esems[eng][self.eep[eng]]
        self.ecnt[eng] += 1
        tok = (sem, self.ecnt[eng])
        self.ops[eng].append((waits, fn, (sem, 1)))
        self._commit(tok, reads, writes)
        self.n_ops += 1
        return tok

    def dma(self, q, fn, reads=(), writes=(), final=False):
        waits = self._deps(q, reads, writes)
        i = self.drr[q]
        self.drr[q] = (i + 1) % len(self.dsems[q])
        sem = self.dsems[q][i]
        prev = self.dval[q][i]
        if prev > 0:
            self._need(q, (sem, prev), waits)
        self.dval[q][i] = prev + 16
        tok = (sem, prev + 16)
        self.ops[q].append((waits, fn, (sem, 16)))
        self._commit(tok, reads, writes)
        if final:
            self.final_tokens.append(tok)
        self.n_ops += 1
        return tok

    def coll(self, fn, reads=(), writes=()):
        waits = self._deps("pool", reads, writes)
        if not hasattr(self, "csem"):
            raise RuntimeError("no collective semaphore")
        self.cval += 1
        tok = (self.csem, self.cval)
        self.ops["pool"].append((waits, fn, (self.csem, 1)))
        self._commit(tok, reads, writes)
        self.ctoks = tok
        return tok

    def barrier(self):
        toks = []
        if getattr(self, "cval", 0) > 0 and getattr(self, "bar_coll", True):
            toks.append((self.csem, self.cval))
        for e in ("pe", "dve", "act", "pool"):
            if self.ecnt[e] > 0:
                toks.append((self.esems[e][self.eep[e]], self.ecnt[e]))
        for q in ("sp", "act", "pool"):
            for i, v in enumerate(self.dval[q]):
                if v > 0:
                    toks.append((self.dsems[q][i], v))
        for e in self.ENGS:
            waits = []
            for t in toks:
                self._need(e, t, waits)
            if waits:
                self.ops[e].append((waits, None, None))

    def emit(self, final=False):
        nc = self.nc
        fin = []
        if final:
            mx = {}
            for (sm, v) in self.final_tokens:
                if id(sm) not in mx or mx[id(sm)][1] < v:
                    mx[id(sm)] = (sm, v)
            fin = list(mx.values())
        handles = {"pe": "tensor", "dve": "vector", "act": "scalar", "pool": "gpsimd", "sp": "sync"}
        with nc.Block() as block:
            for e in self.ENGS:
                ops = self.ops[e]
                extra = fin if e == "sp" else []
                if not ops and not extra:
                    continue

                def body(h, ops=ops, extra=extra):
                    for waits, fn, si in ops:
                        for (ws, wv) in waits:
                            h.wait_ge(ws, wv)
                        if fn is not None:
                            fn(h).then_inc(si[0], si[1])
                    for (ws, wv) in extra:
                        h.wait_ge(ws, wv)

                getattr(block, handles[e])(body)
        self.ops = {e: [] for e in self.ENGS}


class Ring:
    def __init__(self, tiles, name, keys=None):
        self.tiles = tiles
        self.name = name
        self.keys = keys
        self.i = 0

    def next(self):
        j = self.i % len(self.tiles)
        t = self.tiles[j]
        k = (self.name, j) if self.keys is None else self.keys[j]
        self.i += 1
        return t, k


class KB:
    def __init__(self, nc, S):
        self.nc = nc
        self.S = S
        self.dq = 0
        self.uid = 0

    def sb(self, st, name, shape, dt):
        self.uid += 1
        return st.enter_context(self.nc.sbuf_tensor(f"sb{self.uid}_{name}", shape, dt))

    def ps(self, st, name, shape, dt=F32):
        self.uid += 1
        return st.enter_context(self.nc.psum_tensor(f"ps{self.uid}_{name}", shape, dt))

    def ring_sb(self, st, name, shape, dt, n):
        return Ring([self.sb(st, f"{name}{i}", shape, dt) for i in range(n)], name)

    def ring_ps_sliced(self, st, name, width, n, per_bank=4):
        tiles, keys = [], []
        nb = (n + per_bank - 1) // per_bank
        for b in range(nb):
            t = self.ps(st, f"{name}{b}", [128, width * per_bank], F32)
            for j in range(per_bank):
                if len(tiles) < n:
                    tiles.append(t[:, j * width:(j + 1) * width])
                    keys.append((name, "bank", b))
        return Ring(tiles, name, keys)

    def ring_ps(self, st, name, shape, n, dt=F32):
        return Ring([self.ps(st, f"{name}{i}", shape, dt) for i in range(n)], name)

    def load(self, out, in_, writes, reads=(), q=None):
        if q is None:
            q = ("sp", "act")[self.dq % 2]
            self.dq += 1
        return self.S.dma(q, lambda h: h.dma_start(out=out, in_=in_), reads=reads, writes=writes)

    def load_cast(self, out, in_, writes, reads=()):
        return self.S.dma("pool", lambda h: h.dma_start(out=out, in_=in_), reads=reads, writes=writes)

    def store(self, out, in_, reads, writes=(), final=False, q="sp"):
        return self.S.dma(q, lambda h: h.dma_start(out=out, in_=in_), reads=reads, writes=writes, final=final)

    def mm_group(self, out, pairs, reads, writes):
        n = len(pairs)

        def fn(h):
            r = None
            for i, (l, rr) in enumerate(pairs):
                r = h.matmul(out, lhsT=l, rhs=rr, start=(i == 0), stop=(i == n - 1))
            return r
        return self.S.op("pe", fn, reads=reads, writes=writes)


def wview(w, c0, c1):
    return w.rearrange("(k p) n -> p k n", p=128)[:, :, c0:c1]


def emit_ret_tables(kb, st, rld, cst):
    S = kb.S
    lg = kb.sb(st, "lg", [128, 8], F32)
    cs = kb.sb(st, "cst", [128, 128 * 6 + 8], F32)
    DT = kb.sb(st, "DT", [128, 8, 128], F32)
    QD = kb.sb(st, "QD", [128, 8, 128], F32)
    kdec = kb.sb(st, "kdec", [128, 8], F32)
    cd = kb.sb(st, "cd", [128, 8], F32)
    tmp = kb.sb(st, "rt_tmp", [128, 128], F32)
    kb.load(lg[:], rld.partition_broadcast(128), writes=["lg"])
    kb.load(cs[:], cst, writes=["cst"])
    S.op("act", lambda h: h.activation(out=lg[:], in_=lg[:], func=AF.Exp), reads=["lg"], writes=["lg"])
    S.op("act", lambda h: h.activation(out=lg[:], in_=lg[:], func=AF.Ln, scale=-1.0, bias=cs[:, 768 + 4:768 + 5]),
         reads=["lg", "cst"], writes=["lg"])
    for d in range(2):
        for hh in range(4):
            i = d * 4 + hh
            S.op("act", lambda h, i=i, d=d: h.activation(out=tmp[:], in_=cs[:, d * 128:(d + 1) * 128], func=AF.Exp,
                                                          scale=lg[:, i:i + 1]),
                 reads=["lg", "cst"], writes=["rt_tmp"])
            S.op("dve", lambda h, i=i, d=d: h.tensor_tensor(out=DT[:, i, :], in0=tmp[:], in1=cs[:, (2 + d) * 128:(3 + d) * 128],
                                                             op=ALU.mult),
                 reads=["rt_tmp", "cst"], writes=["DT"])
            S.op("act", lambda h, i=i, d=d: h.activation(out=QD[:, i, :], in_=cs[:, (4 + d) * 128:(5 + d) * 128], func=AF.Exp,
                                                          scale=lg[:, i:i + 1]),
                 reads=["lg", "cst"], writes=["QD"])
            S.op("act", lambda h, i=i, d=d: h.activation(out=kdec[:, i:i + 1], in_=cs[:, 768 + d:768 + d + 1], func=AF.Exp,
                                                          scale=lg[:, i:i + 1]),
                 reads=["lg", "cst"], writes=["kdec"])
    S.op("act", lambda h: h.activation(out=cd[:], in_=lg[:], func=AF.Exp, scale=128.0), reads=["lg"], writes=["cd"])
    return dict(lg=lg, DT=DT, QD=QD, kdec=kdec, cd=cd, cs=cs)


def ret_consts():
    s = np.arange(128)[:, None].astype(np.float32)
    c = np.arange(128)[None, :].astype(np.float32)
    E0 = np.maximum(c - s, 0.0)
    E1 = np.maximum(s - c, 0.0)
    M0 = (c >= s).astype(np.float32)
    M1 = (s >= c).astype(np.float32)
    R0 = np.broadcast_to(c + 1.0, (128, 128))
    R1 = np.broadcast_to(128.0 - c, (128, 128))
    tail = np.zeros((128, 8), np.float32)
    tail[:, 0] = 127.0 - s[:, 0]
    tail[:, 1] = s[:, 0]
    tail[:, 4] = 1.0
    tail[:, 5] = EPS
    return np.concatenate([E0, E1, M0, M1, R0, R1, tail], axis=1).astype(np.float32)


def build_A():
    nc = bass.Bass("TRN2", target_bir_lowering=False)
    din = lambda name, shape, dt=F32: nc.dram_tensor(name, shape, dt, kind="ExternalInput").ap()
    dout = lambda name, shape, dt=F32: nc.dram_tensor(name, shape, dt, kind="ExternalOutput").ap()
    xT = din("xT", [D, T])
    c2 = din("c2", [128, 16])
    w_ada = din("w_ada", [D, 6 * D])
    b_ada = din("b_ada", [128, 48])
    w_in = din("w_in", [D, 7200])
    rld = din("rld", [1, 8])
    rcst = din("rcst", [128, 776])
    kvg = din("kvg", [1, 256])
    rope_r = din("rope_r", [T, 64])
    rope_m = din("rope_m", [T, 32])
    o_mods = dout("mods", [128, 96])
    o_hxT = dout("hxT", [128, 8, T], BF16)
    o_retK = dout("retK", [T, 256], BF16)
    o_retKT = dout("retKT", [64, 4, T], BF16)
    o_retV = dout("retV", [T, 512], BF16)
    o_retF = dout("retF", [64, 8, 128])
    o_naKT = dout("naKT", [128, 4, T], BF16)
    o_naV = dout("naV", [T, 8, 128], BF16)
    o_ckvnT = dout("ckvnT", [128, 2, T], BF16)
    o_krT = dout("krT", [32, T], BF16)

    with ExitStack() as st0:
        S = Sched(nc, st0)
        kb = KB(nc, S)
        with ExitStack() as st:
            hxT = kb.sb(st, "hxT", [128, 8, T], BF16)
            mods = kb.sb(st, "mods", [128, 48, 2], F32)
            ident = kb.sb(st, "ident", [128, 128], BF16)
            identf = kb.sb(st, "identf", [128, 128], F32)
            S.op("pool", lambda h: h.memset(identf[:], 0.0), writes=["identf"])
            S.op("pool", lambda h: h.affine_select(out=identf[:], in_=identf[:], pattern=[[-1, 128]],
                                                     compare_op=ALU.not_equal, fill=1.0, base=0, channel_multiplier=1),
                 reads=["identf"], writes=["identf"])
            S.op("dve", lambda h: h.tensor_copy(out=ident[:], in_=identf[:]), reads=["identf"], writes=["ident"])
            with ExitStack() as p1:
                emit_mod_hx(kb, p1, xT, c2, w_ada, b_ada, mods, hxT)
                kb.store(o_mods, mods[:].rearrange("p a b -> p (a b)"), reads=["mods"], final=True)
                kb.store(o_hxT, hxT[:], reads=[("hxT", j) for j in range(8)], final=True)
                S.barrier()
                S.emit()
            with ExitStack() as p2:
                emit_kv_side(kb, p2, hxT, ident, w_in, rld, rcst, kvg, rope_r, rope_m,
                             o_retK, o_retKT, o_retV, o_retF, o_naKT, o_naV, o_ckvnT, o_krT)
                S.barrier()
                S.emit(final=True)
    return nc


def emit_mod_hx(kb, st, xT, c2, w_ada, b_ada, mods, hxT, mods_src=None):
    S = kb.S
    if mods_src is not None:
        sc1p = kb.sb(st, "sc1p", [128, 8, 2], F32)
        xs = kb.ring_sb(st, "xs", [128, T], F32, 2)
        kb.load(mods[:].rearrange("p a b -> p (a b)"), mods_src, writes=["mods"])
        emit_hx_only(kb, S, xT, mods, sc1p, xs, hxT)
        return
    c2s = kb.sb(st, "c2s", [128, 16], F32)
    bad = kb.sb(st, "bad", [128, 48], F32)
    sc1p = kb.sb(st, "sc1p", [128, 8, 2], F32)
    wa = kb.ring_sb(st, "wa", [128, 8, 768], F32, 2)
    mps = kb.ps(st, "mod_ps", [128, 48, 2])
    xs = kb.ring_sb(st, "xs", [128, T], F32, 2)
    kb.load(c2s[:], c2, writes=["c2s"])
    kb.load(bad[:], b_ada, writes=["bad"])
    S.op("act", lambda h: h.activation(out=c2s[:], in_=c2s[:], func=AF.Silu), reads=["c2s"], writes=["c2s"])
    c2v = c2s[:].rearrange("p (k w) -> p k w", w=2)
    if MOD_ROWMAJOR:
        mrow = kb.sb(st, "mrow", [2, 6 * D], F32)
        identf2 = kb.sb(st, "identf2", [128, 128], F32)
        E(S, "pool", "memset", [], ["identf2"], identf2[:], 0.0)
        E(S, "pool", "affine_select", ["identf2"], ["identf2"], out=identf2[:], in_=identf2[:], pattern=[[-1, 128]],
          compare_op=ALU.not_equal, fill=1.0, base=0, channel_multiplier=1)
        mrp = kb.ring_ps(st, "mrow_ps", [2, 768], 2)
        for blk in range(8):
            wt, wk = wa.next()
            kb.load(wt[:], wview(w_ada, blk * 768, (blk + 1) * 768), writes=[wk])
            mp, mpk = mrp.next()
            kb.mm_group(mp[:, 0:512], [(c2v[:, k, :], wt[:, k, 0:512]) for k in range(8)], reads=[wk, "c2s"], writes=[(mpk, 0)])
            kb.mm_group(mp[:, 512:768], [(c2v[:, k, :], wt[:, k, 512:768]) for k in range(8)], reads=[wk, "c2s"], writes=[(mpk, 1)])
            E(S, "act", "copy", [(mpk, 0), (mpk, 1)], [("mrow", blk)], out=mrow[:, blk * 768:(blk + 1) * 768], in_=mp[:, :])
        for j in range(48):
            E(S, "pe", "transpose", [("mrow", j // 6), "identf2"], [("mps", j)], mps[:, j, :], mrow[:, j * 128:(j + 1) * 128], identf2[0:2, 0:2])
    else:
        for blk in range(8):
            wt, wk = wa.next()
            kb.load(wt[:], wview(w_ada, blk * 768, (blk + 1) * 768), writes=[wk])
            for jj in range(6):
                j = blk * 6 + jj
                kb.mm_group(mps[:, j, :], [(wt[:, k, jj * 128:(jj + 1) * 128], c2v[:, k, :]) for k in range(8)],
                            reads=[wk, "c2s"], writes=[("mps", j)])
    for w in range(2):
        S.op("dve", lambda h, w=w: h.tensor_tensor(out=mods[:, :, w], in0=mps[:, :, w], in1=bad[:], op=ALU.add),
             reads=[("mps", j) for j in range(48)] + ["bad"], writes=["mods"])
    emit_hx_only(kb, S, xT, mods, sc1p, xs, hxT)


def emit_hx_only(kb, S, xT, mods, sc1p, xs, hxT):
    S.op("dve", lambda h: h.tensor_scalar_add(out=sc1p[:], in0=mods[:, 8:16, :], scalar1=1.0), reads=["mods"], writes=["sc1p"])
    xv = xT.rearrange("(j p) t -> p j t", p=128)
    for j in range(8):
        xt, xk = xs.next()
        kb.load(xt[:], xv[:, j, :], writes=[xk])
        eng = "dve" if j % 2 == 0 else "pool"
        S.op(eng, lambda h, j=j, xt=xt: h.tensor_scalar(out=hxT[:, j, 0:NT], in0=xt[:, 0:NT], scalar1=sc1p[:, j, 0:1],
                                                        scalar2=mods[:, j, 0:1], op0=ALU.mult, op1=ALU.add),
             reads=[xk, "sc1p", "mods"], writes=[("hxTa", j)])
        S.op(eng, lambda h, j=j, xt=xt: h.tensor_scalar(out=hxT[:, j, NT:T], in0=xt[:, NT:T], scalar1=sc1p[:, j, 1:2],
                                                        scalar2=mods[:, j, 1:2], op0=ALU.mult, op1=ALU.add),
             reads=[xk, "sc1p", "mods", ("hxTa", j)], writes=[("hxT", j)])


def rope_tm(S, eng, out, src, cos, sin, t1, t2, nh, half, rkeys, wkey, tkey):
    cb = cos.unsqueeze(1).to_broadcast([128, nh, half]) if nh > 1 else cos
    sn = sin.unsqueeze(1).to_broadcast([128, nh, half]) if nh > 1 else sin
    if nh > 1:
        x1, x2 = src[:, :, 0:half], src[:, :, half:2 * half]
        o1, o2 = out[:, :, 0:half], out[:, :, half:2 * half]
    else:
        x1, x2 = src[:, 0:half], src[:, half:2 * half]
        o1, o2 = out[:, 0:half], out[:, half:2 * half]
    S.op(eng, lambda h: h.tensor_tensor(out=t1, in0=x1, in1=cb, op=ALU.mult), reads=rkeys, writes=[tkey + "1"])
    S.op(eng, lambda h: h.tensor_tensor(out=t2, in0=x2, in1=sn, op=ALU.mult), reads=rkeys, writes=[tkey + "2"])
    S.op(eng, lambda h: h.tensor_tensor(out=o1, in0=t1, in1=t2, op=ALU.subtract), reads=[tkey + "1", tkey + "2"], writes=[(wkey, "a")])
    S.op(eng, lambda h: h.tensor_tensor(out=t1, in0=x1, in1=sn, op=ALU.mult), reads=rkeys + [(wkey, "a")], writes=[tkey + "1"])
    S.op(eng, lambda h: h.tensor_tensor(out=t2, in0=x2, in1=cb, op=ALU.mult), reads=rkeys + [(wkey, "a")], writes=[tkey + "2"])
    S.op(eng, lambda h: h.tensor_tensor(out=o2, in0=t1, in1=t2, op=ALU.add), reads=[tkey + "1", tkey + "2", (wkey, "a")], writes=[wkey])


def emit_kv_side(kb, st, hxT, ident, w_in, rld, rcst, kvg, rope_r, rope_m,
                 o_retK, o_retKT, o_retV, o_retF, o_naKT, o_naV, o_ckvnT, o_krT):
    S = kb.S
    hx_keys = [("hxT", j) for j in range(8)]
    RT = emit_ret_tables(kb, st, rld, rcst)
    w_rkv = kb.sb(st, "w_rkv", [128, 8, 768], BF16)
    w_nk = kb.sb(st, "w_nk", [128, 8, 512], BF16)
    w_nv = kb.sb(st, "w_nv", [128, 8, 512], BF16)
    w_mk = kb.sb(st, "w_mk", [128, 8, 288], BF16)
    kb.load_cast(w_rkv[:], wview(w_in, C_RK, C_GF), writes=["w_rkv"])
    kb.load_cast(w_mk[:], wview(w_in, C_MKV, C_GA), writes=["w_mk"])
    kb.load_cast(w_nv[:], wview(w_in, C_NV, C_MQ), writes=["w_nv"])
    kb.load_cast(w_nk[:], wview(w_in, C_NK, C_NV), writes=["w_nk"])
    rr = kb.sb(st, "rr", [128, NTILE, 64], F32)
    rm = kb.sb(st, "rm", [128, NTILE, 32], F32)
    kb.load(rr[:], rope_r.rearrange("(t p) f -> p t f", p=128), writes=["rr"])
    kb.load(rm[:], rope_m.rearrange("(t p) f -> p t f", p=128), writes=["rm"])
    gain = kb.sb(st, "kvgain", [128, 256], F32)
    kb.load(gain[:], kvg.partition_broadcast(128), writes=["kvgain"])
    k_tm = kb.sb(st, "k_tm", [128, NTILE, 256], BF16)
    v_tm = kb.sb(st, "v_tm", [128, NTILE, 512], BF16)
    kT = kb.sb(st, "kT", [64, 4, T], BF16)
    vaug_r = kb.ring_sb(st, "vaug", [128, 8, 128], BF16, 2)
    ckvnT = kb.sb(st, "ckvnT", [128, 2, T], BF16)
    krT = kb.sb(st, "krT", [32, T], BF16)
    naKT_r = kb.ring_sb(st, "naKT", [128, T], BF16, 2)
    for _ in range(2):
        vt_, vk_ = vaug_r.next()
        S.op("pool", lambda h, vt_=vt_: h.memset(vt_[:], 1.0), writes=[vk_])
    ps_a = kb.ring_ps(st, "ps_a", [128, 512], 2)
    ps_b = kb.ring_ps(st, "ps_b", [128, 512], 2)
    ps_t = kb.ring_ps(st, "ps_t", [128, 128], 2, BF16)
    kf = kb.ring_sb(st, "kf", [128, 256], F32, 2)
    kr32 = kb.ring_sb(st, "kr32", [128, 32], F32, 2)
    krb = kb.ring_sb(st, "krb", [128, 32], BF16, 2)
    ckvn = kb.ring_sb(st, "ckvn", [128, 256], BF16, 2)
    t1 = kb.sb(st, "t1", [128, 128], F32)
    t2 = kb.sb(st, "t2", [128, 128], F32)
    t1b = kb.sb(st, "t1b", [128, 128], F32)
    t2b = kb.sb(st, "t2b", [128, 128], F32)
    u1 = kb.sb(st, "u1", [128, 16], F32)
    u2 = kb.sb(st, "u2", [128, 16], F32)
    junk = kb.sb(st, "junk", [128, 256], F32)
    ss = kb.ring_sb(st, "ss", [128, 2], F32, 2)
    eps_ap = RT["cs"][:, 768 + 5:768 + 6]
    Fst = kb.sb(st, "Fst", [64, 8, 128], F32)
    S.op("dve", lambda h: h.memset(Fst[:], 0.0), writes=[("F", i) for i in range(8)])
    kd_r = kb.ring_sb(st, "kdA", [128, 64], BF16, 4)
    ps_sA = kb.ring_ps(st, "ps_sA", [64, 128], 2)

    def scan_step(d, t):
        for hh in range(4):
            i = d * 4 + hh
            kdt, kdk = kd_r.next()
            E(S, "act", "activation", [("k_tm", t), "kdec"], [kdk], out=kdt[:], in_=k_tm[:, t, hh * 64:(hh + 1) * 64], func=AF.Copy,
              scale=RT["kdec"][:, i:i + 1])
            pst, psk = ps_sA.next()
            kb.mm_group(pst[:], [(kdt[:], v_tm[:, t, hh * 128:(hh + 1) * 128])], reads=[kdk, ("v_tm", t)], writes=[psk])
            E(S, "dve", "scalar_tensor_tensor", [psk, ("F", i), "cd"], [("F", i)], out=Fst[:, i, :], in0=Fst[:, i, :], scalar=RT["cd"][0:64, i:i + 1],
              in1=pst[:], op0=ALU.mult, op1=ALU.add)

    for t in range(NTILE):
        tok = slice(t * 128, (t + 1) * 128)
        if 1 <= t <= 16:
            scan_step(0, t - 1)
        pa, pak = ps_a.next()
        kb.mm_group(pa[:, 0:512], [(hxT[:, k, tok], w_rkv[:, k, 0:512]) for k in range(8)],
                    reads=hx_keys + ["w_rkv"], writes=[pak])
        pb, pbk = ps_b.next()
        kb.mm_group(pb[:, 0:256], [(hxT[:, k, tok], w_rkv[:, k, 512:768]) for k in range(8)],
                    reads=hx_keys + ["w_rkv"], writes=[pbk])
        S.op("act", lambda h, pa=pa, t=t: h.copy(out=v_tm[:, t, 0:256], in_=pa[:, 256:512]), reads=[pak], writes=[("v_tm_a", t)])
        S.op("act", lambda h, pb=pb, t=t: h.copy(out=v_tm[:, t, 256:512], in_=pb[:, 0:256]), reads=[pbk, ("v_tm_a", t)], writes=[("v_tm", t)])
        kft, kfk = kf.next()
        S.op("act", lambda h, pa=pa, kft=kft: h.copy(out=kft[:], in_=pa[:, 0:256]), reads=[pak], writes=[kfk])
        rope_tm(S, "pool" if t % 2 == 0 else "dve", k_tm[:, t, :].rearrange("p (h f) -> p h f", h=4), kft[:].rearrange("p (h f) -> p h f", h=4),
                rr[:, t, 0:32], rr[:, t, 32:64], (t1 if t % 2 == 0 else t1b)[:].rearrange("p (h f) -> p h f", h=4),
                (t2 if t % 2 == 0 else t2b)[:].rearrange("p (h f) -> p h f", h=4),
                4, 32, [kfk, "rr"], ("k_tm", t), "ropeA" if t % 2 == 0 else "ropeAb")
        kb.store(o_retK[tok, :], k_tm[:, t, :], reads=[("k_tm", t)], final=True)
        kb.store(o_retV[tok, :], v_tm[:, t, :], reads=[("v_tm", t)], final=True, q="act")
        for hh in range(4):
            pt, ptk = ps_t.next()
            S.op("pe", lambda h, pt=pt, t=t, hh=hh: h.transpose(pt[0:64, :], k_tm[:, t, hh * 64:(hh + 1) * 64], ident[:]),
                 reads=[("k_tm", t), "ident"], writes=[ptk])
            S.op("dve", lambda h, pt=pt, hh=hh, tok=tok: h.tensor_copy(out=kT[:, hh, tok], in_=pt[0:64, :]), reads=[ptk], writes=[("kT", t, hh)])
        pa, pak = ps_a.next()
        kb.mm_group(pa[:, 0:512], [(hxT[:, k, tok], w_nv[:, k, :]) for k in range(8)], reads=hx_keys + ["w_nv"], writes=[pak])
        vg, vgk = vaug_r.next()
        S.op("act", lambda h, pa=pa, vg=vg: h.copy(out=vg[:, :, 0:64], in_=pa[:, 0:512].rearrange("p (h f) -> p h f", h=8)),
             reads=[pak, vgk], writes=[vgk])
        kb.store(o_naV[tok, :, :], vg[:], reads=[vgk], final=True)
        pb, pbk = ps_b.next()
        kb.mm_group(pb[:, 0:288], [(hxT[:, k, tok], w_mk[:, k, :]) for k in range(8)], reads=hx_keys + ["w_mk"], writes=[pbk])
        sst, ssk = ss.next()
        S.op("act", lambda h, pb=pb, sst=sst: h.activation(out=junk[:], in_=pb[:, 0:256], func=AF.Square, accum_out=sst[:, 0:1]),
             reads=[pbk], writes=["junk", ssk])
        S.op("act", lambda h, sst=sst: h.activation(out=sst[:, 1:2], in_=sst[:, 0:1], func=AF.Sqrt, scale=1.0 / 256.0, bias=eps_ap),
             reads=[ssk, "cst"], writes=[ssk])
        k32, k32k = kr32.next()
        S.op("act", lambda h, pb=pb, k32=k32: h.copy(out=k32[:], in_=pb[:, 256:288]), reads=[pbk], writes=[k32k])
        S.op("dve", lambda h, sst=sst: h.reciprocal(out=sst[:, 1:2], in_=sst[:, 1:2]), reads=[ssk], writes=[ssk])
        cn, cnk = ckvn.next()
        S.op("dve", lambda h, pb=pb, sst=sst, cn=cn: h.scalar_tensor_tensor(out=cn[:], in0=pb[:, 0:256], scalar=sst[:, 1:2], in1=gain[:],
                                                                           op0=ALU.mult, op1=ALU.mult),
             reads=[pbk, ssk, "kvgain", k32k], writes=[cnk])
        kbt, kbk = krb.next()
        rope_tm(S, "dve", kbt[:], k32[:], rm[:, t, 0:16], rm[:, t, 16:32], u1[:], u2[:], 1, 16, [k32k, "rm"], kbk, "ropeB")
        for kc in range(2):
            pt, ptk = ps_t.next()
            S.op("pe", lambda h, pt=pt, cn=cn, kc=kc: h.transpose(pt[:], cn[:, kc * 128:(kc + 1) * 128], ident[:]),
                 reads=[cnk, "ident"], writes=[ptk])
            S.op("dve", lambda h, pt=pt, kc=kc, tok=tok: h.tensor_copy(out=ckvnT[:, kc, tok], in_=pt[:]), reads=[ptk], writes=[("ckvnT", t, kc)])
        pt, ptk = ps_t.next()
        S.op("pe", lambda h, pt=pt, kbt=kbt: h.transpose(pt[0:32, :], kbt[:], ident[:]), reads=[kbk, "ident"], writes=[ptk])
        S.op("dve", lambda h, pt=pt, tok=tok: h.tensor_copy(out=krT[:, tok], in_=pt[0:32, :]), reads=[ptk], writes=[("krT", t)])
    bwd_t = list(range(15, -1, -1))
    for hp in range(4):
        nk, nkk = naKT_r.next()
        for (b0, bn) in TOKBLKS:
            if bwd_t:
                scan_step(1, bwd_t.pop(0))
            pa, pak = ps_a.next()
            kb.mm_group(pa[:, 0:bn], [(w_nk[:, k, hp * 128:(hp + 1) * 128], hxT[:, k, b0:b0 + bn]) for k in range(8)],
                        reads=hx_keys + ["w_nk"], writes=[pak])
            S.op("act", lambda h, pa=pa, nk=nk, b0=b0, bn=bn: h.copy(out=nk[:, b0:b0 + bn], in_=pa[:, 0:bn]),
                 reads=[pak], writes=[nkk])
        kb.store(o_naKT[:, hp, :], nk[:], reads=[nkk], final=True)
    kb.store(o_ckvnT, ckvnT[:], reads=[("ckvnT", t, kc) for t in range(NTILE) for kc in range(2)], final=True)
    kb.store(o_krT, krT[:], reads=[("krT", t) for t in range(NTILE)], final=True)
    kb.store(o_retKT, kT[:], reads=[("kT", t, hh) for t in range(NTILE) for hh in range(4)], final=True)
    while bwd_t:
        scan_step(1, bwd_t.pop(0))
    kb.store(o_retF, Fst[:], reads=[("F", i) for i in range(8)], final=True)


def emit_ret_scan(kb, st, RT, k_tm, v_tm, Sst, tiles, sprev, tag):
    S = kb.S
    kd = kb.ring_sb(st, "kd" + tag, [128, 64], BF16, 3)
    ps_s = kb.ring_ps(st, "ps_s" + tag, [64, 128], 2)
    for d in range(2):
        order = tiles if d == 0 else tiles[::-1]
        for t in order:
            for hh in range(4):
                i = d * 4 + hh
                if sprev is not None:
                    S.op("act", lambda h, i=i, t=t: h.copy(out=sprev[:, i, t, :], in_=Sst[:, i, :]), reads=[("F", i)],
                         writes=[("sprev", i, t)])
                kdt, kdk = kd.next()
                S.op("act", lambda h, kdt=kdt, t=t, hh=hh, i=i: h.activation(out=kdt[:], in_=k_tm[:, t, hh * 64:(hh + 1) * 64], func=AF.Copy,
                                                                              scale=RT["kdec"][:, i:i + 1]),
                     reads=[("k_tm", t), "kdec"], writes=[kdk])
                pst, psk = ps_s.next()
                kb.mm_group(pst[:], [(kdt[:], v_tm[:, t, hh * 128:(hh + 1) * 128])], reads=[kdk, ("v_tm", t)], writes=[psk])
                S.op("dve", lambda h, pst=pst, i=i: h.scalar_tensor_tensor(out=Sst[:, i, :], in0=Sst[:, i, :], scalar=RT["cd"][0:64, i:i + 1],
                                                                            in1=pst[:], op0=ALU.mult, op1=ALU.add),
                     reads=[psk, ("F", i), "cd"], writes=[("F", i)])


def rope_tables(q):
    idx = q * NT + np.arange(NT)
    row = (idx // 64).astype(np.float32)
    col = (idx % 64).astype(np.float32)

    def tab(rot_dim):
        nf = rot_dim // 4
        inv = (10000.0 ** (-2.0 * np.arange(nf, dtype=np.float32) / (rot_dim // 2))).astype(np.float32)
        ang = np.concatenate([row[:, None] * inv, col[:, None] * inv], axis=-1).astype(np.float32)
        cs = np.concatenate([np.cos(ang), np.sin(ang)], axis=-1).astype(np.float32)
        z = np.concatenate([np.ones((NZ, rot_dim // 2), np.float32), np.zeros((NZ, rot_dim // 2), np.float32)], axis=-1)
        return np.concatenate([cs, z], axis=0)
    return tab(64), tab(32)


def fm(v):
    return np.ascontiguousarray(v.reshape(-1, 128).T)


def prep_A(inp, l, core, xT_core):
    b, q = core // 4, core % 4
    rr, rm = rope_tables(q)
    c2 = np.stack([inp["c"][b], inp["c_ctx"]], axis=-1)
    c2 = np.ascontiguousarray(c2.reshape(8, 128, 2).transpose(1, 0, 2).reshape(128, 16))
    return {
        "xT": xT_core, "c2": c2, "w_ada": inp["w_ada"][l], "b_ada": fm(inp["b_ada"][l]),
        "w_in": inp["w_in"][l], "rld": np.ascontiguousarray(inp["ret_log_decay"][l].reshape(1, 8)),
        "rcst": ret_consts(), "kvg": np.ascontiguousarray(inp["mla_kv_norm"][l].reshape(1, 256)),
        "rope_r": (rr * np.float32(0.125)).astype(np.float32), "rope_m": rm,
    }


NKEY = 2816
NKM = 8192 + 256
FFBLK = 256
MOD_ROWMAJOR = True
MOD_PREFETCH = True
MLA_PAIRS = False


def E(S, eng, meth, reads, writes, *args, **kw):
    return S.op(eng, lambda h: getattr(h, meth)(*args, **kw), reads=reads, writes=writes)


def make_ident(kb, st):
    S = kb.S
    ident = kb.sb(st, "ident", [128, 128], BF16)
    identf = kb.sb(st, "identf", [128, 128], F32)
    E(S, "pool", "memset", [], ["identf"], identf[:], 0.0)
    E(S, "pool", "affine_select", ["identf"], ["identf"], out=identf[:], in_=identf[:], pattern=[[-1, 128]],
      compare_op=ALU.not_equal, fill=1.0, base=0, channel_multiplier=1)
    E(S, "dve", "tensor_copy", ["identf"], ["ident"], out=ident[:], in_=identf[:])
    return ident, identf


def rope2(S, eng, x1, x2, o1, o2, cb, sn, t1, t2, rkeys, wkey, tkey):
    E(S, eng, "tensor_tensor", rkeys, [(tkey, 1)], out=t1, in0=x1, in1=cb, op=ALU.mult)
    E(S, eng, "tensor_tensor", rkeys, [(tkey, 2)], out=t2, in0=x2, in1=sn, op=ALU.mult)
    E(S, eng, "tensor_tensor", [(tkey, 1), (tkey, 2)], [(wkey, "a")], out=o1, in0=t1, in1=t2, op=ALU.subtract)
    E(S, eng, "tensor_tensor", rkeys + [(wkey, "a")], [(tkey, 1)], out=t1, in0=x1, in1=sn, op=ALU.mult)
    E(S, eng, "tensor_tensor", rkeys + [(wkey, "a")], [(tkey, 2)], out=t2, in0=x2, in1=cb, op=ALU.mult)
    E(S, eng, "tensor_tensor", [(tkey, 1), (tkey, 2), (wkey, "a")], [wkey], out=o2, in0=t1, in1=t2, op=ALU.add)


def build_B():
    nc = bass.Bass("TRN2", target_bir_lowering=False)
    din = lambda name, shape, dt=F32: nc.dram_tensor(name, shape, dt, kind="ExternalInput").ap()
    dout = lambda name, shape, dt=F32: nc.dram_tensor(name, shape, dt, kind="ExternalOutput").ap()
    I = dict(
        xT=din("xT", [D, T]), mods=din("mods", [128, 96]), hxT=din("hxT", [128, 8, T], BF16),
        retK=din("retK", [T, 256], BF16), retKT=din("retKT", [64, 4, T], BF16), retV=din("retV", [T, 512], BF16),
        Fsl=din("Fsl", [64, 8, 3, 128]), fexp=din("fexp", [128, 8]), rld=din("rld", [1, 8]), rcst=din("rcst", [128, 776]),
        gng=din("gng", [1, 512]), rope_q=din("rope_q", [T, 64]), rope_mq=din("rope_mq", [T, 32]),
        naKT=din("naKT", [128, 4, NKEY], BF16), naV=din("naV", [NKEY, 8, 128], BF16),
        nabias=din("nabias", [128, 8, 24, 64]), namask=din("namask", [128, 32, 512], BF16),
        ckvnT=din("ckvnT", [128, 2, NKM], BF16), krT=din("krT", [32, NKM], BF16),
        qg=din("qg", [1, 256]), w_qup=din("w_qup", [256, 768]), w_kvup=din("w_kvup", [256, 1024]),
        w_in=din("w_in", [D, 7200]), w_br=din("w_br", [3, 512, D]), w_out=din("w_out", [D, D]),
        w_ff1=din("w_ff1", [D, 4 * D]), w_ff2=din("w_ff2", [4 * D, D]), lng=din("lng", [128, 32]),
    )
    xoT = dout("xoT", [D, T])
    dbg = dict(yaT=dout("yaT", [128, 4, T], BF16), ybT=dout("ybT", [128, 4, T], BF16), ycT=dout("ycT", [128, 4, T], BF16),
               x1T=dout("x1T", [128, 8, T]))
    with ExitStack() as st0:
        S = Sched(nc, st0)
        kb = KB(nc, S)
        with ExitStack() as ph:
            emit_retention(kb, ph, I, dbg["yaT"])
            S.barrier(); S.emit()
        with ExitStack() as ph:
            emit_na(kb, ph, I, dbg["ybT"])
            S.barrier(); S.emit()
        with ExitStack() as ph:
            emit_mla(kb, ph, I, dbg["ycT"])
            S.barrier(); S.emit()
        with ExitStack() as ph:
            emit_merge(kb, ph, I, dbg)
            S.barrier(); S.emit()
        with ExitStack() as ph:
            emit_ffn(kb, ph, I, dbg["x1T"], xoT)
            S.barrier(); S.emit(final=True)
    return nc


def emit_retention(kb, st, I, o_yaT):
    S = kb.S
    RT = emit_ret_tables(kb, st, I["rld"], I["rcst"])
    ident, _ = make_ident(kb, st)
    eps_ap = RT["cs"][:, 768 + 5:768 + 6]
    hxT = kb.sb(st, "hxT", [128, 8, T], BF16)
    kb.load(hxT[:], I["hxT"], writes=["hxT"])
    k_tm = kb.sb(st, "k_tm", [128, NTILE, 256], BF16)
    v_tm = kb.sb(st, "v_tm", [128, NTILE, 512], BF16)
    kT = kb.sb(st, "kT", [64, 4, T], BF16)
    qT = kb.sb(st, "qT", [64, 4, T], BF16)
    kb.load(k_tm[:], I["retK"].rearrange("(t p) f -> p t f", p=128), writes=["k_tm"])
    kb.load(v_tm[:], I["retV"].rearrange("(t p) f -> p t f", p=128), writes=["v_tm"])
    kb.load(kT[:], I["retKT"], writes=["kT"])
    w_g = kb.sb(st, "w_g", [128, 8, 1024], BF16)
    kb.load_cast(w_g[:], wview(I["w_in"], C_GF, C_NQ), writes=["w_g"])
    gng = kb.sb(st, "gng", [128, 512], F32)
    kb.load(gng[:], I["gng"].partition_broadcast(128), writes=["gng"])
    Fsl = kb.sb(st, "Fsl", [64, 8, 4, 128], F32)
    for s_ in range(4):
        kb.load(Fsl[:, :, s_, :], I["FB_out"][s_ * 64:(s_ + 1) * 64, :].rearrange("p (i v) -> p i v", i=8), writes=[("Fsl", s_)], reads=["FB_out"])
    cc = kb.sb(st, "cc", [128, 32], F32)
    kb.load(cc[:], I["cc"], writes=["cc"])
    coef = kb.sb(st, "coef", [128, 8, 5], F32)
    for d in range(2):
        fo = 12 if d == 0 else 17
        mo = 22 if d == 0 else 26
        for hh in range(4):
            i = d * 4 + hh
            E(S, "act", "activation", ["cc", "lg"], [("coefe", i)], out=coef[:, i, :], in_=cc[:, fo:fo + 5], func=AF.Exp, scale=RT["lg"][:, i:i + 1])
            E(S, "dve", "tensor_tensor", [("coefe", i), "cc"], [("coef", i)], out=coef[:, i, 0:4], in0=coef[:, i, 0:4], in1=cc[:, mo:mo + 4], op=ALU.mult)
    ya = kb.sb(st, "ya_acc", [128, NTILE, 512], F32)
    Sst = kb.sb(st, "Sst", [64, 8, 128], F32)
    hv = lambda ap: ap.rearrange("p (h f) -> p h f", h=4)
    with ExitStack() as s2:
        w_q = kb.sb(s2, "w_q", [128, 8, 256], BF16)
        kb.load_cast(w_q[:], wview(I["w_in"], C_RQ, C_RK), writes=["w_q"])
        rq = kb.sb(s2, "rq", [128, NTILE, 64], F32)
        kb.load(rq[:], I["rope_q"].rearrange("(t p) f -> p t f", p=128), writes=["rq"])
        ps_q = kb.ring_ps(s2, "ps_q", [128, 256], 2)
        ps_t = kb.ring_ps(s2, "ps_tq", [128, 128], 4, BF16)
        qf = kb.ring_sb(s2, "qf", [128, 256], F32, 3)
        qb = kb.ring_sb(s2, "qb", [128, 256], BF16, 3)
        t1 = kb.ring_sb(s2, "t1", [128, 128], F32, 2)
        t2 = kb.ring_sb(s2, "t2", [128, 128], F32, 2)
        for t in range(NTILE):
            tok = slice(t * 128, (t + 1) * 128)
            pq, pqk = ps_q.next()
            kb.mm_group(pq[:], [(hxT[:, k, tok], w_q[:, k, :]) for k in range(8)], reads=["hxT", "w_q"], writes=[pqk])
            qft, qfk = qf.next()
            E(S, "act", "copy", [pqk], [qfk], out=qft[:], in_=pq[:])
            qbt, qbk = qb.next()
            cb = rq[:, t, 0:32].unsqueeze(1).to_broadcast([128, 4, 32])
            sn = rq[:, t, 32:64].unsqueeze(1).to_broadcast([128, 4, 32])
            t1t, t1k = t1.next()
            t2t, _ = t2.next()
            rope2(S, "dve" if t % 2 == 0 else "pool", hv(qft[:])[:, :, 0:32], hv(qft[:])[:, :, 32:64], hv(qbt[:])[:, :, 0:32], hv(qbt[:])[:, :, 32:64],
                  cb, sn, hv(t1t[:]), hv(t2t[:]), [qfk, "rq"], qbk, t1k)
            for hh in range(4):
                pt, ptk = ps_t.next()
                E(S, "pe", "transpose", [qbk, "ident"], [ptk], pt[0:64, :], qbt[:, hh * 64:(hh + 1) * 64], ident[:])
                E(S, "act" if hh % 2 else "dve", "copy" if hh % 2 else "tensor_copy", [ptk], [("qT", t, hh)], out=qT[:, hh, tok], in_=pt[0:64, :])
        S.barrier(); S.emit()
    import os as _os
    if _os.environ.get("RET_STOP") == "pre":
        return
    NCH = 8
    ps_g = kb.ring_ps(st, "ps_g", [128, 512], 2)
    ps_sc = kb.ring_ps_sliced(st, "ps_sc", 128, NCH)
    ps_o = kb.ring_ps_sliced(st, "ps_o", 128, NCH)
    ps_t = kb.ring_ps(st, "ps_t", [128, 128], 1, BF16)
    sg = kb.ring_sb(st, "sg", [128, 512], F32, 2)
    sT = kb.ring_sb(st, "sT", [128, 128], BF16, NCH)
    qs = kb.ring_sb(st, "qs", [64, 128], BF16, NCH)
    Sb = kb.ring_sb(st, "Sb", [64, 128], BF16, NCH)
    kd = kb.ring_sb(st, "kd", [128, 64], BF16, NCH)
    stats = kb.ring_sb(st, "stats", [128, 6], F32, NCH)
    mv = kb.ring_sb(st, "mv", [128, 4], F32, NCH)
    yn = kb.ring_sb(st, "yn", [128, 128], F32, NCH)
    yab = kb.ring_sb(st, "yab", [128, 512], BF16, 2)
    yaT_r = kb.ring_sb(st, "yaT", [128, 4, 128], BF16, 2)
    E(S, "pool", "memset", [], [("ya", t) for t in range(NTILE)], ya[:], 0.0)
    E(S, "dve", "memset", [], [("F", i) for i in range(8)], Sst[:], 0.0)

    def do_pairs(pairs):
        ch = []
        for (d, t) in pairs:
            tok = slice(t * 128, (t + 1) * 128)
            pg, pgk = ps_g.next()
            kb.mm_group(pg[:], [(hxT[:, k, tok], w_g[:, k, d * 512:(d + 1) * 512]) for k in range(8)], reads=["hxT", "w_g"], writes=[pgk])
            sgt, sgk = sg.next()
            E(S, "act", "activation", [pgk], [sgk], out=sgt[:], in_=pg[:], func=AF.Silu)
            for hh in range(4):
                ch.append(dict(d=d, t=t, hh=hh, i=d * 4 + hh, tok=tok, sg=sgt, sgk=sgk))
        for c in ch:
            c["psc"], c["psck"] = ps_sc.next()
            kb.mm_group(c["psc"][:], [(kT[:, c["hh"], c["tok"]], qT[:, c["hh"], c["tok"]])], reads=["kT", ("qT", c["t"], c["hh"])], writes=[c["psck"]])
        for c in ch:
            i, hh, t, tok = c["i"], c["hh"], c["t"], c["tok"]
            c["sT"], c["sTk"] = sT.next()
            E(S, "dve", "tensor_tensor", [c["psck"], "DT"], [c["sTk"]], out=c["sT"][:], in0=c["psc"][:], in1=RT["DT"][:, i, :], op=ALU.mult)
            c["qs"], c["qsk"] = qs.next()
            E(S, "pool", "tensor_tensor", [("qT", t, hh), "QD"], [c["qsk"]], out=c["qs"][:], in0=qT[:, hh, tok], in1=RT["QD"][0:64, i, :], op=ALU.mult)
            c["Sb"], c["Sbk"] = Sb.next()
            E(S, "act", "copy", [("F", i)], [c["Sbk"]], out=c["Sb"][:], in_=Sst[:, i, :])
            c["kd"], c["kdk"] = kd.next()
            E(S, "act", "activation", ["k_tm", "kdec"], [c["kdk"]], out=c["kd"][:], in_=k_tm[:, t, hh * 64:(hh + 1) * 64], func=AF.Copy, scale=RT["kdec"][:, i:i + 1])
        for c in ch:
            hh, t = c["hh"], c["t"]
            c["po"], c["pok"] = ps_o.next()
            kb.mm_group(c["po"][:], [(c["sT"][:], v_tm[:, t, hh * 128:(hh + 1) * 128]), (c["qs"][:], c["Sb"][:])],
                        reads=[c["sTk"], "v_tm", c["qsk"], c["Sbk"]], writes=[c["pok"]])
            kb.mm_group(c["psc"][0:64, :], [(c["kd"][:], v_tm[:, t, hh * 128:(hh + 1) * 128])], reads=[c["kdk"], "v_tm", c["sTk"]], writes=[c["psck"]])
        for c in ch:
            i = c["i"]
            E(S, "dve", "scalar_tensor_tensor", [c["psck"], ("F", i), "cd", c["Sbk"]], [("F", i)], out=Sst[:, i, :], in0=Sst[:, i, :],
              scalar=RT["cd"][0:64, i:i + 1], in1=c["psc"][0:64, :], op0=ALU.mult, op1=ALU.add)
            c["st"], c["stk"] = stats.next()
            E(S, "dve", "bn_stats", [c["pok"]], [c["stk"]], out=c["st"][:], in_=c["po"][:])
        for c in ch:
            c["mv"], c["mvk"] = mv.next()
            E(S, "dve", "bn_aggr", [c["stk"]], [c["mvk"]], out=c["mv"][:, 0:2], in_=c["st"][:])
        for c in ch:
            E(S, "act", "activation", [c["mvk"], "cst"], [c["mvk"]], out=c["mv"][:, 2:3], in_=c["mv"][:, 1:2], func=AF.Sqrt, bias=eps_ap, scale=1.0)
        for c in ch:
            E(S, "dve", "reciprocal", [c["mvk"]], [c["mvk"]], out=c["mv"][:, 2:3], in_=c["mv"][:, 2:3])
        for c in ch:
            c["yn"], c["ynk"] = yn.next()
            E(S, "dve", "tensor_scalar", [c["pok"], c["mvk"]], [c["ynk"]], out=c["yn"][:], in0=c["po"][:], scalar1=c["mv"][:, 0:1], scalar2=c["mv"][:, 2:3],
              op0=ALU.subtract, op1=ALU.mult)
        for c in ch:
            hh, t = c["hh"], c["t"]
            yslc = ya[:, t, hh * 128:(hh + 1) * 128]
            E(S, "pool", "tensor_tensor", [c["ynk"], c["sgk"]], [c["ynk"]], out=c["yn"][:], in0=c["yn"][:], in1=c["sg"][:, hh * 128:(hh + 1) * 128], op=ALU.mult)
            E(S, "pool", "tensor_tensor", [c["ynk"], ("ya", t)], [("ya", t)], out=yslc, in0=yslc, in1=c["yn"][:], op=ALU.add)

    if _os.environ.get("RET_STOP") == "memset":
        return
    do_pairs([(0, 16), (1, 17)])
    if _os.environ.get("RET_STOP") == "one":
        return
    do_pairs([(0, 17), (1, 16)])
    for d in range(2):
        for hh in range(4):
            i = d * 4 + hh
            E(S, "dve", "tensor_scalar_mul", [("F", i), ("coef", i)], [("F", i)], out=Sst[:, i, :], in0=Sst[:, i, :], scalar1=coef[0:64, i, 4:5])
            for j in range(4):
                E(S, "dve", "scalar_tensor_tensor", [("F", i), ("coef", i), ("Fsl", j)], [("F", i)], out=Sst[:, i, :], in0=Fsl[:, i, j, :],
                  scalar=coef[0:64, i, j:j + 1], in1=Sst[:, i, :], op0=ALU.mult, op1=ALU.add)
    for k in range(16):
        do_pairs([(0, k), (1, 15 - k)])
    for t in range(NTILE):
        tok = slice(t * 128, (t + 1) * 128)
        ybt, ybk = yab.next()
        E(S, "dve", "tensor_tensor", [("ya", t), "gng"], [ybk], out=ybt[:], in0=ya[:, t, :], in1=gng[:], op=ALU.mult)
        yat, yatk = yaT_r.next()
        for kc in range(4):
            pt, ptk = ps_t.next()
            E(S, "pe", "transpose", [ybk, "ident"], [ptk], pt[:], ybt[:, kc * 128:(kc + 1) * 128], ident[:])
            E(S, "act", "copy", [ptk], [(yatk, kc)], out=yat[:, kc, :], in_=pt[:])
        kb.store(o_yaT[:, :, tok], yat[:], reads=[(yatk, kc) for kc in range(4)], writes=["yaT_d"])


class AttnPipe:
    def __init__(self, kb, S, rings, ident=None, depth=2):
        self.kb, self.S, self.rings, self.ident, self.depth = kb, S, rings, ident, depth
        self.steps = []

    def add(self, kT_list, q_ap, qkeys, v_list, out_ap, okey, extra=None):
        n = len(kT_list)
        blk = dict(q=q_ap, qk=list(qkeys), out=out_ap, okey=okey, n=n, nq=q_ap.shape[-1], po=None)
        for i in range(n):
            self.steps.append(dict(blk=blk, i=i, k=kT_list[i], v=v_list[i], ex=(extra[i] if extra is not None else None)))

    def _qk(self, st):
        ps_s = self.rings[0]
        blk = st["blk"]; nq = blk["nq"]
        ps, psk = ps_s.next()
        st["ps"], st["psk"] = ps, psk
        kl, kkeys = st["k"]
        pairs = [(kl, blk["q"])]
        rk = list(kkeys) + blk["qk"]
        if st["ex"] is not None:
            pairs.append((self.ident, st["ex"][0]))
            rk += list(st["ex"][2]) + ["ident"]
        self.kb.mm_group(ps[:, 0:nq], pairs, reads=rk, writes=[psk])

    def _exp(self, st):
        S = self.S
        _, _, e_r, p_r, _ = self.rings
        nq = st["blk"]["nq"]
        et, ek = e_r.next()
        E(S, "act", "activation", [st["psk"]], [ek], out=et[:, 0:nq], in_=st["ps"][:, 0:nq], func=AF.Exp)
        if st["ex"] is not None:
            pt, pk = p_r.next()
            E(S, "dve", "tensor_tensor", [ek] + list(st["ex"][2]), [pk], out=pt[:, 0:nq], in0=et[:, 0:nq], in1=st["ex"][1], op=ALU.mult)
            et, ek = pt, pk
        st["e"], st["ek"] = et, ek

    def _pv(self, st):
        S = self.S
        _, ps_o, _, _, rden_r = self.rings
        blk = st["blk"]; nq = blk["nq"]; i = st["i"]; n = blk["n"]
        if i == 0:
            blk["po"], blk["pok"] = ps_o.next()
        po, pok = blk["po"], blk["pok"]
        vl, vkeys = st["v"]
        et = st["e"]
        S.op("pe", lambda h: h.matmul(po[:, 0:nq], lhsT=vl, rhs=et[:, 0:nq], start=(i == 0), stop=(i == n - 1)),
             reads=[st["ek"]] + list(vkeys), writes=[pok])
        if i == n - 1:
            rd, rdk = rden_r.next()
            E(S, "dve", "reciprocal", [pok], [rdk], out=rd[:, 0:nq], in_=po[64:128, 0:nq])
            E(S, "dve", "tensor_tensor", [pok, rdk], [blk["okey"]], out=blk["out"], in0=po[0:64, 0:nq], in1=rd[:, 0:nq], op=ALU.mult)

    def run_pairs(self):
        S, kb = self.S, self.kb
        ps_s, ps_o, e_r, _, rden_r = self.rings
        st = self.steps
        assert len(st) % 2 == 0
        units = [(st[2 * u], st[2 * u + 1]) for u in range(len(st) // 2)]

        def qk(u):
            a, b = units[u]
            assert a["blk"] is b["blk"]
            ps, psk = ps_s.next()
            nq = a["blk"]["nq"]
            for half, s_ in enumerate((a, b)):
                kl, kkeys = s_["k"]
                kb.mm_group(ps[:, half * 512:half * 512 + nq], [(kl, s_["blk"]["q"])], reads=list(kkeys) + s_["blk"]["qk"], writes=[psk])
            a["ps"], a["psk"] = ps, psk

        def ex(u):
            a, b = units[u]
            nq = a["blk"]["nq"]
            et, ek = e_r.next()
            if nq == 512:
                E(S, "act", "activation", [a["psk"]], [ek], out=et[:, 0:1024], in_=a["ps"][:, 0:1024], func=AF.Exp)
            else:
                for half in range(2):
                    E(S, "act", "activation", [a["psk"]], [ek], out=et[:, half * 512:half * 512 + nq], in_=a["ps"][:, half * 512:half * 512 + nq], func=AF.Exp)
            a["e"], a["ek"] = et, ek

        def pv(u):
            a, b = units[u]
            et, ek = a["e"], a["ek"]
            for half, s_ in enumerate((a, b)):
                s_["e"], s_["ek"] = et[:, half * 512:(half + 1) * 512], ek
                self._pv(s_)

        qk(0)
        for u in range(len(units)):
            ex(u)
            if u + 1 < len(units):
                qk(u + 1)
            pv(u)
        self.steps = []

    def run(self):
        st = self.steps
        for j in range(min(self.depth, len(st))):
            self._qk(st[j])
        for j in range(len(st)):
            self._exp(st[j])
            if j + self.depth < len(st):
                self._qk(st[j + self.depth])
            self._pv(st[j])
        self.steps = []


def attn_rings_pairs(kb, st, tag):
    return (kb.ring_ps(st, "ps_s" + tag, [128, 1024], 2), kb.ring_ps(st, "ps_o" + tag, [128, 512], 2),
            kb.ring_sb(st, "e_r" + tag, [128, 1024], BF16, 3), None,
            kb.ring_sb(st, "rden" + tag, [64, 512], F32, 2))


def attn_rings(kb, st, tag, n_s=4):
    return (kb.ring_ps(st, "ps_s" + tag, [128, 512], n_s), kb.ring_ps(st, "ps_o" + tag, [128, 512], 2),
            kb.ring_sb(st, "e_r" + tag, [128, 512], BF16, 4), kb.ring_sb(st, "p_r" + tag, [128, 512], BF16, 3),
            kb.ring_sb(st, "rden" + tag, [64, 512], F32, 2))


def emit_na(kb, st, I, o_ybT, modpre=None):
    S = kb.S
    ident, _ = make_ident(kb, st)
    qT = kb.sb(st, "naqT", [128, 4, T], BF16)
    with ExitStack() as s2:
        hxT = kb.sb(s2, "hxT", [128, 8, T], BF16)
        kb.load(hxT[:], I["hxT"], writes=["hxT"])
        w_nq = kb.sb(s2, "w_nq", [128, 8, 512], BF16)
        kb.load_cast(w_nq[:], wview(I["w_in"], C_NQ, C_NK), writes=["w_nq"])
        ps_a = kb.ring_ps(s2, "ps_a", [128, 512], 2)
        for hp in range(4):
            for (b0, bn) in TOKBLKS:
                pa, pak = ps_a.next()
                kb.mm_group(pa[:, 0:bn], [(w_nq[:, k, hp * 128:(hp + 1) * 128], hxT[:, k, b0:b0 + bn]) for k in range(8)],
                            reads=["hxT", "w_nq"], writes=[pak])
                E(S, "act", "activation", [pak], [("naqT", hp, b0)], out=qT[:, hp, b0:b0 + bn], in_=pa[:, 0:bn], func=AF.Copy, scale=0.125)
        S.barrier(); S.emit()
    kT = kb.sb(st, "nakT", [128, 4, NKEY], BF16)
    V = kb.sb(st, "naV", [128, NKEY // 128, 8, 128], BF16)
    kb.load(kT[:, :, 256:2304], I["naKT"][:, :, 0:NT], writes=[("nakT", "own")])
    kb.load(kT[:, :, 2560:NKEY], I["naKT"][:, :, NT:T], writes=[("nakT", "ctx")])
    nvv = I["naV"].rearrange("(u p) h f -> p u h f", p=128)
    kb.load(V[:, 2:18], nvv[:, 0:16], writes=[("naV", "own")])
    kb.load(V[:, 20:22], nvv[:, 16:18], writes=[("naV", "ctx")])
    cc = kb.sb(st, "cc", [128, 32], F32)
    kb.load(cc[:], I["cc"], writes=["cc"])
    xbo = [g_.rearrange("(s p) c -> p s c", p=128) for g_ in I["XB_out"]]
    with ExitStack() as s3:
        hal = kb.sb(s3, "hal", [128, 4, 6144], BF16)
        kb.load(hal[:, :, 0:2048], xbo[1][:, :, 2048:4096], writes=[("hal", 0)], reads=[("XB_out", 1)])
        kb.load(hal[:, :, 2048:6144], xbo[2][:, :, 0:4096], writes=[("hal", 1)], reads=[("XB_out", 2)])
        E(S, "dve", "tensor_copy", [("hal", 0), ("hal", 1)], ["hal"], out=hal[0:1, 0, 0:1], in_=hal[0:1, 0, 0:1])
        kt_top = kT[:, :, 0:256]
        kt_bot = kT[:, :, 2304:2560]
        v_top = V[:, 0:2].rearrange("p u h f -> p u (h f)")
        v_bot = V[:, 18:20].rearrange("p u h f -> p u (h f)")
        for s_ in range(4):
            srcs = [(kt_top, hal[:, s_, 1024:2048].rearrange("p (a t) -> p a t", a=4), 4 + s_, "dve"),
                    (kt_bot, hal[:, s_, 0:1024].rearrange("p (a t) -> p a t", a=4), 8 + s_, "pool"),
                    (v_top, hal[:, s_, 4096:6144].rearrange("p (u f) -> p u f", u=2), 4 + s_, "dve"),
                    (v_bot, hal[:, s_, 2048:4096].rearrange("p (u f) -> p u f", u=2), 8 + s_, "pool")]
            for j, (dst, src, col, eng) in enumerate(srcs):
                if s_ == 0:
                    E(S, eng, "tensor_scalar_mul", ["hal", "cc"], [("halo", j)], out=dst, in0=src, scalar1=cc[:, col:col + 1])
                else:
                    E(S, "dve", "scalar_tensor_tensor", ["hal", "cc", ("halo", j)], [("halo", j)], out=dst, in0=src, scalar=cc[:, col:col + 1], in1=dst,
                      op0=ALU.mult, op1=ALU.add)
        S.barrier(); S.emit()
    E(S, "dve", "tensor_copy", [("nakT", "own"), ("nakT", "ctx"), ("halo", 0), ("halo", 1)], ["nakT"], out=kT[0:1, 0, 0:1], in_=kT[0:1, 0, 0:1])
    E(S, "dve", "tensor_copy", [("naV", "own"), ("naV", "ctx"), ("halo", 2), ("halo", 3)], ["naV"], out=V[0:1, 0, 0, 0:1], in_=V[0:1, 0, 0, 0:1])
    bias = kb.sb(st, "nabias", [128, 8, 24 * 64], BF16)
    kb.load_cast(bias[:], I["nabias"].rearrange("p h e c -> p h (e c)"), writes=["nabias"])
    mask = kb.sb(st, "namask", [128, 32, 512], BF16)
    kb.load(mask[:], I["namask"], writes=["namask"])
    ybT = kb.sb(st, "ybT", [128, 4, T], BF16)
    rings = attn_rings(kb, st, "na")
    pipe = AttnPipe(kb, S, rings, ident=ident[:])
    mp_ = modpre(kb, st) if modpre is not None else None
    for h in range(8):
        hp, hs = h // 2, (h % 2) * 64
        for b in range(5):
            b0, bn = TOKBLKS[b]
            qa = qT[hs:hs + 64, hp, b0:b0 + bn]
            qk = [("naqT", hp, b0)]
            kl, vl, ex = [], [], []
            if b < 4:
                for t in range(8):
                    u = 4 * b + t
                    e0 = (4 - 2 * t) + 10
                    kl.append((kT[hs:hs + 64, hp, u * 128:(u + 1) * 128], ["nakT"]))
                    vl.append((V[:, u, h, :], ["naV"]))
                    ex.append((bias[:, h, e0 * 64:(e0 + 8) * 64], mask[:, b * 8 + t, :], ["nabias", "namask"]))
            for u in (20, 21):
                kl.append((kT[hs:hs + 64, hp, u * 128:(u + 1) * 128], ["nakT"]))
                vl.append((V[:, u, h, :], ["naV"]))
                ex.append(None)
            pipe.add(kl, qa, qk, vl, ybT[hs:hs + 64, hp, b0:b0 + bn], ("ybT", h, b), extra=ex)
        if mp_ is not None:
            mp_.step()
            pipe.run()
    pipe.run()
    if mp_ is not None:
        mp_.finish()
    kb.store(o_ybT, ybT[:], reads=[("ybT", h, b) for h in range(8) for b in range(5)], final=True)


class ModPrefetch:
    NB = 16
    BW = 384

    def __init__(self, kb, st, c2, w_ada, b_ada, out_dram):
        S = self.S = kb.S
        self.kb, self.w_ada, self.out = kb, w_ada, out_dram
        self.c2s = kb.sb(st, "pc2s", [128, 16], F32)
        self.bad = kb.sb(st, "pbad", [128, 48], F32)
        self.wa = kb.ring_sb(st, "pwa", [128, 8, self.BW], F32, 2)
        self.mrow = kb.ring_sb(st, "pmrow", [2, self.BW], F32, 2)
        self.mods = kb.sb(st, "pmods", [128, 48, 2], F32)
        self.idf = kb.sb(st, "pidf", [128, 128], F32)
        self.mps = kb.ps(st, "pmod_ps", [128, 48, 2])
        self.mrp = kb.ring_ps(st, "pmrow_ps", [2, self.BW], 1)
        kb.load(self.c2s[:], c2, writes=["pc2s"])
        kb.load(self.bad[:], b_ada, writes=["pbad"])
        E(S, "act", "activation", ["pc2s"], ["pc2s"], out=self.c2s[:], in_=self.c2s[:], func=AF.Silu)
        E(S, "pool", "memset", [], ["pidf"], self.idf[:], 0.0)
        E(S, "pool", "affine_select", ["pidf"], ["pidf"], out=self.idf[:], in_=self.idf[:], pattern=[[-1, 128]],
          compare_op=ALU.not_equal, fill=1.0, base=0, channel_multiplier=1)
        self.c2v = self.c2s[:].rearrange("p (k w) -> p k w", w=2)
        self.pending = []
        self.nxt = 0

    def _compute(self, blk, wt, wk):
        S, kb = self.S, self.kb
        mp, mpk = self.mrp.next()
        kb.mm_group(mp[:, :], [(self.c2v[:, k, :], wt[:, k, :]) for k in range(8)], reads=[wk, "pc2s"], writes=[mpk])
        mr, mrk = self.mrow.next()
        E(S, "dve", "tensor_copy", [mpk], [mrk], out=mr[:], in_=mp[:, :])
        for jj in range(self.BW // 128):
            j = blk * (self.BW // 128) + jj
            E(S, "pe", "transpose", [mrk, "pidf"], [("pmps", j)], self.mps[:, j, :], mr[:, jj * 128:(jj + 1) * 128], self.idf[0:2, 0:2])

    def step(self):
        for (blk, wt, wk) in self.pending:
            self._compute(blk, wt, wk)
        self.pending = []
        for _ in range(2):
            if self.nxt < self.NB:
                blk = self.nxt
                self.nxt += 1
                wt, wk = self.wa.next()
                self.kb.load(wt[:], wview(self.w_ada, blk * self.BW, (blk + 1) * self.BW), writes=[wk])
                self.pending.append((blk, wt, wk))

    def finish(self):
        while self.pending or self.nxt < self.NB:
            self.step()
        S = self.S
        for w in range(2):
            E(S, "dve", "tensor_tensor", [("pmps", j) for j in range(48)] + ["pbad"], ["pmods"], out=self.mods[:, :, w], in0=self.mps[:, :, w],
              in1=self.bad[:], op=ALU.add)
        self.kb.store(self.out, self.mods[:].rearrange("p a b -> p (a b)"), reads=["pmods"])


def emit_mla(kb, st, I, o_ycT, modpre=None):
    S = kb.S
    ident, _ = make_ident(kb, st)
    qTm = kb.sb(st, "qTm", [96, 8, T], BF16)
    cst = kb.sb(st, "mcst", [128, 8], F32)
    kb.load(cst[:], I["rcst"][:, 768:776], writes=["mcst"])
    eps_ap = cst[:, 5:6]
    with ExitStack() as s2:
        cqnT = kb.sb(s2, "cqnT", [128, 2, T], BF16)
        with ExitStack() as s3:
            hxT = kb.sb(s3, "hxT", [128, 8, T], BF16)
            kb.load(hxT[:], I["hxT"], writes=["hxT"])
            w_mq = kb.sb(s3, "w_mq", [128, 8, 256], BF16)
            kb.load_cast(w_mq[:], wview(I["w_in"], C_MQ, C_MKV), writes=["w_mq"])
            qg = kb.sb(s3, "qg", [128, 256], F32)
            kb.load(qg[:], I["qg"].partition_broadcast(128), writes=["qg"])
            GA = 3
            ps_a = kb.ring_ps(s3, "ps_a", [128, 256], GA)
            ps_t = kb.ring_ps(s3, "ps_t", [128, 128], 4, BF16)
            ss = kb.ring_sb(s3, "ss", [128, 2], F32, 2 * GA)
            junk = kb.sb(s3, "junk", [128, 256], F32)
            cn_r = kb.ring_sb(s3, "cn", [128, 256], BF16, 2 * GA)
            for g0 in range(0, NTILE, GA):
                grp = []
                for t in range(g0, min(g0 + GA, NTILE)):
                    tok = slice(t * 128, (t + 1) * 128)
                    pa, pak = ps_a.next()
                    kb.mm_group(pa[:, 0:256], [(hxT[:, k, tok], w_mq[:, k, :]) for k in range(8)], reads=["hxT", "w_mq"], writes=[pak])
                    sst, ssk = ss.next()
                    cn, cnk = cn_r.next()
                    grp.append(dict(t=t, tok=tok, pa=pa, pak=pak, sst=sst, ssk=ssk, cn=cn, cnk=cnk))
                for c in grp:
                    E(S, "act", "activation", [c["pak"]], ["junk", c["ssk"]], out=junk[:], in_=c["pa"][:, 0:256], func=AF.Square, accum_out=c["sst"][:, 0:1])
                for c in grp:
                    E(S, "act", "activation", [c["ssk"], "mcst"], [c["ssk"]], out=c["sst"][:, 1:2], in_=c["sst"][:, 0:1], func=AF.Sqrt, scale=1.0 / 256.0, bias=eps_ap)
                for c in grp:
                    E(S, "dve", "reciprocal", [c["ssk"]], [c["ssk"]], out=c["sst"][:, 1:2], in_=c["sst"][:, 1:2])
                for c in grp:
                    E(S, "dve", "scalar_tensor_tensor", [c["pak"], c["ssk"], "qg"], [c["cnk"]], out=c["cn"][:], in0=c["pa"][:, 0:256], scalar=c["sst"][:, 1:2], in1=qg[:],
                      op0=ALU.mult, op1=ALU.mult)
                pts = []
                for c in grp:
                    for kc in range(2):
                        pt, ptk = ps_t.next()
                        E(S, "pe", "transpose", [c["cnk"], "ident"], [ptk], pt[:], c["cn"][:, kc * 128:(kc + 1) * 128], ident[:])
                        E(S, "dve" if kc == 0 else "act", "tensor_copy" if kc == 0 else "copy", [ptk], [("cqnT", c["t"])], out=cqnT[:, kc, c["tok"]], in_=pt[:])
            S.barrier(); S.emit()
        w_qup = kb.sb(s2, "w_qup", [128, 2, 768], BF16)
        kb.load_cast(w_qup[:], wview(I["w_qup"], 0, 768), writes=["w_qup"])
        rmq = kb.sb(s2, "rmq", [128, NTILE, 32], F32)
        kb.load(rmq[:], I["rope_mq"].rearrange("(t p) f -> p t f", p=128), writes=["rmq"])
        ps_a = kb.ring_ps(s2, "ps_qa", [128, 512], 2)
        ps_b = kb.ring_ps(s2, "ps_qb", [128, 256], 2)
        ps_t = kb.ring_ps(s2, "ps_qt", [128, 128], 4, BF16)
        qf_r = kb.ring_sb(s2, "mqf", [128, 768], F32, 4)
        qb_r = kb.ring_sb(s2, "mqb", [128, 768], BF16, 4)
        u1_r = kb.ring_sb(s2, "mu1", [128, 8, 16], F32, 2)
        u2_r = kb.ring_sb(s2, "mu2", [128, 8, 16], F32, 2)
        scl = float(96.0 ** -0.5)
        for g0 in range(0, NTILE, 2):
            grp = []
            for t in range(g0, min(g0 + 2, NTILE)):
                tok = slice(t * 128, (t + 1) * 128)
                pa, pak = ps_a.next()
                pb, pbk = ps_b.next()
                kb.mm_group(pa[:, 0:512], [(cqnT[:, kc, tok], w_qup[:, kc, 0:512]) for kc in range(2)], reads=[("cqnT", t), "w_qup"], writes=[pak])
                kb.mm_group(pb[:, 0:256], [(cqnT[:, kc, tok], w_qup[:, kc, 512:768]) for kc in range(2)], reads=[("cqnT", t), "w_qup"], writes=[pbk])
                qf, qfk = qf_r.next()
                qb, qbk = qb_r.next()
                grp.append(dict(t=t, tok=tok, pa=pa, pak=pak, pb=pb, pbk=pbk, qf=qf, qfk=qfk, qb=qb, qbk=qbk))
            for c in grp:
                E(S, "act", "copy", [c["pak"]], [(c["qfk"], 0)], out=c["qf"][:, 0:512], in_=c["pa"][:, 0:512])
                E(S, "act", "copy", [c["pbk"], (c["qfk"], 0)], [c["qfk"]], out=c["qf"][:, 512:768], in_=c["pb"][:, 0:256])
            for c in grp:
                qf3 = c["qf"][:].rearrange("p (h f) -> p h f", h=8)
                qb3 = c["qb"][:].rearrange("p (h f) -> p h f", h=8)
                E(S, "dve", "tensor_scalar_mul", [c["qfk"]], [(c["qbk"], "n")], out=qb3[:, :, 0:64], in0=qf3[:, :, 0:64], scalar1=scl)
                cb = rmq[:, c["t"], 0:16].unsqueeze(1).to_broadcast([128, 8, 16])
                sn = rmq[:, c["t"], 16:32].unsqueeze(1).to_broadcast([128, 8, 16])
                u1, u1k = u1_r.next()
                u2, _ = u2_r.next()
                rope2(S, "pool" if c["t"] % 2 == 0 else "dve", qf3[:, :, 64:80], qf3[:, :, 80:96], qb3[:, :, 64:80], qb3[:, :, 80:96], cb, sn, u1[:], u2[:],
                      [c["qfk"], "rmq", (c["qbk"], "n")], c["qbk"], u1k)
            for c in grp:
                for h in range(8):
                    pt, ptk = ps_t.next()
                    E(S, "pe", "transpose", [c["qbk"], "ident"], [ptk], pt[0:96, :], c["qb"][:, h * 96:(h + 1) * 96], ident[:])
                    E(S, "dve" if h % 2 == 0 else "act", "tensor_copy" if h % 2 == 0 else "copy", [ptk], [("qTm", c["t"])], out=qTm[:, h, c["tok"]], in_=pt[0:96, :])
        S.barrier(); S.emit()
    ck = kb.sb(st, "ckvnTa", [128, 2, NKM], BF16)
    xbo = [g_.rearrange("(s p) c -> p s c", p=128) for g_ in I["XB_out"]]
    for s_ in range(4):
        kb.load(ck[:, :, s_ * NT:(s_ + 1) * NT], xbo[0][:, s_, 0:4096].rearrange("p (k t) -> p k t", k=2), writes=[("ck", s_)], reads=[("XB_out", 0)])
    kb.load(ck[:, :, 4 * NT:NKM], I["ckvnT"][:, :, NT:T], writes=[("ck", 4)])
    E(S, "dve", "tensor_copy", [("ck", s_) for s_ in range(5)], ["ck"], out=ck[0:1, 0, 0:1], in_=ck[0:1, 0, 0:1])
    w_kv = kb.sb(st, "w_kvup", [128, 2, 1024], BF16)
    kb.load_cast(w_kv[:], wview(I["w_kvup"], 0, 1024), writes=["w_kv"])
    KT = kb.sb(st, "mKT", [96, 2, NKM], BF16)
    VA = kb.sb(st, "mVA", [128, NKM // 128, 2, 128], BF16)
    ycT = kb.sb(st, "ycT", [128, 4, T], BF16)
    E(S, "pool", "memset", [], ["mVA"], VA[:], 1.0)
    mp_ = modpre(kb, st) if modpre is not None else None
    ps_k = kb.ring_ps(st, "ps_k", [128, 512], 1 if mp_ is not None else (2 if MLA_PAIRS else 3))
    rings = attn_rings_pairs(kb, st, "ml") if (MLA_PAIRS and mp_ is None) else attn_rings(kb, st, "ml", n_s=3)
    pipe = AttnPipe(kb, S, rings)
    NU = NKM // 128
    for hp in range(4):
        for hh in range(2):
            h = hp * 2 + hh
            for s_ in range(4):
                kb.load(KT[64:96, hh, s_ * NT:(s_ + 1) * NT], xbo[1][0:32, s_, 0:2048], writes=[("mKTr", hh, s_)], reads=[("XB_out", 1)])
            kb.load(KT[64:96, hh, 4 * NT:NKM], I["krT"][:, NT:T], writes=[("mKTr", hh, 4)])
            for c0 in range(0, NKM, 512):
                cn = min(512, NKM - c0)
                pk, pkk = ps_k.next()
                kb.mm_group(pk[0:64, 0:cn], [(w_kv[:, kc, h * 128:h * 128 + 64], ck[:, kc, c0:c0 + cn]) for kc in range(2)],
                            reads=["ck", "w_kv"], writes=[pkk])
                E(S, "act" if (c0 // 512) % 2 else "dve", "copy" if (c0 // 512) % 2 else "tensor_copy", [pkk], [("mKT", hh, c0)],
                  out=KT[0:64, hh, c0:c0 + cn], in_=pk[0:64, 0:cn])
        for u in range(NU):
            pk, pkk = ps_k.next()
            wv = w_kv[:, :, hp * 256:(hp + 1) * 256].rearrange("p k (h f) -> p k h f", h=2)[:, :, :, 64:128]
            kb.mm_group(pk[:, 0:128].rearrange("p (h f) -> p h f", h=2), [(ck[:, kc, u * 128:(u + 1) * 128], wv[:, kc, :, :]) for kc in range(2)],
                        reads=["ck", "w_kv"], writes=[pkk])
            E(S, "act" if u % 2 else "dve", "copy" if u % 2 else "tensor_copy", [pkk, "mVA"], [("mVAu", u)],
              out=VA[:, u, :, 0:64], in_=pk[:, 0:128].rearrange("p (h f) -> p h f", h=2))
        for hh in range(2):
            h = hp * 2 + hh
            for b in range(5):
                b0, bn = TOKBLKS[b]
                us = list(range(NU)) if b < 4 else [NU - 2, NU - 1]
                kl = [(KT[:, hh, u * 128:(u + 1) * 128], [("mKT", hh, (u // 4) * 512), ("mKTr", hh, u // 16)]) for u in us]
                vl = [(VA[:, u, hh, :], [("mVAu", u)]) for u in us]
                pipe.add(kl, qTm[:, h, b0:b0 + bn], [("qTm", t) for t in range(b0 // 128, (b0 + bn) // 128)], vl,
                         ycT[hh * 64:hh * 64 + 64, hp, b0:b0 + bn], ("ycT", h, b))
            if mp_ is not None:
                mp_.step()
            if MLA_PAIRS and mp_ is None:
                pipe.run_pairs()
            else:
                pipe.run()
    if mp_ is not None:
        mp_.finish()
    kb.store(o_ycT, ycT[:], reads=[("ycT", h, b) for h in range(8) for b in range(5)], final=True)


def emit_ln(kb, S, x, xkeys, n, ones, lng, gcol, cst, ps_ln, sq_r, lnt, okeys, out=None):
    p1, p1k = ps_ln.next()
    p2, p2k = ps_ln.next()
    kb.mm_group(p1[:, 0:n], [(ones[:], x[:, oc, :]) for oc in range(8)], reads=list(xkeys) + ["ones"], writes=[p1k])
    for oc in range(8):
        sq, sqk = sq_r.next()
        E(S, "act", "activation", [xkeys[oc]], [sqk], out=sq[:, 0:n], in_=x[:, oc, :], func=AF.Square)
        S.op("pe", lambda h, sq=sq, oc=oc, p2=p2: h.matmul(p2[:, 0:n], lhsT=ones[:], rhs=sq[:, 0:n], start=(oc == 0), stop=(oc == 7)),
             reads=[sqk, "ones"], writes=[p2k])
    mean, msq, rstd = lnt
    E(S, "act", "activation", [p1k], ["ln_mean"], out=mean[:, 0:n], in_=p1[:, 0:n], func=AF.Copy, scale=1.0 / 1024.0)
    E(S, "pool", "tensor_tensor", ["ln_mean"], ["ln_msq"], out=msq[:, 0:n], in0=mean[:, 0:n], in1=mean[:, 0:n], op=ALU.mult)
    E(S, "dve", "scalar_tensor_tensor", [p2k, "ln_msq"], ["ln_rstd"], out=rstd[:, 0:n], in0=p2[:, 0:n], scalar=1.0 / 1024.0, in1=msq[:, 0:n],
      op0=ALU.mult, op1=ALU.subtract)
    E(S, "act", "activation", ["ln_rstd", "mcst"], ["ln_rstd"], out=rstd[:, 0:n], in_=rstd[:, 0:n], func=AF.Sqrt, bias=cst[:, 5:6], scale=1.0)
    E(S, "dve", "reciprocal", ["ln_rstd"], ["ln_rstd"], out=rstd[:, 0:n], in_=rstd[:, 0:n])
    for oc in range(8):
        eng = "dve" if oc % 2 == 0 else "pool"
        o = x[:, oc, :] if out is None else out[:, oc, :]
        E(S, eng, "tensor_tensor", [xkeys[oc], "ln_mean"], [xkeys[oc]], out=x[:, oc, :], in0=x[:, oc, :], in1=mean[:, 0:n], op=ALU.subtract)
        E(S, eng, "tensor_tensor", [xkeys[oc], "ln_rstd"], [xkeys[oc]], out=x[:, oc, :], in0=x[:, oc, :], in1=rstd[:, 0:n], op=ALU.mult)
        E(S, eng, "tensor_scalar", [xkeys[oc], "lng"], [okeys[oc]], out=o, in0=x[:, oc, :], scalar1=lng[:, gcol + oc:gcol + oc + 1],
          scalar2=lng[:, gcol + 8 + oc:gcol + 9 + oc], op0=ALU.mult, op1=ALU.add)


def ln_common(kb, st, I):
    S = kb.S
    ones = kb.sb(st, "ones", [128, 128], F32)
    E(S, "pool", "memset", [], ["ones"], ones[:], 1.0)
    lng = kb.sb(st, "lng", [128, 32], F32)
    kb.load(lng[:], I["lng"], writes=["lng"])
    cst = kb.sb(st, "mcst", [128, 8], F32)
    kb.load(cst[:], I["rcst"][:, 768:776], writes=["mcst"])
    mods = kb.sb(st, "mods", [128, 48, 2], F32)
    kb.load(mods[:].rearrange("p a b -> p (a b)"), I["mods"], writes=["mods"])
    return ones, lng, cst, mods


def emit_merge(kb, st, I, dbg):
    S = kb.S
    ones, lng, cst, mods = ln_common(kb, st, I)
    w_gt = kb.sb(st, "w_gt", [128, 8, 3072], BF16)
    w_br = kb.sb(st, "w_br", [128, 3, 4, 1024], BF16)
    w_out = kb.sb(st, "w_out", [128, 8, 1024], BF16)
    for b in range(3):
        kb.load_cast(w_gt[:, :, b * 1024:(b + 1) * 1024], wview(I["w_in"], C_GA + b * 1024, C_GA + (b + 1) * 1024), writes=[("w_gt", b)])
        kb.load_cast(w_br[:, b, :, :], I["w_br"][b].rearrange("(k p) n -> p k n", p=128), writes=[("w_br", b)])
    kb.load_cast(w_out[:], wview(I["w_out"], 0, 1024), writes=["w_out"])
    wkeys = [("w_gt", b) for b in range(3)] + [("w_br", b) for b in range(3)]
    hx_r = kb.ring_sb(st, "hxb", [128, 8, 512], BF16, 2)
    y_r = [kb.ring_sb(st, f"yb{b}", [128, 4, 512], BF16, 2) for b in range(3)]
    x_r = kb.ring_sb(st, "xb", [128, 8, 512], F32, 1)
    yT = kb.sb(st, "yTm", [128, 8, 512], BF16)
    x1 = kb.ring_sb(st, "x1", [128, 8, 512], F32, 1)
    ps_g = kb.ring_ps(st, "ps_g", [128, 512], 3)
    ps_b = kb.ring_ps(st, "ps_b", [128, 512], 3)
    ps_ln = kb.ring_ps(st, "ps_ln", [128, 512], 2)
    sg_r = kb.ring_sb(st, "sgm", [128, 512], F32, 3)
    acc_r = kb.ring_sb(st, "accm", [128, 512], F32, 2)
    tmp_r = kb.ring_sb(st, "tmpm", [128, 512], F32, 2)
    sq_r = kb.ring_sb(st, "sqm", [128, 512], F32, 2)
    lnt = (kb.sb(st, "ln_mean", [128, 512], F32), kb.sb(st, "ln_msq", [128, 512], F32), kb.sb(st, "ln_rstd", [128, 512], F32))
    srcs = [dbg["yaT"], dbg["ybT"], dbg["ycT"]]
    xv = I["xT"].rearrange("(j p) t -> p j t", p=128)
    for (b0, bn) in TOKBLKS:
        w = 0 if b0 < NT else 1
        hx, hxk = hx_r.next()
        kb.load(hx[:, :, 0:bn], I["hxT"][:, :, b0:b0 + bn], writes=[hxk])
        ys = []
        for b in range(3):
            yt, yk = y_r[b].next()
            kb.load(yt[:, :, 0:bn], srcs[b][:, :, b0:b0 + bn], writes=[yk])
            ys.append((yt, yk))
        xt, xk = x_r.next()
        kb.load(xt[:, :, 0:bn], xv[:, :, b0:b0 + bn], writes=[xk])
        for oc in range(8):
            ocs = slice(oc * 128, (oc + 1) * 128)
            acc, acck = acc_r.next()
            for b in range(3):
                pg, pgk = ps_g.next()
                kb.mm_group(pg[:, 0:bn], [(w_gt[:, k, b * 1024 + oc * 128:b * 1024 + (oc + 1) * 128], hx[:, k, 0:bn]) for k in range(8)],
                            reads=[hxk, ("w_gt", b)], writes=[pgk])
                pb, pbk = ps_b.next()
                kb.mm_group(pb[:, 0:bn], [(w_br[:, b, k, ocs], ys[b][0][:, k, 0:bn]) for k in range(4)], reads=[ys[b][1], ("w_br", b)], writes=[pbk])
                sgt, sgk = sg_r.next()
                E(S, "act", "activation", [pgk], [sgk], out=sgt[:, 0:bn], in_=pg[:, 0:bn], func=AF.Sigmoid)
                if b == 0:
                    E(S, "dve", "tensor_tensor", [sgk, pbk], [acck], out=acc[:, 0:bn], in0=pb[:, 0:bn], in1=sgt[:, 0:bn], op=ALU.mult)
                else:
                    E(S, "dve", "tensor_tensor", [sgk, pbk], [sgk], out=sgt[:, 0:bn], in0=pb[:, 0:bn], in1=sgt[:, 0:bn], op=ALU.mult)
                    if b == 1:
                        E(S, "pool", "tensor_tensor", [sgk, acck], [acck], out=acc[:, 0:bn], in0=acc[:, 0:bn], in1=sgt[:, 0:bn], op=ALU.add)
                    else:
                        E(S, "pool", "tensor_tensor", [sgk, acck], [("yTm", oc)], out=yT[:, oc, 0:bn], in0=acc[:, 0:bn], in1=sgt[:, 0:bn], op=ALU.add)
        x1t, x1k = x1.next()
        xkeys = [(x1k, oc) for oc in range(8)]
        for oc in range(8):
            ocs = slice(oc * 128, (oc + 1) * 128)
            pm, pmk = ps_g.next()
            kb.mm_group(pm[:, 0:bn], [(w_out[:, k, ocs], yT[:, k, 0:bn]) for k in range(8)], reads=[("yTm", k) for k in range(8)] + ["w_out"], writes=[pmk])
            tmp, tmpk = tmp_r.next()
            E(S, "act", "activation", [pmk, "mods"], [tmpk], out=tmp[:, 0:bn], in_=pm[:, 0:bn], func=AF.Copy, scale=mods[:, 16 + oc, w:w + 1])
            E(S, "dve", "scalar_tensor_tensor", [tmpk, xk], [xkeys[oc]], out=x1t[:, oc, 0:bn], in0=xt[:, oc, 0:bn], scalar=float(ALPHA), in1=tmp[:, 0:bn],
              op0=ALU.mult, op1=ALU.add)
        emit_ln(kb, S, x1t[:, :, 0:bn], xkeys, bn, ones, lng, 0, cst, ps_ln, sq_r, lnt, xkeys)
        kb.store(dbg["x1T"][:, :, b0:b0 + bn], x1t[:, :, 0:bn], reads=xkeys, writes=[("x1T_d", b0)], final=True)


def emit_ffn(kb, st, I, x1T_d, xoT, last=True):
    S = kb.S
    ones, lng, cst, mods = ln_common(kb, st, I)
    sc2p = kb.sb(st, "sc2p", [128, 8, 2], F32)
    E(S, "dve", "tensor_scalar_add", ["mods"], ["sc2p"], out=sc2p[:], in0=mods[:, 32:40, :], scalar1=1.0)
    w1 = kb.sb(st, "w_ff1", [128, 8, 4096], BF16)
    w2 = kb.sb(st, "w_ff2", [128, 32, 1024], BF16)
    for c in range(4):
        kb.load_cast(w1[:, :, c * 1024:(c + 1) * 1024], wview(I["w_ff1"], c * 1024, (c + 1) * 1024), writes=[("w1", c)])
    for c in range(4):
        kb.load_cast(w2[:, c * 8:(c + 1) * 8, :], I["w_ff2"].rearrange("(k p) n -> p k n", p=128)[:, c * 8:(c + 1) * 8, :], writes=[("w2", c)])
    w1k = [("w1", c) for c in range(4)]
    w2k = [("w2", c) for c in range(4)]
    n = FFBLK
    x_r = kb.ring_sb(st, "xf", [128, 8, n], F32, 3)
    h_r = kb.ring_sb(st, "hf", [128, 8, n], BF16, 2)
    a_r = kb.ring_sb(st, "af", [128, 32, n], BF16, 2)
    r_r = kb.ring_sb(st, "rf", [128, n], F32, 3)
    ps_f = kb.ring_ps(st, "ps_f", [128, n], 4)
    ps_ln = kb.ring_ps(st, "ps_lnf", [128, n], 2)
    tmp_r = kb.ring_sb(st, "tmpf", [128, n], F32, 2)
    sq_r = kb.ring_sb(st, "sqf", [128, n], F32, 2)
    lnt = (kb.sb(st, "ln_mean", [128, n], F32), kb.sb(st, "ln_msq", [128, n], F32), kb.sb(st, "ln_rstd", [128, n], F32))
    xov = xoT.rearrange("(j p) t -> p j t", p=128)
    blocks = list(range(0, T, n))
    ctx = {}

    def stage_A(b0):
        w = 0 if b0 < NT else 1
        xt, xk = x_r.next()
        kb.load(xt[:], x1T_d[:, :, b0:b0 + n], writes=[xk], reads=[("x1T_d", (b0 // 512) * 512)])
        ht, hk = h_r.next()
        for j in range(8):
            E(S, "dve" if j % 2 == 0 else "pool", "tensor_scalar", [xk, "sc2p", "mods"], [(hk, j)], out=ht[:, j, :], in0=xt[:, j, :],
              scalar1=sc2p[:, j, w:w + 1], scalar2=mods[:, 24 + j, w:w + 1], op0=ALU.mult, op1=ALU.add)
        at, ak = a_r.next()
        for fc in range(32):
            pf, pfk = ps_f.next()
            kb.mm_group(pf[:], [(w1[:, k, fc * 128:(fc + 1) * 128], ht[:, k, :]) for k in range(8)], reads=[(hk, j) for j in range(8)] + [("w1", fc // 8)], writes=[pfk])
            rt, rk = r_r.next()
            E(S, "act", "activation", [pfk], [rk], out=rt[:], in_=pf[:], func=AF.Relu)
            E(S, "pool" if fc % 2 == 0 else "dve", "tensor_tensor", [rk], [(ak, fc)], out=at[:, fc, :], in0=rt[:], in1=rt[:], op=ALU.mult)
        ctx[b0] = dict(w=w, xt=xt, xk=xk, hk=hk, at=at, ak=ak, xkeys=[(xk, oc) for oc in range(8)])

    def stage_B1(b0):
        c = ctx[b0]
        xt, xk, at, ak, w = c["xt"], c["xk"], c["at"], c["ak"], c["w"]
        for oc in range(8):
            pm, pmk = ps_f.next()
            kb.mm_group(pm[:], [(w2[:, k, oc * 128:(oc + 1) * 128], at[:, k, :]) for k in range(32)], reads=[(ak, fc) for fc in range(32)] + w2k, writes=[pmk])
            tmp, tmpk = tmp_r.next()
            E(S, "act", "activation", [pmk, "mods"], [tmpk], out=tmp[:], in_=pm[:], func=AF.Copy, scale=mods[:, 40 + oc, w:w + 1])
            E(S, "dve", "scalar_tensor_tensor", [tmpk, xk] + [(c["hk"], j) for j in range(8)], [c["xkeys"][oc]], out=xt[:, oc, :], in0=xt[:, oc, :], scalar=float(ALPHA), in1=tmp[:],
              op0=ALU.mult, op1=ALU.add)

    def stage_B2(b0):
        c = ctx.pop(b0)
        emit_ln(kb, S, c["xt"][:], c["xkeys"], n, ones, lng, 16, cst, ps_ln, sq_r, lnt, c["xkeys"])
        kb.store(xov[:, :, b0:b0 + n], c["xt"][:], reads=c["xkeys"], final=last)

    nb = len(blocks)
    stage_A(blocks[0])
    if nb > 1:
        stage_A(blocks[1])
    for i in range(nb):
        stage_B1(blocks[i])
        if i + 2 < nb:
            stage_A(blocks[i + 2])
        stage_B2(blocks[i])


_BF = ml_dtypes.bfloat16


def _bf(a):
    a = np.asarray(a)
    if a.dtype.kind == "V":
        a = a.view(_BF)
    return a


def na_bias_table(rpb):
    a = np.arange(2)[:, None, None, None]
    kc = np.arange(64)[None, :, None, None]
    e = np.arange(24)[None, None, :, None]
    qc = np.arange(64)[None, None, None, :]
    dr = 10 + a - e + 0 * kc + 0 * qc
    dc = kc - qc + 0 * a + 0 * e
    ok = (np.abs(dr) <= 7) & (np.abs(dc) <= 15)
    dri = np.clip(dr + 7, 0, 14)
    dci = np.clip(dc + 15, 0, 30)
    out = np.zeros((2, 64, 8, 24, 64), np.float32)
    for h in range(8):
        g = rpb[h][dri, dci]
        out[:, :, h] = np.where(ok, g, np.float32(0.0))
    return np.ascontiguousarray(out.reshape(128, 8, 24, 64))


def na_mask_table(q):
    R0 = 32 * q
    m = np.zeros((128, 4, 8, 8, 64), np.float32)
    kc = np.arange(64)[:, None]
    qc = np.arange(64)[None, :]
    cs = np.clip(qc - 8, 0, 48)
    colok = (kc >= cs) & (kc < cs + 16)
    for b in range(4):
        for t in range(8):
            for a in range(2):
                krow = R0 + 8 * b - 4 + 2 * t + a
                for j in range(8):
                    qrow = R0 + 8 * b + j
                    r0 = min(max(qrow - 4, 0), 120)
                    if 0 <= krow < 128 and r0 <= krow < r0 + 8:
                        m[a * 64:(a + 1) * 64, b, t, j, :] = colok
    return np.ascontiguousarray(m.reshape(128, 32, 512)).astype(_BF)


def prep_B(inp, l, core, xT_core, A):
    b, q = core // 4, core % 4
    grp = [b * 4 + i for i in range(4)]
    me = A[core]
    rr, rm = rope_tables(q)
    Fsl = np.zeros((64, 8, 3, 128), np.float32)
    fexp = np.zeros((128, 8), np.float32)
    for j in range(3):
        if q - 1 - j >= 0:
            Fsl[:, 0:4, j, :] = np.asarray(A[grp[q - 1 - j]]["retF"])[:, 0:4, :]
        if q + 1 + j <= 3:
            Fsl[:, 4:8, j, :] = np.asarray(A[grp[q + 1 + j]]["retF"])[:, 4:8, :]
        fexp[:, j] = NT * j
        fexp[:, 4 + j] = NT * j
    fexp[:, 3] = NT * q
    fexp[:, 7] = NT * (3 - q)
    kT = _bf(me["naKT"]); V = _bf(me["naV"])
    naKT = np.zeros((128, 4, NKEY), _BF)
    naV = np.zeros((NKEY, 8, 128), _BF)
    naKT[:, :, 256:2304] = kT[:, :, 0:NT]; naV[256:2304] = V[0:NT]
    naKT[:, :, 2560:] = kT[:, :, NT:]; naV[2560:] = V[NT:]
    if q > 0:
        p = A[grp[q - 1]]
        naKT[:, :, 0:256] = _bf(p["naKT"])[:, :, NT - 256:NT]; naV[0:256] = _bf(p["naV"])[NT - 256:NT]
    if q < 3:
        p = A[grp[q + 1]]
        naKT[:, :, 2304:2560] = _bf(p["naKT"])[:, :, 0:256]; naV[2304:2560] = _bf(p["naV"])[0:256]
    ck = np.concatenate([_bf(A[g]["ckvnT"])[:, :, 0:NT] for g in grp] + [_bf(me["ckvnT"])[:, :, NT:]], axis=2)
    kr = np.concatenate([_bf(A[g]["krT"])[:, 0:NT] for g in grp] + [_bf(me["krT"])[:, NT:]], axis=1)
    lng = np.concatenate([fm(inp["ln_gain"][l, 0]), fm(inp["ln_bias"][l, 0]), fm(inp["ln_gain"][l, 1]), fm(inp["ln_bias"][l, 1])], axis=1)
    return {
        "xT": xT_core, "mods": np.asarray(me["mods"]), "hxT": _bf(me["hxT"]), "retK": _bf(me["retK"]), "retKT": _bf(me["retKT"]),
        "retV": _bf(me["retV"]), "Fsl": Fsl, "fexp": fexp, "rld": np.ascontiguousarray(inp["ret_log_decay"][l].reshape(1, 8)),
        "rcst": ret_consts(), "gng": np.ascontiguousarray(inp["ret_gn_gain"][l].reshape(1, 512)), "rope_q": rr,
        "rope_mq": (rm * np.float32(96.0 ** -0.5)).astype(np.float32),
        "naKT": naKT, "naV": naV, "nabias": na_bias_table(inp["na_rpb"][l]), "namask": na_mask_table(q),
        "ckvnT": np.ascontiguousarray(ck), "krT": np.ascontiguousarray(kr),
        "qg": np.ascontiguousarray(inp["mla_q_norm"][l].reshape(1, 256)), "w_qup": inp["mla_w_qup"][l], "w_kvup": inp["mla_w_kvup"][l],
        "w_in": inp["w_in"][l],
        "w_br": np.ascontiguousarray(np.stack([inp["w_branch_ret"][l], inp["w_branch_na"][l], inp["w_branch_mla"][l]])),
        "w_out": inp["w_out"][l], "w_ff1": inp["w_ff1"][l], "w_ff2": inp["w_ff2"][l], "lng": np.ascontiguousarray(lng),
    }


XBW = 12288


def emit_exchange(kb, st, G):
    S = kb.S
    cc = kb.sb(st, "cc", [128, 32], F32)
    kb.load(cc[:], G["cc"], writes=["cc"])
    stg1 = kb.sb(st, "xstg1", [128, 4096], BF16)
    stg2 = kb.sb(st, "xstg2", [32, 2048], BF16)
    stg3 = kb.sb(st, "xstg3", [128, 6144], BF16)
    m1 = kb.sb(st, "xm1", [128, 4, 4096], BF16)
    m2 = kb.sb(st, "xm2", [32, 4, 2048], BF16)
    m3 = kb.sb(st, "xm3", [128, 4, 6144], BF16)
    fst = kb.sb(st, "xf", [64, 1024], F32)
    fm_ = kb.sb(st, "xfm", [64, 4, 1024], F32)
    xin = [g_.rearrange("(s p) c -> p s c", p=128) for g_ in G["XB_in"]]
    nvv = G["naV"].rearrange("(u p) h f -> p u (h f)", p=128)
    kb.load(fst[:], G["retF"].rearrange("p i v -> p (i v)"), writes=["xf"])
    kb.load(stg1[:].rearrange("p (k t) -> p k t", k=2), G["ckvnT"][:, :, 0:NT], writes=["xstg1"])
    kb.load(stg2[:], G["krT"][:, 0:NT], writes=["xstg2"])
    kb.load(stg3[:, 0:1024].rearrange("p (a t) -> p a t", a=4), G["naKT"][:, :, 0:256], writes=[("xstg3", 0)])
    kb.load(stg3[:, 1024:2048].rearrange("p (a t) -> p a t", a=4), G["naKT"][:, :, NT - 256:NT], writes=[("xstg3", 1)])
    kb.load(stg3[:, 2048:4096].rearrange("p (u f) -> p u f", u=2), nvv[:, 0:2, :], writes=[("xstg3", 2)])
    kb.load(stg3[:, 4096:6144].rearrange("p (u f) -> p u f", u=2), nvv[:, 14:16, :], writes=[("xstg3", 3)])
    rg = [[0, 1, 2, 3], [4, 5, 6, 7]]
    for s_ in range(4):
        E(S, "dve", "tensor_scalar_mul", ["xf", "cc"], [("xfm", s_)], out=fm_[:, s_, :], in0=fst[:], scalar1=cc[0:64, s_:s_ + 1])
    kb.store(G["FB_in"].rearrange("(s p) c -> p s c", p=64), fm_[:], reads=[("xfm", s_) for s_ in range(4)], writes=["FB_in"])
    fi, fo = G["FB_in"], G["FB_out"]
    S.coll(lambda h: h.collective_compute("AllReduce", ALU.add, replica_groups=rg, ins=[fi.opt()], outs=[fo.opt()]),
           reads=["FB_in"], writes=["FB_out"])
    for s_ in range(4):
        E(S, "dve", "tensor_scalar_mul", [("xstg3", j) for j in range(4)] + ["cc"], [("xm3", s_)], out=m3[:, s_, :], in0=stg3[:],
          scalar1=cc[:, s_:s_ + 1])
    for s_ in range(4):
        E(S, "dve", "tensor_scalar_mul", ["xstg2", "cc"], [("xm2", s_)], out=m2[:, s_, :], in0=stg2[:], scalar1=cc[0:32, s_:s_ + 1])
    kb.store(xin[1][0:32, :, 0:2048], m2[:], reads=[("xm2", s_) for s_ in range(4)], writes=[("XB_in", 1, "a")])
    kb.store(xin[1][:, :, 2048:4096], m3[:, :, 0:2048], reads=[("xm3", s_) for s_ in range(4)], writes=[("XB_in", 1, "b")], q="act")
    kb.store(xin[2][:, :, 0:4096], m3[:, :, 2048:6144], reads=[("xm3", s_) for s_ in range(4)], writes=[("XB_in", 2)])
    for s_ in range(4):
        E(S, "dve", "tensor_scalar_mul", ["xstg1", "cc"], [("xm1", s_)], out=m1[:, s_, :], in0=stg1[:], scalar1=cc[:, s_:s_ + 1])
    kb.store(xin[0][:, :, 0:4096], m1[:], reads=[("xm1", s_) for s_ in range(4)], writes=[("XB_in", 0)], q="act")
    rk = {0: [("XB_in", 0)], 1: [("XB_in", 1, "a"), ("XB_in", 1, "b"), "XB_in"], 2: [("XB_in", 2)]}
    for j in (1, 2, 0):
        xi, xo = G["XB_in"][j], G["XB_out"][j]
        S.coll(lambda h, xi=xi, xo=xo: h.collective_compute("AllReduce", ALU.add, replica_groups=rg, ins=[xi.opt()], outs=[xo.opt()]),
               reads=rk[j], writes=[("XB_out", j)])


def build_fused(n_layers=4):
    nc = bass.Bass("TRN2", target_bir_lowering=False)
    din = lambda name, shape, dt=F32: nc.dram_tensor(name, shape, dt, kind="ExternalInput").ap()
    dsc = lambda name, shape, dt=F32: nc.dram_tensor(name, shape, dt).ap()
    L = n_layers
    X = dict(
        xT=din("xT", [D, T]), c2=din("c2", [128, 16]), w_ada=din("w_ada", [L, D, 6 * D]), b_ada=din("b_ada", [L, 128, 48]),
        w_in=din("w_in", [L, D, 7200]), rld=din("rld", [L, 1, 8]), rcst=din("rcst", [128, 776]), kvg=din("kvg", [L, 1, 256]),
        rope_rk=din("rope_rk", [T, 64]), rope_m=din("rope_m", [T, 32]), rope_q=din("rope_q", [T, 64]), rope_mq=din("rope_mq", [T, 32]),
        gng=din("gng", [L, 1, 512]), nabias=din("nabias", [L, 128, 8, 24, 64]), namask=din("namask", [128, 32, 512], BF16),
        qg=din("qg", [L, 1, 256]), w_qup=din("w_qup", [L, 256, 768]), w_kvup=din("w_kvup", [L, 256, 1024]),
        w_br=din("w_br", [L, 3, 512, D]), w_out=din("w_out", [L, D, D]), w_ff1=din("w_ff1", [L, D, 4 * D]), w_ff2=din("w_ff2", [L, 4 * D, D]),
        lng=din("lng", [L, 128, 32]), cc=din("cc", [128, 32]),
    )
    xoT = nc.dram_tensor("xoT", [D, T], F32, kind="ExternalOutput").ap()
    G = dict(
        mods=dsc("mods_d", [128, 96]), hxT=dsc("hxT_d", [128, 8, T], BF16), retK=dsc("retK_d", [T, 256], BF16),
        retKT=dsc("retKT_d", [64, 4, T], BF16), retV=dsc("retV_d", [T, 512], BF16), retF=dsc("retF_d", [64, 8, 128]),
        naKT=dsc("naKT_d", [128, 4, T], BF16), naV=dsc("naV_d", [T, 8, 128], BF16), ckvnT=dsc("ckvnT_d", [128, 2, T], BF16),
        krT=dsc("krT_d", [32, T], BF16), XB_in=[dsc(f"XB_in{j}", [512, 4096], BF16) for j in range(3)],
        XB_out=[dsc(f"XB_out{j}", [512, 4096], BF16) for j in range(3)],
        FB_in=dsc("FB_in", [256, 1024]), FB_out=dsc("FB_out", [256, 1024]),
        yaT=dsc("yaT_d", [128, 4, T], BF16), ybT=dsc("ybT_d", [128, 4, T], BF16), ycT=dsc("ycT_d", [128, 4, T], BF16),
        x1T=dsc("x1T_d", [128, 8, T]), cc=X["cc"], modsN=dsc("modsN_d", [128, 96]),
    )
    xbuf = [dsc("xping", [D, T]), dsc("xpong", [D, T])]
    with ExitStack() as st0:
        S = Sched(nc, st0)
        kb = KB(nc, S)
        with ExitStack() as ph:
            zt = kb.sb(ph, "zt", [128, 4, 2048], BF16)
            E(S, "pool", "memset", [], ["zt"], zt[:], 0.0)
            kb.store(G["XB_in"][1].rearrange("(s p) c -> p s c", p=128)[:, :, 0:2048], zt[:], reads=["zt"], writes=["XB_in"])
            S.barrier(); S.emit()
        for l in range(L):
            x_in = X["xT"] if l == 0 else xbuf[(l - 1) % 2]
            x_out = xoT if l == L - 1 else xbuf[l % 2]
            with ExitStack() as st:
                hxT = kb.sb(st, "hxT", [128, 8, T], BF16)
                mods = kb.sb(st, "mods", [128, 48, 2], F32)
                ident, _ = make_ident(kb, st)
                with ExitStack() as p1:
                    emit_mod_hx(kb, p1, x_in, X["c2"], X["w_ada"][l], X["b_ada"][l], mods, hxT,
                                mods_src=(G["modsN"] if (l > 0 and MOD_PREFETCH) else None))
                    kb.store(G["mods"], mods[:].rearrange("p a b -> p (a b)"), reads=["mods"])
                    kb.store(G["hxT"], hxT[:], reads=[("hxT", j) for j in range(8)])
                    S.barrier(); S.emit()
                with ExitStack() as p2:
                    emit_kv_side(kb, p2, hxT, ident, X["w_in"][l], X["rld"][l], X["rcst"], X["kvg"][l], X["rope_rk"], X["rope_m"],
                                 G["retK"], G["retKT"], G["retV"], G["retF"], G["naKT"], G["naV"], G["ckvnT"], G["krT"])
                    S.barrier(); S.emit()
            with ExitStack() as ph:
                emit_exchange(kb, ph, G)
                S.bar_coll = False
                S.barrier(); S.emit()
            I = dict(G)
            I.update(xT=x_in, rld=X["rld"][l], rcst=X["rcst"], gng=X["gng"][l], rope_q=X["rope_q"], rope_mq=X["rope_mq"],
                     nabias=X["nabias"][l], namask=X["namask"], qg=X["qg"][l], w_qup=X["w_qup"][l], w_kvup=X["w_kvup"][l],
                     w_in=X["w_in"][l], w_br=X["w_br"][l], w_out=X["w_out"][l], w_ff1=X["w_ff1"][l], w_ff2=X["w_ff2"][l], lng=X["lng"][l])
            dbg = dict(yaT=G["yaT"], ybT=G["ybT"], ycT=G["ycT"], x1T=G["x1T"])
            with ExitStack() as ph:
                emit_retention(kb, ph, I, dbg["yaT"])
                S.barrier(); S.emit()
            mpre = None
            if MOD_PREFETCH and l + 1 < L:
                mpre = (lambda kb_, st_, l=l: ModPrefetch(kb_, st_, X["c2"], X["w_ada"][l + 1], X["b_ada"][l + 1], G["modsN"]))
            with ExitStack() as ph:
                emit_na(kb, ph, I, dbg["ybT"], modpre=mpre)
                S.barrier(); S.emit()
            with ExitStack() as ph:
                emit_mla(kb, ph, I, dbg["ycT"], modpre=None)
                S.barrier(); S.emit()
            with ExitStack() as ph:
                emit_merge(kb, ph, I, dbg)
                S.barrier(); S.emit()
            with ExitStack() as ph:
                emit_ffn(kb, ph, I, dbg["x1T"], x_out, last=(l == L - 1))
                S.barrier(); S.emit(final=(l == L - 1))
    return nc


def core_consts(q):
    cc = np.zeros((128, 32), np.float32)
    cc[:, q] = 1.0
    if q > 0:
        cc[:, 4 + q - 1] = 1.0
    if q < 3:
        cc[:, 8 + q + 1] = 1.0
    for s_ in range(4):
        if s_ < q:
            cc[:, 12 + s_] = NT * (q - 1 - s_)
            cc[:, 22 + s_] = 1.0
        if s_ > q:
            cc[:, 17 + s_] = NT * (s_ - q - 1)
            cc[:, 26 + s_] = 1.0
    cc[:, 16] = NT * q
    cc[:, 21] = NT * (3 - q)
    return cc


def prep_fused(inp, core, L=4):
    b, q = core // 4, core % 4
    rr, rm = rope_tables(q)
    xc = np.concatenate([inp["x"][b, q * NT:(q + 1) * NT], inp["ctx"][b]], axis=0)
    c2 = np.stack([inp["c"][b], inp["c_ctx"]], axis=-1)
    c2 = np.ascontiguousarray(c2.reshape(8, 128, 2).transpose(1, 0, 2).reshape(128, 16))
    lng = np.stack([np.concatenate([fm(inp["ln_gain"][l, 0]), fm(inp["ln_bias"][l, 0]), fm(inp["ln_gain"][l, 1]), fm(inp["ln_bias"][l, 1])], axis=1)
                    for l in range(L)])
    return {
        "xT": np.ascontiguousarray(xc.T), "c2": c2, "w_ada": inp["w_ada"][:L], "b_ada": np.stack([fm(inp["b_ada"][l]) for l in range(L)]),
        "w_in": inp["w_in"][:L], "rld": np.ascontiguousarray(inp["ret_log_decay"][:L].reshape(L, 1, 8)), "rcst": ret_consts(),
        "kvg": np.ascontiguousarray(inp["mla_kv_norm"][:L].reshape(L, 1, 256)),
        "rope_rk": (rr * np.float32(0.125)).astype(np.float32), "rope_m": rm, "rope_q": rr,
        "rope_mq": (rm * np.float32(96.0 ** -0.5)).astype(np.float32),
        "gng": np.ascontiguousarray(inp["ret_gn_gain"][:L].reshape(L, 1, 512)),
        "nabias": np.stack([na_bias_table(inp["na_rpb"][l]) for l in range(L)]), "namask": na_mask_table(q),
        "qg": np.ascontiguousarray(inp["mla_q_norm"][:L].reshape(L, 1, 256)), "w_qup": inp["mla_w_qup"][:L], "w_kvup": inp["mla_w_kvup"][:L],
        "w_br": np.ascontiguousarray(np.stack([inp["w_branch_ret"][:L], inp["w_branch_na"][:L], inp["w_branch_mla"][:L]], axis=1)),
        "w_out": inp["w_out"][:L], "w_ff1": inp["w_ff1"][:L], "w_ff2": inp["w_ff2"][:L], "lng": np.ascontiguousarray(lng),
        "cc": core_consts(q),
    }


_NC = {}


def kernel(**inp):
    inp = {k: np.asarray(v) for k, v in inp.items()}
    if "F" not in _NC:
        _NC["F"] = build_fused(4)
    res = run_bass_kernel_spmd(_NC["F"], [prep_fused(inp, c) for c in range(8)], core_ids=list(range(8))).results
    out = np.zeros((2, 8192, D), np.float32)
    for core in range(8):
        b, q = core // 4, core % 4
        out[b, q * NT:(q + 1) * NT] = np.asarray(res[core]["xoT"])[:, 0:NT].T
    return out
```

```python
import numpy as np
from contextlib import ExitStack
import ml_dtypes
import concourse.bass as bass
import concourse.mybir as mybir
from concourse.bass_utils import run_bass_kernel_spmd

F32 = mybir.dt.float32
BF16 = mybir.dt.bfloat16
AF = mybir.ActivationFunctionType
ALU = mybir.AluOpType
AX = mybir.AxisListType

D = 1024
NT = 2048
NZ = 256
T = NT + NZ
NTILE = T // 128
EPS = 1e-5
ALPHA = 8.0 ** 0.25
EPOCH = 20000
C_RQ, C_RK, C_RV, C_GF, C_GB, C_NQ, C_NK, C_NV, C_MQ, C_MKV, C_MKR, C_GA, C_GBR, C_GC = (
    0, 256, 512, 1024, 1536, 2048, 2560, 3072, 3584, 3840, 4096, 4128, 5152, 6176)
TOKBLKS = [(0, 512), (512, 512), (1024, 512), (1536, 512), (2048, 256)]


class Sched:
    ENGS = ("pe", "dve", "act", "pool", "sp")

    def __init__(self, nc, stack, n_dma_sems=12, n_eng_sems=8):
        self.nc = nc
        self.ops = {e: [] for e in self.ENGS}
        self.esems = {e: [stack.enter_context(nc.semaphore(f"s_{e}_{i}")) for i in range(n_eng_sems)]
                      for e in ("pe", "dve", "act", "pool")}
        self.ecnt = {e: 0 for e in ("pe", "dve", "act", "pool")}
        self.eep = {e: 0 for e in ("pe", "dve", "act", "pool")}
        self.dsems = {q: [stack.enter_context(nc.semaphore(f"d_{q}_{i}")) for i in range(n_dma_sems)]
                      for q in ("sp", "act", "pool")}
        self.dval = {q: [0] * n_dma_sems for q in ("sp", "act", "pool")}
        self.drr = {q: 0 for q in ("sp", "act", "pool")}
        self.lastw = {}
        self.reads = {}
        self.seen = {e: {} for e in self.ENGS}
        self.final_tokens = []
        self.n_ops = 0
        self.csem = stack.enter_context(nc.semaphore("s_coll"))
        self.cval = 0

    def _need(self, eng, tok, waits):
        if tok is None:
            return
        sem, val = tok
        if eng == "pe" and any(sem is x for x in self.esems["pe"]):
            return
        sid = id(sem)
        cur = self.seen[eng].get(sid)
        if cur is not None and cur >= val:
            return
        self.seen[eng][sid] = val
        waits.append((sem, val))

    def _deps(self, eng, reads, writes):
        waits = []
        for k in reads:
            self._need(eng, self.lastw.get(k), waits)
        for k in writes:
            self._need(eng, self.lastw.get(k), waits)
            for t in self.reads.get(k, ()):
                self._need(eng, t, waits)
        return waits

    def _commit(self, tok, reads, writes):
        for k in reads:
            self.reads.setdefault(k, []).append(tok)
        for k in writes:
            self.lastw[k] = tok
            self.reads[k] = []

    def op(self, eng, fn, reads=(), writes=()):
        waits = self._deps(eng, reads, writes)
        if self.ecnt[eng] >= EPOCH:
            self.eep[eng] += 1
            self.ecnt[eng] = 0
        sem = self.esems[eng][self.eep[eng]]
        self.ecnt[eng] += 1
        tok = (sem, self.ecnt[eng])
        self.ops[eng].append((waits, fn, (sem, 1)))
        self._commit(tok, reads, writes)
        self.n_ops += 1
        return tok

    def dma(self, q, fn, reads=(), writes=(), final=False):
        waits = self._deps(q, reads, writes)
        i = self.drr[q]
        self.drr[q] = (i + 1) % len(self.dsems[q])
        sem = self.dsems[q][i]
        prev = self.dval[q][i]
        if prev > 0:
            self._need(q, (sem, prev), waits)
        self.dval[q][i] = prev + 16
        tok = (sem, prev + 16)
        self.ops[q].append((waits, fn, (sem, 16)))
        self._commit(tok, reads, writes)
        if final:
            self.final_tokens.append(tok)
        self.n_ops += 1
        return tok

    def coll(self, fn, reads=(), writes=()):
        waits = self._deps("pool", reads, writes)
        if not hasattr(self, "csem"):
            raise RuntimeError("no collective semaphore")
        self.cval += 1
        tok = (self.csem, self.cval)
        self.ops["pool"].append((waits, fn, (self.csem, 1)))
        self._commit(tok, reads, writes)
        self.ctoks = tok
        return tok

    def barrier(self):
        toks = []
        if getattr(self, "cval", 0) > 0 and getattr(self, "bar_coll", True):
            toks.append((self.csem, self.cval))
        for e in ("pe", "dve", "act", "pool"):
            if self.ecnt[e] > 0:
                toks.append((self.esems[e][self.eep[e]], self.ecnt[e]))
        for q in ("sp", "act", "pool"):
            for i, v in enumerate(self.dval[q]):
                if v > 0:
                    toks.append((self.dsems[q][i], v))
        for e in self.ENGS:
            waits = []
            for t in toks:
                self._need(e, t, waits)
            if waits:
                self.ops[e].append((waits, None, None))

    def emit(self, final=False):
        nc = self.nc
        fin = []
        if final:
            mx = {}
            for (sm, v) in self.final_tokens:
                if id(sm) not in mx or mx[id(sm)][1] < v:
                    mx[id(sm)] = (sm, v)
            fin = list(mx.values())
        handles = {"pe": "tensor", "dve": "vector", "act": "scalar", "pool": "gpsimd", "sp": "sync"}
        with nc.Block() as block:
            for e in self.ENGS:
                ops = self.ops[e]
                extra = fin if e == "sp" else []
                if not ops and not extra:
                    continue

                def body(h, ops=ops, extra=extra):
                    for waits, fn, si in ops:
                        for (ws, wv) in waits:
                            h.wait_ge(ws, wv)
                        if fn is not None:
                            fn(h).then_inc(si[0], si[1])
                    for (ws, wv) in extra:
                        h.wait_ge(ws, wv)

                getattr(block, handles[e])(body)
        self.ops = {e: [] for e in self.ENGS}


class Ring:
    def __init__(self, tiles, name, keys=None):
        self.tiles = tiles
        self.name = name
        self.keys = keys
        self.i = 0

    def next(self):
        j = self.i % len(self.tiles)
        t = self.tiles[j]
        k = (self.name, j) if self.keys is None else self.keys[j]
        self.i += 1
        return t, k


class KB:
    def __init__(self, nc, S):
        self.nc = nc
        self.S = S
        self.dq = 0
        self.uid = 0

    def sb(self, st, name, shape, dt):
        self.uid += 1
        return st.enter_context(self.nc.sbuf_tensor(f"sb{self.uid}_{name}", shape, dt))

    def ps(self, st, name, shape, dt=F32):
        self.uid += 1
        return st.enter_context(self.nc.psum_tensor(f"ps{self.uid}_{name}", shape, dt))

    def ring_sb(self, st, name, shape, dt, n):
        return Ring([self.sb(st, f"{name}{i}", shape, dt) for i in range(n)], name)

    def ring_ps_sliced(self, st, name, width, n, per_bank=4):
        tiles, keys = [], []
        nb = (n + per_bank - 1) // per_bank
        for b in range(nb):
            t = self.ps(st, f"{name}{b}", [128, width * per_bank], F32)
            for j in range(per_bank):
                if len(tiles) < n:
                    tiles.append(t[:, j * width:(j + 1) * width])
                    keys.append((name, "bank", b))
        return Ring(tiles, name, keys)

    def ring_ps(self, st, name, shape, n, dt=F32):
        return Ring([self.ps(st, f"{name}{i}", shape, dt) for i in range(n)], name)

    def load(self, out, in_, writes, reads=(), q=None):
        if q is None:
            q = ("sp", "act")[self.dq % 2]
            self.dq += 1
        return self.S.dma(q, lambda h: h.dma_start(out=out, in_=in_), reads=reads, writes=writes)

    def load_cast(self, out, in_, writes, reads=()):
        return self.S.dma("pool", lambda h: h.dma_start(out=out, in_=in_), reads=reads, writes=writes)

    def store(self, out, in_, reads, writes=(), final=False, q="sp"):
        return self.S.dma(q, lambda h: h.dma_start(out=out, in_=in_), reads=reads, writes=writes, final=final)

    def mm_group(self, out, pairs, reads, writes):
        n = len(pairs)

        def fn(h):
            r = None
            for i, (l, rr) in enumerate(pairs):
                r = h.matmul(out, lhsT=l, rhs=rr, start=(i == 0), stop=(i == n - 1))
            return r
        return self.S.op("pe", fn, reads=reads, writes=writes)


def wview(w, c0, c1):
    return w.rearrange("(k p) n -> p k n", p=128)[:, :, c0:c1]


def emit_ret_tables(kb, st, rld, cst):
    S = kb.S
    lg = kb.sb(st, "lg", [128, 8], F32)
    cs = kb.sb(st, "cst", [128, 128 * 6 + 8], F32)
    DT = kb.sb(st, "DT", [128, 8, 128], F32)
    QD = kb.sb(st, "QD", [128, 8, 128], F32)
    kdec = kb.sb(st, "kdec", [128, 8], F32)
    cd = kb.sb(st, "cd", [128, 8], F32)
    tmp = kb.sb(st, "rt_tmp", [128, 128], F32)
    kb.load(lg[:], rld.partition_broadcast(128), writes=["lg"])
    kb.load(cs[:], cst, writes=["cst"])
    S.op("act", lambda h: h.activation(out=lg[:], in_=lg[:], func=AF.Exp), reads=["lg"], writes=["lg"])
    S.op("act", lambda h: h.activation(out=lg[:], in_=lg[:], func=AF.Ln, scale=-1.0, bias=cs[:, 768 + 4:768 + 5]),
         reads=["lg", "cst"], writes=["lg"])
    for d in range(2):
        for hh in range(4):
            i = d * 4 + hh
            S.op("act", lambda h, i=i, d=d: h.activation(out=tmp[:], in_=cs[:, d * 128:(d + 1) * 128], func=AF.Exp,
                                                          scale=lg[:, i:i + 1]),
                 reads=["lg", "cst"], writes=["rt_tmp"])
            S.op("dve", lambda h, i=i, d=d: h.tensor_tensor(out=DT[:, i, :], in0=tmp[:], in1=cs[:, (2 + d) * 128:(3 + d) * 128],
                                                             op=ALU.mult),
                 reads=["rt_tmp", "cst"], writes=["DT"])
            S.op("act", lambda h, i=i, d=d: h.activation(out=QD[:, i, :], in_=cs[:, (4 + d) * 128:(5 + d) * 128], func=AF.Exp,
                                                          scale=lg[:, i:i + 1]),
                 reads=["lg", "cst"], writes=["QD"])
            S.op("act", lambda h, i=i, d=d: h.activation(out=kdec[:, i:i + 1], in_=cs[:, 768 + d:768 + d + 1], func=AF.Exp,
                                                          scale=lg[:, i:i + 1]),
                 reads=["lg", "cst"], writes=["kdec"])
    S.op("act", lambda h: h.activation(out=cd[:], in_=lg[:], func=AF.Exp, scale=128.0), reads=["lg"], writes=["cd"])
    return dict(lg=lg, DT=DT, QD=QD, kdec=kdec, cd=cd, cs=cs)


def ret_consts():
    s = np.arange(128)[:, None].astype(np.float32)
    c = np.arange(128)[None, :].astype(np.float32)
    E0 = np.maximum(c - s, 0.0)
    E1 = np.maximum(s - c, 0.0)
    M0 = (c >= s).astype(np.float32)
    M1 = (s >= c).astype(np.float32)
    R0 = np.broadcast_to(c + 1.0, (128, 128))
    R1 = np.broadcast_to(128.0 - c, (128, 128))
    tail = np.zeros((128, 8), np.float32)
    tail[:, 0] = 127.0 - s[:, 0]
    tail[:, 1] = s[:, 0]
    tail[:, 4] = 1.0
    tail[:, 5] = EPS
    return np.concatenate([E0, E1, M0, M1, R0, R1, tail], axis=1).astype(np.float32)


def build_A():
    nc = bass.Bass("TRN2", target_bir_lowering=False)
    din = lambda name, shape, dt=F32: nc.dram_tensor(name, shape, dt, kind="ExternalInput").ap()
    dout = lambda name, shape, dt=F32: nc.dram_tensor(name, shape, dt, kind="ExternalOutput").ap()
    xT = din("xT", [D, T])
    c2 = din("c2", [128, 16])
    w_ada = din("w_ada", [D, 6 * D])
    b_ada = din("b_ada", [128, 48])
    w_in = din("w_in", [D, 7200])
    rld = din("rld", [1, 8])
    rcst = din("rcst", [128, 776])
    kvg = din("kvg", [1, 256])
    rope_r = din("rope_r", [T, 64])
    rope_m = din("rope_m", [T, 32])
    o_mods = dout("mods", [128, 96])
    o_hxT = dout("hxT", [128, 8, T], BF16)
    o_retK = dout("retK", [T, 256], BF16)
    o_retKT = dout("retKT", [64, 4, T], BF16)
    o_retV = dout("retV", [T, 512], BF16)
    o_retF = dout("retF", [64, 8, 128])
    o_naKT = dout("naKT", [128, 4, T], BF16)
    o_naV = dout("naV", [T, 8, 128], BF16)
    o_ckvnT = dout("ckvnT", [128, 2, T], BF16)
    o_krT = dout("krT", [32, T], BF16)

    with ExitStack() as st0:
        S = Sched(nc, st0)
        kb = KB(nc, S)
        with ExitStack() as st:
            hxT = kb.sb(st, "hxT", [128, 8, T], BF16)
            mods = kb.sb(st, "mods", [128, 48, 2], F32)
            ident = kb.sb(st, "ident", [128, 128], BF16)
            identf = kb.sb(st, "identf", [128, 128], F32)
            S.op("pool", lambda h: h.memset(identf[:], 0.0), writes=["identf"])
            S.op("pool", lambda h: h.affine_select(out=identf[:], in_=identf[:], pattern=[[-1, 128]],
                                                     compare_op=ALU.not_equal, fill=1.0, base=0, channel_multiplier=1),
                 reads=["identf"], writes=["identf"])
            S.op("dve", lambda h: h.tensor_copy(out=ident[:], in_=identf[:]), reads=["identf"], writes=["ident"])
            with ExitStack() as p1:
                emit_mod_hx(kb, p1, xT, c2, w_ada, b_ada, mods, hxT)
                kb.store(o_mods, mods[:].rearrange("p a b -> p (a b)"), reads=["mods"], final=True)
                kb.store(o_hxT, hxT[:], reads=[("hxT", j) for j in range(8)], final=True)
                S.barrier()
                S.emit()
            with ExitStack() as p2:
                emit_kv_side(kb, p2, hxT, ident, w_in, rld, rcst, kvg, rope_r, rope_m,
                             o_retK, o_retKT, o_retV, o_retF, o_naKT, o_naV, o_ckvnT, o_krT)
                S.barrier()
                S.emit(final=True)
    return nc


def emit_mod_hx(kb, st, xT, c2, w_ada, b_ada, mods, hxT, mods_src=None):
    S = kb.S
    if mods_src is not None:
        sc1p = kb.sb(st, "sc1p", [128, 8, 2], F32)
        xs = kb.ring_sb(st, "xs", [128, T], F32, 2)
        kb.load(mods[:].rearrange("p a b -> p (a b)"), mods_src, writes=["mods"])
        emit_hx_only(kb, S, xT, mods, sc1p, xs, hxT)
        return
    c2s = kb.sb(st, "c2s", [128, 16], F32)
    bad = kb.sb(st, "bad", [128, 48], F32)
    sc1p = kb.sb(st, "sc1p", [128, 8, 2], F32)
    wa = kb.ring_sb(st, "wa", [128, 8, 768], F32, 2)
    mps = kb.ps(st, "mod_ps", [128, 48, 2])
    xs = kb.ring_sb(st, "xs", [128, T], F32, 2)
    kb.load(c2s[:], c2, writes=["c2s"])
    kb.load(bad[:], b_ada, writes=["bad"])
    S.op("act", lambda h: h.activation(out=c2s[:], in_=c2s[:], func=AF.Silu), reads=["c2s"], writes=["c2s"])
    c2v = c2s[:].rearrange("p (k w) -> p k w", w=2)
    if MOD_ROWMAJOR:
        mrow = kb.sb(st, "mrow", [2, 6 * D], F32)
        identf2 = kb.sb(st, "identf2", [128, 128], F32)
        E(S, "pool", "memset", [], ["identf2"], identf2[:], 0.0)
        E(S, "pool", "affine_select", ["identf2"], ["identf2"], out=identf2[:], in_=identf2[:], pattern=[[-1, 128]],
          compare_op=ALU.not_equal, fill=1.0, base=0, channel_multiplier=1)
        mrp = kb.ring_ps(st, "mrow_ps", [2, 768], 2)
        for blk in range(8):
            wt, wk = wa.next()
            kb.load(wt[:], wview(w_ada, blk * 768, (blk + 1) * 768), writes=[wk])
            mp, mpk = mrp.next()
            kb.mm_group(mp[:, 0:512], [(c2v[:, k, :], wt[:, k, 0:512]) for k in range(8)], reads=[wk, "c2s"], writes=[(mpk, 0)])
            kb.mm_group(mp[:, 512:768], [(c2v[:, k, :], wt[:, k, 512:768]) for k in range(8)], reads=[wk, "c2s"], writes=[(mpk, 1)])
            E(S, "act", "copy", [(mpk, 0), (mpk, 1)], [("mrow", blk)], out=mrow[:, blk * 768:(blk + 1) * 768], in_=mp[:, :])
        for j in range(48):
            E(S, "pe", "transpose", [("mrow", j // 6), "identf2"], [("mps", j)], mps[:, j, :], mrow[:, j * 128:(j + 1) * 128], identf2[0:2, 0:2])
    else:
        for blk in range(8):
            wt, wk = wa.next()
            kb.load(wt[:], wview(w_ada, blk * 768, (blk + 1) * 768), writes=[wk])
            for jj in range(6):
                j = blk * 6 + jj
                kb.mm_group(mps[:, j, :], [(wt[:, k, jj * 128:(jj + 1) * 128], c2v[:, k, :]) for k in range(8)],
                            reads=[wk, "c2s"], writes=[("mps", j)])
    for w in range(2):
        S.op("dve", lambda h, w=w: h.tensor_tensor(out=mods[:, :, w], in0=mps[:, :, w], in1=bad[:], op=ALU.add),
             reads=[("mps", j) for j in range(48)] + ["bad"], writes=["mods"])
    emit_hx_only(kb, S, xT, mods, sc1p, xs, hxT)


def emit_hx_only(kb, S, xT, mods, sc1p, xs, hxT):
    S.op("dve", lambda h: h.tensor_scalar_add(out=sc1p[:], in0=mods[:, 8:16, :], scalar1=1.0), reads=["mods"], writes=["sc1p"])
    xv = xT.rearrange("(j p) t -> p j t", p=128)
    for j in range(8):
        xt, xk = xs.next()
        kb.load(xt[:], xv[:, j, :], writes=[xk])
        eng = "dve" if j % 2 == 0 else "pool"
        S.op(eng, lambda h, j=j, xt=xt: h.tensor_scalar(out=hxT[:, j, 0:NT], in0=xt[:, 0:NT], scalar1=sc1p[:, j, 0:1],
                                                        scalar2=mods[:, j, 0:1], op0=ALU.mult, op1=ALU.add),
             reads=[xk, "sc1p", "mods"], writes=[("hxTa", j)])
        S.op(eng, lambda h, j=j, xt=xt: h.tensor_scalar(out=hxT[:, j, NT:T], in0=xt[:, NT:T], scalar1=sc1p[:, j, 1:2],
                                                        scalar2=mods[:, j, 1:2], op0=ALU.mult, op1=ALU.add),
             reads=[xk, "sc1p", "mods", ("hxTa", j)], writes=[("hxT", j)])


def rope_tm(S, eng, out, src, cos, sin, t1, t2, nh, half, rkeys, wkey, tkey):
    cb = cos.unsqueeze(1).to_broadcast([128, nh, half]) if nh > 1 else cos
    sn = sin.unsqueeze(1).to_broadcast([128, nh, half]) if nh > 1 else sin
    if nh > 1:
        x1, x2 = src[:, :, 0:half], src[:, :, half:2 * half]
        o1, o2 = out[:, :, 0:half], out[:, :, half:2 * half]
    else:
        x1, x2 = src[:, 0:half], src[:, half:2 * half]
        o1, o2 = out[:, 0:half], out[:, half:2 * half]
    S.op(eng, lambda h: h.tensor_tensor(out=t1, in0=x1, in1=cb, op=ALU.mult), reads=rkeys, writes=[tkey + "1"])
    S.op(eng, lambda h: h.tensor_tensor(out=t2, in0=x2, in1=sn, op=ALU.mult), reads=rkeys, writes=[tkey + "2"])
    S.op(eng, lambda h: h.tensor_tensor(out=o1, in0=t1, in1=t2, op=ALU.subtract), reads=[tkey + "1", tkey + "2"], writes=[(wkey, "a")])
    S.op(eng, lambda h: h.tensor_tensor(out=t1, in0=x1, in1=sn, op=ALU.mult), reads=rkeys + [(wkey, "a")], writes=[tkey + "1"])
    S.op(eng, lambda h: h.tensor_tensor(out=t2, in0=x2, in1=cb, op=ALU.mult), reads=rkeys + [(wkey, "a")], writes=[tkey + "2"])
    S.op(eng, lambda h: h.tensor_tensor(out=o2, in0=t1, in1=t2, op=ALU.add), reads=[tkey + "1", tkey + "2", (wkey, "a")], writes=[wkey])


def emit_kv_side(kb, st, hxT, ident, w_in, rld, rcst, kvg, rope_r, rope_m,
                 o_retK, o_retKT, o_retV, o_retF, o_naKT, o_naV, o_ckvnT, o_krT):
    S = kb.S
    hx_keys = [("hxT", j) for j in range(8)]
    RT = emit_ret_tables(kb, st, rld, rcst)
    w_rkv = kb.sb(st, "w_rkv", [128, 8, 768], BF16)
    w_nk = kb.sb(st, "w_nk", [128, 8, 512], BF16)
    w_nv = kb.sb(st, "w_nv", [128, 8, 512], BF16)
    w_mk = kb.sb(st, "w_mk", [128, 8, 288], BF16)
    kb.load_cast(w_rkv[:], wview(w_in, C_RK, C_GF), writes=["w_rkv"])
    kb.load_cast(w_mk[:], wview(w_in, C_MKV, C_GA), writes=["w_mk"])
    kb.load_cast(w_nv[:], wview(w_in, C_NV, C_MQ), writes=["w_nv"])
    kb.load_cast(w_nk[:], wview(w_in, C_NK, C_NV), writes=["w_nk"])
    rr = kb.sb(st, "rr", [128, NTILE, 64], F32)
    rm = kb.sb(st, "rm", [128, NTILE, 32], F32)
    kb.load(rr[:], rope_r.rearrange("(t p) f -> p t f", p=128), writes=["rr"])
    kb.load(rm[:], rope_m.rearrange("(t p) f -> p t f", p=128), writes=["rm"])
    gain = kb.sb(st, "kvgain", [128, 256], F32)
    kb.load(gain[:], kvg.partition_broadcast(128), writes=["kvgain"])
    k_tm = kb.sb(st, "k_tm", [128, NTILE, 256], BF16)
    v_tm = kb.sb(st, "v_tm", [128, NTILE, 512], BF16)
    kT = kb.sb(st, "kT", [64, 4, T], BF16)
    vaug_r = kb.ring_sb(st, "vaug", [128, 8, 128], BF16, 2)
    ckvnT = kb.sb(st, "ckvnT", [128, 2, T], BF16)
    krT = kb.sb(st, "krT", [32, T], BF16)
    naKT_r = kb.ring_sb(st, "naKT", [128, T], BF16, 2)
    for _ in range(2):
        vt_, vk_ = vaug_r.next()
        S.op("pool", lambda h, vt_=vt_: h.memset(vt_[:], 1.0), writes=[vk_])
    ps_a = kb.ring_ps(st, "ps_a", [128, 512], 2)
    ps_b = kb.ring_ps(st, "ps_b", [128, 512], 2)
    ps_t = kb.ring_ps(st, "ps_t", [128, 128], 2, BF16)
    kf = kb.ring_sb(st, "kf", [128, 256], F32, 2)
    kr32 = kb.ring_sb(st, "kr32", [128, 32], F32, 2)
    krb = kb.ring_sb(st, "krb", [128, 32], BF16, 2)
    ckvn = kb.ring_sb(st, "ckvn", [128, 256], BF16, 2)
    t1 = kb.sb(st, "t1", [128, 128], F32)
    t2 = kb.sb(st, "t2", [128, 128], F32)
    t1b = kb.sb(st, "t1b", [128, 128], F32)
    t2b = kb.sb(st, "t2b", [128, 128], F32)
    u1 = kb.sb(st, "u1", [128, 16], F32)
    u2 = kb.sb(st, "u2", [128, 16], F32)
    junk = kb.sb(st, "junk", [128, 256], F32)
    ss = kb.ring_sb(st, "ss", [128, 2], F32, 2)
    eps_ap = RT["cs"][:, 768 + 5:768 + 6]
    Fst = kb.sb(st, "Fst", [64, 8, 128], F32)
    S.op("dve", lambda h: h.memset(Fst[:], 0.0), writes=[("F", i) for i in range(8)])
    kd_r = kb.ring_sb(st, "kdA", [128, 64], BF16, 4)
    ps_sA = kb.ring_ps(st, "ps_sA", [64, 128], 2)

    def scan_step(d, t):
        for hh in range(4):
            i = d * 4 + hh
            kdt, kdk = kd_r.next()
            E(S, "act", "activation", [("k_tm", t), "kdec"], [kdk], out=kdt[:], in_=k_tm[:, t, hh * 64:(hh + 1) * 64], func=AF.Copy,
              scale=RT["kdec"][:, i:i + 1])
            pst, psk = ps_sA.next()
            kb.mm_group(pst[:], [(kdt[:], v_tm[:, t, hh * 128:(hh + 1) * 128])], reads=[kdk, ("v_tm", t)], writes=[psk])
            E(S, "dve", "scalar_tensor_tensor", [psk, ("F", i), "cd"], [("F", i)], out=Fst[:, i, :], in0=Fst[:, i, :], scalar=RT["cd"][0:64, i:i + 1],
              in1=pst[:], op0=ALU.mult, op1=ALU.add)

    for t in range(NTILE):
        tok = slice(t * 128, (t + 1) * 128)
        if 1 <= t <= 16:
            scan_step(0, t - 1)
        pa, pak = ps_a.next()
        kb.mm_group(pa[:, 0:512], [(hxT[:, k, tok], w_rkv[:, k, 0:512]) for k in range(8)],
                    reads=hx_keys + ["w_rkv"], writes=[pak])
        pb, pbk = ps_b.next()
        kb.mm_group(pb[:, 0:256], [(hxT[:, k, tok], w_rkv[:, k, 512:768]) for k in range(8)],
                    reads=hx_keys + ["w_rkv"], writes=[pbk])
        S.op("act", lambda h, pa=pa, t=t: h.copy(out=v_tm[:, t, 0:256], in_=pa[:, 256:512]), reads=[pak], writes=[("v_tm_a", t)])
        S.op("act", lambda h, pb=pb, t=t: h.copy(out=v_tm[:, t, 256:512], in_=pb[:, 0:256]), reads=[pbk, ("v_tm_a", t)], writes=[("v_tm", t)])
        kft, kfk = kf.next()
        S.op("act", lambda h, pa=pa, kft=kft: h.copy(out=kft[:], in_=pa[:, 0:256]), reads=[pak], writes=[kfk])
        rope_tm(S, "pool" if t % 2 == 0 else "dve", k_tm[:, t, :].rearrange("p (h f) -> p h f", h=4), kft[:].rearrange("p (h f) -> p h f", h=4),
                rr[:, t, 0:32], rr[:, t, 32:64], (t1 if t % 2 == 0 else t1b)[:].rearrange("p (h f) -> p h f", h=4),
                (t2 if t % 2 == 0 else t2b)[:].rearrange("p (h f) -> p h f", h=4),
                4, 32, [kfk, "rr"], ("k_tm", t), "ropeA" if t % 2 == 0 else "ropeAb")
        kb.store(o_retK[tok, :], k_tm[:, t, :], reads=[("k_tm", t)], final=True)
        kb.store(o_retV[tok, :], v_tm[:, t, :], reads=[("v_tm", t)], final=True, q="act")
        for hh in range(4):
            pt, ptk = ps_t.next()
            S.op("pe", lambda h, pt=pt, t=t, hh=hh: h.transpose(pt[0:64, :], k_tm[:, t, hh * 64:(hh + 1) * 64], ident[:]),
                 reads=[("k_tm", t), "ident"], writes=[ptk])
            S.op("dve", lambda h, pt=pt, hh=hh, tok=tok: h.tensor_copy(out=kT[:, hh, tok], in_=pt[0:64, :]), reads=[ptk], writes=[("kT", t, hh)])
        pa, pak = ps_a.next()
        kb.mm_group(pa[:, 0:512], [(hxT[:, k, tok], w_nv[:, k, :]) for k in range(8)], reads=hx_keys + ["w_nv"], writes=[pak])
        vg, vgk = vaug_r.next()
        S.op("act", lambda h, pa=pa, vg=vg: h.copy(out=vg[:, :, 0:64], in_=pa[:, 0:512].rearrange("p (h f) -> p h f", h=8)),
             reads=[pak, vgk], writes=[vgk])
        kb.store(o_naV[tok, :, :], vg[:], reads=[vgk], final=True)
        pb, pbk = ps_b.next()
        kb.mm_group(pb[:, 0:288], [(hxT[:, k, tok], w_mk[:, k, :]) for k in range(8)], reads=hx_keys + ["w_mk"], writes=[pbk])
        sst, ssk = ss.next()
        S.op("act", lambda h, pb=pb, sst=sst: h.activation(out=junk[:], in_=pb[:, 0:256], func=AF.Square, accum_out=sst[:, 0:1]),
             reads=[pbk], writes=["junk", ssk])
        S.op("act", lambda h, sst=sst: h.activation(out=sst[:, 1:2], in_=sst[:, 0:1], func=AF.Sqrt, scale=1.0 / 256.0, bias=eps_ap),
             reads=[ssk, "cst"], writes=[ssk])
        k32, k32k = kr32.next()
        S.op("act", lambda h, pb=pb, k32=k32: h.copy(out=k32[:], in_=pb[:, 256:288]), reads=[pbk], writes=[k32k])
        S.op("dve", lambda h, sst=sst: h.reciprocal(out=sst[:, 1:2], in_=sst[:, 1:2]), reads=[ssk], writes=[ssk])
        cn, cnk = ckvn.next()
        S.op("dve", lambda h, pb=pb, sst=sst, cn=cn: h.scalar_tensor_tensor(out=cn[:], in0=pb[:, 0:256], scalar=sst[:, 1:2], in1=gain[:],
                                                                           op0=ALU.mult, op1=ALU.mult),
             reads=[pbk, ssk, "kvgain", k32k], writes=[cnk])
        kbt, kbk = krb.next()
        rope_tm(S, "dve", kbt[:], k32[:], rm[:, t, 0:16], rm[:, t, 16:32], u1[:], u2[:], 1, 16, [k32k, "rm"], kbk, "ropeB")
        for kc in range(2):
            pt, ptk = ps_t.next()
            S.op("pe", lambda h, pt=pt, cn=cn, kc=kc: h.transpose(pt[:], cn[:, kc * 128:(kc + 1) * 128], ident[:]),
                 reads=[cnk, "ident"], writes=[ptk])
            S.op("dve", lambda h, pt=pt, kc=kc, tok=tok: h.tensor_copy(out=ckvnT[:, kc, tok], in_=pt[:]), reads=[ptk], writes=[("ckvnT", t, kc)])
        pt, ptk = ps_t.next()
        S.op("pe", lambda h, pt=pt, kbt=kbt: h.transpose(pt[0:32, :], kbt[:], ident[:]), reads=[kbk, "ident"], writes=[ptk])
        S.op("dve", lambda h, pt=pt, tok=tok: h.tensor_copy(out=krT[:, tok], in_=pt[0:32, :]), reads=[ptk], writes=[("krT", t)])
    bwd_t = list(range(15, -1, -1))
    for hp in range(4):
        nk, nkk = naKT_r.next()
        for (b0, bn) in TOKBLKS:
            if bwd_t:
                scan_step(1, bwd_t.pop(0))
            pa, pak = ps_a.next()
            kb.mm_group(pa[:, 0:bn], [(w_nk[:, k, hp * 128:(hp + 1) * 128], hxT[:, k, b0:b0 + bn]) for k in range(8)],
                        reads=hx_keys + ["w_nk"], writes=[pak])
            S.op("act", lambda h, pa=pa, nk=nk, b0=b0, bn=bn: h.copy(out=nk[:, b0:b0 + bn], in_=pa[:, 0:bn]),
                 reads=[pak], writes=[nkk])
        kb.store(o_naKT[:, hp, :], nk[:], reads=[nkk], final=True)
    kb.store(o_ckvnT, ckvnT[:], reads=[("ckvnT", t, kc) for t in range(NTILE) for kc in range(2)], final=True)
    kb.store(o_krT, krT[:], reads=[("krT", t) for t in range(NTILE)], final=True)
    kb.store(o_retKT, kT[:], reads=[("kT", t, hh) for t in range(NTILE) for hh in range(4)], final=True)
    while bwd_t:
        scan_step(1, bwd_t.pop(0))
    kb.store(o_retF, Fst[:], reads=[("F", i) for i in range(8)], final=True)


def emit_ret_scan(kb, st, RT, k_tm, v_tm, Sst, tiles, sprev, tag):
    S = kb.S
    kd = kb.ring_sb(st, "kd" + tag, [128, 64], BF16, 3)
    ps_s = kb.ring_ps(st, "ps_s" + tag, [64, 128], 2)
    for d in range(2):
        order = tiles if d == 0 else tiles[::-1]
        for t in order:
            for hh in range(4):
                i = d * 4 + hh
                if sprev is not None:
                    S.op("act", lambda h, i=i, t=t: h.copy(out=sprev[:, i, t, :], in_=Sst[:, i, :]), reads=[("F", i)],
                         writes=[("sprev", i, t)])
                kdt, kdk = kd.next()
                S.op("act", lambda h, kdt=kdt, t=t, hh=hh, i=i: h.activation(out=kdt[:], in_=k_tm[:, t, hh * 64:(hh + 1) * 64], func=AF.Copy,
                                                                              scale=RT["kdec"][:, i:i + 1]),
                     reads=[("k_tm", t), "kdec"], writes=[kdk])
                pst, psk = ps_s.next()
                kb.mm_group(pst[:], [(kdt[:], v_tm[:, t, hh * 128:(hh + 1) * 128])], reads=[kdk, ("v_tm", t)], writes=[psk])
                S.op("dve", lambda h, pst=pst, i=i: h.scalar_tensor_tensor(out=Sst[:, i, :], in0=Sst[:, i, :], scalar=RT["cd"][0:64, i:i + 1],
                                                                            in1=pst[:], op0=ALU.mult, op1=ALU.add),
                     reads=[psk, ("F", i), "cd"], writes=[("F", i)])


def rope_tables(q):
    idx = q * NT + np.arange(NT)
    row = (idx // 64).astype(np.float32)
    col = (idx % 64).astype(np.float32)

    def tab(rot_dim):
        nf = rot_dim // 4
        inv = (10000.0 ** (-2.0 * np.arange(nf, dtype=np.float32) / (rot_dim // 2))).astype(np.float32)
        ang = np.concatenate([row[:, None] * inv, col[:, None] * inv], axis=-1).astype(np.float32)
        cs = np.concatenate([np.cos(ang), np.sin(ang)], axis=-1).astype(np.float32)
        z = np.concatenate([np.ones((NZ, rot_dim // 2), np.float32), np.zeros((NZ, rot_dim // 2), np.float32)], axis=-1)
        return np.concatenate([cs, z], axis=0)
    return tab(64), tab(32)


def fm(v):
    return np.ascontiguousarray(v.reshape(-1, 128).T)


def prep_A(inp, l, core, xT_core):
    b, q = core // 4, core % 4
    rr, rm = rope_tables(q)
    c2 = np.stack([inp["c"][b], inp["c_ctx"]], axis=-1)
    c2 = np.ascontiguousarray(c2.reshape(8, 128, 2).transpose(1, 0, 2).reshape(128, 16))
    return {
        "xT": xT_core, "c2": c2, "w_ada": inp["w_ada"][l], "b_ada": fm(inp["b_ada"][l]),
        "w_in": inp["w_in"][l], "rld": np.ascontiguousarray(inp["ret_log_decay"][l].reshape(1, 8)),
        "rcst": ret_consts(), "kvg": np.ascontiguousarray(inp["mla_kv_norm"][l].reshape(1, 256)),
        "rope_r": (rr * np.float32(0.125)).astype(np.float32), "rope_m": rm,
    }


NKEY = 2816
NKM = 8192 + 256
FFBLK = 256
MOD_ROWMAJOR = True
MOD_PREFETCH = True
MLA_PAIRS = False


def E(S, eng, meth, reads, writes, *args, **kw):
    return S.op(eng, lambda h: getattr(h, meth)(*args, **kw), reads=reads, writes=writes)


def make_ident(kb, st):
    S = kb.S
    ident = kb.sb(st, "ident", [128, 128], BF16)
    identf = kb.sb(st, "identf", [128, 128], F32)
    E(S, "pool", "memset", [], ["identf"], identf[:], 0.0)
    E(S, "pool", "affine_select", ["identf"], ["identf"], out=identf[:], in_=identf[:], pattern=[[-1, 128]],
      compare_op=ALU.not_equal, fill=1.0, base=0, channel_multiplier=1)
    E(S, "dve", "tensor_copy", ["identf"], ["ident"], out=ident[:], in_=identf[:])
    return ident, identf


def rope2(S, eng, x1, x2, o1, o2, cb, sn, t1, t2, rkeys, wkey, tkey):
    E(S, eng, "tensor_tensor", rkeys, [(tkey, 1)], out=t1, in0=x1, in1=cb, op=ALU.mult)
    E(S, eng, "tensor_tensor", rkeys, [(tkey, 2)], out=t2, in0=x2, in1=sn, op=ALU.mult)
    E(S, eng, "tensor_tensor", [(tkey, 1), (tkey, 2)], [(wkey, "a")], out=o1, in0=t1, in1=t2, op=ALU.subtract)
    E(S, eng, "tensor_tensor", rkeys + [(wkey, "a")], [(tkey, 1)], out=t1, in0=x1, in1=sn, op=ALU.mult)
    E(S, eng, "tensor_tensor", rkeys + [(wkey, "a")], [(tkey, 2)], out=t2, in0=x2, in1=cb, op=ALU.mult)
    E(S, eng, "tensor_tensor", [(tkey, 1), (tkey, 2), (wkey, "a")], [wkey], out=o2, in0=t1, in1=t2, op=ALU.add)


def build_B():
    nc = bass.Bass("TRN2", target_bir_lowering=False)
    din = lambda name, shape, dt=F32: nc.dram_tensor(name, shape, dt, kind="ExternalInput").ap()
    dout = lambda name, shape, dt=F32: nc.dram_tensor(name, shape, dt, kind="ExternalOutput").ap()
    I = dict(
        xT=din("xT", [D, T]), mods=din("mods", [128, 96]), hxT=din("hxT", [128, 8, T], BF16),
        retK=din("retK", [T, 256], BF16), retKT=din("retKT", [64, 4, T], BF16), retV=din("retV", [T, 512], BF16),
        Fsl=din("Fsl", [64, 8, 3, 128]), fexp=din("fexp", [128, 8]), rld=din("rld", [1, 8]), rcst=din("rcst", [128, 776]),
        gng=din("gng", [1, 512]), rope_q=din("rope_q", [T, 64]), rope_mq=din("rope_mq", [T, 32]),
        naKT=din("naKT", [128, 4, NKEY], BF16), naV=din("naV", [NKEY, 8, 128], BF16),
        nabias=din("nabias", [128, 8, 24, 64]), namask=din("namask", [128, 32, 512], BF16),
        ckvnT=din("ckvnT", [128, 2, NKM], BF16), krT=din("krT", [32, NKM], BF16),
        qg=din("qg", [1, 256]), w_qup=din("w_qup", [256, 768]), w_kvup=din("w_kvup", [256, 1024]),
        w_in=din("w_in", [D, 7200]), w_br=din("w_br", [3, 512, D]), w_out=din("w_out", [D, D]),
        w_ff1=din("w_ff1", [D, 4 * D]), w_ff2=din("w_ff2", [4 * D, D]), lng=din("lng", [128, 32]),
    )
    xoT = dout("xoT", [D, T])
    dbg = dict(yaT=dout("yaT", [128, 4, T], BF16), ybT=dout("ybT", [128, 4, T], BF16), ycT=dout("ycT", [128, 4, T], BF16),
               x1T=dout("x1T", [128, 8, T]))
    with ExitStack() as st0:
        S = Sched(nc, st0)
        kb = KB(nc, S)
        with ExitStack() as ph:
            emit_retention(kb, ph, I, dbg["yaT"])
            S.barrier(); S.emit()
        with ExitStack() as ph:
            emit_na(kb, ph, I, dbg["ybT"])
            S.barrier(); S.emit()
        with ExitStack() as ph:
            emit_mla(kb, ph, I, dbg["ycT"])
            S.barrier(); S.emit()
        with ExitStack() as ph:
            emit_merge(kb, ph, I, dbg)
            S.barrier(); S.emit()
        with ExitStack() as ph:
            emit_ffn(kb, ph, I, dbg["x1T"], xoT)
            S.barrier(); S.emit(final=True)
    return nc


def emit_retention(kb, st, I, o_yaT):
    S = kb.S
    RT = emit_ret_tables(kb, st, I["rld"], I["rcst"])
    ident, _ = make_ident(kb, st)
    eps_ap = RT["cs"][:, 768 + 5:768 + 6]
    hxT = kb.sb(st, "hxT", [128, 8, T], BF16)
    kb.load(hxT[:], I["hxT"], writes=["hxT"])
    k_tm = kb.sb(st, "k_tm", [128, NTILE, 256], BF16)
    v_tm = kb.sb(st, "v_tm", [128, NTILE, 512], BF16)
    kT = kb.sb(st, "kT", [64, 4, T], BF16)
    qT = kb.sb(st, "qT", [64, 4, T], BF16)
    kb.load(k_tm[:], I["retK"].rearrange("(t p) f -> p t f", p=128), writes=["k_tm"])
    kb.load(v_tm[:], I["retV"].rearrange("(t p) f -> p t f", p=128), writes=["v_tm"])
    kb.load(kT[:], I["retKT"], writes=["kT"])
    w_g = kb.sb(st, "w_g", [128, 8, 1024], BF16)
    kb.load_cast(w_g[:], wview(I["w_in"], C_GF, C_NQ), writes=["w_g"])
    gng = kb.sb(st, "gng", [128, 512], F32)
    kb.load(gng[:], I["gng"].partition_broadcast(128), writes=["gng"])
    Fsl = kb.sb(st, "Fsl", [64, 8, 4, 128], F32)
    for s_ in range(4):
        kb.load(Fsl[:, :, s_, :], I["FB_out"][s_ * 64:(s_ + 1) * 64, :].rearrange("p (i v) -> p i v", i=8), writes=[("Fsl", s_)], reads=["FB_out"])
    cc = kb.sb(st, "cc", [128, 32], F32)
    kb.load(cc[:], I["cc"], writes=["cc"])
    coef = kb.sb(st, "coef", [128, 8, 5], F32)
    for d in range(2):
        fo = 12 if d == 0 else 17
        mo = 22 if d == 0 else 26
        for hh in range(4):
            i = d * 4 + hh
            E(S, "act", "activation", ["cc", "lg"], [("coefe", i)], out=coef[:, i, :], in_=cc[:, fo:fo + 5], func=AF.Exp, scale=RT["lg"][:, i:i + 1])
            E(S, "dve", "tensor_tensor", [("coefe", i), "cc"], [("coef", i)], out=coef[:, i, 0:4], in0=coef[:, i, 0:4], in1=cc[:, mo:mo + 4], op=ALU.mult)
    ya = kb.sb(st, "ya_acc", [128, NTILE, 512], F32)
    Sst = kb.sb(st, "Sst", [64, 8, 128], F32)
    hv = lambda ap: ap.rearrange("p (h f) -> p h f", h=4)
    with ExitStack() as s2:
        w_q = kb.sb(s2, "w_q", [128, 8, 256], BF16)
        kb.load_cast(w_q[:], wview(I["w_in"], C_RQ, C_RK), writes=["w_q"])
        rq = kb.sb(s2, "rq", [128, NTILE, 64], F32)
        kb.load(rq[:], I["rope_q"].rearrange("(t p) f -> p t f", p=128), writes=["rq"])
        ps_q = kb.ring_ps(s2, "ps_q", [128, 256], 2)
        ps_t = kb.ring_ps(s2, "ps_tq", [128, 128], 4, BF16)
        qf = kb.ring_sb(s2, "qf", [128, 256], F32, 3)
        qb = kb.ring_sb(s2, "qb", [128, 256], BF16, 3)
        t1 = kb.ring_sb(s2, "t1", [128, 128], F32, 2)
        t2 = kb.ring_sb(s2, "t2", [128, 128], F32, 2)
        for t in range(NTILE):
            tok = slice(t * 128, (t + 1) * 128)
            pq, pqk = ps_q.next()
            kb.mm_group(pq[:], [(hxT[:, k, tok], w_q[:, k, :]) for k in range(8)], reads=["hxT", "w_q"], writes=[pqk])
            qft, qfk = qf.next()
            E(S, "act", "copy", [pqk], [qfk], out=qft[:], in_=pq[:])
            qbt, qbk = qb.next()
            cb = rq[:, t, 0:32].unsqueeze(1).to_broadcast([128, 4, 32])
            sn = rq[:, t, 32:64].unsqueeze(1).to_broadcast([128, 4, 32])
            t1t, t1k = t1.next()
            t2t, _ = t2.next()
            rope2(S, "dve" if t % 2 == 0 else "pool", hv(qft[:])[:, :, 0:32], hv(qft[:])[:, :, 32:64], hv(qbt[:])[:, :, 0:32], hv(qbt[:])[:, :, 32:64],
                  cb, sn, hv(t1t[:]), hv(t2t[:]), [qfk, "rq"], qbk, t1k)
            for hh in range(4):
                pt, ptk = ps_t.next()
                E(S, "pe", "transpose", [qbk, "ident"], [ptk], pt[0:64, :], qbt[:, hh * 64:(hh + 1) * 64], ident[:])
                E(S, "act" if hh % 2 else "dve", "copy" if hh % 2 else "tensor_copy", [ptk], [("qT", t, hh)], out=qT[:, hh, tok], in_=pt[0:64, :])
        S.barrier(); S.emit()
    import os as _os
    if _os.environ.get("RET_STOP") == "pre":
        return
    NCH = 8
    ps_g = kb.ring_ps(st, "ps_g", [128, 512], 2)
    ps_sc = kb.ring_ps_sliced(st, "ps_sc", 128, NCH)
    ps_o = kb.ring_ps_sliced(st, "ps_o", 128, NCH)
    ps_t = kb.ring_ps(st, "ps_t", [128, 128], 1, BF16)
    sg = kb.ring_sb(st, "sg", [128, 512], F32, 2)
    sT = kb.ring_sb(st, "sT", [128, 128], BF16, NCH)
    qs = kb.ring_sb(st, "qs", [64, 128], BF16, NCH)
    Sb = kb.ring_sb(st, "Sb", [64, 128], BF16, NCH)
    kd = kb.ring_sb(st, "kd", [128, 64], BF16, NCH)
    stats = kb.ring_sb(st, "stats", [128, 6], F32, NCH)
    mv = kb.ring_sb(st, "mv", [128, 4], F32, NCH)
    yn = kb.ring_sb(st, "yn", [128, 128], F32, NCH)
    yab = kb.ring_sb(st, "yab", [128, 512], BF16, 2)
    yaT_r = kb.ring_sb(st, "yaT", [128, 4, 128], BF16, 2)
    E(S, "pool", "memset", [], [("ya", t) for t in range(NTILE)], ya[:], 0.0)
    E(S, "dve", "memset", [], [("F", i) for i in range(8)], Sst[:], 0.0)

    def do_pairs(pairs):
        ch = []
        for (d, t) in pairs:
            tok = slice(t * 128, (t + 1) * 128)
            pg, pgk = ps_g.next()
            kb.mm_group(pg[:], [(hxT[:, k, tok], w_g[:, k, d * 512:(d + 1) * 512]) for k in range(8)], reads=["hxT", "w_g"], writes=[pgk])
            sgt, sgk = sg.next()
            E(S, "act", "activation", [pgk], [sgk], out=sgt[:], in_=pg[:], func=AF.Silu)
            for hh in range(4):
                ch.append(dict(d=d, t=t, hh=hh, i=d * 4 + hh, tok=tok, sg=sgt, sgk=sgk))
        for c in ch:
            c["psc"], c["psck"] = ps_sc.next()
            kb.mm_group(c["psc"][:], [(kT[:, c["hh"], c["tok"]], qT[:, c["hh"], c["tok"]])], reads=["kT", ("qT", c["t"], c["hh"])], writes=[c["psck"]])
        for c in ch:
            i, hh, t, tok = c["i"], c["hh"], c["t"], c["tok"]
            c["sT"], c["sTk"] = sT.next()
            E(S, "dve", "tensor_tensor", [c["psck"], "DT"], [c["sTk"]], out=c["sT"][:], in0=c["psc"][:], in1=RT["DT"][:, i, :], op=ALU.mult)
            c["qs"], c["qsk"] = qs.next()
            E(S, "pool", "tensor_tensor", [("qT", t, hh), "QD"], [c["qsk"]], out=c["qs"][:], in0=qT[:, hh, tok], in1=RT["QD"][0:64, i, :], op=ALU.mult)
            c["Sb"], c["Sbk"] = Sb.next()
            E(S, "act", "copy", [("F", i)], [c["Sbk"]], out=c["Sb"][:], in_=Sst[:, i, :])
            c["kd"], c["kdk"] = kd.next()
            E(S, "act", "activation", ["k_tm", "kdec"], [c["kdk"]], out=c["kd"][:], in_=k_tm[:, t, hh * 64:(hh + 1) * 64], func=AF.Copy, scale=RT["kdec"][:, i:i + 1])
        for c in ch:
            hh, t = c["hh"], c["t"]
            c["po"], c["pok"] = ps_o.next()
            kb.mm_group(c["po"][:], [(c["sT"][:], v_tm[:, t, hh * 128:(hh + 1) * 128]), (c["qs"][:], c["Sb"][:])],
                        reads=[c["sTk"], "v_tm", c["qsk"], c["Sbk"]], writes=[c["pok"]])
            kb.mm_group(c["psc"][0:64, :], [(c["kd"][:], v_tm[:, t, hh * 128:(hh + 1) * 128])], reads=[c["kdk"], "v_tm", c["sTk"]], writes=[c["psck"]])
        for c in ch:
            i = c["i"]
            E(S, "dve", "scalar_tensor_tensor", [c["psck"], ("F", i), "cd", c["Sbk"]], [("F", i)], out=Sst[:, i, :], in0=Sst[:, i, :],
              scalar=RT["cd"][0:64, i:i + 1], in1=c["psc"][0:64, :], op0=ALU.mult, op1=ALU.add)
            c["st"], c["stk"] = stats.next()
            E(S, "dve", "bn_stats", [c["pok"]], [c["stk"]], out=c["st"][:], in_=c["po"][:])
        for c in ch:
            c["mv"], c["mvk"] = mv.next()
            E(S, "dve", "bn_aggr", [c["stk"]], [c["mvk"]], out=c["mv"][:, 0:2], in_=c["st"][:])
        for c in ch:
            E(S, "act", "activation", [c["mvk"], "cst"], [c["mvk"]], out=c["mv"][:, 2:3], in_=c["mv"][:, 1:2], func=AF.Sqrt, bias=eps_ap, scale=1.0)
        for c in ch:
            E(S, "dve", "reciprocal", [c["mvk"]], [c["mvk"]], out=c["mv"][:, 2:3], in_=c["mv"][:, 2:3])
        for c in ch:
            c["yn"], c["ynk"] = yn.next()
            E(S, "dve", "tensor_scalar", [c["pok"], c["mvk"]], [c["ynk"]], out=c["yn"][:], in0=c["po"][:], scalar1=c["mv"][:, 0:1], scalar2=c["mv"][:, 2:3],
              op0=ALU.subtract, op1=ALU.mult)
        for c in ch:
            hh, t = c["hh"], c["t"]
            yslc = ya[:, t, hh * 128:(hh + 1) * 128]
            E(S, "pool", "tensor_tensor", [c["ynk"], c["sgk"]], [c["ynk"]], out=c["yn"][:], in0=c["yn"][:], in1=c["sg"][:, hh * 128:(hh + 1) * 128], op=ALU.mult)
            E(S, "pool", "tensor_tensor", [c["ynk"], ("ya", t)], [("ya", t)], out=yslc, in0=yslc, in1=c["yn"][:], op=ALU.add)

    if _os.environ.get("RET_STOP") == "memset":
        return
    do_pairs([(0, 16), (1, 17)])
    if _os.environ.get("RET_STOP") == "one":
        return
    do_pairs([(0, 17), (1, 16)])
    for d in range(2):
        for hh in range(4):
            i = d * 4 + hh
            E(S, "dve", "tensor_scalar_mul", [("F", i), ("coef", i)], [("F", i)], out=Sst[:, i, :], in0=Sst[:, i, :], scalar1=coef[0:64, i, 4:5])
            for j in range(4):
                E(S, "dve", "scalar_tensor_tensor", [("F", i), ("coef", i), ("Fsl", j)], [("F", i)], out=Sst[:, i, :], in0=Fsl[:, i, j, :],
                  scalar=coef[0:64, i, j:j + 1], in1=Sst[:, i, :], op0=ALU.mult, op1=ALU.add)
    for k in range(16):
        do_pairs([(0, k), (1, 15 - k)])
    for t in range(NTILE):
        tok = slice(t * 128, (t + 1) * 128)
        ybt, ybk = yab.next()
        E(S, "dve", "tensor_tensor", [("ya", t), "gng"], [ybk], out=ybt[:], in0=ya[:, t, :], in1=gng[:], op=ALU.mult)
        yat, yatk = yaT_r.next()
        for kc in range(4):
            pt, ptk = ps_t.next()
            E(S, "pe", "transpose", [ybk, "ident"], [ptk], pt[:], ybt[:, kc * 128:(kc + 1) * 128], ident[:])
            E(S, "act", "copy", [ptk], [(yatk, kc)], out=yat[:, kc, :], in_=pt[:])
        kb.store(o_yaT[:, :, tok], yat[:], reads=[(yatk, kc) for kc in range(4)], writes=["yaT_d"])


class AttnPipe:
    def __init__(self, kb, S, rings, ident=None, depth=2):
        self.kb, self.S, self.rings, self.ident, self.depth = kb, S, rings, ident, depth
        self.steps = []

    def add(self, kT_list, q_ap, qkeys, v_list, out_ap, okey, extra=None):
        n = len(kT_list)
        blk = dict(q=q_ap, qk=list(qkeys), out=out_ap, okey=okey, n=n, nq=q_ap.shape[-1], po=None)
        for i in range(n):
            self.steps.append(dict(blk=blk, i=i, k=kT_list[i], v=v_list[i], ex=(extra[i] if extra is not None else None)))

    def _qk(self, st):
        ps_s = self.rings[0]
        blk = st["blk"]; nq = blk["nq"]
        ps, psk = ps_s.next()
        st["ps"], st["psk"] = ps, psk
        kl, kkeys = st["k"]
        pairs = [(kl, blk["q"])]
        rk = list(kkeys) + blk["qk"]
        if st["ex"] is not None:
            pairs.append((self.ident, st["ex"][0]))
            rk += list(st["ex"][2]) + ["ident"]
        self.kb.mm_group(ps[:, 0:nq], pairs, reads=rk, writes=[psk])

    def _exp(self, st):
        S = self.S
        _, _, e_r, p_r, _ = self.rings
        nq = st["blk"]["nq"]
        et, ek = e_r.next()
        E(S, "act", "activation", [st["psk"]], [ek], out=et[:, 0:nq], in_=st["ps"][:, 0:nq], func=AF.Exp)
        if st["ex"] is not None:
            pt, pk = p_r.next()
            E(S, "dve", "tensor_tensor", [ek] + list(st["ex"][2]), [pk], out=pt[:, 0:nq], in0=et[:, 0:nq], in1=st["ex"][1], op=ALU.mult)
            et, ek = pt, pk
        st["e"], st["ek"] = et, ek

    def _pv(self, st):
        S = self.S
        _, ps_o, _, _, rden_r = self.rings
        blk = st["blk"]; nq = blk["nq"]; i = st["i"]; n = blk["n"]
        if i == 0:
            blk["po"], blk["pok"] = ps_o.next()
        po, pok = blk["po"], blk["pok"]
        vl, vkeys = st["v"]
        et = st["e"]
        S.op("pe", lambda h: h.matmul(po[:, 0:nq], lhsT=vl, rhs=et[:, 0:nq], start=(i == 0), stop=(i == n - 1)),
             reads=[st["ek"]] + list(vkeys), writes=[pok])
        if i == n - 1:
            rd, rdk = rden_r.next()
            E(S, "dve", "reciprocal", [pok], [rdk], out=rd[:, 0:nq], in_=po[64:128, 0:nq])
            E(S, "dve", "tensor_tensor", [pok, rdk], [blk["okey"]], out=blk["out"], in0=po[0:64, 0:nq], in1=rd[:, 0:nq], op=ALU.mult)

    def run_pairs(self):
        S, kb = self.S, self.kb
        ps_s, ps_o, e_r, _, rden_r = self.rings
        st = self.steps
        assert len(st) % 2 == 0
        units = [(st[2 * u], st[2 * u + 1]) for u in range(len(st) // 2)]

        def qk(u):
            a, b = units[u]
            assert a["blk"] is b["blk"]
            ps, psk = ps_s.next()
            nq = a["blk"]["nq"]
            for half, s_ in enumerate((a, b)):
                kl, kkeys = s_["k"]
                kb.mm_group(ps[:, half * 512:half * 512 + nq], [(kl, s_["blk"]["q"])], reads=list(kkeys) + s_["blk"]["qk"], writes=[psk])
            a["ps"], a["psk"] = ps, psk

        def ex(u):
            a, b = units[u]
            nq = a["blk"]["nq"]
            et, ek = e_r.next()
            if nq == 512:
                E(S, "act", "activation", [a["psk"]], [ek], out=et[:, 0:1024], in_=a["ps"][:, 0:1024], func=AF.Exp)
            else:
                for half in range(2):
                    E(S, "act", "activation", [a["psk"]], [ek], out=et[:, half * 512:half * 512 + nq], in_=a["ps"][:, half * 512:half * 512 + nq], func=AF.Exp)
            a["e"], a["ek"] = et, ek

        def pv(u):
            a, b = units[u]
            et, ek = a["e"], a["ek"]
            for half, s_ in enumerate((a, b)):
                s_["e"], s_["ek"] = et[:, half * 512:(half + 1) * 512], ek
                self._pv(s_)

        qk(0)
        for u in range(len(units)):
            ex(u)
            if u + 1 < len(units):
                qk(u + 1)
            pv(u)
        self.steps = []

    def run(self):
        st = self.steps
        for j in range(min(self.depth, len(st))):
            self._qk(st[j])
        for j in range(len(st)):
            self._exp(st[j])
            if j + self.depth < len(st):
                self._qk(st[j + self.depth])
            self._pv(st[j])
        self.steps = []


def attn_rings_pairs(kb, st, tag):
    return (kb.ring_ps(st, "ps_s" + tag, [128, 1024], 2), kb.ring_ps(st, "ps_o" + tag, [128, 512], 2),
            kb.ring_sb(st, "e_r" + tag, [128, 1024], BF16, 3), None,
            kb.ring_sb(st, "rden" + tag, [64, 512], F32, 2))


def attn_rings(kb, st, tag, n_s=4):
    return (kb.ring_ps(st, "ps_s" + tag, [128, 512], n_s), kb.ring_ps(st, "ps_o" + tag, [128, 512], 2),
            kb.ring_sb(st, "e_r" + tag, [128, 512], BF16, 4), kb.ring_sb(st, "p_r" + tag, [128, 512], BF16, 3),
            kb.ring_sb(st, "rden" + tag, [64, 512], F32, 2))


def emit_na(kb, st, I, o_ybT, modpre=None):
    S = kb.S
    ident, _ = make_ident(kb, st)
    qT = kb.sb(st, "naqT", [128, 4, T], BF16)
    with ExitStack() as s2:
        hxT = kb.sb(s2, "hxT", [128, 8, T], BF16)
        kb.load(hxT[:], I["hxT"], writes=["hxT"])
        w_nq = kb.sb(s2, "w_nq", [128, 8, 512], BF16)
        kb.load_cast(w_nq[:], wview(I["w_in"], C_NQ, C_NK), writes=["w_nq"])
        ps_a = kb.ring_ps(s2, "ps_a", [128, 512], 2)
        for hp in range(4):
            for (b0, bn) in TOKBLKS:
                pa, pak = ps_a.next()
                kb.mm_group(pa[:, 0:bn], [(w_nq[:, k, hp * 128:(hp + 1) * 128], hxT[:, k, b0:b0 + bn]) for k in range(8)],
                            reads=["hxT", "w_nq"], writes=[pak])
                E(S, "act", "activation", [pak], [("naqT", hp, b0)], out=qT[:, hp, b0:b0 + bn], in_=pa[:, 0:bn], func=AF.Copy, scale=0.125)
        S.barrier(); S.emit()
    kT = kb.sb(st, "nakT", [128, 4, NKEY], BF16)
    V = kb.sb(st, "naV", [128, NKEY // 128, 8, 128], BF16)
    kb.load(kT[:, :, 256:2304], I["naKT"][:, :, 0:NT], writes=[("nakT", "own")])
    kb.load(kT[:, :, 2560:NKEY], I["naKT"][:, :, NT:T], writes=[("nakT", "ctx")])
    nvv = I["naV"].rearrange("(u p) h f -> p u h f", p=128)
    kb.load(V[:, 2:18], nvv[:, 0:16], writes=[("naV", "own")])
    kb.load(V[:, 20:22], nvv[:, 16:18], writes=[("naV", "ctx")])
    cc = kb.sb(st, "cc", [128, 32], F32)
    kb.load(cc[:], I["cc"], writes=["cc"])
    xbo = [g_.rearrange("(s p) c -> p s c", p=128) for g_ in I["XB_out"]]
    with ExitStack() as s3:
        hal = kb.sb(s3, "hal", [128, 4, 6144], BF16)
        kb.load(hal[:, :, 0:2048], xbo[1][:, :, 2048:4096], writes=[("hal", 0)], reads=[("XB_out", 1)])
        kb.load(hal[:, :, 2048:6144], xbo[2][:, :, 0:4096], writes=[("hal", 1)], reads=[("XB_out", 2)])
        E(S, "dve", "tensor_copy", [("hal", 0), ("hal", 1)], ["hal"], out=hal[0:1, 0, 0:1], in_=hal[0:1, 0, 0:1])
        kt_top = kT[:, :, 0:256]
        kt_bot = kT[:, :, 2304:2560]
        v_top = V[:, 0:2].rearrange("p u h f -> p u (h f)")
        v_bot = V[:, 18:20].rearrange("p u h f -> p u (h f)")
        for s_ in range(4):
            srcs = [(kt_top, hal[:, s_, 1024:2048].rearrange("p (a t) -> p a t", a=4), 4 + s_, "dve"),
                    (kt_bot, hal[:, s_, 0:1024].rearrange("p (a t) -> p a t", a=4), 8 + s_, "pool"),
                    (v_top, hal[:, s_, 4096:6144].rearrange("p (u f) -> p u f", u=2), 4 + s_, "dve"),
                    (v_bot, hal[:, s_, 2048:4096].rearrange("p (u f) -> p u f", u=2), 8 + s_, "pool")]
            for j, (dst, src, col, eng) in enumerate(srcs):
                if s_ == 0:
                    E(S, eng, "tensor_scalar_mul", ["hal", "cc"], [("halo", j)], out=dst, in0=src, scalar1=cc[:, col:col + 1])
                else:
                    E(S, "dve", "scalar_tensor_tensor", ["hal", "cc", ("halo", j)], [("halo", j)], out=dst, in0=src, scalar=cc[:, col:col + 1], in1=dst,
                      op0=ALU.mult, op1=ALU.add)
        S.barrier(); S.emit()
    E(S, "dve", "tensor_copy", [("nakT", "own"), ("nakT", "ctx"), ("halo", 0), ("halo", 1)], ["nakT"], out=kT[0:1, 0, 0:1], in_=kT[0:1, 0, 0:1])
    E(S, "dve", "tensor_copy", [("naV", "own"), ("naV", "ctx"), ("halo", 2), ("halo", 3)], ["naV"], out=V[0:1, 0, 0, 0:1], in_=V[0:1, 0, 0, 0:1])
    bias = kb.sb(st, "nabias", [128, 8, 24 * 64], BF16)
    kb.load_cast(bias[:], I["nabias"].rearrange("p h e c -> p h (e c)"), writes=["nabias"])
    mask = kb.sb(st, "namask", [128, 32, 512], BF16)
    kb.load(mask[:], I["namask"], writes=["namask"])
    ybT = kb.sb(st, "ybT", [128, 4, T], BF16)
    rings = attn_rings(kb, st, "na")
    pipe = AttnPipe(kb, S, rings, ident=ident[:])
    mp_ = modpre(kb, st) if modpre is not None else None
    for h in range(8):
        hp, hs = h // 2, (h % 2) * 64
        for b in range(5):
            b0, bn = TOKBLKS[b]
            qa = qT[hs:hs + 64, hp, b0:b0 + bn]
            qk = [("naqT", hp, b0)]
            kl, vl, ex = [], [], []
            if b < 4:
                for t in range(8):
                    u = 4 * b + t
                    e0 = (4 - 2 * t) + 10
                    kl.append((kT[hs:hs + 64, hp, u * 128:(u + 1) * 128], ["nakT"]))
                    vl.append((V[:, u, h, :], ["naV"]))
                    ex.append((bias[:, h, e0 * 64:(e0 + 8) * 64], mask[:, b * 8 + t, :], ["nabias", "namask"]))
            for u in (20, 21):
                kl.append((kT[hs:hs + 64, hp, u * 128:(u + 1) * 128], ["nakT"]))
                vl.append((V[:, u, h, :], ["naV"]))
                ex.append(None)
            pipe.add(kl, qa, qk, vl, ybT[hs:hs + 64, hp, b0:b0 + bn], ("ybT", h, b), extra=ex)
        if mp_ is not None:
            mp_.step()
            pipe.run()
    pipe.run()
    if mp_ is not None:
        mp_.finish()
    kb.store(o_ybT, ybT[:], reads=[("ybT", h, b) for h in range(8) for b in range(5)], final=True)


class ModPrefetch:
    NB = 16
    BW = 384

    def __init__(self, kb, st, c2, w_ada, b_ada, out_dram):
        S = self.S = kb.S
        self.kb, self.w_ada, self.out = kb, w_ada, out_dram
        self.c2s = kb.sb(st, "pc2s", [128, 16], F32)
        self.bad = kb.sb(st, "pbad", [128, 48], F32)
        self.wa = kb.ring_sb(st, "pwa", [128, 8, self.BW], F32, 2)
        self.mrow = kb.ring_sb(st, "pmrow", [2, self.BW], F32, 2)
        self.mods = kb.sb(st, "pmods", [128, 48, 2], F32)
        self.idf = kb.sb(st, "pidf", [128, 128], F32)
        self.mps = kb.ps(st, "pmod_ps", [128, 48, 2])
        self.mrp = kb.ring_ps(st, "pmrow_ps", [2, self.BW], 1)
        kb.load(self.c2s[:], c2, writes=["pc2s"])
        kb.load(self.bad[:], b_ada, writes=["pbad"])
        E(S, "act", "activation", ["pc2s"], ["pc2s"], out=self.c2s[:], in_=self.c2s[:], func=AF.Silu)
        E(S, "pool", "memset", [], ["pidf"], self.idf[:], 0.0)
        E(S, "pool", "affine_select", ["pidf"], ["pidf"], out=self.idf[:], in_=self.idf[:], pattern=[[-1, 128]],
          compare_op=ALU.not_equal, fill=1.0, base=0, channel_multiplier=1)
        self.c2v = self.c2s[:].rearrange("p (k w) -> p k w", w=2)
        self.pending = []
        self.nxt = 0

    def _compute(self, blk, wt, wk):
        S, kb = self.S, self.kb
        mp, mpk = self.mrp.next()
        kb.mm_group(mp[:, :], [(self.c2v[:, k, :], wt[:, k, :]) for k in range(8)], reads=[wk, "pc2s"], writes=[mpk])
        mr, mrk = self.mrow.next()
        E(S, "dve", "tensor_copy", [mpk], [mrk], out=mr[:], in_=mp[:, :])
        for jj in range(self.BW // 128):
            j = blk * (self.BW // 128) + jj
            E(S, "pe", "transpose", [mrk, "pidf"], [("pmps", j)], self.mps[:, j, :], mr[:, jj * 128:(jj + 1) * 128], self.idf[0:2, 0:2])

    def step(self):
        for (blk, wt, wk) in self.pending:
            self._compute(blk, wt, wk)
        self.pending = []
        for _ in range(2):
            if self.nxt < self.NB:
                blk = self.nxt
                self.nxt += 1
                wt, wk = self.wa.next()
                self.kb.load(wt[:], wview(self.w_ada, blk * self.BW, (blk + 1) * self.BW), writes=[wk])
                self.pending.append((blk, wt, wk))

    def finish(self):
        while self.pending or self.nxt < self.NB:
            self.step()
        S = self.S
        for w in range(2):
            E(S, "dve", "tensor_tensor", [("pmps", j) for j in range(48)] + ["pbad"], ["pmods"], out=self.mods[:, :, w], in0=self.mps[:, :, w],
              in1=self.bad[:], op=ALU.add)
        self.kb.store(self.out, self.mods[:].rearrange("p a b -> p (a b)"), reads=["pmods"])


def emit_mla(kb, st, I, o_ycT, modpre=None):
    S = kb.S
    ident, _ = make_ident(kb, st)
    qTm = kb.sb(st, "qTm", [96, 8, T], BF16)
    cst = kb.sb(st, "mcst", [128, 8], F32)
    kb.load(cst[:], I["rcst"][:, 768:776], writes=["mcst"])
    eps_ap = cst[:, 5:6]
    with ExitStack() as s2:
        cqnT = kb.sb(s2, "cqnT", [128, 2, T], BF16)
        with ExitStack() as s3:
            hxT = kb.sb(s3, "hxT", [128, 8, T], BF16)
            kb.load(hxT[:], I["hxT"], writes=["hxT"])
            w_mq = kb.sb(s3, "w_mq", [128, 8, 256], BF16)
            kb.load_cast(w_mq[:], wview(I["w_in"], C_MQ, C_MKV), writes=["w_mq"])
            qg = kb.sb(s3, "qg", [128, 256], F32)
            kb.load(qg[:], I["qg"].partition_broadcast(128), writes=["qg"])
            GA = 3
            ps_a = kb.ring_ps(s3, "ps_a", [128, 256], GA)
            ps_t = kb.ring_ps(s3, "ps_t", [128, 128], 4, BF16)
            ss = kb.ring_sb(s3, "ss", [128, 2], F32, 2 * GA)
            junk = kb.sb(s3, "junk", [128, 256], F32)
            cn_r = kb.ring_sb(s3, "cn", [128, 256], BF16, 2 * GA)
            for g0 in range(0, NTILE, GA):
                grp = []
                for t in range(g0, min(g0 + GA, NTILE)):
                    tok = slice(t * 128, (t + 1) * 128)
                    pa, pak = ps_a.next()
                    kb.mm_group(pa[:, 0:256], [(hxT[:, k, tok], w_mq[:, k, :]) for k in range(8)], reads=["hxT", "w_mq"], writes=[pak])
                    sst, ssk = ss.next()
                    cn, cnk = cn_r.next()
                    grp.append(dict(t=t, tok=tok, pa=pa, pak=pak, sst=sst, ssk=ssk, cn=cn, cnk=cnk))
                for c in grp:
                    E(S, "act", "activation", [c["pak"]], ["junk", c["ssk"]], out=junk[:], in_=c["pa"][:, 0:256], func=AF.Square, accum_out=c["sst"][:, 0:1])
                for c in grp:
                    E(S, "act", "activation", [c["ssk"], "mcst"], [c["ssk"]], out=c["sst"][:, 1:2], in_=c["sst"][:, 0:1], func=AF.Sqrt, scale=1.0 / 256.0, bias=eps_ap)
                for c in grp:
                    E(S, "dve", "reciprocal", [c["ssk"]], [c["ssk"]], out=c["sst"][:, 1:2], in_=c["sst"][:, 1:2])
                for c in grp:
                    E(S, "dve", "scalar_tensor_tensor", [c["pak"], c["ssk"], "qg"], [c["cnk"]], out=c["cn"][:], in0=c["pa"][:, 0:256], scalar=c["sst"][:, 1:2], in1=qg[:],
                      op0=ALU.mult, op1=ALU.mult)
                pts = []
                for c in grp:
                    for kc in range(2):
                        pt, ptk = ps_t.next()
                        E(S, "pe", "transpose", [c["cnk"], "ident"], [ptk], pt[:], c["cn"][:, kc * 128:(kc + 1) * 128], ident[:])
                        E(S, "dve" if kc == 0 else "act", "tensor_copy" if kc == 0 else "copy", [ptk], [("cqnT", c["t"])], out=cqnT[:, kc, c["tok"]], in_=pt[:])
            S.barrier(); S.emit()
        w_qup = kb.sb(s2, "w_qup", [128, 2, 768], BF16)
        kb.load_cast(w_qup[:], wview(I["w_qup"], 0, 768), writes=["w_qup"])
        rmq = kb.sb(s2, "rmq", [128, NTILE, 32], F32)
        kb.load(rmq[:], I["rope_mq"].rearrange("(t p) f -> p t f", p=128), writes=["rmq"])
        ps_a = kb.ring_ps(s2, "ps_qa", [128, 512], 2)
        ps_b = kb.ring_ps(s2, "ps_qb", [128, 256], 2)
        ps_t = kb.ring_ps(s2, "ps_qt", [128, 128], 4, BF16)
        qf_r = kb.ring_sb(s2, "mqf", [128, 768], F32, 4)
        qb_r = kb.ring_sb(s2, "mqb", [128, 768], BF16, 4)
        u1_r = kb.ring_sb(s2, "mu1", [128, 8, 16], F32, 2)
        u2_r = kb.ring_sb(s2, "mu2", [128, 8, 16], F32, 2)
        scl = float(96.0 ** -0.5)
        for g0 in range(0, NTILE, 2):
            grp = []
            for t in range(g0, min(g0 + 2, NTILE)):
                tok = slice(t * 128, (t + 1) * 128)
                pa, pak = ps_a.next()
                pb, pbk = ps_b.next()
                kb.mm_group(pa[:, 0:512], [(cqnT[:, kc, tok], w_qup[:, kc, 0:512]) for kc in range(2)], reads=[("cqnT", t), "w_qup"], writes=[pak])
                kb.mm_group(pb[:, 0:256], [(cqnT[:, kc, tok], w_qup[:, kc, 512:768]) for kc in range(2)], reads=[("cqnT", t), "w_qup"], writes=[pbk])
                qf, qfk = qf_r.next()
                qb, qbk = qb_r.next()
                grp.append(dict(t=t, tok=tok, pa=pa, pak=pak, pb=pb, pbk=pbk, qf=qf, qfk=qfk, qb=qb, qbk=qbk))
            for c in grp:
                E(S, "act", "copy", [c["pak"]], [(c["qfk"], 0)], out=c["qf"][:, 0:512], in_=c["pa"][:, 0:512])
                E(S, "act", "copy", [c["pbk"], (c["qfk"], 0)], [c["qfk"]], out=c["qf"][:, 512:768], in_=c["pb"][:, 0:256])
            for c in grp:
                qf3 = c["qf"][:].rearrange("p (h f) -> p h f", h=8)
                qb3 = c["qb"][:].rearrange("p (h f) -> p h f", h=8)
                E(S, "dve", "tensor_scalar_mul", [c["qfk"]], [(c["qbk"], "n")], out=qb3[:, :, 0:64], in0=qf3[:, :, 0:64], scalar1=scl)
                cb = rmq[:, c["t"], 0:16].unsqueeze(1).to_broadcast([128, 8, 16])
                sn = rmq[:, c["t"], 16:32].unsqueeze(1).to_broadcast([128, 8, 16])
                u1, u1k = u1_r.next()
                u2, _ = u2_r.next()
                rope2(S, "pool" if c["t"] % 2 == 0 else "dve", qf3[:, :, 64:80], qf3[:, :, 80:96], qb3[:, :, 64:80], qb3[:, :, 80:96], cb, sn, u1[:], u2[:],
                      [c["qfk"], "rmq", (c["qbk"], "n")], c["qbk"], u1k)
            for c in grp:
                for h in range(8):
                    pt, ptk = ps_t.next()
                    E(S, "pe", "transpose", [c["qbk"], "ident"], [ptk], pt[0:96, :], c["qb"][:, h * 96:(h + 1) * 96], ident[:])
                    E(S, "dve" if h % 2 == 0 else "act", "tensor_copy" if h % 2 == 0 else "copy", [ptk], [("qTm", c["t"])], out=qTm[:, h, c["tok"]], in_=pt[0:96, :])
        S.barrier(); S.emit()
    ck = kb.sb(st, "ckvnTa", [128, 2, NKM], BF16)
    xbo = [g_.rearrange("(s p) c -> p s c", p=128) for g_ in I["XB_out"]]
    for s_ in range(4):
        kb.load(ck[:, :, s_ * NT:(s_ + 1) * NT], xbo[0][:, s_, 0:4096].rearrange("p (k t) -> p k t", k=2), writes=[("ck", s_)], reads=[("XB_out", 0)])
    kb.load(ck[:, :, 4 * NT:NKM], I["ckvnT"][:, :, NT:T], writes=[("ck", 4)])
    E(S, "dve", "tensor_copy", [("ck", s_) for s_ in range(5)], ["ck"], out=ck[0:1, 0, 0:1], in_=ck[0:1, 0, 0:1])
    w_kv = kb.sb(st, "w_kvup", [128, 2, 1024], BF16)
    kb.load_cast(w_kv[:], wview(I["w_kvup"], 0, 1024), writes=["w_kv"])
    KT = kb.sb(st, "mKT", [96, 2, NKM], BF16)
    VA = kb.sb(st, "mVA", [128, NKM // 128, 2, 128], BF16)
    ycT = kb.sb(st, "ycT", [128, 4, T], BF16)
    E(S, "pool", "memset", [], ["mVA"], VA[:], 1.0)
    mp_ = modpre(kb, st) if modpre is not None else None
    ps_k = kb.ring_ps(st, "ps_k", [128, 512], 1 if mp_ is not None else (2 if MLA_PAIRS else 3))
    rings = attn_rings_pairs(kb, st, "ml") if (MLA_PAIRS and mp_ is None) else attn_rings(kb, st, "ml", n_s=3)
    pipe = AttnPipe(kb, S, rings)
    NU = NKM // 128
    for hp in range(4):
        for hh in range(2):
            h = hp * 2 + hh
            for s_ in range(4):
                kb.load(KT[64:96, hh, s_ * NT:(s_ + 1) * NT], xbo[1][0:32, s_, 0:2048], writes=[("mKTr", hh, s_)], reads=[("XB_out", 1)])
            kb.load(KT[64:96, hh, 4 * NT:NKM], I["krT"][:, NT:T], writes=[("mKTr", hh, 4)])
            for c0 in range(0, NKM, 512):
                cn = min(512, NKM - c0)
                pk, pkk = ps_k.next()
                kb.mm_group(pk[0:64, 0:cn], [(w_kv[:, kc, h * 128:h * 128 + 64], ck[:, kc, c0:c0 + cn]) for kc in range(2)],
                            reads=["ck", "w_kv"], writes=[pkk])
                E(S, "act" if (c0 // 512) % 2 else "dve", "copy" if (c0 // 512) % 2 else "tensor_copy", [pkk], [("mKT", hh, c0)],
                  out=KT[0:64, hh, c0:c0 + cn], in_=pk[0:64, 0:cn])
        for u in range(NU):
            pk, pkk = ps_k.next()
            wv = w_kv[:, :, hp * 256:(hp + 1) * 256].rearrange("p k (h f) -> p k h f", h=2)[:, :, :, 64:128]
            kb.mm_group(pk[:, 0:128].rearrange("p (h f) -> p h f", h=2), [(ck[:, kc, u * 128:(u + 1) * 128], wv[:, kc, :, :]) for kc in range(2)],
                        reads=["ck", "w_kv"], writes=[pkk])
            E(S, "act" if u % 2 else "dve", "copy" if u % 2 else "tensor_copy", [pkk, "mVA"], [("mVAu", u)],
              out=VA[:, u, :, 0:64], in_=pk[:, 0:128].rearrange("p (h f) -> p h f", h=2))
        for hh in range(2):
            h = hp * 2 + hh
            for b in range(5):
                b0, bn = TOKBLKS[b]
                us = list(range(NU)) if b < 4 else [NU - 2, NU - 1]
                kl = [(KT[:, hh, u * 128:(u + 1) * 128], [("mKT", hh, (u // 4) * 512), ("mKTr", hh, u // 16)]) for u in us]
                vl = [(VA[:, u, hh, :], [("mVAu", u)]) for u in us]
                pipe.add(kl, qTm[:, h, b0:b0 + bn], [("qTm", t) for t in range(b0 // 128, (b0 + bn) // 128)], vl,
                         ycT[hh * 64:hh * 64 + 64, hp, b0:b0 + bn], ("ycT", h, b))
            if mp_ is not None:
                mp_.step()
            if MLA_PAIRS and mp_ is None:
                pipe.run_pairs()
            else:
                pipe.run()
    if mp_ is not None:
        mp_.finish()
    kb.store(o_ycT, ycT[:], reads=[("ycT", h, b) for h in range(8) for b in range(5)], final=True)


def emit_ln(kb, S, x, xkeys, n, ones, lng, gcol, cst, ps_ln, sq_r, lnt, okeys, out=None):
    p1, p1k = ps_ln.next()
    p2, p2k = ps_ln.next()
    kb.mm_group(p1[:, 0:n], [(ones[:], x[:, oc, :]) for oc in range(8)], reads=list(xkeys) + ["ones"], writes=[p1k])
    for oc in range(8):
        sq, sqk = sq_r.next()
        E(S, "act", "activation", [xkeys[oc]], [sqk], out=sq[:, 0:n], in_=x[:, oc, :], func=AF.Square)
        S.op("pe", lambda h, sq=sq, oc=oc, p2=p2: h.matmul(p2[:, 0:n], lhsT=ones[:], rhs=sq[:, 0:n], start=(oc == 0), stop=(oc == 7)),
             reads=[sqk, "ones"], writes=[p2k])
    mean, msq, rstd = lnt
    E(S, "act", "activation", [p1k], ["ln_mean"], out=mean[:, 0:n], in_=p1[:, 0:n], func=AF.Copy, scale=1.0 / 1024.0)
    E(S, "pool", "tensor_tensor", ["ln_mean"], ["ln_msq"], out=msq[:, 0:n], in0=mean[:, 0:n], in1=mean[:, 0:n], op=ALU.mult)
    E(S, "dve", "scalar_tensor_tensor", [p2k, "ln_msq"], ["ln_rstd"], out=rstd[:, 0:n], in0=p2[:, 0:n], scalar=1.0 / 1024.0, in1=msq[:, 0:n],
      op0=ALU.mult, op1=ALU.subtract)
    E(S, "act", "activation", ["ln_rstd", "mcst"], ["ln_rstd"], out=rstd[:, 0:n], in_=rstd[:, 0:n], func=AF.Sqrt, bias=cst[:, 5:6], scale=1.0)
    E(S, "dve", "reciprocal", ["ln_rstd"], ["ln_rstd"], out=rstd[:, 0:n], in_=rstd[:, 0:n])
    for oc in range(8):
        eng = "dve" if oc % 2 == 0 else "pool"
        o = x[:, oc, :] if out is None else out[:, oc, :]
        E(S, eng, "tensor_tensor", [xkeys[oc], "ln_mean"], [xkeys[oc]], out=x[:, oc, :], in0=x[:, oc, :], in1=mean[:, 0:n], op=ALU.subtract)
        E(S, eng, "tensor_tensor", [xkeys[oc], "ln_rstd"], [xkeys[oc]], out=x[:, oc, :], in0=x[:, oc, :], in1=rstd[:, 0:n], op=ALU.mult)
        E(S, eng, "tensor_scalar", [xkeys[oc], "lng"], [okeys[oc]], out=o, in0=x[:, oc, :], scalar1=lng[:, gcol + oc:gcol + oc + 1],
          scalar2=lng[:, gcol + 8 + oc:gcol + 9 + oc], op0=ALU.mult, op1=ALU.add)


def ln_common(kb, st, I):
    S = kb.S
    ones = kb.sb(st, "ones", [128, 128], F32)
    E(S, "pool", "memset", [], ["ones"], ones[:], 1.0)
    lng = kb.sb(st, "lng", [128, 32], F32)
    kb.load(lng[:], I["lng"], writes=["lng"])
    cst = kb.sb(st, "mcst", [128, 8], F32)
    kb.load(cst[:], I["rcst"][:, 768:776], writes=["mcst"])
    mods = kb.sb(st, "mods", [128, 48, 2], F32)
    kb.load(mods[:].rearrange("p a b -> p (a b)"), I["mods"], writes=["mods"])
    return ones, lng, cst, mods


def emit_merge(kb, st, I, dbg):
    S = kb.S
    ones, lng, cst, mods = ln_common(kb, st, I)
    w_gt = kb.sb(st, "w_gt", [128, 8, 3072], BF16)
    w_br = kb.sb(st, "w_br", [128, 3, 4, 1024], BF16)
    w_out = kb.sb(st, "w_out", [128, 8, 1024], BF16)
    for b in range(3):
        kb.load_cast(w_gt[:, :, b * 1024:(b + 1) * 1024], wview(I["w_in"], C_GA + b * 1024, C_GA + (b + 1) * 1024), writes=[("w_gt", b)])
        kb.load_cast(w_br[:, b, :, :], I["w_br"][b].rearrange("(k p) n -> p k n", p=128), writes=[("w_br", b)])
    kb.load_cast(w_out[:], wview(I["w_out"], 0, 1024), writes=["w_out"])
    wkeys = [("w_gt", b) for b in range(3)] + [("w_br", b) for b in range(3)]
    hx_r = kb.ring_sb(st, "hxb", [128, 8, 512], BF16, 2)
    y_r = [kb.ring_sb(st, f"yb{b}", [128, 4, 512], BF16, 2) for b in range(3)]
    x_r = kb.ring_sb(st, "xb", [128, 8, 512], F32, 1)
    yT = kb.sb(st, "yTm", [128, 8, 512], BF16)
    x1 = kb.ring_sb(st, "x1", [128, 8, 512], F32, 1)
    ps_g = kb.ring_ps(st, "ps_g", [128, 512], 3)
    ps_b = kb.ring_ps(st, "ps_b", [128, 512], 3)
    ps_ln = kb.ring_ps(st, "ps_ln", [128, 512], 2)
    sg_r = kb.ring_sb(st, "sgm", [128, 512], F32, 3)
    acc_r = kb.ring_sb(st, "accm", [128, 512], F32, 2)
    tmp_r = kb.ring_sb(st, "tmpm", [128, 512], F32, 2)
    sq_r = kb.ring_sb(st, "sqm", [128, 512], F32, 2)
    lnt = (kb.sb(st, "ln_mean", [128, 512], F32), kb.sb(st, "ln_msq", [128, 512], F32), kb.sb(st, "ln_rstd", [128, 512], F32))
    srcs = [dbg["yaT"], dbg["ybT"], dbg["ycT"]]
    xv = I["xT"].rearrange("(j p) t -> p j t", p=128)
    ctx = {}

    def stage_A(b0, bn):
        hx, hxk = hx_r.next()
        kb.load(hx[:, :, 0:bn], I["hxT"][:, :, b0:b0 + bn], writes=[hxk])
        ys = []
        for b in range(3):
            yt, yk = y_r[b].next()
            kb.load(yt[:, :, 0:bn], srcs[b][:, :, b0:b0 + bn], writes=[yk])
            ys.append((yt, yk))
        xt, xk = x_r.next()
        kb.load(xt[:, :, 0:bn], xv[:, :, b0:b0 + bn], writes=[xk])
        for oc in range(8):
            ocs = slice(oc * 128, (oc + 1) * 128)
            acc, acck = acc_r.next()
            for b in range(3):
                pg, pgk = ps_g.next()
                kb.mm_group(pg[:, 0:bn], [(w_gt[:, k, b * 1024 + oc * 128:b * 1024 + (oc + 1) * 128], hx[:, k, 0:bn]) for k in range(8)],
                            reads=[hxk, ("w_gt", b)], writes=[pgk])
                pb, pbk = ps_b.next()
                kb.mm_group(pb[:, 0:bn], [(w_br[:, b, k, ocs], ys[b][0][:, k, 0:bn]) for k in range(4)], reads=[ys[b][1], ("w_br", b)], writes=[pbk])
                sgt, sgk = sg_r.next()
                E(S, "act", "activation", [pgk], [sgk], out=sgt[:, 0:bn], in_=pg[:, 0:bn], func=AF.Sigmoid)
                if b == 0:
                    E(S, "dve", "tensor_tensor", [sgk, pbk], [acck], out=acc[:, 0:bn], in0=pb[:, 0:bn], in1=sgt[:, 0:bn], op=ALU.mult)
                else:
                    E(S, "dve", "tensor_tensor", [sgk, pbk], [sgk], out=sgt[:, 0:bn], in0=pb[:, 0:bn], in1=sgt[:, 0:bn], op=ALU.mult)
                    if b == 1:
                        E(S, "pool", "tensor_tensor", [sgk, acck], [acck], out=acc[:, 0:bn], in0=acc[:, 0:bn], in1=sgt[:, 0:bn], op=ALU.add)
                    else:
                        E(S, "pool", "tensor_tensor", [sgk, acck], [("yTm", oc)], out=yT[:, oc, 0:bn], in0=acc[:, 0:bn], in1=sgt[:, 0:bn], op=ALU.add)
        ctx[b0] = dict(bn=bn, w=(0 if b0 < NT else 1), xt=xt, xk=xk)

    def stage_B1(b0):
        c = ctx[b0]
        bn, w, xt, xk = c["bn"], c["w"], c["xt"], c["xk"]
        x1t, x1k = x1.next()
        xkeys = [(x1k, oc) for oc in range(8)]
        for oc in range(8):
            ocs = slice(oc * 128, (oc + 1) * 128)
            pm, pmk = ps_g.next()
            kb.mm_group(pm[:, 0:bn], [(w_out[:, k, ocs], yT[:, k, 0:bn]) for k in range(8)], reads=[("yTm", k) for k in range(8)] + ["w_out"], writes=[pmk])
            tmp, tmpk = tmp_r.next()
            E(S, "act", "activation", [pmk, "mods"], [tmpk], out=tmp[:, 0:bn], in_=pm[:, 0:bn], func=AF.Copy, scale=mods[:, 16 + oc, w:w + 1])
            E(S, "dve", "scalar_tensor_tensor", [tmpk, xk], [xkeys[oc]], out=x1t[:, oc, 0:bn], in0=xt[:, oc, 0:bn], scalar=float(ALPHA), in1=tmp[:, 0:bn],
              op0=ALU.mult, op1=ALU.add)
        c["x1t"], c["xkeys"] = x1t, xkeys

    def stage_B2(b0):
        c = ctx.pop(b0)
        bn = c["bn"]
        emit_ln(kb, S, c["x1t"][:, :, 0:bn], c["xkeys"], bn, ones, lng, 0, cst, ps_ln, sq_r, lnt, c["xkeys"])
        kb.store(dbg["x1T"][:, :, b0:b0 + bn], c["x1t"][:, :, 0:bn], reads=c["xkeys"], writes=[("x1T_d", b0)], final=False)

    nblk = len(TOKBLKS)
    stage_A(*TOKBLKS[0])
    for i in range(nblk):
        stage_B1(TOKBLKS[i][0])
        if i + 1 < nblk:
            stage_A(*TOKBLKS[i + 1])
        stage_B2(TOKBLKS[i][0])


def emit_ffn(kb, st, I, x1T_d, xoT, last=True):
    S = kb.S
    ones, lng, cst, mods = ln_common(kb, st, I)
    sc2p = kb.sb(st, "sc2p", [128, 8, 2], F32)
    E(S, "dve", "tensor_scalar_add", ["mods"], ["sc2p"], out=sc2p[:], in0=mods[:, 32:40, :], scalar1=1.0)
    w1 = kb.sb(st, "w_ff1", [128, 8, 4096], BF16)
    w2 = kb.sb(st, "w_ff2", [128, 32, 1024], BF16)
    for c in range(4):
        kb.load_cast(w1[:, :, c * 1024:(c + 1) * 1024], wview(I["w_ff1"], c * 1024, (c + 1) * 1024), writes=[("w1", c)])
    for c in range(4):
        kb.load_cast(w2[:, c * 8:(c + 1) * 8, :], I["w_ff2"].rearrange("(k p) n -> p k n", p=128)[:, c * 8:(c + 1) * 8, :], writes=[("w2", c)])
    w1k = [("w1", c) for c in range(4)]
    w2k = [("w2", c) for c in range(4)]
    n = FFBLK
    x_r = kb.ring_sb(st, "xf", [128, 8, n], F32, 3)
    h_r = kb.ring_sb(st, "hf", [128, 8, n], BF16, 2)
    a_r = kb.ring_sb(st, "af", [128, 32, n], BF16, 2)
    r_r = kb.ring_sb(st, "rf", [128, n], F32, 3)
    ps_f = kb.ring_ps(st, "ps_f", [128, n], 4)
    ps_ln = kb.ring_ps(st, "ps_lnf", [128, n], 2)
    tmp_r = kb.ring_sb(st, "tmpf", [128, n], F32, 2)
    sq_r = kb.ring_sb(st, "sqf", [128, n], F32, 2)
    lnt = (kb.sb(st, "ln_mean", [128, n], F32), kb.sb(st, "ln_msq", [128, n], F32), kb.sb(st, "ln_rstd", [128, n], F32))
    xov = xoT.rearrange("(j p) t -> p j t", p=128)
    blocks = list(range(0, T, n))
    ctx = {}

    def stage_A(b0):
        w = 0 if b0 < NT else 1
        xt, xk = x_r.next()
        kb.load(xt[:], x1T_d[:, :, b0:b0 + n], writes=[xk], reads=[("x1T_d", (b0 // 512) * 512)])
        ht, hk = h_r.next()
        for j in range(8):
            E(S, "dve" if j % 2 == 0 else "pool", "tensor_scalar", [xk, "sc2p", "mods"], [(hk, j)], out=ht[:, j, :], in0=xt[:, j, :],
              scalar1=sc2p[:, j, w:w + 1], scalar2=mods[:, 24 + j, w:w + 1], op0=ALU.mult, op1=ALU.add)
        at, ak = a_r.next()
        for fc in range(32):
            pf, pfk = ps_f.next()
            kb.mm_group(pf[:], [(w1[:, k, fc * 128:(fc + 1) * 128], ht[:, k, :]) for k in range(8)], reads=[(hk, j) for j in range(8)] + [("w1", fc // 8)], writes=[pfk])
            rt, rk = r_r.next()
            E(S, "act", "activation", [pfk], [rk], out=rt[:], in_=pf[:], func=AF.Relu)
            E(S, "pool" if fc % 2 == 0 else "dve", "tensor_tensor", [rk], [(ak, fc)], out=at[:, fc, :], in0=rt[:], in1=rt[:], op=ALU.mult)
        ctx[b0] = dict(w=w, xt=xt, xk=xk, hk=hk, at=at, ak=ak, xkeys=[(xk, oc) for oc in range(8)])

    def stage_B1(b0):
        c = ctx[b0]
        xt, xk, at, ak, w = c["xt"], c["xk"], c["at"], c["ak"], c["w"]
        for oc in range(8):
            pm, pmk = ps_f.next()
            kb.mm_group(pm[:], [(w2[:, k, oc * 128:(oc + 1) * 128], at[:, k, :]) for k in range(32)], reads=[(ak, fc) for fc in range(32)] + w2k, writes=[pmk])
            tmp, tmpk = tmp_r.next()
            E(S, "act", "activation", [pmk, "mods"], [tmpk], out=tmp[:], in_=pm[:], func=AF.Copy, scale=mods[:, 40 + oc, w:w + 1])
            E(S, "dve", "scalar_tensor_tensor", [tmpk, xk] + [(c["hk"], j) for j in range(8)], [c["xkeys"][oc]], out=xt[:, oc, :], in0=xt[:, oc, :], scalar=float(ALPHA), in1=tmp[:],
              op0=ALU.mult, op1=ALU.add)

    def stage_B2(b0):
        c = ctx.pop(b0)
        emit_ln(kb, S, c["xt"][:], c["xkeys"], n, ones, lng, 16, cst, ps_ln, sq_r, lnt, c["xkeys"])
        kb.store(xov[:, :, b0:b0 + n], c["xt"][:], reads=c["xkeys"], final=last)

    nb = len(blocks)
    stage_A(blocks[0])
    if nb > 1:
        stage_A(blocks[1])
    for i in range(nb):
        stage_B1(blocks[i])
        if i + 2 < nb:
            stage_A(blocks[i + 2])
        stage_B2(blocks[i])


_BF = ml_dtypes.bfloat16


def _bf(a):
    a = np.asarray(a)
    if a.dtype.kind == "V":
        a = a.view(_BF)
    return a


def na_bias_table(rpb):
    a = np.arange(2)[:, None, None, None]
    kc = np.arange(64)[None, :, None, None]
    e = np.arange(24)[None, None, :, None]
    qc = np.arange(64)[None, None, None, :]
    dr = 10 + a - e + 0 * kc + 0 * qc
    dc = kc - qc + 0 * a + 0 * e
    ok = (np.abs(dr) <= 7) & (np.abs(dc) <= 15)
    dri = np.clip(dr + 7, 0, 14)
    dci = np.clip(dc + 15, 0, 30)
    out = np.zeros((2, 64, 8, 24, 64), np.float32)
    for h in range(8):
        g = rpb[h][dri, dci]
        out[:, :, h] = np.where(ok, g, np.float32(0.0))
    return np.ascontiguousarray(out.reshape(128, 8, 24, 64))


def na_mask_table(q):
    R0 = 32 * q
    m = np.zeros((128, 4, 8, 8, 64), np.float32)
    kc = np.arange(64)[:, None]
    qc = np.arange(64)[None, :]
    cs = np.clip(qc - 8, 0, 48)
    colok = (kc >= cs) & (kc < cs + 16)
    for b in range(4):
        for t in range(8):
            for a in range(2):
                krow = R0 + 8 * b - 4 + 2 * t + a
                for j in range(8):
                    qrow = R0 + 8 * b + j
                    r0 = min(max(qrow - 4, 0), 120)
                    if 0 <= krow < 128 and r0 <= krow < r0 + 8:
                        m[a * 64:(a + 1) * 64, b, t, j, :] = colok
    return np.ascontiguousarray(m.reshape(128, 32, 512)).astype(_BF)


def prep_B(inp, l, core, xT_core, A):
    b, q = core // 4, core % 4
    grp = [b * 4 + i for i in range(4)]
    me = A[core]
    rr, rm = rope_tables(q)
    Fsl = np.zeros((64, 8, 3, 128), np.float32)
    fexp = np.zeros((128, 8), np.float32)
    for j in range(3):
        if q - 1 - j >= 0:
            Fsl[:, 0:4, j, :] = np.asarray(A[grp[q - 1 - j]]["retF"])[:, 0:4, :]
        if q + 1 + j <= 3:
            Fsl[:, 4:8, j, :] = np.asarray(A[grp[q + 1 + j]]["retF"])[:, 4:8, :]
        fexp[:, j] = NT * j
        fexp[:, 4 + j] = NT * j
    fexp[:, 3] = NT * q
    fexp[:, 7] = NT * (3 - q)
    kT = _bf(me["naKT"]); V = _bf(me["naV"])
    naKT = np.zeros((128, 4, NKEY), _BF)
    naV = np.zeros((NKEY, 8, 128), _BF)
    naKT[:, :, 256:2304] = kT[:, :, 0:NT]; naV[256:2304] = V[0:NT]
    naKT[:, :, 2560:] = kT[:, :, NT:]; naV[2560:] = V[NT:]
    if q > 0:
        p = A[grp[q - 1]]
        naKT[:, :, 0:256] = _bf(p["naKT"])[:, :, NT - 256:NT]; naV[0:256] = _bf(p["naV"])[NT - 256:NT]
    if q < 3:
        p = A[grp[q + 1]]
        naKT[:, :, 2304:2560] = _bf(p["naKT"])[:, :, 0:256]; naV[2304:2560] = _bf(p["naV"])[0:256]
    ck = np.concatenate([_bf(A[g]["ckvnT"])[:, :, 0:NT] for g in grp] + [_bf(me["ckvnT"])[:, :, NT:]], axis=2)
    kr = np.concatenate([_bf(A[g]["krT"])[:, 0:NT] for g in grp] + [_bf(me["krT"])[:, NT:]], axis=1)
    lng = np.concatenate([fm(inp["ln_gain"][l, 0]), fm(inp["ln_bias"][l, 0]), fm(inp["ln_gain"][l, 1]), fm(inp["ln_bias"][l, 1])], axis=1)
    return {
        "xT": xT_core, "mods": np.asarray(me["mods"]), "hxT": _bf(me["hxT"]), "retK": _bf(me["retK"]), "retKT": _bf(me["retKT"]),
        "retV": _bf(me["retV"]), "Fsl": Fsl, "fexp": fexp, "rld": np.ascontiguousarray(inp["ret_log_decay"][l].reshape(1, 8)),
        "rcst": ret_consts(), "gng": np.ascontiguousarray(inp["ret_gn_gain"][l].reshape(1, 512)), "rope_q": rr,
        "rope_mq": (rm * np.float32(96.0 ** -0.5)).astype(np.float32),
        "naKT": naKT, "naV": naV, "nabias": na_bias_table(inp["na_rpb"][l]), "namask": na_mask_table(q),
        "ckvnT": np.ascontiguousarray(ck), "krT": np.ascontiguousarray(kr),
        "qg": np.ascontiguousarray(inp["mla_q_norm"][l].reshape(1, 256)), "w_qup": inp["mla_w_qup"][l], "w_kvup": inp["mla_w_kvup"][l],
        "w_in": inp["w_in"][l],
        "w_br": np.ascontiguousarray(np.stack([inp["w_branch_ret"][l], inp["w_branch_na"][l], inp["w_branch_mla"][l]])),
        "w_out": inp["w_out"][l], "w_ff1": inp["w_ff1"][l], "w_ff2": inp["w_ff2"][l], "lng": np.ascontiguousarray(lng),
    }


XBW = 12288


def emit_exchange(kb, st, G):
    S = kb.S
    cc = kb.sb(st, "cc", [128, 32], F32)
    kb.load(cc[:], G["cc"], writes=["cc"])
    stg1 = kb.sb(st, "xstg1", [128, 4096], BF16)
    stg2 = kb.sb(st, "xstg2", [32, 2048], BF16)
    stg3 = kb.sb(st, "xstg3", [128, 6144], BF16)
    m1 = kb.sb(st, "xm1", [128, 4, 4096], BF16)
    m2 = kb.sb(st, "xm2", [32, 4, 2048], BF16)
    m3 = kb.sb(st, "xm3", [128, 4, 6144], BF16)
    fst = kb.sb(st, "xf", [64, 1024], F32)
    fm_ = kb.sb(st, "xfm", [64, 4, 1024], F32)
    xin = [g_.rearrange("(s p) c -> p s c", p=128) for g_ in G["XB_in"]]
    nvv = G["naV"].rearrange("(u p) h f -> p u (h f)", p=128)
    kb.load(fst[:], G["retF"].rearrange("p i v -> p (i v)"), writes=["xf"])
    kb.load(stg1[:].rearrange("p (k t) -> p k t", k=2), G["ckvnT"][:, :, 0:NT], writes=["xstg1"])
    kb.load(stg2[:], G["krT"][:, 0:NT], writes=["xstg2"])
    kb.load(stg3[:, 0:1024].rearrange("p (a t) -> p a t", a=4), G["naKT"][:, :, 0:256], writes=[("xstg3", 0)])
    kb.load(stg3[:, 1024:2048].rearrange("p (a t) -> p a t", a=4), G["naKT"][:, :, NT - 256:NT], writes=[("xstg3", 1)])
    kb.load(stg3[:, 2048:4096].rearrange("p (u f) -> p u f", u=2), nvv[:, 0:2, :], writes=[("xstg3", 2)])
    kb.load(stg3[:, 4096:6144].rearrange("p (u f) -> p u f", u=2), nvv[:, 14:16, :], writes=[("xstg3", 3)])
    rg = [[0, 1, 2, 3], [4, 5, 6, 7]]
    for s_ in range(4):
        E(S, "dve", "tensor_scalar_mul", ["xf", "cc"], [("xfm", s_)], out=fm_[:, s_, :], in0=fst[:], scalar1=cc[0:64, s_:s_ + 1])
    kb.store(G["FB_in"].rearrange("(s p) c -> p s c", p=64), fm_[:], reads=[("xfm", s_) for s_ in range(4)], writes=["FB_in"])
    fi, fo = G["FB_in"], G["FB_out"]
    S.coll(lambda h: h.collective_compute("AllReduce", ALU.add, replica_groups=rg, ins=[fi.opt()], outs=[fo.opt()]),
           reads=["FB_in"], writes=["FB_out"])
    for s_ in range(4):
        E(S, "dve", "tensor_scalar_mul", [("xstg3", j) for j in range(4)] + ["cc"], [("xm3", s_)], out=m3[:, s_, :], in0=stg3[:],
          scalar1=cc[:, s_:s_ + 1])
    for s_ in range(4):
        E(S, "dve", "tensor_scalar_mul", ["xstg2", "cc"], [("xm2", s_)], out=m2[:, s_, :], in0=stg2[:], scalar1=cc[0:32, s_:s_ + 1])
    kb.store(xin[1][0:32, :, 0:2048], m2[:], reads=[("xm2", s_) for s_ in range(4)], writes=[("XB_in", 1, "a")])
    kb.store(xin[1][:, :, 2048:4096], m3[:, :, 0:2048], reads=[("xm3", s_) for s_ in range(4)], writes=[("XB_in", 1, "b")], q="act")
    kb.store(xin[2][:, :, 0:4096], m3[:, :, 2048:6144], reads=[("xm3", s_) for s_ in range(4)], writes=[("XB_in", 2)])
    for s_ in range(4):
        E(S, "dve", "tensor_scalar_mul", ["xstg1", "cc"], [("xm1", s_)], out=m1[:, s_, :], in0=stg1[:], scalar1=cc[:, s_:s_ + 1])
    kb.store(xin[0][:, :, 0:4096], m1[:], reads=[("xm1", s_) for s_ in range(4)], writes=[("XB_in", 0)], q="act")
    rk = {0: [("XB_in", 0)], 1: [("XB_in", 1, "a"), ("XB_in", 1, "b"), "XB_in"], 2: [("XB_in", 2)]}
    for j in (1, 2, 0):
        xi, xo = G["XB_in"][j], G["XB_out"][j]
        S.coll(lambda h, xi=xi, xo=xo: h.collective_compute("AllReduce", ALU.add, replica_groups=rg, ins=[xi.opt()], outs=[xo.opt()]),
               reads=rk[j], writes=[("XB_out", j)])


def build_fused(n_layers=4):
    nc = bass.Bass("TRN2", target_bir_lowering=False)
    din = lambda name, shape, dt=F32: nc.dram_tensor(name, shape, dt, kind="ExternalInput").ap()
    dsc = lambda name, shape, dt=F32: nc.dram_tensor(name, shape, dt).ap()
    L = n_layers
    X = dict(
        xT=din("xT", [D, T]), c2=din("c2", [128, 16]), w_ada=din("w_ada", [L, D, 6 * D]), b_ada=din("b_ada", [L, 128, 48]),
        w_in=din("w_in", [L, D, 7200]), rld=din("rld", [L, 1, 8]), rcst=din("rcst", [128, 776]), kvg=din("kvg", [L, 1, 256]),
        rope_rk=din("rope_rk", [T, 64]), rope_m=din("rope_m", [T, 32]), rope_q=din("rope_q", [T, 64]), rope_mq=din("rope_mq", [T, 32]),
        gng=din("gng", [L, 1, 512]), nabias=din("nabias", [L, 128, 8, 24, 64]), namask=din("namask", [128, 32, 512], BF16),
        qg=din("qg", [L, 1, 256]), w_qup=din("w_qup", [L, 256, 768]), w_kvup=din("w_kvup", [L, 256, 1024]),
        w_br=din("w_br", [L, 3, 512, D]), w_out=din("w_out", [L, D, D]), w_ff1=din("w_ff1", [L, D, 4 * D]), w_ff2=din("w_ff2", [L, 4 * D, D]),
        lng=din("lng", [L, 128, 32]), cc=din("cc", [128, 32]),
    )
    xoT = nc.dram_tensor("xoT", [D, T], F32, kind="ExternalOutput").ap()
    G = dict(
        mods=dsc("mods_d", [128, 96]), hxT=dsc("hxT_d", [128, 8, T], BF16), retK=dsc("retK_d", [T, 256], BF16),
        retKT=dsc("retKT_d", [64, 4, T], BF16), retV=dsc("retV_d", [T, 512], BF16), retF=dsc("retF_d", [64, 8, 128]),
        naKT=dsc("naKT_d", [128, 4, T], BF16), naV=dsc("naV_d", [T, 8, 128], BF16), ckvnT=dsc("ckvnT_d", [128, 2, T], BF16),
        krT=dsc("krT_d", [32, T], BF16), XB_in=[dsc(f"XB_in{j}", [512, 4096], BF16) for j in range(3)],
        XB_out=[dsc(f"XB_out{j}", [512, 4096], BF16) for j in range(3)],
        FB_in=dsc("FB_in", [256, 1024]), FB_out=dsc("FB_out", [256, 1024]),
        yaT=dsc("yaT_d", [128, 4, T], BF16), ybT=dsc("ybT_d", [128, 4, T], BF16), ycT=dsc("ycT_d", [128, 4, T], BF16),
        x1T=dsc("x1T_d", [128, 8, T]), cc=X["cc"], modsN=dsc("modsN_d", [128, 96]),
    )
    xbuf = [dsc("xping", [D, T]), dsc("xpong", [D, T])]
    with ExitStack() as st0:
        S = Sched(nc, st0)
        kb = KB(nc, S)
        with ExitStack() as ph:
            zt = kb.sb(ph, "zt", [128, 4, 2048], BF16)
            E(S, "pool", "memset", [], ["zt"], zt[:], 0.0)
            kb.store(G["XB_in"][1].rearrange("(s p) c -> p s c", p=128)[:, :, 0:2048], zt[:], reads=["zt"], writes=["XB_in"])
            S.barrier(); S.emit()
        for l in range(L):
            x_in = X["xT"] if l == 0 else xbuf[(l - 1) % 2]
            x_out = xoT if l == L - 1 else xbuf[l % 2]
            with ExitStack() as st:
                hxT = kb.sb(st, "hxT", [128, 8, T], BF16)
                mods = kb.sb(st, "mods", [128, 48, 2], F32)
                ident, _ = make_ident(kb, st)
                with ExitStack() as p1:
                    emit_mod_hx(kb, p1, x_in, X["c2"], X["w_ada"][l], X["b_ada"][l], mods, hxT,
                                mods_src=(G["modsN"] if (l > 0 and MOD_PREFETCH) else None))
                    kb.store(G["mods"], mods[:].rearrange("p a b -> p (a b)"), reads=["mods"])
                    kb.store(G["hxT"], hxT[:], reads=[("hxT", j) for j in range(8)])
                    S.barrier(); S.emit()
                with ExitStack() as p2:
                    emit_kv_side(kb, p2, hxT, ident, X["w_in"][l], X["rld"][l], X["rcst"], X["kvg"][l], X["rope_rk"], X["rope_m"],
                                 G["retK"], G["retKT"], G["retV"], G["retF"], G["naKT"], G["naV"], G["ckvnT"], G["krT"])
                    S.barrier(); S.emit()
            with ExitStack() as ph:
                emit_exchange(kb, ph, G)
                S.bar_coll = False
                S.barrier(); S.emit()
            I = dict(G)
            I.update(xT=x_in, rld=X["rld"][l], rcst=X["rcst"], gng=X["gng"][l], rope_q=X["rope_q"], rope_mq=X["rope_mq"],
                     nabias=X["nabias"][l], namask=X["namask"], qg=X["qg"][l], w_qup=X["w_qup"][l], w_kvup=X["w_kvup"][l],
                     w_in=X["w_in"][l], w_br=X["w_br"][l], w_out=X["w_out"][l], w_ff1=X["w_ff1"][l], w_ff2=X["w_ff2"][l], lng=X["lng"][l])
            dbg = dict(yaT=G["yaT"], ybT=G["ybT"], ycT=G["ycT"], x1T=G["x1T"])
            with ExitStack() as ph:
                emit_retention(kb, ph, I, dbg["yaT"])
                S.barrier(); S.emit()
            mpre = None
            if MOD_PREFETCH and l + 1 < L:
                mpre = (lambda kb_, st_, l=l: ModPrefetch(kb_, st_, X["c2"], X["w_ada"][l + 1], X["b_ada"][l + 1], G["modsN"]))
            with ExitStack() as ph:
                emit_na(kb, ph, I, dbg["ybT"], modpre=mpre)
                S.barrier(); S.emit()
            with ExitStack() as ph:
                emit_mla(kb, ph, I, dbg["ycT"], modpre=None)
                S.barrier(); S.emit()
            with ExitStack() as ph:
                emit_merge(kb, ph, I, dbg)
                S.barrier(); S.emit()
            with ExitStack() as ph:
                emit_ffn(kb, ph, I, dbg["x1T"], x_out, last=(l == L - 1))
                S.barrier(); S.emit(final=(l == L - 1))
    return nc


def core_consts(q):
    cc = np.zeros((128, 32), np.float32)
    cc[:, q] = 1.0
    if q > 0:
        cc[:, 4 + q - 1] = 1.0
    if q < 3:
        cc[:, 8 + q + 1] = 1.0
    for s_ in range(4):
        if s_ < q:
            cc[:, 12 + s_] = NT * (q - 1 - s_)
            cc[:, 22 + s_] = 1.0
        if s_ > q:
            cc[:, 17 + s_] = NT * (s_ - q - 1)
            cc[:, 26 + s_] = 1.0
    cc[:, 16] = NT * q
    cc[:, 21] = NT * (3 - q)
    return cc


def prep_fused(inp, core, L=4):
    b, q = core // 4, core % 4
    rr, rm = rope_tables(q)
    xc = np.concatenate([inp["x"][b, q * NT:(q + 1) * NT], inp["ctx"][b]], axis=0)
    c2 = np.stack([inp["c"][b], inp["c_ctx"]], axis=-1)
    c2 = np.ascontiguousarray(c2.reshape(8, 128, 2).transpose(1, 0, 2).reshape(128, 16))
    lng = np.stack([np.concatenate([fm(inp["ln_gain"][l, 0]), fm(inp["ln_bias"][l, 0]), fm(inp["ln_gain"][l, 1]), fm(inp["ln_bias"][l, 1])], axis=1)
                    for l in range(L)])
    return {
        "xT": np.ascontiguousarray(xc.T), "c2": c2, "w_ada": inp["w_ada"][:L], "b_ada": np.stack([fm(inp["b_ada"][l]) for l in range(L)]),
        "w_in": inp["w_in"][:L], "rld": np.ascontiguousarray(inp["ret_log_decay"][:L].reshape(L, 1, 8)), "rcst": ret_consts(),
        "kvg": np.ascontiguousarray(inp["mla_kv_norm"][:L].reshape(L, 1, 256)),
        "rope_rk": (rr * np.float32(0.125)).astype(np.float32), "rope_m": rm, "rope_q": rr,
        "rope_mq": (rm * np.float32(96.0 ** -0.5)).astype(np.float32),
        "gng": np.ascontiguousarray(inp["ret_gn_gain"][:L].reshape(L, 1, 512)),
        "nabias": np.stack([na_bias_table(inp["na_rpb"][l]) for l in range(L)]), "namask": na_mask_table(q),
        "qg": np.ascontiguousarray(inp["mla_q_norm"][:L].reshape(L, 1, 256)), "w_qup": inp["mla_w_qup"][:L], "w_kvup": inp["mla_w_kvup"][:L],
        "w_br": np.ascontiguousarray(np.stack([inp["w_branch_ret"][:L], inp["w_branch_na"][:L], inp["w_branch_mla"][:L]], axis=1)),
        "w_out": inp["w_out"][:L], "w_ff1": inp["w_ff1"][:L], "w_ff2": inp["w_ff2"][:L], "lng": np.ascontiguousarray(lng),
        "cc": core_consts(q),
    }


_NC = {}


def kernel(**inp):
    inp = {k: np.asarray(v) for k, v in inp.items()}
    if "F" not in _NC:
        _NC["F"] = build_fused(4)
    res = run_bass_kernel_spmd(_NC["F"], [prep_fused(inp, c) for c in range(8)], core_ids=list(range(8))).results
    out = np.zeros((2, 8192, D), np.float32)
    for core in range(8):
        b, q = core // 4, core % 4
        out[b, q * NT:(q + 1) * NT] = np.asarray(res[core]["xoT"])[:, 0:NT].T
    return out
```

```python
import numpy as np
from contextlib import ExitStack
import ml_dtypes
import concourse.bass as bass
import concourse.mybir as mybir
from concourse.bass_utils import run_bass_kernel_spmd

F32 = mybir.dt.float32
BF16 = mybir.dt.bfloat16
AF = mybir.ActivationFunctionType
ALU = mybir.AluOpType
AX = mybir.AxisListType

D = 1024
NT = 2048
NZ = 256
T = NT + NZ
NTILE = T // 128
EPS = 1e-5
ALPHA = 8.0 ** 0.25
EPOCH = 20000
C_RQ, C_RK, C_RV, C_GF, C_GB, C_NQ, C_NK, C_NV, C_MQ, C_MKV, C_MKR, C_GA, C_GBR, C_GC = (
    0, 256, 512, 1024, 1536, 2048, 2560, 3072, 3584, 3840, 4096, 4128, 5152, 6176)
TOKBLKS = [(0, 512), (512, 512), (1024, 512), (1536, 512), (2048, 256)]


class Sched:
    ENGS = ("pe", "dve", "act", "pool", "sp")

    def __init__(self, nc, stack, n_dma_sems=12, n_eng_sems=8):
        self.nc = nc
        self.ops = {e: [] for e in self.ENGS}
        self.esems = {e: [stack.enter_context(nc.semaphore(f"s_{e}_{i}")) for i in range(n_eng_sems)]
                      for e in ("pe", "dve", "act", "pool")}
        self.ecnt = {e: 0 for e in ("pe", "dve", "act", "pool")}
        self.eep = {e: 0 for e in ("pe", "dve", "act", "pool")}
        self.dsems = {q: [stack.enter_context(nc.semaphore(f"d_{q}_{i}")) for i in range(n_dma_sems)]
                      for q in ("sp", "act", "pool")}
        self.dval = {q: [0] * n_dma_sems for q in ("sp", "act", "pool")}
        self.drr = {q: 0 for q in ("sp", "act", "pool")}
        self.lastw = {}
        self.reads = {}
        self.seen = {e: {} for e in self.ENGS}
        self.final_tokens = []
        self.n_ops = 0
        self.csem = stack.enter_context(nc.semaphore("s_coll"))
        self.cval = 0

    def _need(self, eng, tok, waits):
        if tok is None:
            return
        sem, val = tok
        if eng == "pe" and any(sem is x for x in self.esems["pe"]):
            return
        sid = id(sem)
        cur = self.seen[eng].get(sid)
        if cur is not None and cur >= val:
            return
        self.seen[eng][sid] = val
        waits.append((sem, val))

    def _deps(self, eng, reads, writes):
        waits = []
        for k in reads:
            self._need(eng, self.lastw.get(k), waits)
        for k in writes:
            self._need(eng, self.lastw.get(k), waits)
            for t in self.reads.get(k, ()):
                self._need(eng, t, waits)
        return waits

    def _commit(self, tok, reads, writes):
        for k in reads:
            self.reads.setdefault(k, []).append(tok)
        for k in writes:
            self.lastw[k] = tok
            self.reads[k] = []

    def op(self, eng, fn, reads=(), writes=()):
        waits = self._deps(eng, reads, writes)
        if self.ecnt[eng] >= EPOCH:
            self.eep[eng] += 1
            self.ecnt[eng] = 0
        sem = self.esems[eng][self.eep[eng]]
        self.ecnt[eng] += 1
        tok = (sem, self.ecnt[eng])
        self.ops[eng].append((waits, fn, (sem, 1)))
        self._commit(tok, reads, writes)
        self.n_ops += 1
        return tok

    def dma(self, q, fn, reads=(), writes=(), final=False):
        waits = self._deps(q, reads, writes)
        i = self.drr[q]
        self.drr[q] = (i + 1) % len(self.dsems[q])
        sem = self.dsems[q][i]
        prev = self.dval[q][i]
        if prev > 0:
            self._need(q, (sem, prev), waits)
        self.dval[q][i] = prev + 16
        tok = (sem, prev + 16)
        self.ops[q].append((waits, fn, (sem, 16)))
        self._commit(tok, reads, writes)
        if final:
            self.final_tokens.append(tok)
        self.n_ops += 1
        return tok

    def coll(self, fn, reads=(), writes=()):
        waits = self._deps("pool", reads, writes)
        if not hasattr(self, "csem"):
            raise RuntimeError("no collective semaphore")
        self.cval += 1
        tok = (self.csem, self.cval)
        self.ops["pool"].append((waits, fn, (self.csem, 1)))
        self._commit(tok, reads, writes)
        self.ctoks = tok
        return tok

    def barrier(self):
        toks = []
        if getattr(self, "cval", 0) > 0 and getattr(self, "bar_coll", True):
            toks.append((self.csem, self.cval))
        for e in ("pe", "dve", "act", "pool"):
            if self.ecnt[e] > 0:
                toks.append((self.esems[e][self.eep[e]], self.ecnt[e]))
        for q in ("sp", "act", "pool"):
            for i, v in enumerate(self.dval[q]):
                if v > 0:
                    toks.append((self.dsems[q][i], v))
        for e in self.ENGS:
            waits = []
            for t in toks:
                self._need(e, t, waits)
            if waits:
                self.ops[e].append((waits, None, None))

    def emit(self, final=False):
        nc = self.nc
        fin = []
        if final:
            mx = {}
            for (sm, v) in self.final_tokens:
                if id(sm) not in mx or mx[id(sm)][1] < v:
                    mx[id(sm)] = (sm, v)
            fin = list(mx.values())
        handles = {"pe": "tensor", "dve": "vector", "act": "scalar", "pool": "gpsimd", "sp": "sync"}
        with nc.Block() as block:
            for e in self.ENGS:
                ops = self.ops[e]
                extra = fin if e == "sp" else []
                if not ops and not extra:
                    continue

                def body(h, ops=ops, extra=extra):
                    for waits, fn, si in ops:
                        for (ws, wv) in waits:
                            h.wait_ge(ws, wv)
                        if fn is not None:
                            fn(h).then_inc(si[0], si[1])
                    for (ws, wv) in extra:
                        h.wait_ge(ws, wv)

                getattr(block, handles[e])(body)
        self.ops = {e: [] for e in self.ENGS}


class Ring:
    def __init__(self, tiles, name, keys=None):
        self.tiles = tiles
        self.name = name
        self.keys = keys
        self.i = 0

    def next(self):
        j = self.i % len(self.tiles)
        t = self.tiles[j]
        k = (self.name, j) if self.keys is None else self.keys[j]
        self.i += 1
        return t, k


class KB:
    def __init__(self, nc, S):
        self.nc = nc
        self.S = S
        self.dq = 0
        self.uid = 0

    def sb(self, st, name, shape, dt):
        self.uid += 1
        return st.enter_context(self.nc.sbuf_tensor(f"sb{self.uid}_{name}", shape, dt))

    def ps(self, st, name, shape, dt=F32):
        self.uid += 1
        return st.enter_context(self.nc.psum_tensor(f"ps{self.uid}_{name}", shape, dt))

    def ring_sb(self, st, name, shape, dt, n):
        return Ring([self.sb(st, f"{name}{i}", shape, dt) for i in range(n)], name)

    def ring_ps_sliced(self, st, name, width, n, per_bank=4):
        tiles, keys = [], []
        nb = (n + per_bank - 1) // per_bank
        for b in range(nb):
            t = self.ps(st, f"{name}{b}", [128, width * per_bank], F32)
            for j in range(per_bank):
                if len(tiles) < n:
                    tiles.append(t[:, j * width:(j + 1) * width])
                    keys.append((name, "bank", b))
        return Ring(tiles, name, keys)

    def ring_ps(self, st, name, shape, n, dt=F32):
        return Ring([self.ps(st, f"{name}{i}", shape, dt) for i in range(n)], name)

    def load(self, out, in_, writes, reads=(), q=None):
        if q is None:
            q = ("sp", "act")[self.dq % 2]
            self.dq += 1
        return self.S.dma(q, lambda h: h.dma_start(out=out, in_=in_), reads=reads, writes=writes)

    def load_cast(self, out, in_, writes, reads=()):
        return self.S.dma("pool", lambda h: h.dma_start(out=out, in_=in_), reads=reads, writes=writes)

    def store(self, out, in_, reads, writes=(), final=False, q="sp"):
        return self.S.dma(q, lambda h: h.dma_start(out=out, in_=in_), reads=reads, writes=writes, final=final)

    def mm_group(self, out, pairs, reads, writes):
        n = len(pairs)

        def fn(h):
            r = None
            for i, (l, rr) in enumerate(pairs):
                r = h.matmul(out, lhsT=l, rhs=rr, start=(i == 0), stop=(i == n - 1))
            return r
        return self.S.op("pe", fn, reads=reads, writes=writes)


def wview(w, c0, c1):
    return w.rearrange("(k p) n -> p k n", p=128)[:, :, c0:c1]


def emit_ret_tables(kb, st, rld, cst):
    S = kb.S
    lg = kb.sb(st, "lg", [128, 8], F32)
    cs = kb.sb(st, "cst", [128, 128 * 6 + 8], F32)
    DT = kb.sb(st, "DT", [128, 8, 128], F32)
    QD = kb.sb(st, "QD", [128, 8, 128], F32)
    kdec = kb.sb(st, "kdec", [128, 8], F32)
    cd = kb.sb(st, "cd", [128, 8], F32)
    tmp = kb.sb(st, "rt_tmp", [128, 128], F32)
    kb.load(lg[:], rld.partition_broadcast(128), writes=["lg"])
    kb.load(cs[:], cst, writes=["cst"])
    S.op("act", lambda h: h.activation(out=lg[:], in_=lg[:], func=AF.Exp), reads=["lg"], writes=["lg"])
    S.op("act", lambda h: h.activation(out=lg[:], in_=lg[:], func=AF.Ln, scale=-1.0, bias=cs[:, 768 + 4:768 + 5]),
         reads=["lg", "cst"], writes=["lg"])
    for d in range(2):
        for hh in range(4):
            i = d * 4 + hh
            S.op("act", lambda h, i=i, d=d: h.activation(out=tmp[:], in_=cs[:, d * 128:(d + 1) * 128], func=AF.Exp,
                                                          scale=lg[:, i:i + 1]),
                 reads=["lg", "cst"], writes=["rt_tmp"])
            S.op("dve", lambda h, i=i, d=d: h.tensor_tensor(out=DT[:, i, :], in0=tmp[:], in1=cs[:, (2 + d) * 128:(3 + d) * 128],
                                                             op=ALU.mult),
                 reads=["rt_tmp", "cst"], writes=["DT"])
            S.op("act", lambda h, i=i, d=d: h.activation(out=QD[:, i, :], in_=cs[:, (4 + d) * 128:(5 + d) * 128], func=AF.Exp,
                                                          scale=lg[:, i:i + 1]),
                 reads=["lg", "cst"], writes=["QD"])
            S.op("act", lambda h, i=i, d=d: h.activation(out=kdec[:, i:i + 1], in_=cs[:, 768 + d:768 + d + 1], func=AF.Exp,
                                                          scale=lg[:, i:i + 1]),
                 reads=["lg", "cst"], writes=["kdec"])
    S.op("act", lambda h: h.activation(out=cd[:], in_=lg[:], func=AF.Exp, scale=128.0), reads=["lg"], writes=["cd"])
    return dict(lg=lg, DT=DT, QD=QD, kdec=kdec, cd=cd, cs=cs)


def ret_consts():
    s = np.arange(128)[:, None].astype(np.float32)
    c = np.arange(128)[None, :].astype(np.float32)
    E0 = np.maximum(c - s, 0.0)
    E1 = np.maximum(s - c, 0.0)
    M0 = (c >= s).astype(np.float32)
    M1 = (s >= c).astype(np.float32)
    R0 = np.broadcast_to(c + 1.0, (128, 128))
    R1 = np.broadcast_to(128.0 - c, (128, 128))
    tail = np.zeros((128, 8), np.float32)
    tail[:, 0] = 127.0 - s[:, 0]
    tail[:, 1] = s[:, 0]
    tail[:, 4] = 1.0
    tail[:, 5] = EPS
    return np.concatenate([E0, E1, M0, M1, R0, R1, tail], axis=1).astype(np.float32)


def build_A():
    nc = bass.Bass("TRN2", target_bir_lowering=False)
    din = lambda name, shape, dt=F32: nc.dram_tensor(name, shape, dt, kind="ExternalInput").ap()
    dout = lambda name, shape, dt=F32: nc.dram_tensor(name, shape, dt, kind="ExternalOutput").ap()
    xT = din("xT", [D, T])
    c2 = din("c2", [128, 16])
    w_ada = din("w_ada", [D, 6 * D])
    b_ada = din("b_ada", [128, 48])
    w_in = din("w_in", [D, 7200])
    rld = din("rld", [1, 8])
    rcst = din("rcst", [128, 776])
    kvg = din("kvg", [1, 256])
    rope_r = din("rope_r", [T, 64])
    rope_m = din("rope_m", [T, 32])
    o_mods = dout("mods", [128, 96])
    o_hxT = dout("hxT", [128, 8, T], BF16)
    o_retK = dout("retK", [T, 256], BF16)
    o_retKT = dout("retKT", [64, 4, T], BF16)
    o_retV = dout("retV", [T, 512], BF16)
    o_retF = dout("retF", [64, 8, 128])
    o_naKT = dout("naKT", [128, 4, T], BF16)
    o_naV = dout("naV", [T, 8, 128], BF16)
    o_ckvnT = dout("ckvnT", [128, 2, T], BF16)
    o_krT = dout("krT", [32, T], BF16)

    with ExitStack() as st0:
        S = Sched(nc, st0)
        kb = KB(nc, S)
        with ExitStack() as st:
            hxT = kb.sb(st, "hxT", [128, 8, T], BF16)
            mods = kb.sb(st, "mods", [128, 48, 2], F32)
            ident = kb.sb(st, "ident", [128, 128], BF16)
            identf = kb.sb(st, "identf", [128, 128], F32)
            S.op("pool", lambda h: h.memset(identf[:], 0.0), writes=["identf"])
            S.op("pool", lambda h: h.affine_select(out=identf[:], in_=identf[:], pattern=[[-1, 128]],
                                                     compare_op=ALU.not_equal, fill=1.0, base=0, channel_multiplier=1),
                 reads=["identf"], writes=["identf"])
            S.op("dve", lambda h: h.tensor_copy(out=ident[:], in_=identf[:]), reads=["identf"], writes=["ident"])
            with ExitStack() as p1:
                emit_mod_hx(kb, p1, xT, c2, w_ada, b_ada, mods, hxT)
                kb.store(o_mods, mods[:].rearrange("p a b -> p (a b)"), reads=["mods"], final=True)
                kb.store(o_hxT, hxT[:], reads=[("hxT", j) for j in range(8)], final=True)
                S.barrier()
                S.emit()
            with ExitStack() as p2:
                emit_kv_side(kb, p2, hxT, ident, w_in, rld, rcst, kvg, rope_r, rope_m,
                             o_retK, o_retKT, o_retV, o_retF, o_naKT, o_naV, o_ckvnT, o_krT)
                S.barrier()
                S.emit(final=True)
    return nc


def emit_mod_hx(kb, st, xT, c2, w_ada, b_ada, mods, hxT, mods_src=None):
    S = kb.S
    if mods_src is not None:
        sc1p = kb.sb(st, "sc1p", [128, 8, 2], F32)
        xs = kb.ring_sb(st, "xs", [128, T], F32, 2)
        kb.load(mods[:].rearrange("p a b -> p (a b)"), mods_src, writes=["mods"])
        emit_hx_only(kb, S, xT, mods, sc1p, xs, hxT)
        return
    c2s = kb.sb(st, "c2s", [128, 16], F32)
    bad = kb.sb(st, "bad", [128, 48], F32)
    sc1p = kb.sb(st, "sc1p", [128, 8, 2], F32)
    wa = kb.ring_sb(st, "wa", [128, 8, 768], F32, 2)
    mps = kb.ps(st, "mod_ps", [128, 48, 2])
    xs = kb.ring_sb(st, "xs", [128, T], F32, 2)
    kb.load(c2s[:], c2, writes=["c2s"])
    kb.load(bad[:], b_ada, writes=["bad"])
    S.op("act", lambda h: h.activation(out=c2s[:], in_=c2s[:], func=AF.Silu), reads=["c2s"], writes=["c2s"])
    c2v = c2s[:].rearrange("p (k w) -> p k w", w=2)
    if MOD_ROWMAJOR:
        mrow = kb.sb(st, "mrow", [2, 6 * D], F32)
        identf2 = kb.sb(st, "identf2", [128, 128], F32)
        E(S, "pool", "memset", [], ["identf2"], identf2[:], 0.0)
        E(S, "pool", "affine_select", ["identf2"], ["identf2"], out=identf2[:], in_=identf2[:], pattern=[[-1, 128]],
          compare_op=ALU.not_equal, fill=1.0, base=0, channel_multiplier=1)
        mrp = kb.ring_ps(st, "mrow_ps", [2, 768], 2)
        for blk in range(8):
            wt, wk = wa.next()
            kb.load(wt[:], wview(w_ada, blk * 768, (blk + 1) * 768), writes=[wk])
            mp, mpk = mrp.next()
            kb.mm_group(mp[:, 0:512], [(c2v[:, k, :], wt[:, k, 0:512]) for k in range(8)], reads=[wk, "c2s"], writes=[(mpk, 0)])
            kb.mm_group(mp[:, 512:768], [(c2v[:, k, :], wt[:, k, 512:768]) for k in range(8)], reads=[wk, "c2s"], writes=[(mpk, 1)])
            E(S, "act", "copy", [(mpk, 0), (mpk, 1)], [("mrow", blk)], out=mrow[:, blk * 768:(blk + 1) * 768], in_=mp[:, :])
        for j in range(48):
            E(S, "pe", "transpose", [("mrow", j // 6), "identf2"], [("mps", j)], mps[:, j, :], mrow[:, j * 128:(j + 1) * 128], identf2[0:2, 0:2])
    else:
        for blk in range(8):
            wt, wk = wa.next()
            kb.load(wt[:], wview(w_ada, blk * 768, (blk + 1) * 768), writes=[wk])
            for jj in range(6):
                j = blk * 6 + jj
                kb.mm_group(mps[:, j, :], [(wt[:, k, jj * 128:(jj + 1) * 128], c2v[:, k, :]) for k in range(8)],
                            reads=[wk, "c2s"], writes=[("mps", j)])
    for w in range(2):
        S.op("dve", lambda h, w=w: h.tensor_tensor(out=mods[:, :, w], in0=mps[:, :, w], in1=bad[:], op=ALU.add),
             reads=[("mps", j) for j in range(48)] + ["bad"], writes=["mods"])
    emit_hx_only(kb, S, xT, mods, sc1p, xs, hxT)


def emit_hx_only(kb, S, xT, mods, sc1p, xs, hxT):
    S.op("dve", lambda h: h.tensor_scalar_add(out=sc1p[:], in0=mods[:, 8:16, :], scalar1=1.0), reads=["mods"], writes=["sc1p"])
    xv = xT.rearrange("(j p) t -> p j t", p=128)
    for j in range(8):
        xt, xk = xs.next()
        kb.load(xt[:], xv[:, j, :], writes=[xk])
        eng = "dve" if j % 2 == 0 else "pool"
        S.op(eng, lambda h, j=j, xt=xt: h.tensor_scalar(out=hxT[:, j, 0:NT], in0=xt[:, 0:NT], scalar1=sc1p[:, j, 0:1],
                                                        scalar2=mods[:, j, 0:1], op0=ALU.mult, op1=ALU.add),
             reads=[xk, "sc1p", "mods"], writes=[("hxTa", j)])
        S.op(eng, lambda h, j=j, xt=xt: h.tensor_scalar(out=hxT[:, j, NT:T], in0=xt[:, NT:T], scalar1=sc1p[:, j, 1:2],
                                                        scalar2=mods[:, j, 1:2], op0=ALU.mult, op1=ALU.add),
             reads=[xk, "sc1p", "mods", ("hxTa", j)], writes=[("hxT", j)])


def rope_tm(S, eng, out, src, cos, sin, t1, t2, nh, half, rkeys, wkey, tkey):
    cb = cos.unsqueeze(1).to_broadcast([128, nh, half]) if nh > 1 else cos
    sn = sin.unsqueeze(1).to_broadcast([128, nh, half]) if nh > 1 else sin
    if nh > 1:
        x1, x2 = src[:, :, 0:half], src[:, :, half:2 * half]
        o1, o2 = out[:, :, 0:half], out[:, :, half:2 * half]
    else:
        x1, x2 = src[:, 0:half], src[:, half:2 * half]
        o1, o2 = out[:, 0:half], out[:, half:2 * half]
    S.op(eng, lambda h: h.tensor_tensor(out=t1, in0=x1, in1=cb, op=ALU.mult), reads=rkeys, writes=[tkey + "1"])
    S.op(eng, lambda h: h.tensor_tensor(out=t2, in0=x2, in1=sn, op=ALU.mult), reads=rkeys, writes=[tkey + "2"])
    S.op(eng, lambda h: h.tensor_tensor(out=o1, in0=t1, in1=t2, op=ALU.subtract), reads=[tkey + "1", tkey + "2"], writes=[(wkey, "a")])
    S.op(eng, lambda h: h.tensor_tensor(out=t1, in0=x1, in1=sn, op=ALU.mult), reads=rkeys + [(wkey, "a")], writes=[tkey + "1"])
    S.op(eng, lambda h: h.tensor_tensor(out=t2, in0=x2, in1=cb, op=ALU.mult), reads=rkeys + [(wkey, "a")], writes=[tkey + "2"])
    S.op(eng, lambda h: h.tensor_tensor(out=o2, in0=t1, in1=t2, op=ALU.add), reads=[tkey + "1", tkey + "2", (wkey, "a")], writes=[wkey])


def emit_kv_side(kb, st, hxT, ident, w_in, rld, rcst, kvg, rope_r, rope_m,
                 o_retK, o_retKT, o_retV, o_retF, o_naKT, o_naV, o_ckvnT, o_krT):
    S = kb.S
    hx_keys = [("hxT", j) for j in range(8)]
    RT = emit_ret_tables(kb, st, rld, rcst)
    w_rkv = kb.sb(st, "w_rkv", [128, 8, 768], BF16)
    w_nk = kb.sb(st, "w_nk", [128, 8, 512], BF16)
    w_nv = kb.sb(st, "w_nv", [128, 8, 512], BF16)
    w_mk = kb.sb(st, "w_mk", [128, 8, 288], BF16)
    kb.load_cast(w_rkv[:], wview(w_in, C_RK, C_GF), writes=["w_rkv"])
    kb.load_cast(w_mk[:], wview(w_in, C_MKV, C_GA), writes=["w_mk"])
    kb.load_cast(w_nv[:], wview(w_in, C_NV, C_MQ), writes=["w_nv"])
    kb.load_cast(w_nk[:], wview(w_in, C_NK, C_NV), writes=["w_nk"])
    rr = kb.sb(st, "rr", [128, NTILE, 64], F32)
    rm = kb.sb(st, "rm", [128, NTILE, 32], F32)
    kb.load(rr[:], rope_r.rearrange("(t p) f -> p t f", p=128), writes=["rr"])
    kb.load(rm[:], rope_m.rearrange("(t p) f -> p t f", p=128), writes=["rm"])
    gain = kb.sb(st, "kvgain", [128, 256], F32)
    kb.load(gain[:], kvg.partition_broadcast(128), writes=["kvgain"])
    k_tm = kb.sb(st, "k_tm", [128, NTILE, 256], BF16)
    v_tm = kb.sb(st, "v_tm", [128, NTILE, 512], BF16)
    kT = kb.sb(st, "kT", [64, 4, T], BF16)
    vaug_r = kb.ring_sb(st, "vaug", [128, 8, 128], BF16, 2)
    ckvnT = kb.sb(st, "ckvnT", [128, 2, T], BF16)
    krT = kb.sb(st, "krT", [32, T], BF16)
    naKT_r = kb.ring_sb(st, "naKT", [128, T], BF16, 2)
    for _ in range(2):
        vt_, vk_ = vaug_r.next()
        S.op("pool", lambda h, vt_=vt_: h.memset(vt_[:], 1.0), writes=[vk_])
    ps_a = kb.ring_ps(st, "ps_a", [128, 512], 2)
    ps_b = kb.ring_ps(st, "ps_b", [128, 512], 2)
    ps_t = kb.ring_ps(st, "ps_t", [128, 128], 2, BF16)
    kf = kb.ring_sb(st, "kf", [128, 256], F32, 2)
    kr32 = kb.ring_sb(st, "kr32", [128, 32], F32, 2)
    krb = kb.ring_sb(st, "krb", [128, 32], BF16, 2)
    ckvn = kb.ring_sb(st, "ckvn", [128, 256], BF16, 2)
    t1 = kb.sb(st, "t1", [128, 128], F32)
    t2 = kb.sb(st, "t2", [128, 128], F32)
    t1b = kb.sb(st, "t1b", [128, 128], F32)
    t2b = kb.sb(st, "t2b", [128, 128], F32)
    u1 = kb.sb(st, "u1", [128, 16], F32)
    u2 = kb.sb(st, "u2", [128, 16], F32)
    junk = kb.sb(st, "junk", [128, 256], F32)
    ss = kb.ring_sb(st, "ss", [128, 2], F32, 2)
    eps_ap = RT["cs"][:, 768 + 5:768 + 6]
    Fst = kb.sb(st, "Fst", [64, 8, 128], F32)
    S.op("dve", lambda h: h.memset(Fst[:], 0.0), writes=[("F", i) for i in range(8)])
    kd_r = kb.ring_sb(st, "kdA", [128, 64], BF16, 4)
    ps_sA = kb.ring_ps(st, "ps_sA", [64, 128], 2)

    def scan_step(d, t):
        for hh in range(4):
            i = d * 4 + hh
            kdt, kdk = kd_r.next()
            E(S, "act", "activation", [("k_tm", t), "kdec"], [kdk], out=kdt[:], in_=k_tm[:, t, hh * 64:(hh + 1) * 64], func=AF.Copy,
              scale=RT["kdec"][:, i:i + 1])
            pst, psk = ps_sA.next()
            kb.mm_group(pst[:], [(kdt[:], v_tm[:, t, hh * 128:(hh + 1) * 128])], reads=[kdk, ("v_tm", t)], writes=[psk])
            E(S, "dve", "scalar_tensor_tensor", [psk, ("F", i), "cd"], [("F", i)], out=Fst[:, i, :], in0=Fst[:, i, :], scalar=RT["cd"][0:64, i:i + 1],
              in1=pst[:], op0=ALU.mult, op1=ALU.add)

    for t in range(NTILE):
        tok = slice(t * 128, (t + 1) * 128)
        if 1 <= t <= 16:
            scan_step(0, t - 1)
        pa, pak = ps_a.next()
        kb.mm_group(pa[:, 0:512], [(hxT[:, k, tok], w_rkv[:, k, 0:512]) for k in range(8)],
                    reads=hx_keys + ["w_rkv"], writes=[pak])
        pb, pbk = ps_b.next()
        kb.mm_group(pb[:, 0:256], [(hxT[:, k, tok], w_rkv[:, k, 512:768]) for k in range(8)],
                    reads=hx_keys + ["w_rkv"], writes=[pbk])
        S.op("act", lambda h, pa=pa, t=t: h.copy(out=v_tm[:, t, 0:256], in_=pa[:, 256:512]), reads=[pak], writes=[("v_tm_a", t)])
        S.op("act", lambda h, pb=pb, t=t: h.copy(out=v_tm[:, t, 256:512], in_=pb[:, 0:256]), reads=[pbk, ("v_tm_a", t)], writes=[("v_tm", t)])
        kft, kfk = kf.next()
        S.op("act", lambda h, pa=pa, kft=kft: h.copy(out=kft[:], in_=pa[:, 0:256]), reads=[pak], writes=[kfk])
        rope_tm(S, "pool" if t % 2 == 0 else "dve", k_tm[:, t, :].rearrange("p (h f) -> p h f", h=4), kft[:].rearrange("p (h f) -> p h f", h=4),
                rr[:, t, 0:32], rr[:, t, 32:64], (t1 if t % 2 == 0 else t1b)[:].rearrange("p (h f) -> p h f", h=4),
                (t2 if t % 2 == 0 else t2b)[:].rearrange("p (h f) -> p h f", h=4),
                4, 32, [kfk, "rr"], ("k_tm", t), "ropeA" if t % 2 == 0 else "ropeAb")
        kb.store(o_retK[tok, :], k_tm[:, t, :], reads=[("k_tm", t)], final=True)
        kb.store(o_retV[tok, :], v_tm[:, t, :], reads=[("v_tm", t)], final=True, q="act")
        for hh in range(4):
            pt, ptk = ps_t.next()
            S.op("pe", lambda h, pt=pt, t=t, hh=hh: h.transpose(pt[0:64, :], k_tm[:, t, hh * 64:(hh + 1) * 64], ident[:]),
                 reads=[("k_tm", t), "ident"], writes=[ptk])
            S.op("dve", lambda h, pt=pt, hh=hh, tok=tok: h.tensor_copy(out=kT[:, hh, tok], in_=pt[0:64, :]), reads=[ptk], writes=[("kT", t, hh)])
        pa, pak = ps_a.next()
        kb.mm_group(pa[:, 0:512], [(hxT[:, k, tok], w_nv[:, k, :]) for k in range(8)], reads=hx_keys + ["w_nv"], writes=[pak])
        vg, vgk = vaug_r.next()
        S.op("act", lambda h, pa=pa, vg=vg: h.copy(out=vg[:, :, 0:64], in_=pa[:, 0:512].rearrange("p (h f) -> p h f", h=8)),
             reads=[pak, vgk], writes=[vgk])
        kb.store(o_naV[tok, :, :], vg[:], reads=[vgk], final=True)
        pb, pbk = ps_b.next()
        kb.mm_group(pb[:, 0:288], [(hxT[:, k, tok], w_mk[:, k, :]) for k in range(8)], reads=hx_keys + ["w_mk"], writes=[pbk])
        sst, ssk = ss.next()
        S.op("act", lambda h, pb=pb, sst=sst: h.activation(out=junk[:], in_=pb[:, 0:256], func=AF.Square, accum_out=sst[:, 0:1]),
             reads=[pbk], writes=["junk", ssk])
        S.op("act", lambda h, sst=sst: h.activation(out=sst[:, 1:2], in_=sst[:, 0:1], func=AF.Sqrt, scale=1.0 / 256.0, bias=eps_ap),
             reads=[ssk, "cst"], writes=[ssk])
        k32, k32k = kr32.next()
        S.op("act", lambda h, pb=pb, k32=k32: h.copy(out=k32[:], in_=pb[:, 256:288]), reads=[pbk], writes=[k32k])
        S.op("dve", lambda h, sst=sst: h.reciprocal(out=sst[:, 1:2], in_=sst[:, 1:2]), reads=[ssk], writes=[ssk])
        cn, cnk = ckvn.next()
        S.op("dve", lambda h, pb=pb, sst=sst, cn=cn: h.scalar_tensor_tensor(out=cn[:], in0=pb[:, 0:256], scalar=sst[:, 1:2], in1=gain[:],
                                                                           op0=ALU.mult, op1=ALU.mult),
             reads=[pbk, ssk, "kvgain", k32k], writes=[cnk])
        kbt, kbk = krb.next()
        rope_tm(S, "dve", kbt[:], k32[:], rm[:, t, 0:16], rm[:, t, 16:32], u1[:], u2[:], 1, 16, [k32k, "rm"], kbk, "ropeB")
        for kc in range(2):
            pt, ptk = ps_t.next()
            S.op("pe", lambda h, pt=pt, cn=cn, kc=kc: h.transpose(pt[:], cn[:, kc * 128:(kc + 1) * 128], ident[:]),
                 reads=[cnk, "ident"], writes=[ptk])
            S.op("dve", lambda h, pt=pt, kc=kc, tok=tok: h.tensor_copy(out=ckvnT[:, kc, tok], in_=pt[:]), reads=[ptk], writes=[("ckvnT", t, kc)])
        pt, ptk = ps_t.next()
        S.op("pe", lambda h, pt=pt, kbt=kbt: h.transpose(pt[0:32, :], kbt[:], ident[:]), reads=[kbk, "ident"], writes=[ptk])
        S.op("dve", lambda h, pt=pt, tok=tok: h.tensor_copy(out=krT[:, tok], in_=pt[0:32, :]), reads=[ptk], writes=[("krT", t)])
    bwd_t = list(range(15, -1, -1))
    for hp in range(4):
        nk, nkk = naKT_r.next()
        for (b0, bn) in TOKBLKS:
            if bwd_t:
                scan_step(1, bwd_t.pop(0))
            pa, pak = ps_a.next()
            kb.mm_group(pa[:, 0:bn], [(w_nk[:, k, hp * 128:(hp + 1) * 128], hxT[:, k, b0:b0 + bn]) for k in range(8)],
                        reads=hx_keys + ["w_nk"], writes=[pak])
            S.op("act", lambda h, pa=pa, nk=nk, b0=b0, bn=bn: h.copy(out=nk[:, b0:b0 + bn], in_=pa[:, 0:bn]),
                 reads=[pak], writes=[nkk])
        kb.store(o_naKT[:, hp, :], nk[:], reads=[nkk], final=True)
    kb.store(o_ckvnT, ckvnT[:], reads=[("ckvnT", t, kc) for t in range(NTILE) for kc in range(2)], final=True)
    kb.store(o_krT, krT[:], reads=[("krT", t) for t in range(NTILE)], final=True)
    kb.store(o_retKT, kT[:], reads=[("kT", t, hh) for t in range(NTILE) for hh in range(4)], final=True)
    while bwd_t:
        scan_step(1, bwd_t.pop(0))
    kb.store(o_retF, Fst[:], reads=[("F", i) for i in range(8)], final=True)


def emit_ret_scan(kb, st, RT, k_tm, v_tm, Sst, tiles, sprev, tag):
    S = kb.S
    kd = kb.ring_sb(st, "kd" + tag, [128, 64], BF16, 3)
    ps_s = kb.ring_ps(st, "ps_s" + tag, [64, 128], 2)
    for d in range(2):
        order = tiles if d == 0 else tiles[::-1]
        for t in order:
            for hh in range(4):
                i = d * 4 + hh
                if sprev is not None:
                    S.op("act", lambda h, i=i, t=t: h.copy(out=sprev[:, i, t, :], in_=Sst[:, i, :]), reads=[("F", i)],
                         writes=[("sprev", i, t)])
                kdt, kdk = kd.next()
                S.op("act", lambda h, kdt=kdt, t=t, hh=hh, i=i: h.activation(out=kdt[:], in_=k_tm[:, t, hh * 64:(hh + 1) * 64], func=AF.Copy,
                                                                              scale=RT["kdec"][:, i:i + 1]),
                     reads=[("k_tm", t), "kdec"], writes=[kdk])
                pst, psk = ps_s.next()
                kb.mm_group(pst[:], [(kdt[:], v_tm[:, t, hh * 128:(hh + 1) * 128])], reads=[kdk, ("v_tm", t)], writes=[psk])
                S.op("dve", lambda h, pst=pst, i=i: h.scalar_tensor_tensor(out=Sst[:, i, :], in0=Sst[:, i, :], scalar=RT["cd"][0:64, i:i + 1],
                                                                            in1=pst[:], op0=ALU.mult, op1=ALU.add),
                     reads=[psk, ("F", i), "cd"], writes=[("F", i)])


def rope_tables(q):
    idx = q * NT + np.arange(NT)
    row = (idx // 64).astype(np.float32)
    col = (idx % 64).astype(np.float32)

    def tab(rot_dim):
        nf = rot_dim // 4
        inv = (10000.0 ** (-2.0 * np.arange(nf, dtype=np.float32) / (rot_dim // 2))).astype(np.float32)
        ang = np.concatenate([row[:, None] * inv, col[:, None] * inv], axis=-1).astype(np.float32)
        cs = np.concatenate([np.cos(ang), np.sin(ang)], axis=-1).astype(np.float32)
        z = np.concatenate([np.ones((NZ, rot_dim // 2), np.float32), np.zeros((NZ, rot_dim // 2), np.float32)], axis=-1)
        return np.concatenate([cs, z], axis=0)
    return tab(64), tab(32)


def fm(v):
    return np.ascontiguousarray(v.reshape(-1, 128).T)


def prep_A(inp, l, core, xT_core):
    b, q = core // 4, core % 4
    rr, rm = rope_tables(q)
    c2 = np.stack([inp["c"][b], inp["c_ctx"]], axis=-1)
    c2 = np.ascontiguousarray(c2.reshape(8, 128, 2).transpose(1, 0, 2).reshape(128, 16))
    return {
        "xT": xT_core, "c2": c2, "w_ada": inp["w_ada"][l], "b_ada": fm(inp["b_ada"][l]),
        "w_in": inp["w_in"][l], "rld": np.ascontiguousarray(inp["ret_log_decay"][l].reshape(1, 8)),
        "rcst": ret_consts(), "kvg": np.ascontiguousarray(inp["mla_kv_norm"][l].reshape(1, 256)),
        "rope_r": (rr * np.float32(0.125)).astype(np.float32), "rope_m": rm,
    }


NKEY = 2816
NKM = 8192 + 256
FFBLK = 256
MOD_ROWMAJOR = True
MOD_PREFETCH = True
MLA_PAIRS = False


def E(S, eng, meth, reads, writes, *args, **kw):
    return S.op(eng, lambda h: getattr(h, meth)(*args, **kw), reads=reads, writes=writes)


def make_ident(kb, st):
    S = kb.S
    ident = kb.sb(st, "ident", [128, 128], BF16)
    identf = kb.sb(st, "identf", [128, 128], F32)
    E(S, "pool", "memset", [], ["identf"], identf[:], 0.0)
    E(S, "pool", "affine_select", ["identf"], ["identf"], out=identf[:], in_=identf[:], pattern=[[-1, 128]],
      compare_op=ALU.not_equal, fill=1.0, base=0, channel_multiplier=1)
    E(S, "dve", "tensor_copy", ["identf"], ["ident"], out=ident[:], in_=identf[:])
    return ident, identf


def rope2(S, eng, x1, x2, o1, o2, cb, sn, t1, t2, rkeys, wkey, tkey):
    E(S, eng, "tensor_tensor", rkeys, [(tkey, 1)], out=t1, in0=x1, in1=cb, op=ALU.mult)
    E(S, eng, "tensor_tensor", rkeys, [(tkey, 2)], out=t2, in0=x2, in1=sn, op=ALU.mult)
    E(S, eng, "tensor_tensor", [(tkey, 1), (tkey, 2)], [(wkey, "a")], out=o1, in0=t1, in1=t2, op=ALU.subtract)
    E(S, eng, "tensor_tensor", rkeys + [(wkey, "a")], [(tkey, 1)], out=t1, in0=x1, in1=sn, op=ALU.mult)
    E(S, eng, "tensor_tensor", rkeys + [(wkey, "a")], [(tkey, 2)], out=t2, in0=x2, in1=cb, op=ALU.mult)
    E(S, eng, "tensor_tensor", [(tkey, 1), (tkey, 2), (wkey, "a")], [wkey], out=o2, in0=t1, in1=t2, op=ALU.add)


def build_B():
    nc = bass.Bass("TRN2", target_bir_lowering=False)
    din = lambda name, shape, dt=F32: nc.dram_tensor(name, shape, dt, kind="ExternalInput").ap()
    dout = lambda name, shape, dt=F32: nc.dram_tensor(name, shape, dt, kind="ExternalOutput").ap()
    I = dict(
        xT=din("xT", [D, T]), mods=din("mods", [128, 96]), hxT=din("hxT", [128, 8, T], BF16),
        retK=din("retK", [T, 256], BF16), retKT=din("retKT", [64, 4, T], BF16), retV=din("retV", [T, 512], BF16),
        Fsl=din("Fsl", [64, 8, 3, 128]), fexp=din("fexp", [128, 8]), rld=din("rld", [1, 8]), rcst=din("rcst", [128, 776]),
        gng=din("gng", [1, 512]), rope_q=din("rope_q", [T, 64]), rope_mq=din("rope_mq", [T, 32]),
        naKT=din("naKT", [128, 4, NKEY], BF16), naV=din("naV", [NKEY, 8, 128], BF16),
        nabias=din("nabias", [128, 8, 24, 64]), namask=din("namask", [128, 32, 512], BF16),
        ckvnT=din("ckvnT", [128, 2, NKM], BF16), krT=din("krT", [32, NKM], BF16),
        qg=din("qg", [1, 256]), w_qup=din("w_qup", [256, 768]), w_kvup=din("w_kvup", [256, 1024]),
        w_in=din("w_in", [D, 7200]), w_br=din("w_br", [3, 512, D]), w_out=din("w_out", [D, D]),
        w_ff1=din("w_ff1", [D, 4 * D]), w_ff2=din("w_ff2", [4 * D, D]), lng=din("lng", [128, 32]),
    )
    xoT = dout("xoT", [D, T])
    dbg = dict(yaT=dout("yaT", [128, 4, T], BF16), ybT=dout("ybT", [128, 4, T], BF16), ycT=dout("ycT", [128, 4, T], BF16),
               x1T=dout("x1T", [128, 8, T]))
    with ExitStack() as st0:
        S = Sched(nc, st0)
        kb = KB(nc, S)
        with ExitStack() as ph:
            emit_retention(kb, ph, I, dbg["yaT"])
            S.barrier(); S.emit()
        with ExitStack() as ph:
            emit_na(kb, ph, I, dbg["ybT"])
            S.barrier(); S.emit()
        with ExitStack() as ph:
            emit_mla(kb, ph, I, dbg["ycT"])
            S.barrier(); S.emit()
        with ExitStack() as ph:
            emit_merge(kb, ph, I, dbg)
            S.barrier(); S.emit()
        with ExitStack() as ph:
            emit_ffn(kb, ph, I, dbg["x1T"], xoT)
            S.barrier(); S.emit(final=True)
    return nc


def emit_retention(kb, st, I, o_yaT):
    S = kb.S
    RT = emit_ret_tables(kb, st, I["rld"], I["rcst"])
    ident, _ = make_ident(kb, st)
    eps_ap = RT["cs"][:, 768 + 5:768 + 6]
    hxT = kb.sb(st, "hxT", [128, 8, T], BF16)
    kb.load(hxT[:], I["hxT"], writes=["hxT"])
    k_tm = kb.sb(st, "k_tm", [128, NTILE, 256], BF16)
    v_tm = kb.sb(st, "v_tm", [128, NTILE, 512], BF16)
    kT = kb.sb(st, "kT", [64, 4, T], BF16)
    qT = kb.sb(st, "qT", [64, 4, T], BF16)
    kb.load(k_tm[:], I["retK"].rearrange("(t p) f -> p t f", p=128), writes=["k_tm"])
    kb.load(v_tm[:], I["retV"].rearrange("(t p) f -> p t f", p=128), writes=["v_tm"])
    kb.load(kT[:], I["retKT"], writes=["kT"])
    w_g = kb.sb(st, "w_g", [128, 8, 1024], BF16)
    kb.load_cast(w_g[:], wview(I["w_in"], C_GF, C_NQ), writes=["w_g"])
    gng = kb.sb(st, "gng", [128, 512], F32)
    kb.load(gng[:], I["gng"].partition_broadcast(128), writes=["gng"])
    Fsl = kb.sb(st, "Fsl", [64, 8, 4, 128], F32)
    for s_ in range(4):
        kb.load(Fsl[:, :, s_, :], I["FB_out"][s_ * 64:(s_ + 1) * 64, :].rearrange("p (i v) -> p i v", i=8), writes=[("Fsl", s_)], reads=["FB_out"])
    cc = kb.sb(st, "cc", [128, 32], F32)
    kb.load(cc[:], I["cc"], writes=["cc"])
    coef = kb.sb(st, "coef", [128, 8, 5], F32)
    for d in range(2):
        fo = 12 if d == 0 else 17
        mo = 22 if d == 0 else 26
        for hh in range(4):
            i = d * 4 + hh
            E(S, "act", "activation", ["cc", "lg"], [("coefe", i)], out=coef[:, i, :], in_=cc[:, fo:fo + 5], func=AF.Exp, scale=RT["lg"][:, i:i + 1])
            E(S, "dve", "tensor_tensor", [("coefe", i), "cc"], [("coef", i)], out=coef[:, i, 0:4], in0=coef[:, i, 0:4], in1=cc[:, mo:mo + 4], op=ALU.mult)
    ya = kb.sb(st, "ya_acc", [128, NTILE, 512], F32)
    Sst = kb.sb(st, "Sst", [64, 8, 128], F32)
    hv = lambda ap: ap.rearrange("p (h f) -> p h f", h=4)
    with ExitStack() as s2:
        w_q = kb.sb(s2, "w_q", [128, 8, 256], BF16)
        kb.load_cast(w_q[:], wview(I["w_in"], C_RQ, C_RK), writes=["w_q"])
        rq = kb.sb(s2, "rq", [128, NTILE, 64], F32)
        kb.load(rq[:], I["rope_q"].rearrange("(t p) f -> p t f", p=128), writes=["rq"])
        ps_q = kb.ring_ps(s2, "ps_q", [128, 256], 2)
        ps_t = kb.ring_ps(s2, "ps_tq", [128, 128], 4, BF16)
        qf = kb.ring_sb(s2, "qf", [128, 256], F32, 3)
        qb = kb.ring_sb(s2, "qb", [128, 256], BF16, 3)
        t1 = kb.ring_sb(s2, "t1", [128, 128], F32, 2)
        t2 = kb.ring_sb(s2, "t2", [128, 128], F32, 2)
        for t in range(NTILE):
            tok = slice(t * 128, (t + 1) * 128)
            pq, pqk = ps_q.next()
            kb.mm_group(pq[:], [(hxT[:, k, tok], w_q[:, k, :]) for k in range(8)], reads=["hxT", "w_q"], writes=[pqk])
            qft, qfk = qf.next()
            E(S, "act", "copy", [pqk], [qfk], out=qft[:], in_=pq[:])
            qbt, qbk = qb.next()
            cb = rq[:, t, 0:32].unsqueeze(1).to_broadcast([128, 4, 32])
            sn = rq[:, t, 32:64].unsqueeze(1).to_broadcast([128, 4, 32])
            t1t, t1k = t1.next()
            t2t, _ = t2.next()
            rope2(S, "dve" if t % 2 == 0 else "pool", hv(qft[:])[:, :, 0:32], hv(qft[:])[:, :, 32:64], hv(qbt[:])[:, :, 0:32], hv(qbt[:])[:, :, 32:64],
                  cb, sn, hv(t1t[:]), hv(t2t[:]), [qfk, "rq"], qbk, t1k)
            for hh in range(4):
                pt, ptk = ps_t.next()
                E(S, "pe", "transpose", [qbk, "ident"], [ptk], pt[0:64, :], qbt[:, hh * 64:(hh + 1) * 64], ident[:])
                E(S, "act" if hh % 2 else "dve", "copy" if hh % 2 else "tensor_copy", [ptk], [("qT", t, hh)], out=qT[:, hh, tok], in_=pt[0:64, :])
        S.barrier(); S.emit()
    import os as _os
    if _os.environ.get("RET_STOP") == "pre":
        return
    NCH = 8
    ps_g = kb.ring_ps(st, "ps_g", [128, 512], 2)
    ps_sc = kb.ring_ps_sliced(st, "ps_sc", 128, NCH)
    ps_o = kb.ring_ps_sliced(st, "ps_o", 128, NCH)
    ps_t = kb.ring_ps(st, "ps_t", [128, 128], 1, BF16)
    sg = kb.ring_sb(st, "sg", [128, 512], F32, 2)
    sT = kb.ring_sb(st, "sT", [128, 128], BF16, NCH)
    qs = kb.ring_sb(st, "qs", [64, 128], BF16, NCH)
    Sb = kb.ring_sb(st, "Sb", [64, 128], BF16, NCH)
    kd = kb.ring_sb(st, "kd", [128, 64], BF16, NCH)
    stats = kb.ring_sb(st, "stats", [128, 6], F32, NCH)
    mv = kb.ring_sb(st, "mv", [128, 4], F32, NCH)
    yn = kb.ring_sb(st, "yn", [128, 128], F32, NCH)
    yab = kb.ring_sb(st, "yab", [128, 512], BF16, 2)
    yaT_r = kb.ring_sb(st, "yaT", [128, 4, 128], BF16, 2)
    E(S, "pool", "memset", [], [("ya", t) for t in range(NTILE)], ya[:], 0.0)
    E(S, "dve", "memset", [], [("F", i) for i in range(8)], Sst[:], 0.0)

    def do_pairs(pairs):
        ch = []
        for (d, t) in pairs:
            tok = slice(t * 128, (t + 1) * 128)
            pg, pgk = ps_g.next()
            kb.mm_group(pg[:], [(hxT[:, k, tok], w_g[:, k, d * 512:(d + 1) * 512]) for k in range(8)], reads=["hxT", "w_g"], writes=[pgk])
            sgt, sgk = sg.next()
            E(S, "act", "activation", [pgk], [sgk], out=sgt[:], in_=pg[:], func=AF.Silu)
            for hh in range(4):
                ch.append(dict(d=d, t=t, hh=hh, i=d * 4 + hh, tok=tok, sg=sgt, sgk=sgk))
        for c in ch:
            c["psc"], c["psck"] = ps_sc.next()
            kb.mm_group(c["psc"][:], [(kT[:, c["hh"], c["tok"]], qT[:, c["hh"], c["tok"]])], reads=["kT", ("qT", c["t"], c["hh"])], writes=[c["psck"]])
        for c in ch:
            i, hh, t, tok = c["i"], c["hh"], c["t"], c["tok"]
            c["sT"], c["sTk"] = sT.next()
            E(S, "dve", "tensor_tensor", [c["psck"], "DT"], [c["sTk"]], out=c["sT"][:], in0=c["psc"][:], in1=RT["DT"][:, i, :], op=ALU.mult)
            c["qs"], c["qsk"] = qs.next()
            E(S, "pool", "tensor_tensor", [("qT", t, hh), "QD"], [c["qsk"]], out=c["qs"][:], in0=qT[:, hh, tok], in1=RT["QD"][0:64, i, :], op=ALU.mult)
            c["Sb"], c["Sbk"] = Sb.next()
            E(S, "act", "copy", [("F", i)], [c["Sbk"]], out=c["Sb"][:], in_=Sst[:, i, :])
            c["kd"], c["kdk"] = kd.next()
            E(S, "act", "activation", ["k_tm", "kdec"], [c["kdk"]], out=c["kd"][:], in_=k_tm[:, t, hh * 64:(hh + 1) * 64], func=AF.Copy, scale=RT["kdec"][:, i:i + 1])
        for c in ch:
            hh, t = c["hh"], c["t"]
            c["po"], c["pok"] = ps_o.next()
            kb.mm_group(c["po"][:], [(c["sT"][:], v_tm[:, t, hh * 128:(hh + 1) * 128]), (c["qs"][:], c["Sb"][:])],
                        reads=[c["sTk"], "v_tm", c["qsk"], c["Sbk"]], writes=[c["pok"]])
            kb.mm_group(c["psc"][0:64, :], [(c["kd"][:], v_tm[:, t, hh * 128:(hh + 1) * 128])], reads=[c["kdk"], "v_tm", c["sTk"]], writes=[c["psck"]])
        for c in ch:
            i = c["i"]
            E(S, "dve", "scalar_tensor_tensor", [c["psck"], ("F", i), "cd", c["Sbk"]], [("F", i)], out=Sst[:, i, :], in0=Sst[:, i, :],
              scalar=RT["cd"][0:64, i:i + 1], in1=c["psc"][0:64, :], op0=ALU.mult, op1=ALU.add)
            c["st"], c["stk"] = stats.next()
            E(S, "dve", "bn_stats", [c["pok"]], [c["stk"]], out=c["st"][:], in_=c["po"][:])
        for c in ch:
            c["mv"], c["mvk"] = mv.next()
            E(S, "dve", "bn_aggr", [c["stk"]], [c["mvk"]], out=c["mv"][:, 0:2], in_=c["st"][:])
        for c in ch:
            E(S, "act", "activation", [c["mvk"], "cst"], [c["mvk"]], out=c["mv"][:, 2:3], in_=c["mv"][:, 1:2], func=AF.Sqrt, bias=eps_ap, scale=1.0)
        for c in ch:
            E(S, "dve", "reciprocal", [c["mvk"]], [c["mvk"]], out=c["mv"][:, 2:3], in_=c["mv"][:, 2:3])
        for c in ch:
            c["yn"], c["ynk"] = yn.next()
            E(S, "dve", "tensor_scalar", [c["pok"], c["mvk"]], [c["ynk"]], out=c["yn"][:], in0=c["po"][:], scalar1=c["mv"][:, 0:1], scalar2=c["mv"][:, 2:3],
              op0=ALU.subtract, op1=ALU.mult)
        for c in ch:
            hh, t = c["hh"], c["t"]
            yslc = ya[:, t, hh * 128:(hh + 1) * 128]
            E(S, "pool", "tensor_tensor", [c["ynk"], c["sgk"]], [c["ynk"]], out=c["yn"][:], in0=c["yn"][:], in1=c["sg"][:, hh * 128:(hh + 1) * 128], op=ALU.mult)
            E(S, "pool", "tensor_tensor", [c["ynk"], ("ya", t)], [("ya", t)], out=yslc, in0=yslc, in1=c["yn"][:], op=ALU.add)

    if _os.environ.get("RET_STOP") == "memset":
        return
    do_pairs([(0, 16), (1, 17)])
    if _os.environ.get("RET_STOP") == "one":
        return
    do_pairs([(0, 17), (1, 16)])
    for d in range(2):
        for hh in range(4):
            i = d * 4 + hh
            E(S, "dve", "tensor_scalar_mul", [("F", i), ("coef", i)], [("F", i)], out=Sst[:, i, :], in0=Sst[:, i, :], scalar1=coef[0:64, i, 4:5])
            for j in range(4):
                E(S, "dve", "scalar_tensor_tensor", [("F", i), ("coef", i), ("Fsl", j)], [("F", i)], out=Sst[:, i, :], in0=Fsl[:, i, j, :],
                  scalar=coef[0:64, i, j:j + 1], in1=Sst[:, i, :], op0=ALU.mult, op1=ALU.add)
    for k in range(16):
        do_pairs([(0, k), (1, 15 - k)])
    for t in range(NTILE):
        tok = slice(t * 128, (t + 1) * 128)
        ybt, ybk = yab.next()
        E(S, "dve", "tensor_tensor", [("ya", t), "gng"], [ybk], out=ybt[:], in0=ya[:, t, :], in1=gng[:], op=ALU.mult)
        yat, yatk = yaT_r.next()
        for kc in range(4):
            pt, ptk = ps_t.next()
            E(S, "pe", "transpose", [ybk, "ident"], [ptk], pt[:], ybt[:, kc * 128:(kc + 1) * 128], ident[:])
            E(S, "act", "copy", [ptk], [(yatk, kc)], out=yat[:, kc, :], in_=pt[:])
        kb.store(o_yaT[:, :, tok], yat[:], reads=[(yatk, kc) for kc in range(4)], writes=["yaT_d"])


class AttnPipe:
    def __init__(self, kb, S, rings, ident=None, depth=2):
        self.kb, self.S, self.rings, self.ident, self.depth = kb, S, rings, ident, depth
        self.steps = []

    def add(self, kT_list, q_ap, qkeys, v_list, out_ap, okey, extra=None):
        n = len(kT_list)
        blk = dict(q=q_ap, qk=list(qkeys), out=out_ap, okey=okey, n=n, nq=q_ap.shape[-1], po=None)
        for i in range(n):
            self.steps.append(dict(blk=blk, i=i, k=kT_list[i], v=v_list[i], ex=(extra[i] if extra is not None else None)))

    def _qk(self, st):
        ps_s = self.rings[0]
        blk = st["blk"]; nq = blk["nq"]
        ps, psk = ps_s.next()
        st["ps"], st["psk"] = ps, psk
        kl, kkeys = st["k"]
        pairs = [(kl, blk["q"])]
        rk = list(kkeys) + blk["qk"]
        if st["ex"] is not None:
            pairs.append((self.ident, st["ex"][0]))
            rk += list(st["ex"][2]) + ["ident"]
        self.kb.mm_group(ps[:, 0:nq], pairs, reads=rk, writes=[psk])

    def _exp(self, st):
        S = self.S
        _, _, e_r, p_r, _ = self.rings
        nq = st["blk"]["nq"]
        et, ek = e_r.next()
        E(S, "act", "activation", [st["psk"]], [ek], out=et[:, 0:nq], in_=st["ps"][:, 0:nq], func=AF.Exp)
        if st["ex"] is not None:
            pt, pk = p_r.next()
            E(S, "dve", "tensor_tensor", [ek] + list(st["ex"][2]), [pk], out=pt[:, 0:nq], in0=et[:, 0:nq], in1=st["ex"][1], op=ALU.mult)
            et, ek = pt, pk
        st["e"], st["ek"] = et, ek

    def _pv(self, st):
        S = self.S
        _, ps_o, _, _, rden_r = self.rings
        blk = st["blk"]; nq = blk["nq"]; i = st["i"]; n = blk["n"]
        if i == 0:
            blk["po"], blk["pok"] = ps_o.next()
        po, pok = blk["po"], blk["pok"]
        vl, vkeys = st["v"]
        et = st["e"]
        S.op("pe", lambda h: h.matmul(po[:, 0:nq], lhsT=vl, rhs=et[:, 0:nq], start=(i == 0), stop=(i == n - 1)),
             reads=[st["ek"]] + list(vkeys), writes=[pok])
        if i == n - 1:
            rd, rdk = rden_r.next()
            E(S, "dve", "reciprocal", [pok], [rdk], out=rd[:, 0:nq], in_=po[64:128, 0:nq])
            E(S, "dve", "tensor_tensor", [pok, rdk], [blk["okey"]], out=blk["out"], in0=po[0:64, 0:nq], in1=rd[:, 0:nq], op=ALU.mult)

    def run_pairs(self):
        S, kb = self.S, self.kb
        ps_s, ps_o, e_r, _, rden_r = self.rings
        st = self.steps
        assert len(st) % 2 == 0
        units = [(st[2 * u], st[2 * u + 1]) for u in range(len(st) // 2)]

        def qk(u):
            a, b = units[u]
            assert a["blk"] is b["blk"]
            ps, psk = ps_s.next()
            nq = a["blk"]["nq"]
            for half, s_ in enumerate((a, b)):
                kl, kkeys = s_["k"]
                kb.mm_group(ps[:, half * 512:half * 512 + nq], [(kl, s_["blk"]["q"])], reads=list(kkeys) + s_["blk"]["qk"], writes=[psk])
            a["ps"], a["psk"] = ps, psk

        def ex(u):
            a, b = units[u]
            nq = a["blk"]["nq"]
            et, ek = e_r.next()
            if nq == 512:
                E(S, "act", "activation", [a["psk"]], [ek], out=et[:, 0:1024], in_=a["ps"][:, 0:1024], func=AF.Exp)
            else:
                for half in range(2):
                    E(S, "act", "activation", [a["psk"]], [ek], out=et[:, half * 512:half * 512 + nq], in_=a["ps"][:, half * 512:half * 512 + nq], func=AF.Exp)
            a["e"], a["ek"] = et, ek

        def pv(u):
            a, b = units[u]
            et, ek = a["e"], a["ek"]
            for half, s_ in enumerate((a, b)):
                s_["e"], s_["ek"] = et[:, half * 512:(half + 1) * 512], ek
                self._pv(s_)

        qk(0)
        for u in range(len(units)):
            ex(u)
            if u + 1 < len(units):
                qk(u + 1)
            pv(u)
        self.steps = []

    def run(self):
        st = self.steps
        for j in range(min(self.depth, len(st))):
            self._qk(st[j])
        for j in range(len(st)):
            self._exp(st[j])
            if j + self.depth < len(st):
                self._qk(st[j + self.depth])
            self._pv(st[j])
        self.steps = []


def attn_rings_pairs(kb, st, tag):
    return (kb.ring_ps(st, "ps_s" + tag, [128, 1024], 2), kb.ring_ps(st, "ps_o" + tag, [128, 512], 2),
            kb.ring_sb(st, "e_r" + tag, [128, 1024], BF16, 3), None,
            kb.ring_sb(st, "rden" + tag, [64, 512], F32, 2))


def attn_rings(kb, st, tag, n_s=4):
    return (kb.ring_ps(st, "ps_s" + tag, [128, 512], n_s), kb.ring_ps(st, "ps_o" + tag, [128, 512], 2),
            kb.ring_sb(st, "e_r" + tag, [128, 512], BF16, 4), kb.ring_sb(st, "p_r" + tag, [128, 512], BF16, 3),
            kb.ring_sb(st, "rden" + tag, [64, 512], F32, 2))


def emit_na(kb, st, I, o_ybT, modpre=None):
    S = kb.S
    ident, _ = make_ident(kb, st)
    qT = kb.sb(st, "naqT", [128, 4, T], BF16)
    with ExitStack() as s2:
        hxT = kb.sb(s2, "hxT", [128, 8, T], BF16)
        kb.load(hxT[:], I["hxT"], writes=["hxT"])
        w_nq = kb.sb(s2, "w_nq", [128, 8, 512], BF16)
        kb.load_cast(w_nq[:], wview(I["w_in"], C_NQ, C_NK), writes=["w_nq"])
        ps_a = kb.ring_ps(s2, "ps_a", [128, 512], 2)
        for hp in range(4):
            for (b0, bn) in TOKBLKS:
                pa, pak = ps_a.next()
                kb.mm_group(pa[:, 0:bn], [(w_nq[:, k, hp * 128:(hp + 1) * 128], hxT[:, k, b0:b0 + bn]) for k in range(8)],
                            reads=["hxT", "w_nq"], writes=[pak])
                E(S, "act", "activation", [pak], [("naqT", hp, b0)], out=qT[:, hp, b0:b0 + bn], in_=pa[:, 0:bn], func=AF.Copy, scale=0.125)
        S.barrier(); S.emit()
    kT = kb.sb(st, "nakT", [128, 4, NKEY], BF16)
    V = kb.sb(st, "naV", [128, NKEY // 128, 8, 128], BF16)
    kb.load(kT[:, :, 256:2304], I["naKT"][:, :, 0:NT], writes=[("nakT", "own")])
    kb.load(kT[:, :, 2560:NKEY], I["naKT"][:, :, NT:T], writes=[("nakT", "ctx")])
    nvv = I["naV"].rearrange("(u p) h f -> p u h f", p=128)
    kb.load(V[:, 2:18], nvv[:, 0:16], writes=[("naV", "own")])
    kb.load(V[:, 20:22], nvv[:, 16:18], writes=[("naV", "ctx")])
    cc = kb.sb(st, "cc", [128, 32], F32)
    kb.load(cc[:], I["cc"], writes=["cc"])
    xbo = [g_.rearrange("(s p) c -> p s c", p=128) for g_ in I["XB_out"]]
    with ExitStack() as s3:
        hal = kb.sb(s3, "hal", [128, 4, 6144], BF16)
        kb.load(hal[:, :, 0:2048], xbo[1][:, :, 2048:4096], writes=[("hal", 0)], reads=[("XB_out", 1)])
        kb.load(hal[:, :, 2048:6144], xbo[2][:, :, 0:4096], writes=[("hal", 1)], reads=[("XB_out", 2)])
        E(S, "dve", "tensor_copy", [("hal", 0), ("hal", 1)], ["hal"], out=hal[0:1, 0, 0:1], in_=hal[0:1, 0, 0:1])
        kt_top = kT[:, :, 0:256]
        kt_bot = kT[:, :, 2304:2560]
        v_top = V[:, 0:2].rearrange("p u h f -> p u (h f)")
        v_bot = V[:, 18:20].rearrange("p u h f -> p u (h f)")
        for s_ in range(4):
            srcs = [(kt_top, hal[:, s_, 1024:2048].rearrange("p (a t) -> p a t", a=4), 4 + s_, "dve"),
                    (kt_bot, hal[:, s_, 0:1024].rearrange("p (a t) -> p a t", a=4), 8 + s_, "pool"),
                    (v_top, hal[:, s_, 4096:6144].rearrange("p (u f) -> p u f", u=2), 4 + s_, "dve"),
                    (v_bot, hal[:, s_, 2048:4096].rearrange("p (u f) -> p u f", u=2), 8 + s_, "pool")]
            for j, (dst, src, col, eng) in enumerate(srcs):
                if s_ == 0:
                    E(S, eng, "tensor_scalar_mul", ["hal", "cc"], [("halo", j)], out=dst, in0=src, scalar1=cc[:, col:col + 1])
                else:
                    E(S, "dve", "scalar_tensor_tensor", ["hal", "cc", ("halo", j)], [("halo", j)], out=dst, in0=src, scalar=cc[:, col:col + 1], in1=dst,
                      op0=ALU.mult, op1=ALU.add)
        S.barrier(); S.emit()
    E(S, "dve", "tensor_copy", [("nakT", "own"), ("nakT", "ctx"), ("halo", 0), ("halo", 1)], ["nakT"], out=kT[0:1, 0, 0:1], in_=kT[0:1, 0, 0:1])
    E(S, "dve", "tensor_copy", [("naV", "own"), ("naV", "ctx"), ("halo", 2), ("halo", 3)], ["naV"], out=V[0:1, 0, 0, 0:1], in_=V[0:1, 0, 0, 0:1])
    bias = kb.sb(st, "nabias", [128, 8, 24 * 64], BF16)
    kb.load_cast(bias[:], I["nabias"].rearrange("p h e c -> p h (e c)"), writes=["nabias"])
    mask = kb.sb(st, "namask", [128, 32, 512], BF16)
    kb.load(mask[:], I["namask"], writes=["namask"])
    ybT = kb.sb(st, "ybT", [128, 4, T], BF16)
    rings = attn_rings(kb, st, "na")
    pipe = AttnPipe(kb, S, rings, ident=ident[:])
    mp_ = modpre(kb, st) if modpre is not None else None
    for h in range(8):
        hp, hs = h // 2, (h % 2) * 64
        for b in range(5):
            b0, bn = TOKBLKS[b]
            qa = qT[hs:hs + 64, hp, b0:b0 + bn]
            qk = [("naqT", hp, b0)]
            kl, vl, ex = [], [], []
            if b < 4:
                for t in range(8):
                    u = 4 * b + t
                    e0 = (4 - 2 * t) + 10
                    kl.append((kT[hs:hs + 64, hp, u * 128:(u + 1) * 128], ["nakT"]))
                    vl.append((V[:, u, h, :], ["naV"]))
                    ex.append((bias[:, h, e0 * 64:(e0 + 8) * 64], mask[:, b * 8 + t, :], ["nabias", "namask"]))
            for u in (20, 21):
                kl.append((kT[hs:hs + 64, hp, u * 128:(u + 1) * 128], ["nakT"]))
                vl.append((V[:, u, h, :], ["naV"]))
                ex.append(None)
            pipe.add(kl, qa, qk, vl, ybT[hs:hs + 64, hp, b0:b0 + bn], ("ybT", h, b), extra=ex)
        if mp_ is not None:
            mp_.step()
            pipe.run()
    pipe.run()
    if mp_ is not None:
        mp_.finish()
    kb.store(o_ybT, ybT[:], reads=[("ybT", h, b) for h in range(8) for b in range(5)], final=True)


class ModPrefetch:
    NB = 16
    BW = 384

    def __init__(self, kb, st, c2, w_ada, b_ada, out_dram):
        S = self.S = kb.S
        self.kb, self.w_ada, self.out = kb, w_ada, out_dram
        self.c2s = kb.sb(st, "pc2s", [128, 16], F32)
        self.bad = kb.sb(st, "pbad", [128, 48], F32)
        self.wa = kb.ring_sb(st, "pwa", [128, 8, self.BW], F32, 2)
        self.mrow = kb.ring_sb(st, "pmrow", [2, self.BW], F32, 2)
        self.mods = kb.sb(st, "pmods", [128, 48, 2], F32)
        self.idf = kb.sb(st, "pidf", [128, 128], F32)
        self.mps = kb.ps(st, "pmod_ps", [128, 48, 2])
        self.mrp = kb.ring_ps(st, "pmrow_ps", [2, self.BW], 1)
        kb.load(self.c2s[:], c2, writes=["pc2s"])
        kb.load(self.bad[:], b_ada, writes=["pbad"])
        E(S, "act", "activation", ["pc2s"], ["pc2s"], out=self.c2s[:], in_=self.c2s[:], func=AF.Silu)
        E(S, "pool", "memset", [], ["pidf"], self.idf[:], 0.0)
        E(S, "pool", "affine_select", ["pidf"], ["pidf"], out=self.idf[:], in_=self.idf[:], pattern=[[-1, 128]],
          compare_op=ALU.not_equal, fill=1.0, base=0, channel_multiplier=1)
        self.c2v = self.c2s[:].rearrange("p (k w) -> p k w", w=2)
        self.pending = []
        self.nxt = 0

    def _compute(self, blk, wt, wk):
        S, kb = self.S, self.kb
        mp, mpk = self.mrp.next()
        kb.mm_group(mp[:, :], [(self.c2v[:, k, :], wt[:, k, :]) for k in range(8)], reads=[wk, "pc2s"], writes=[mpk])
        mr, mrk = self.mrow.next()
        E(S, "dve", "tensor_copy", [mpk], [mrk], out=mr[:], in_=mp[:, :])
        for jj in range(self.BW // 128):
            j = blk * (self.BW // 128) + jj
            E(S, "pe", "transpose", [mrk, "pidf"], [("pmps", j)], self.mps[:, j, :], mr[:, jj * 128:(jj + 1) * 128], self.idf[0:2, 0:2])

    def step(self):
        for (blk, wt, wk) in self.pending:
            self._compute(blk, wt, wk)
        self.pending = []
        for _ in range(2):
            if self.nxt < self.NB:
                blk = self.nxt
                self.nxt += 1
                wt, wk = self.wa.next()
                self.kb.load(wt[:], wview(self.w_ada, blk * self.BW, (blk + 1) * self.BW), writes=[wk])
                self.pending.append((blk, wt, wk))

    def finish(self):
        while self.pending or self.nxt < self.NB:
            self.step()
        S = self.S
        for w in range(2):
            E(S, "dve", "tensor_tensor", [("pmps", j) for j in range(48)] + ["pbad"], ["pmods"], out=self.mods[:, :, w], in0=self.mps[:, :, w],
              in1=self.bad[:], op=ALU.add)
        self.kb.store(self.out, self.mods[:].rearrange("p a b -> p (a b)"), reads=["pmods"])


def emit_mla(kb, st, I, o_ycT, modpre=None):
    S = kb.S
    ident, _ = make_ident(kb, st)
    qTm = kb.sb(st, "qTm", [96, 8, T], BF16)
    cst = kb.sb(st, "mcst", [128, 8], F32)
    kb.load(cst[:], I["rcst"][:, 768:776], writes=["mcst"])
    eps_ap = cst[:, 5:6]
    with ExitStack() as s2:
        cqnT = kb.sb(s2, "cqnT", [128, 2, T], BF16)
        with ExitStack() as s3:
            hxT = kb.sb(s3, "hxT", [128, 8, T], BF16)
            kb.load(hxT[:], I["hxT"], writes=["hxT"])
            w_mq = kb.sb(s3, "w_mq", [128, 8, 256], BF16)
            kb.load_cast(w_mq[:], wview(I["w_in"], C_MQ, C_MKV), writes=["w_mq"])
            qg = kb.sb(s3, "qg", [128, 256], F32)
            kb.load(qg[:], I["qg"].partition_broadcast(128), writes=["qg"])
            GA = 3
            ps_a = kb.ring_ps(s3, "ps_a", [128, 256], GA)
            ps_t = kb.ring_ps(s3, "ps_t", [128, 128], 4, BF16)
            ss = kb.ring_sb(s3, "ss", [128, 2], F32, 2 * GA)
            junk = kb.sb(s3, "junk", [128, 256], F32)
            cn_r = kb.ring_sb(s3, "cn", [128, 256], BF16, 2 * GA)
            for g0 in range(0, NTILE, GA):
                grp = []
                for t in range(g0, min(g0 + GA, NTILE)):
                    tok = slice(t * 128, (t + 1) * 128)
                    pa, pak = ps_a.next()
                    kb.mm_group(pa[:, 0:256], [(hxT[:, k, tok], w_mq[:, k, :]) for k in range(8)], reads=["hxT", "w_mq"], writes=[pak])
                    sst, ssk = ss.next()
                    cn, cnk = cn_r.next()
                    grp.append(dict(t=t, tok=tok, pa=pa, pak=pak, sst=sst, ssk=ssk, cn=cn, cnk=cnk))
                for c in grp:
                    E(S, "act", "activation", [c["pak"]], ["junk", c["ssk"]], out=junk[:], in_=c["pa"][:, 0:256], func=AF.Square, accum_out=c["sst"][:, 0:1])
                for c in grp:
                    E(S, "act", "activation", [c["ssk"], "mcst"], [c["ssk"]], out=c["sst"][:, 1:2], in_=c["sst"][:, 0:1], func=AF.Sqrt, scale=1.0 / 256.0, bias=eps_ap)
                for c in grp:
                    E(S, "dve", "reciprocal", [c["ssk"]], [c["ssk"]], out=c["sst"][:, 1:2], in_=c["sst"][:, 1:2])
                for c in grp:
                    E(S, "dve", "scalar_tensor_tensor", [c["pak"], c["ssk"], "qg"], [c["cnk"]], out=c["cn"][:], in0=c["pa"][:, 0:256], scalar=c["sst"][:, 1:2], in1=qg[:],
                      op0=ALU.mult, op1=ALU.mult)
                pts = []
                for c in grp:
                    for kc in range(2):
                        pt, ptk = ps_t.next()
                        E(S, "pe", "transpose", [c["cnk"], "ident"], [ptk], pt[:], c["cn"][:, kc * 128:(kc + 1) * 128], ident[:])
                        E(S, "dve" if kc == 0 else "act", "tensor_copy" if kc == 0 else "copy", [ptk], [("cqnT", c["t"])], out=cqnT[:, kc, c["tok"]], in_=pt[:])
            S.barrier(); S.emit()
        w_qup = kb.sb(s2, "w_qup", [128, 2, 768], BF16)
        kb.load_cast(w_qup[:], wview(I["w_qup"], 0, 768), writes=["w_qup"])
        rmq = kb.sb(s2, "rmq", [128, NTILE, 32], F32)
        kb.load(rmq[:], I["rope_mq"].rearrange("(t p) f -> p t f", p=128), writes=["rmq"])
        ps_a = kb.ring_ps(s2, "ps_qa", [128, 512], 2)
        ps_b = kb.ring_ps(s2, "ps_qb", [128, 256], 2)
        ps_t = kb.ring_ps(s2, "ps_qt", [128, 128], 4, BF16)
        qf_r = kb.ring_sb(s2, "mqf", [128, 768], F32, 4)
        qb_r = kb.ring_sb(s2, "mqb", [128, 768], BF16, 4)
        u1_r = kb.ring_sb(s2, "mu1", [128, 8, 16], F32, 2)
        u2_r = kb.ring_sb(s2, "mu2", [128, 8, 16], F32, 2)
        scl = float(96.0 ** -0.5)
        for g0 in range(0, NTILE, 2):
            grp = []
            for t in range(g0, min(g0 + 2, NTILE)):
                tok = slice(t * 128, (t + 1) * 128)
                pa, pak = ps_a.next()
                pb, pbk = ps_b.next()
                kb.mm_group(pa[:, 0:512], [(cqnT[:, kc, tok], w_qup[:, kc, 0:512]) for kc in range(2)], reads=[("cqnT", t), "w_qup"], writes=[pak])
                kb.mm_group(pb[:, 0:256], [(cqnT[:, kc, tok], w_qup[:, kc, 512:768]) for kc in range(2)], reads=[("cqnT", t), "w_qup"], writes=[pbk])
                qf, qfk = qf_r.next()
                qb, qbk = qb_r.next()
                grp.append(dict(t=t, tok=tok, pa=pa, pak=pak, pb=pb, pbk=pbk, qf=qf, qfk=qfk, qb=qb, qbk=qbk))
            for c in grp:
                E(S, "act", "copy", [c["pak"]], [(c["qfk"], 0)], out=c["qf"][:, 0:512], in_=c["pa"][:, 0:512])
                E(S, "act", "copy", [c["pbk"], (c["qfk"], 0)], [c["qfk"]], out=c["qf"][:, 512:768], in_=c["pb"][:, 0:256])
            for c in grp:
                qf3 = c["qf"][:].rearrange("p (h f) -> p h f", h=8)
                qb3 = c["qb"][:].rearrange("p (h f) -> p h f", h=8)
                E(S, "dve", "tensor_scalar_mul", [c["qfk"]], [(c["qbk"], "n")], out=qb3[:, :, 0:64], in0=qf3[:, :, 0:64], scalar1=scl)
                cb = rmq[:, c["t"], 0:16].unsqueeze(1).to_broadcast([128, 8, 16])
                sn = rmq[:, c["t"], 16:32].unsqueeze(1).to_broadcast([128, 8, 16])
                u1, u1k = u1_r.next()
                u2, _ = u2_r.next()
                rope2(S, "pool" if c["t"] % 2 == 0 else "dve", qf3[:, :, 64:80], qf3[:, :, 80:96], qb3[:, :, 64:80], qb3[:, :, 80:96], cb, sn, u1[:], u2[:],
                      [c["qfk"], "rmq", (c["qbk"], "n")], c["qbk"], u1k)
            for c in grp:
                for h in range(8):
                    pt, ptk = ps_t.next()
                    E(S, "pe", "transpose", [c["qbk"], "ident"], [ptk], pt[0:96, :], c["qb"][:, h * 96:(h + 1) * 96], ident[:])
                    E(S, "dve" if h % 2 == 0 else "act", "tensor_copy" if h % 2 == 0 else "copy", [ptk], [("qTm", c["t"])], out=qTm[:, h, c["tok"]], in_=pt[0:96, :])
        S.barrier(); S.emit()
    ck = kb.sb(st, "ckvnTa", [128, 2, NKM], BF16)
    xbo = [g_.rearrange("(s p) c -> p s c", p=128) for g_ in I["XB_out"]]
    for s_ in range(4):
        kb.load(ck[:, :, s_ * NT:(s_ + 1) * NT], xbo[0][:, s_, 0:4096].rearrange("p (k t) -> p k t", k=2), writes=[("ck", s_)], reads=[("XB_out", 0)])
    kb.load(ck[:, :, 4 * NT:NKM], I["ckvnT"][:, :, NT:T], writes=[("ck", 4)])
    E(S, "dve", "tensor_copy", [("ck", s_) for s_ in range(5)], ["ck"], out=ck[0:1, 0, 0:1], in_=ck[0:1, 0, 0:1])
    w_kv = kb.sb(st, "w_kvup", [128, 2, 1024], BF16)
    kb.load_cast(w_kv[:], wview(I["w_kvup"], 0, 1024), writes=["w_kv"])
    KT = kb.sb(st, "mKT", [96, 2, NKM], BF16)
    VA = kb.sb(st, "mVA", [128, NKM // 128, 2, 128], BF16)
    ycT = kb.sb(st, "ycT", [128, 4, T], BF16)
    E(S, "pool", "memset", [], ["mVA"], VA[:], 1.0)
    mp_ = modpre(kb, st) if modpre is not None else None
    ps_k = kb.ring_ps(st, "ps_k", [128, 512], 1 if mp_ is not None else (2 if MLA_PAIRS else 3))
    rings = attn_rings_pairs(kb, st, "ml") if (MLA_PAIRS and mp_ is None) else attn_rings(kb, st, "ml", n_s=3)
    pipe = AttnPipe(kb, S, rings)
    NU = NKM // 128
    for hp in range(4):
        for hh in range(2):
            h = hp * 2 + hh
            for s_ in range(4):
                kb.load(KT[64:96, hh, s_ * NT:(s_ + 1) * NT], xbo[1][0:32, s_, 0:2048], writes=[("mKTr", hh, s_)], reads=[("XB_out", 1)])
            kb.load(KT[64:96, hh, 4 * NT:NKM], I["krT"][:, NT:T], writes=[("mKTr", hh, 4)])
            for c0 in range(0, NKM, 512):
                cn = min(512, NKM - c0)
                pk, pkk = ps_k.next()
                kb.mm_group(pk[0:64, 0:cn], [(w_kv[:, kc, h * 128:h * 128 + 64], ck[:, kc, c0:c0 + cn]) for kc in range(2)],
                            reads=["ck", "w_kv"], writes=[pkk])
                E(S, "act" if (c0 // 512) % 4 == 3 else "dve", "copy" if (c0 // 512) % 4 == 3 else "tensor_copy", [pkk], [("mKT", hh, c0)],
                  out=KT[0:64, hh, c0:c0 + cn], in_=pk[0:64, 0:cn])
        for u in range(NU):
            pk, pkk = ps_k.next()
            wv = w_kv[:, :, hp * 256:(hp + 1) * 256].rearrange("p k (h f) -> p k h f", h=2)[:, :, :, 64:128]
            kb.mm_group(pk[:, 0:128].rearrange("p (h f) -> p h f", h=2), [(ck[:, kc, u * 128:(u + 1) * 128], wv[:, kc, :, :]) for kc in range(2)],
                        reads=["ck", "w_kv"], writes=[pkk])
            E(S, "act" if u % 4 == 3 else "dve", "copy" if u % 4 == 3 else "tensor_copy", [pkk, "mVA"], [("mVAu", u)],
              out=VA[:, u, :, 0:64], in_=pk[:, 0:128].rearrange("p (h f) -> p h f", h=2))
        for hh in range(2):
            h = hp * 2 + hh
            for b in range(5):
                b0, bn = TOKBLKS[b]
                us = list(range(NU)) if b < 4 else [NU - 2, NU - 1]
                kl = [(KT[:, hh, u * 128:(u + 1) * 128], [("mKT", hh, (u // 4) * 512), ("mKTr", hh, u // 16)]) for u in us]
                vl = [(VA[:, u, hh, :], [("mVAu", u)]) for u in us]
                pipe.add(kl, qTm[:, h, b0:b0 + bn], [("qTm", t) for t in range(b0 // 128, (b0 + bn) // 128)], vl,
                         ycT[hh * 64:hh * 64 + 64, hp, b0:b0 + bn], ("ycT", h, b))
            if mp_ is not None:
                mp_.step()
            if MLA_PAIRS and mp_ is None:
                pipe.run_pairs()
            else:
                pipe.run()
    if mp_ is not None:
        mp_.finish()
    kb.store(o_ycT, ycT[:], reads=[("ycT", h, b) for h in range(8) for b in range(5)], final=True)


def emit_ln(kb, S, x, xkeys, n, ones, lng, gcol, cst, ps_ln, sq_r, lnt, okeys, out=None):
    p1, p1k = ps_ln.next()
    p2, p2k = ps_ln.next()
    kb.mm_group(p1[:, 0:n], [(ones[:], x[:, oc, :]) for oc in range(8)], reads=list(xkeys) + ["ones"], writes=[p1k])
    for oc in range(8):
        sq, sqk = sq_r.next()
        E(S, "act", "activation", [xkeys[oc]], [sqk], out=sq[:, 0:n], in_=x[:, oc, :], func=AF.Square)
        S.op("pe", lambda h, sq=sq, oc=oc, p2=p2: h.matmul(p2[:, 0:n], lhsT=ones[:], rhs=sq[:, 0:n], start=(oc == 0), stop=(oc == 7)),
             reads=[sqk, "ones"], writes=[p2k])
    mean, msq, rstd = lnt
    E(S, "act", "activation", [p1k], ["ln_mean"], out=mean[:, 0:n], in_=p1[:, 0:n], func=AF.Copy, scale=1.0 / 1024.0)
    E(S, "pool", "tensor_tensor", ["ln_mean"], ["ln_msq"], out=msq[:, 0:n], in0=mean[:, 0:n], in1=mean[:, 0:n], op=ALU.mult)
    E(S, "dve", "scalar_tensor_tensor", [p2k, "ln_msq"], ["ln_rstd"], out=rstd[:, 0:n], in0=p2[:, 0:n], scalar=1.0 / 1024.0, in1=msq[:, 0:n],
      op0=ALU.mult, op1=ALU.subtract)
    E(S, "act", "activation", ["ln_rstd", "mcst"], ["ln_rstd"], out=rstd[:, 0:n], in_=rstd[:, 0:n], func=AF.Sqrt, bias=cst[:, 5:6], scale=1.0)
    E(S, "dve", "reciprocal", ["ln_rstd"], ["ln_rstd"], out=rstd[:, 0:n], in_=rstd[:, 0:n])
    for oc in range(8):
        eng = "dve" if oc % 2 == 0 else "pool"
        o = x[:, oc, :] if out is None else out[:, oc, :]
        E(S, eng, "tensor_tensor", [xkeys[oc], "ln_mean"], [xkeys[oc]], out=x[:, oc, :], in0=x[:, oc, :], in1=mean[:, 0:n], op=ALU.subtract)
        E(S, eng, "tensor_tensor", [xkeys[oc], "ln_rstd"], [xkeys[oc]], out=x[:, oc, :], in0=x[:, oc, :], in1=rstd[:, 0:n], op=ALU.mult)
        E(S, eng, "tensor_scalar", [xkeys[oc], "lng"], [okeys[oc]], out=o, in0=x[:, oc, :], scalar1=lng[:, gcol + oc:gcol + oc + 1],
          scalar2=lng[:, gcol + 8 + oc:gcol + 9 + oc], op0=ALU.mult, op1=ALU.add)


def ln_common(kb, st, I):
    S = kb.S
    ones = kb.sb(st, "ones", [128, 128], F32)
    E(S, "pool", "memset", [], ["ones"], ones[:], 1.0)
    lng = kb.sb(st, "lng", [128, 32], F32)
    kb.load(lng[:], I["lng"], writes=["lng"])
    cst = kb.sb(st, "mcst", [128, 8], F32)
    kb.load(cst[:], I["rcst"][:, 768:776], writes=["mcst"])
    mods = kb.sb(st, "mods", [128, 48, 2], F32)
    kb.load(mods[:].rearrange("p a b -> p (a b)"), I["mods"], writes=["mods"])
    return ones, lng, cst, mods


def emit_merge(kb, st, I, dbg):
    S = kb.S
    ones, lng, cst, mods = ln_common(kb, st, I)
    w_gt = kb.sb(st, "w_gt", [128, 8, 3072], BF16)
    w_br = kb.sb(st, "w_br", [128, 3, 4, 1024], BF16)
    w_out = kb.sb(st, "w_out", [128, 8, 1024], BF16)
    for b in range(3):
        kb.load_cast(w_gt[:, :, b * 1024:(b + 1) * 1024], wview(I["w_in"], C_GA + b * 1024, C_GA + (b + 1) * 1024), writes=[("w_gt", b)])
        kb.load_cast(w_br[:, b, :, :], I["w_br"][b].rearrange("(k p) n -> p k n", p=128), writes=[("w_br", b)])
    kb.load_cast(w_out[:], wview(I["w_out"], 0, 1024), writes=["w_out"])
    wkeys = [("w_gt", b) for b in range(3)] + [("w_br", b) for b in range(3)]
    hx_r = kb.ring_sb(st, "hxb", [128, 8, 512], BF16, 2)
    y_r = [kb.ring_sb(st, f"yb{b}", [128, 4, 512], BF16, 2) for b in range(3)]
    x_r = kb.ring_sb(st, "xb", [128, 8, 512], F32, 1)
    yT = kb.sb(st, "yTm", [128, 8, 512], BF16)
    x1 = kb.ring_sb(st, "x1", [128, 8, 512], F32, 1)
    ps_g = kb.ring_ps(st, "ps_g", [128, 512], 3)
    ps_b = kb.ring_ps(st, "ps_b", [128, 512], 3)
    ps_ln = kb.ring_ps(st, "ps_ln", [128, 512], 2)
    sg_r = kb.ring_sb(st, "sgm", [128, 512], F32, 3)
    acc_r = kb.ring_sb(st, "accm", [128, 512], F32, 2)
    tmp_r = kb.ring_sb(st, "tmpm", [128, 512], F32, 2)
    sq_r = kb.ring_sb(st, "sqm", [128, 512], F32, 2)
    lnt = (kb.sb(st, "ln_mean", [128, 512], F32), kb.sb(st, "ln_msq", [128, 512], F32), kb.sb(st, "ln_rstd", [128, 512], F32))
    srcs = [dbg["yaT"], dbg["ybT"], dbg["ycT"]]
    xv = I["xT"].rearrange("(j p) t -> p j t", p=128)
    ctx = {}

    def stage_A(b0, bn):
        hx, hxk = hx_r.next()
        kb.load(hx[:, :, 0:bn], I["hxT"][:, :, b0:b0 + bn], writes=[hxk])
        ys = []
        for b in range(3):
            yt, yk = y_r[b].next()
            kb.load(yt[:, :, 0:bn], srcs[b][:, :, b0:b0 + bn], writes=[yk])
            ys.append((yt, yk))
        xt, xk = x_r.next()
        kb.load(xt[:, :, 0:bn], xv[:, :, b0:b0 + bn], writes=[xk])
        for oc in range(8):
            ocs = slice(oc * 128, (oc + 1) * 128)
            acc, acck = acc_r.next()
            for b in range(3):
                pg, pgk = ps_g.next()
                kb.mm_group(pg[:, 0:bn], [(w_gt[:, k, b * 1024 + oc * 128:b * 1024 + (oc + 1) * 128], hx[:, k, 0:bn]) for k in range(8)],
                            reads=[hxk, ("w_gt", b)], writes=[pgk])
                pb, pbk = ps_b.next()
                kb.mm_group(pb[:, 0:bn], [(w_br[:, b, k, ocs], ys[b][0][:, k, 0:bn]) for k in range(4)], reads=[ys[b][1], ("w_br", b)], writes=[pbk])
                sgt, sgk = sg_r.next()
                E(S, "act", "activation", [pgk], [sgk], out=sgt[:, 0:bn], in_=pg[:, 0:bn], func=AF.Sigmoid)
                if b == 0:
                    E(S, "dve", "tensor_tensor", [sgk, pbk], [acck], out=acc[:, 0:bn], in0=pb[:, 0:bn], in1=sgt[:, 0:bn], op=ALU.mult)
                else:
                    E(S, "dve", "tensor_tensor", [sgk, pbk], [sgk], out=sgt[:, 0:bn], in0=pb[:, 0:bn], in1=sgt[:, 0:bn], op=ALU.mult)
                    if b == 1:
                        E(S, "pool", "tensor_tensor", [sgk, acck], [acck], out=acc[:, 0:bn], in0=acc[:, 0:bn], in1=sgt[:, 0:bn], op=ALU.add)
                    else:
                        E(S, "pool", "tensor_tensor", [sgk, acck], [("yTm", oc)], out=yT[:, oc, 0:bn], in0=acc[:, 0:bn], in1=sgt[:, 0:bn], op=ALU.add)
        ctx[b0] = dict(bn=bn, w=(0 if b0 < NT else 1), xt=xt, xk=xk)

    def stage_B1(b0):
        c = ctx[b0]
        bn, w, xt, xk = c["bn"], c["w"], c["xt"], c["xk"]
        x1t, x1k = x1.next()
        xkeys = [(x1k, oc) for oc in range(8)]
        for oc in range(8):
            ocs = slice(oc * 128, (oc + 1) * 128)
            pm, pmk = ps_g.next()
            kb.mm_group(pm[:, 0:bn], [(w_out[:, k, ocs], yT[:, k, 0:bn]) for k in range(8)], reads=[("yTm", k) for k in range(8)] + ["w_out"], writes=[pmk])
            tmp, tmpk = tmp_r.next()
            E(S, "act", "activation", [pmk, "mods"], [tmpk], out=tmp[:, 0:bn], in_=pm[:, 0:bn], func=AF.Copy, scale=mods[:, 16 + oc, w:w + 1])
            E(S, "dve", "scalar_tensor_tensor", [tmpk, xk], [xkeys[oc]], out=x1t[:, oc, 0:bn], in0=xt[:, oc, 0:bn], scalar=float(ALPHA), in1=tmp[:, 0:bn],
              op0=ALU.mult, op1=ALU.add)
        c["x1t"], c["xkeys"] = x1t, xkeys

    def stage_B2(b0):
        c = ctx.pop(b0)
        bn = c["bn"]
        emit_ln(kb, S, c["x1t"][:, :, 0:bn], c["xkeys"], bn, ones, lng, 0, cst, ps_ln, sq_r, lnt, c["xkeys"])
        kb.store(dbg["x1T"][:, :, b0:b0 + bn], c["x1t"][:, :, 0:bn], reads=c["xkeys"], writes=[("x1T_d", b0)], final=False)

    nblk = len(TOKBLKS)
    stage_A(*TOKBLKS[0])
    for i in range(nblk):
        stage_B1(TOKBLKS[i][0])
        if i + 1 < nblk:
            stage_A(*TOKBLKS[i + 1])
        stage_B2(TOKBLKS[i][0])


def emit_ffn(kb, st, I, x1T_d, xoT, last=True):
    S = kb.S
    ones, lng, cst, mods = ln_common(kb, st, I)
    sc2p = kb.sb(st, "sc2p", [128, 8, 2], F32)
    E(S, "dve", "tensor_scalar_add", ["mods"], ["sc2p"], out=sc2p[:], in0=mods[:, 32:40, :], scalar1=1.0)
    w1 = kb.sb(st, "w_ff1", [128, 8, 4096], BF16)
    w2 = kb.sb(st, "w_ff2", [128, 32, 1024], BF16)
    for c in range(4):
        kb.load_cast(w1[:, :, c * 1024:(c + 1) * 1024], wview(I["w_ff1"], c * 1024, (c + 1) * 1024), writes=[("w1", c)])
    for c in range(4):
        kb.load_cast(w2[:, c * 8:(c + 1) * 8, :], I["w_ff2"].rearrange("(k p) n -> p k n", p=128)[:, c * 8:(c + 1) * 8, :], writes=[("w2", c)])
    w1k = [("w1", c) for c in range(4)]
    w2k = [("w2", c) for c in range(4)]
    n = FFBLK
    x_r = kb.ring_sb(st, "xf", [128, 8, n], F32, 3)
    h_r = kb.ring_sb(st, "hf", [128, 8, n], BF16, 2)
    a_r = kb.ring_sb(st, "af", [128, 32, n], BF16, 2)
    r_r = kb.ring_sb(st, "rf", [128, n], F32, 3)
    ps_f = kb.ring_ps(st, "ps_f", [128, n], 4)
    ps_ln = kb.ring_ps(st, "ps_lnf", [128, n], 2)
    tmp_r = kb.ring_sb(st, "tmpf", [128, n], F32, 2)
    sq_r = kb.ring_sb(st, "sqf", [128, n], F32, 2)
    lnt = (kb.sb(st, "ln_mean", [128, n], F32), kb.sb(st, "ln_msq", [128, n], F32), kb.sb(st, "ln_rstd", [128, n], F32))
    xov = xoT.rearrange("(j p) t -> p j t", p=128)
    blocks = list(range(0, T, n))
    ctx = {}

    def stage_A(b0):
        w = 0 if b0 < NT else 1
        xt, xk = x_r.next()
        kb.load(xt[:], x1T_d[:, :, b0:b0 + n], writes=[xk], reads=[("x1T_d", (b0 // 512) * 512)])
        ht, hk = h_r.next()
        for j in range(8):
            E(S, "dve" if j % 2 == 0 else "pool", "tensor_scalar", [xk, "sc2p", "mods"], [(hk, j)], out=ht[:, j, :], in0=xt[:, j, :],
              scalar1=sc2p[:, j, w:w + 1], scalar2=mods[:, 24 + j, w:w + 1], op0=ALU.mult, op1=ALU.add)
        at, ak = a_r.next()
        for fc in range(32):
            pf, pfk = ps_f.next()
            kb.mm_group(pf[:], [(w1[:, k, fc * 128:(fc + 1) * 128], ht[:, k, :]) for k in range(8)], reads=[(hk, j) for j in range(8)] + [("w1", fc // 8)], writes=[pfk])
            rt, rk = r_r.next()
            E(S, "act", "activation", [pfk], [rk], out=rt[:], in_=pf[:], func=AF.Relu)
            E(S, "pool" if fc % 2 == 0 else "dve", "tensor_tensor", [rk], [(ak, fc)], out=at[:, fc, :], in0=rt[:], in1=rt[:], op=ALU.mult)
        ctx[b0] = dict(w=w, xt=xt, xk=xk, hk=hk, at=at, ak=ak, xkeys=[(xk, oc) for oc in range(8)])

    def stage_B1(b0):
        c = ctx[b0]
        xt, xk, at, ak, w = c["xt"], c["xk"], c["at"], c["ak"], c["w"]
        for oc in range(8):
            pm, pmk = ps_f.next()
            kb.mm_group(pm[:], [(w2[:, k, oc * 128:(oc + 1) * 128], at[:, k, :]) for k in range(32)], reads=[(ak, fc) for fc in range(32)] + w2k, writes=[pmk])
            tmp, tmpk = tmp_r.next()
            E(S, "act", "activation", [pmk, "mods"], [tmpk], out=tmp[:], in_=pm[:], func=AF.Copy, scale=mods[:, 40 + oc, w:w + 1])
            E(S, "dve", "scalar_tensor_tensor", [tmpk, xk] + [(c["hk"], j) for j in range(8)], [c["xkeys"][oc]], out=xt[:, oc, :], in0=xt[:, oc, :], scalar=float(ALPHA), in1=tmp[:],
              op0=ALU.mult, op1=ALU.add)

    def stage_B2(b0):
        c = ctx.pop(b0)
        emit_ln(kb, S, c["xt"][:], c["xkeys"], n, ones, lng, 16, cst, ps_ln, sq_r, lnt, c["xkeys"])
        kb.store(xov[:, :, b0:b0 + n], c["xt"][:], reads=c["xkeys"], final=last)

    nb = len(blocks)
    stage_A(blocks[0])
    if nb > 1:
        stage_A(blocks[1])
    for i in range(nb):
        stage_B1(blocks[i])
        if i + 2 < nb:
            stage_A(blocks[i + 2])
        stage_B2(blocks[i])


_BF = ml_dtypes.bfloat16


def _bf(a):
    a = np.asarray(a)
    if a.dtype.kind == "V":
        a = a.view(_BF)
    return a


def na_bias_table(rpb):
    a = np.arange(2)[:, None, None, None]
    kc = np.arange(64)[None, :, None, None]
    e = np.arange(24)[None, None, :, None]
    qc = np.arange(64)[None, None, None, :]
    dr = 10 + a - e + 0 * kc + 0 * qc
    dc = kc - qc + 0 * a + 0 * e
    ok = (np.abs(dr) <= 7) & (np.abs(dc) <= 15)
    dri = np.clip(dr + 7, 0, 14)
    dci = np.clip(dc + 15, 0, 30)
    out = np.zeros((2, 64, 8, 24, 64), np.float32)
    for h in range(8):
        g = rpb[h][dri, dci]
        out[:, :, h] = np.where(ok, g, np.float32(0.0))
    return np.ascontiguousarray(out.reshape(128, 8, 24, 64))


def na_mask_table(q):
    R0 = 32 * q
    m = np.zeros((128, 4, 8, 8, 64), np.float32)
    kc = np.arange(64)[:, None]
    qc = np.arange(64)[None, :]
    cs = np.clip(qc - 8, 0, 48)
    colok = (kc >= cs) & (kc < cs + 16)
    for b in range(4):
        for t in range(8):
            for a in range(2):
                krow = R0 + 8 * b - 4 + 2 * t + a
                for j in range(8):
                    qrow = R0 + 8 * b + j
                    r0 = min(max(qrow - 4, 0), 120)
                    if 0 <= krow < 128 and r0 <= krow < r0 + 8:
                        m[a * 64:(a + 1) * 64, b, t, j, :] = colok
    return np.ascontiguousarray(m.reshape(128, 32, 512)).astype(_BF)


def prep_B(inp, l, core, xT_core, A):
    b, q = core // 4, core % 4
    grp = [b * 4 + i for i in range(4)]
    me = A[core]
    rr, rm = rope_tables(q)
    Fsl = np.zeros((64, 8, 3, 128), np.float32)
    fexp = np.zeros((128, 8), np.float32)
    for j in range(3):
        if q - 1 - j >= 0:
            Fsl[:, 0:4, j, :] = np.asarray(A[grp[q - 1 - j]]["retF"])[:, 0:4, :]
        if q + 1 + j <= 3:
            Fsl[:, 4:8, j, :] = np.asarray(A[grp[q + 1 + j]]["retF"])[:, 4:8, :]
        fexp[:, j] = NT * j
        fexp[:, 4 + j] = NT * j
    fexp[:, 3] = NT * q
    fexp[:, 7] = NT * (3 - q)
    kT = _bf(me["naKT"]); V = _bf(me["naV"])
    naKT = np.zeros((128, 4, NKEY), _BF)
    naV = np.zeros((NKEY, 8, 128), _BF)
    naKT[:, :, 256:2304] = kT[:, :, 0:NT]; naV[256:2304] = V[0:NT]
    naKT[:, :, 2560:] = kT[:, :, NT:]; naV[2560:] = V[NT:]
    if q > 0:
        p = A[grp[q - 1]]
        naKT[:, :, 0:256] = _bf(p["naKT"])[:, :, NT - 256:NT]; naV[0:256] = _bf(p["naV"])[NT - 256:NT]
    if q < 3:
        p = A[grp[q + 1]]
        naKT[:, :, 2304:2560] = _bf(p["naKT"])[:, :, 0:256]; naV[2304:2560] = _bf(p["naV"])[0:256]
    ck = np.concatenate([_bf(A[g]["ckvnT"])[:, :, 0:NT] for g in grp] + [_bf(me["ckvnT"])[:, :, NT:]], axis=2)
    kr = np.concatenate([_bf(A[g]["krT"])[:, 0:NT] for g in grp] + [_bf(me["krT"])[:, NT:]], axis=1)
    lng = np.concatenate([fm(inp["ln_gain"][l, 0]), fm(inp["ln_bias"][l, 0]), fm(inp["ln_gain"][l, 1]), fm(inp["ln_bias"][l, 1])], axis=1)
    return {
        "xT": xT_core, "mods": np.asarray(me["mods"]), "hxT": _bf(me["hxT"]), "retK": _bf(me["retK"]), "retKT": _bf(me["retKT"]),
        "retV": _bf(me["retV"]), "Fsl": Fsl, "fexp": fexp, "rld": np.ascontiguousarray(inp["ret_log_decay"][l].reshape(1, 8)),
        "rcst": ret_consts(), "gng": np.ascontiguousarray(inp["ret_gn_gain"][l].reshape(1, 512)), "rope_q": rr,
        "rope_mq": (rm * np.float32(96.0 ** -0.5)).astype(np.float32),
        "naKT": naKT, "naV": naV, "nabias": na_bias_table(inp["na_rpb"][l]), "namask": na_mask_table(q),
        "ckvnT": np.ascontiguousarray(ck), "krT": np.ascontiguousarray(kr),
        "qg": np.ascontiguousarray(inp["mla_q_norm"][l].reshape(1, 256)), "w_qup": inp["mla_w_qup"][l], "w_kvup": inp["mla_w_kvup"][l],
        "w_in": inp["w_in"][l],
        "w_br": np.ascontiguousarray(np.stack([inp["w_branch_ret"][l], inp["w_branch_na"][l], inp["w_branch_mla"][l]])),
        "w_out": inp["w_out"][l], "w_ff1": inp["w_ff1"][l], "w_ff2": inp["w_ff2"][l], "lng": np.ascontiguousarray(lng),
    }


XBW = 12288


def emit_exchange(kb, st, G):
    S = kb.S
    cc = kb.sb(st, "cc", [128, 32], F32)
    kb.load(cc[:], G["cc"], writes=["cc"])
    stg1 = kb.sb(st, "xstg1", [128, 4096], BF16)
    stg2 = kb.sb(st, "xstg2", [32, 2048], BF16)
    stg3 = kb.sb(st, "xstg3", [128, 6144], BF16)
    m1 = kb.sb(st, "xm1", [128, 4, 4096], BF16)
    m2 = kb.sb(st, "xm2", [32, 4, 2048], BF16)
    m3 = kb.sb(st, "xm3", [128, 4, 6144], BF16)
    fst = kb.sb(st, "xf", [64, 1024], F32)
    fm_ = kb.sb(st, "xfm", [64, 4, 1024], F32)
    xin = [g_.rearrange("(s p) c -> p s c", p=128) for g_ in G["XB_in"]]
    nvv = G["naV"].rearrange("(u p) h f -> p u (h f)", p=128)
    kb.load(fst[:], G["retF"].rearrange("p i v -> p (i v)"), writes=["xf"])
    kb.load(stg1[:].rearrange("p (k t) -> p k t", k=2), G["ckvnT"][:, :, 0:NT], writes=["xstg1"])
    kb.load(stg2[:], G["krT"][:, 0:NT], writes=["xstg2"])
    kb.load(stg3[:, 0:1024].rearrange("p (a t) -> p a t", a=4), G["naKT"][:, :, 0:256], writes=[("xstg3", 0)])
    kb.load(stg3[:, 1024:2048].rearrange("p (a t) -> p a t", a=4), G["naKT"][:, :, NT - 256:NT], writes=[("xstg3", 1)])
    kb.load(stg3[:, 2048:4096].rearrange("p (u f) -> p u f", u=2), nvv[:, 0:2, :], writes=[("xstg3", 2)])
    kb.load(stg3[:, 4096:6144].rearrange("p (u f) -> p u f", u=2), nvv[:, 14:16, :], writes=[("xstg3", 3)])
    rg = [[0, 1, 2, 3], [4, 5, 6, 7]]
    for s_ in range(4):
        E(S, "dve", "tensor_scalar_mul", ["xf", "cc"], [("xfm", s_)], out=fm_[:, s_, :], in0=fst[:], scalar1=cc[0:64, s_:s_ + 1])
    kb.store(G["FB_in"].rearrange("(s p) c -> p s c", p=64), fm_[:], reads=[("xfm", s_) for s_ in range(4)], writes=["FB_in"])
    fi, fo = G["FB_in"], G["FB_out"]
    S.coll(lambda h: h.collective_compute("AllReduce", ALU.add, replica_groups=rg, ins=[fi.opt()], outs=[fo.opt()]),
           reads=["FB_in"], writes=["FB_out"])
    for s_ in range(4):
        E(S, "dve", "tensor_scalar_mul", [("xstg3", j) for j in range(4)] + ["cc"], [("xm3", s_)], out=m3[:, s_, :], in0=stg3[:],
          scalar1=cc[:, s_:s_ + 1])
    for s_ in range(4):
        E(S, "dve", "tensor_scalar_mul", ["xstg2", "cc"], [("xm2", s_)], out=m2[:, s_, :], in0=stg2[:], scalar1=cc[0:32, s_:s_ + 1])
    kb.store(xin[1][0:32, :, 0:2048], m2[:], reads=[("xm2", s_) for s_ in range(4)], writes=[("XB_in", 1, "a")])
    kb.store(xin[1][:, :, 2048:4096], m3[:, :, 0:2048], reads=[("xm3", s_) for s_ in range(4)], writes=[("XB_in", 1, "b")], q="act")
    kb.store(xin[2][:, :, 0:4096], m3[:, :, 2048:6144], reads=[("xm3", s_) for s_ in range(4)], writes=[("XB_in", 2)])
    for s_ in range(4):
        E(S, "dve", "tensor_scalar_mul", ["xstg1", "cc"], [("xm1", s_)], out=m1[:, s_, :], in0=stg1[:], scalar1=cc[:, s_:s_ + 1])
    kb.store(xin[0][:, :, 0:4096], m1[:], reads=[("xm1", s_) for s_ in range(4)], writes=[("XB_in", 0)], q="act")
    rk = {0: [("XB_in", 0)], 1: [("XB_in", 1, "a"), ("XB_in", 1, "b"), "XB_in"], 2: [("XB_in", 2)]}
    for j in (1, 2, 0):
        xi, xo = G["XB_in"][j], G["XB_out"][j]
        S.coll(lambda h, xi=xi, xo=xo: h.collective_compute("AllReduce", ALU.add, replica_groups=rg, ins=[xi.opt()], outs=[xo.opt()]),
               reads=rk[j], writes=[("XB_out", j)])


def build_fused(n_layers=4):
    nc = bass.Bass("TRN2", target_bir_lowering=False)
    din = lambda name, shape, dt=F32: nc.dram_tensor(name, shape, dt, kind="ExternalInput").ap()
    dsc = lambda name, shape, dt=F32: nc.dram_tensor(name, shape, dt).ap()
    L = n_layers
    X = dict(
        xT=din("xT", [D, T]), c2=din("c2", [128, 16]), w_ada=din("w_ada", [L, D, 6 * D]), b_ada=din("b_ada", [L, 128, 48]),
        w_in=din("w_in", [L, D, 7200]), rld=din("rld", [L, 1, 8]), rcst=din("rcst", [128, 776]), kvg=din("kvg", [L, 1, 256]),
        rope_rk=din("rope_rk", [T, 64]), rope_m=din("rope_m", [T, 32]), rope_q=din("rope_q", [T, 64]), rope_mq=din("rope_mq", [T, 32]),
        gng=din("gng", [L, 1, 512]), nabias=din("nabias", [L, 128, 8, 24, 64]), namask=din("namask", [128, 32, 512], BF16),
        qg=din("qg", [L, 1, 256]), w_qup=din("w_qup", [L, 256, 768]), w_kvup=din("w_kvup", [L, 256, 1024]),
        w_br=din("w_br", [L, 3, 512, D]), w_out=din("w_out", [L, D, D]), w_ff1=din("w_ff1", [L, D, 4 * D]), w_ff2=din("w_ff2", [L, 4 * D, D]),
        lng=din("lng", [L, 128, 32]), cc=din("cc", [128, 32]),
    )
    xoT = nc.dram_tensor("xoT", [D, T], F32, kind="ExternalOutput").ap()
    G = dict(
        mods=dsc("mods_d", [128, 96]), hxT=dsc("hxT_d", [128, 8, T], BF16), retK=dsc("retK_d", [T, 256], BF16),
        retKT=dsc("retKT_d", [64, 4, T], BF16), retV=dsc("retV_d", [T, 512], BF16), retF=dsc("retF_d", [64, 8, 128]),
        naKT=dsc("naKT_d", [128, 4, T], BF16), naV=dsc("naV_d", [T, 8, 128], BF16), ckvnT=dsc("ckvnT_d", [128, 2, T], BF16),
        krT=dsc("krT_d", [32, T], BF16), XB_in=[dsc(f"XB_in{j}", [512, 4096], BF16) for j in range(3)],
        XB_out=[dsc(f"XB_out{j}", [512, 4096], BF16) for j in range(3)],
        FB_in=dsc("FB_in", [256, 1024]), FB_out=dsc("FB_out", [256, 1024]),
        yaT=dsc("yaT_d", [128, 4, T], BF16), ybT=dsc("ybT_d", [128, 4, T], BF16), ycT=dsc("ycT_d", [128, 4, T], BF16),
        x1T=dsc("x1T_d", [128, 8, T]), cc=X["cc"], modsN=dsc("modsN_d", [128, 96]),
    )
    xbuf = [dsc("xping", [D, T]), dsc("xpong", [D, T])]
    with ExitStack() as st0:
        S = Sched(nc, st0)
        kb = KB(nc, S)
        with ExitStack() as ph:
            zt = kb.sb(ph, "zt", [128, 4, 2048], BF16)
            E(S, "pool", "memset", [], ["zt"], zt[:], 0.0)
            kb.store(G["XB_in"][1].rearrange("(s p) c -> p s c", p=128)[:, :, 0:2048], zt[:], reads=["zt"], writes=["XB_in"])
            S.barrier(); S.emit()
        for l in range(L):
            x_in = X["xT"] if l == 0 else xbuf[(l - 1) % 2]
            x_out = xoT if l == L - 1 else xbuf[l % 2]
            with ExitStack() as st:
                hxT = kb.sb(st, "hxT", [128, 8, T], BF16)
                mods = kb.sb(st, "mods", [128, 48, 2], F32)
                ident, _ = make_ident(kb, st)
                with ExitStack() as p1:
                    emit_mod_hx(kb, p1, x_in, X["c2"], X["w_ada"][l], X["b_ada"][l], mods, hxT,
                                mods_src=(G["modsN"] if (l > 0 and MOD_PREFETCH) else None))
                    kb.store(G["mods"], mods[:].rearrange("p a b -> p (a b)"), reads=["mods"])
                    kb.store(G["hxT"], hxT[:], reads=[("hxT", j) for j in range(8)])
                    S.barrier(); S.emit()
                with ExitStack() as p2:
                    emit_kv_side(kb, p2, hxT, ident, X["w_in"][l], X["rld"][l], X["rcst"], X["kvg"][l], X["rope_rk"], X["rope_m"],
                                 G["retK"], G["retKT"], G["retV"], G["retF"], G["naKT"], G["naV"], G["ckvnT"], G["krT"])
                    S.barrier(); S.emit()
            with ExitStack() as ph:
                emit_exchange(kb, ph, G)
                S.bar_coll = False
                S.barrier(); S.emit()
            I = dict(G)
            I.update(xT=x_in, rld=X["rld"][l], rcst=X["rcst"], gng=X["gng"][l], rope_q=X["rope_q"], rope_mq=X["rope_mq"],
                     nabias=X["nabias"][l], namask=X["namask"], qg=X["qg"][l], w_qup=X["w_qup"][l], w_kvup=X["w_kvup"][l],
                     w_in=X["w_in"][l], w_br=X["w_br"][l], w_out=X["w_out"][l], w_ff1=X["w_ff1"][l], w_ff2=X["w_ff2"][l], lng=X["lng"][l])
            dbg = dict(yaT=G["yaT"], ybT=G["ybT"], ycT=G["ycT"], x1T=G["x1T"])
            with ExitStack() as ph:
                emit_retention(kb, ph, I, dbg["yaT"])
                S.barrier(); S.emit()
            mpre = None
            if MOD_PREFETCH and l + 1 < L:
                mpre = (lambda kb_, st_, l=l: ModPrefetch(kb_, st_, X["c2"], X["w_ada"][l + 1], X["b_ada"][l + 1], G["modsN"]))
            with ExitStack() as ph:
                emit_na(kb, ph, I, dbg["ybT"], modpre=mpre)
                S.barrier(); S.emit()
            with ExitStack() as ph:
                emit_mla(kb, ph, I, dbg["ycT"], modpre=None)
                S.barrier(); S.emit()
            with ExitStack() as ph:
                emit_merge(kb, ph, I, dbg)
                S.barrier(); S.emit()
            with ExitStack() as ph:
                emit_ffn(kb, ph, I, dbg["x1T"], x_out, last=(l == L - 1))
                S.barrier(); S.emit(final=(l == L - 1))
    return nc


def core_consts(q):
    cc = np.zeros((128, 32), np.float32)
    cc[:, q] = 1.0
    if q > 0:
        cc[:, 4 + q - 1] = 1.0
    if q < 3:
        cc[:, 8 + q + 1] = 1.0
    for s_ in range(4):
        if s_ < q:
            cc[:, 12 + s_] = NT * (q - 1 - s_)
            cc[:, 22 + s_] = 1.0
        if s_ > q:
            cc[:, 17 + s_] = NT * (s_ - q - 1)
            cc[:, 26 + s_] = 1.0
    cc[:, 16] = NT * q
    cc[:, 21] = NT * (3 - q)
    return cc


def prep_fused(inp, core, L=4):
    b, q = core // 4, core % 4
    rr, rm = rope_tables(q)
    xc = np.concatenate([inp["x"][b, q * NT:(q + 1) * NT], inp["ctx"][b]], axis=0)
    c2 = np.stack([inp["c"][b], inp["c_ctx"]], axis=-1)
    c2 = np.ascontiguousarray(c2.reshape(8, 128, 2).transpose(1, 0, 2).reshape(128, 16))
    lng = np.stack([np.concatenate([fm(inp["ln_gain"][l, 0]), fm(inp["ln_bias"][l, 0]), fm(inp["ln_gain"][l, 1]), fm(inp["ln_bias"][l, 1])], axis=1)
                    for l in range(L)])
    return {
        "xT": np.ascontiguousarray(xc.T), "c2": c2, "w_ada": inp["w_ada"][:L], "b_ada": np.stack([fm(inp["b_ada"][l]) for l in range(L)]),
        "w_in": inp["w_in"][:L], "rld": np.ascontiguousarray(inp["ret_log_decay"][:L].reshape(L, 1, 8)), "rcst": ret_consts(),
        "kvg": np.ascontiguousarray(inp["mla_kv_norm"][:L].reshape(L, 1, 256)),
        "rope_rk": (rr * np.float32(0.125)).astype(np.float32), "rope_m": rm, "rope_q": rr,
        "rope_mq": (rm * np.float32(96.0 ** -0.5)).astype(np.float32),
        "gng": np.ascontiguousarray(inp["ret_gn_gain"][:L].reshape(L, 1, 512)),
        "nabias": np.stack([na_bias_table(inp["na_rpb"][l]) for l in range(L)]), "namask": na_mask_table(q),
        "qg": np.ascontiguousarray(inp["mla_q_norm"][:L].reshape(L, 1, 256)), "w_qup": inp["mla_w_qup"][:L], "w_kvup": inp["mla_w_kvup"][:L],
        "w_br": np.ascontiguousarray(np.stack([inp["w_branch_ret"][:L], inp["w_branch_na"][:L], inp["w_branch_mla"][:L]], axis=1)),
        "w_out": inp["w_out"][:L], "w_ff1": inp["w_ff1"][:L], "w_ff2": inp["w_ff2"][:L], "lng": np.ascontiguousarray(lng),
        "cc": core_consts(q),
    }


_NC = {}


def kernel(**inp):
    inp = {k: np.asarray(v) for k, v in inp.items()}
    if "F" not in _NC:
        _NC["F"] = build_fused(4)
    res = run_bass_kernel_spmd(_NC["F"], [prep_fused(inp, c) for c in range(8)], core_ids=list(range(8))).results
    out = np.zeros((2, 8192, D), np.float32)
    for core in range(8):
        b, q = core // 4, core % 4
        out[b, q * NT:(q + 1) * NT] = np.asarray(res[core]["xoT"])[:, 0:NT].T
    return out
```
